# Optimizing a Trainium2 kernel written in Bass

```python
import math
import jax
import jax.numpy as jnp
from jax import lax
import numpy as np

D_MODEL = 1024
BATCH = 4
SEQ = 8192
DEPTH = 4
DEC_BATCH = 16
DEC_SEQ = 64
PAST_LEN = 4096

CHUNK = 64
N_MIXERS = 4
Q_BLOCK = 128
NORM_EPS = 1e-6
ROPE_THETA = 10000.0
NEG_INF = -1e30
D_FF = 4 * D_MODEL

S5_GROUP = 16
S5_GROUPS = D_MODEL // S5_GROUP
S5_STATE = 64
S5_SCAN_BLOCK = 128
S5_DT_MIN = 1e-3
S5_DT_MAX = 1e-1

DIFF_HEADS = 8
DIFF_DH = D_MODEL // (2 * DIFF_HEADS)
DIFF_LAYER = 1
LAMBDA_INIT = 0.8 - 0.6 * math.exp(-0.3 * DIFF_LAYER)

MLA_HEADS = 16
MLA_Q_RANK = 256
MLA_KV_RANK = 128
MLA_NOPE = 64
MLA_ROPE = 32
MLA_V = 64
MLA_SCALE = (MLA_NOPE + MLA_ROPE) ** -0.5

SGU_CHUNK = 128
SGU_WIDTH = 2 * D_MODEL
SGU_GROUPS = 8
SGU_GDIM = SGU_WIDTH // SGU_GROUPS

kernel_name = 'hybrid_streaming_encoder_step'


def rmsnorm(x, g):
    xf = x.astype(jnp.float32)
    y = xf * lax.rsqrt(jnp.mean(xf * xf, axis=-1, keepdims=True) + NORM_EPS)
    return (y * g.astype(jnp.float32)).astype(x.dtype)


def modulate(x, g, shift, scale):
    return rmsnorm(x, g) * (1.0 + scale[:, None, :]) + shift[:, None, :]


def rope(x, pos):
    half = x.shape[-1] // 2
    inv_freq = ROPE_THETA ** (-jnp.arange(half, dtype=jnp.float32) / half)
    ang = pos.astype(jnp.float32)[:, None] * inv_freq[None, :]
    cos = jnp.cos(ang)[:, None, :]
    sin = jnp.sin(ang)[:, None, :]
    xf = x.astype(jnp.float32)
    x1, x2 = xf[..., :half], xf[..., half:]
    return jnp.concatenate([x1 * cos - x2 * sin, x1 * sin + x2 * cos], axis=-1).astype(x.dtype)


def block_causal_sweep(fn, q_parts, seq_len):
    nb = seq_len // Q_BLOCK
    key_chunk = jnp.arange(seq_len) // CHUNK

    def to_blocks(a):
        return jnp.moveaxis(a.reshape((a.shape[0], nb, Q_BLOCK) + a.shape[2:]), 1, 0)

    def body(args):
        j, qs = args
        q_chunk = (j * Q_BLOCK + jnp.arange(Q_BLOCK)) // CHUNK
        mask = key_chunk[None, :] <= q_chunk[:, None]
        return fn(qs, mask)

    out = lax.map(body, (jnp.arange(nb), tuple(to_blocks(a) for a in q_parts)))
    out = jnp.moveaxis(out, 0, 1)
    return out.reshape((out.shape[0], seq_len) + out.shape[3:])


def _lin_combine(e1, e2):
    a1, b1 = e1
    a2, b2 = e2
    return a1 * a2, a2 * b1 + b2


def s5_mixer(h, h0_re, h0_im, a_re, a_im, b_re, b_im, c_re, c_im, d, log_dt, w_glu_a, w_glu_b):
    B, L, _ = h.shape
    f32 = jnp.float32
    lam = lax.complex(a_re.astype(f32), a_im.astype(f32))
    dt = jnp.exp(log_dt.astype(f32))[:, None]
    a_bar = jnp.exp(lam * dt)
    b_bar = ((a_bar - 1.0) / lam)[..., None] * lax.complex(b_re.astype(f32), b_im.astype(f32))
    c_mat = lax.complex(c_re.astype(f32), c_im.astype(f32))
    u = h.astype(f32).reshape(B, L, S5_GROUPS, S5_GROUP)
    if h0_re is None:
        h0 = jnp.zeros((B, S5_GROUPS, S5_STATE), jnp.complex64)
    else:
        h0 = lax.complex(h0_re.astype(f32), h0_im.astype(f32))
    T = S5_SCAN_BLOCK if L % S5_SCAN_BLOCK == 0 else L
    ub = jnp.moveaxis(u.reshape(B, L // T, T, S5_GROUPS, S5_GROUP), 1, 0)

    def block(hc, u_blk):
        bu = jnp.einsum('gpc,btgc->btgp', b_bar, u_blk.astype(jnp.complex64))
        bu = bu.at[:, 0].add(a_bar * hc)
        a = jnp.broadcast_to(a_bar, bu.shape)
        _, hs = lax.associative_scan(_lin_combine, (a, bu), axis=1)
        y = jnp.real(jnp.einsum('gcp,btgp->btgc', c_mat, hs))
        return hs[:, -1], y

    h_last, ys = lax.scan(block, h0, ub)
    y = jnp.moveaxis(ys, 0, 1).reshape(B, L, D_MODEL) + d.astype(f32) * h.astype(f32)
    z = jax.nn.gelu(y).astype(h.dtype)
    out = (z @ w_glu_a) * jax.nn.sigmoid(z @ w_glu_b)
    return out, jnp.real(h_last).astype(h.dtype), jnp.imag(h_last).astype(h.dtype)


def diff_core(q, k, v, lam, mask):
    s = jnp.einsum('bqhcd,bkhcd->bhcqk', q, k, preferred_element_type=jnp.float32) * (DIFF_DH ** -0.5)
    if mask is not None:
        s = jnp.where(mask, s, NEG_INF)
    p = jax.nn.softmax(s, axis=-1)
    a = p[:, :, 0] - lam * p[:, :, 1]
    return jnp.einsum('bhqk,bkhe->bqhe', a, v).astype(v.dtype)


def diff_attn_mixer(h, pos, cache_k, cache_v, w_qkv, lq1, lk1, lq2, lk2, g_sub, w_o):
    B, L, _ = h.shape
    q, k, v = jnp.split(h @ w_qkv, 3, axis=-1)
    q = rope(q.reshape(B, L, 2 * DIFF_HEADS, DIFF_DH), pos).reshape(B, L, DIFF_HEADS, 2, DIFF_DH)
    k = rope(k.reshape(B, L, 2 * DIFF_HEADS, DIFF_DH), pos).reshape(B, L, DIFF_HEADS, 2, DIFF_DH)
    v = v.reshape(B, L, DIFF_HEADS, 2 * DIFF_DH)
    f32 = jnp.float32
    lam = (jnp.exp(jnp.sum(lq1.astype(f32) * lk1.astype(f32)))
           - jnp.exp(jnp.sum(lq2.astype(f32) * lk2.astype(f32))) + LAMBDA_INIT)
    if cache_k is None:
        o = block_causal_sweep(lambda qs, m: diff_core(qs[0], k, v, lam, m), (q,), L)
    else:
        P = cache_k.shape[1]
        k_all = jnp.concatenate([cache_k.reshape(B, P, DIFF_HEADS, 2, DIFF_DH), k], axis=1)
        v_all = jnp.concatenate([cache_v, v], axis=1)
        o = diff_core(q, k_all, v_all, lam, None)
    o = rmsnorm(o, g_sub) * (1.0 - LAMBDA_INIT)
    out = o.reshape(B, L, D_MODEL) @ w_o
    return out, k.reshape(B, L, DIFF_HEADS, 2 * DIFF_DH), v


def mla_core(q_nope, q_rope, k_nope, k_rope, v, mask):
    s = (jnp.einsum('bqhn,bkhn->bhqk', q_nope, k_nope, preferred_element_type=jnp.float32)
         + jnp.einsum('bqhe,bke->bhqk', q_rope, k_rope, preferred_element_type=jnp.float32)) * MLA_SCALE
    if mask is not None:
        s = jnp.where(mask, s, NEG_INF)
    p = jax.nn.softmax(s, axis=-1)
    return jnp.einsum('bhqk,bkhv->bqhv', p, v).astype(v.dtype)


def mla_mixer(h, pos, cache_ckv, cache_krope, w_dq, g_q, w_uq, w_dkv, g_kv, w_uk, w_uv, w_o):
    B, L, _ = h.shape
    q = (rmsnorm(h @ w_dq, g_q) @ w_uq).reshape(B, L, MLA_HEADS, MLA_NOPE + MLA_ROPE)
    q_nope = q[..., :MLA_NOPE]
    q_rope = rope(q[..., MLA_NOPE:], pos)
    kv = h @ w_dkv
    ckv = rmsnorm(kv[..., :MLA_KV_RANK], g_kv)
    krope = rope(kv[..., None, MLA_KV_RANK:], pos)[:, :, 0]
    if cache_ckv is None:
        ckv_all, krope_all = ckv, krope
    else:
        ckv_all = jnp.concatenate([cache_ckv, ckv], axis=1)
        krope_all = jnp.concatenate([cache_krope, krope], axis=1)
    k_nope = jnp.einsum('bkr,rhn->bkhn', ckv_all, w_uk)
    v = jnp.einsum('bkr,rhv->bkhv', ckv_all, w_uv)
    if cache_ckv is None:
        o = block_causal_sweep(lambda qs, m: mla_core(qs[0], qs[1], k_nope, krope_all, v, m),
                               (q_nope, q_rope), L)
    else:
        o = mla_core(q_nope, q_rope, k_nope, krope_all, v, None)
    out = o.reshape(B, L, MLA_HEADS * MLA_V) @ w_o
    return out, ckv, krope


def sgu_mixer(h, w_in, g_v, w_s, b_s, w_out):
    B, L, _ = h.shape
    u, v = jnp.split(jax.nn.gelu(h @ w_in), 2, axis=-1)
    v = rmsnorm(v, g_v)
    T = min(L, SGU_CHUNK)
    vb = v.reshape(B, L // T, T, SGU_GROUPS, SGU_GDIM)
    w = jnp.tril(w_s[:, :T, :T])
    sv = jnp.einsum('gts,bnsgd->bntgd', w, vb) + b_s[:, :T].T[:, :, None]
    out = (u * sv.reshape(B, L, SGU_WIDTH)) @ w_out
    return out, v


def sq_relu_mlp(h, w_up, w_down):
    return jnp.square(jax.nn.relu(h @ w_up)) @ w_down


def run_trunk(x, c, pos, past, p):
    new = {}
    cond = jax.nn.silu(c)
    for i in range(DEPTH):
        mod = cond @ p['w_ada'][i] + p['b_ada'][i]
        sh1, sc1, gt1, sh2, sc2, gt2 = jnp.split(mod, 6, axis=-1)
        h = modulate(x, p['g_mix'][i], sh1, sc1)
        kind = i % N_MIXERS
        if kind == 0:
            h0_re = None if past is None else past['s5_re']
            h0_im = None if past is None else past['s5_im']
            out, new['s5_re'], new['s5_im'] = s5_mixer(
                h, h0_re, h0_im, p['s5_a_re'], p['s5_a_im'], p['s5_b_re'], p['s5_b_im'],
                p['s5_c_re'], p['s5_c_im'], p['s5_d'], p['s5_log_dt'], p['s5_w_glu_a'], p['s5_w_glu_b'])
        elif kind == 1:
            ck = None if past is None else past['diff_k']
            cv = None if past is None else past['diff_v']
            out, new['diff_k'], new['diff_v'] = diff_attn_mixer(
                h, pos, ck, cv, p['diff_w_qkv'], p['diff_lambda_q1'], p['diff_lambda_k1'],
                p['diff_lambda_q2'], p['diff_lambda_k2'], p['diff_g_sub'], p['diff_w_o'])
        elif kind == 2:
            cc = None if past is None else past['mla_ckv']
            cr = None if past is None else past['mla_krope']
            out, new['mla_ckv'], new['mla_krope'] = mla_mixer(
                h, pos, cc, cr, p['mla_w_dq'], p['mla_g_q'], p['mla_w_uq'], p['mla_w_dkv'],
                p['mla_g_kv'], p['mla_w_uk'], p['mla_w_uv'], p['mla_w_o'])
        else:
            out, new['sgu_v'] = sgu_mixer(h, p['sgu_w_in'], p['sgu_g_v'], p['sgu_w_s'],
                                          p['sgu_b_s'], p['sgu_w_out'])
        x = x + (1.0 + gt1)[:, None, :] * out
        h = modulate(x, p['g_ffn'][i], sh2, sc2)
        x = x + (1.0 + gt2)[:, None, :] * sq_relu_mlp(h, p['w_up'][i], p['w_down'][i])
    return rmsnorm(x, p['g_final']), new


def setup_inputs(seed: int = 0) -> dict:
    key = jax.random.key(seed)
    keys = list(jax.random.split(key, 64))
    f32 = jnp.float32
    counter = [0]

    def nxt():
        k = keys[counter[0]]
        counter[0] += 1
        return k

    def nrm(shape, scale=1.0):
        return jax.random.normal(nxt(), shape, f32) * scale

    D = D_MODEL
    G, P, C = S5_GROUPS, S5_STATE, S5_GROUP
    n_idx = jnp.arange(P, dtype=f32)[None, :]
    out = {}
    out['x_prompt'] = nrm((BATCH, SEQ, D))
    out['x_sample'] = nrm((DEC_BATCH, DEC_SEQ, D))
    out['c_prompt'] = nrm((BATCH, D))
    out['c_sample'] = nrm((DEC_BATCH, D))
    out['state_s5_re'] = nrm((DEC_BATCH, G, P), 0.3)
    out['state_s5_im'] = nrm((DEC_BATCH, G, P), 0.3)
    out['cache_diff_k'] = nrm((DEC_BATCH, PAST_LEN, DIFF_HEADS, 2 * DIFF_DH))
    out['cache_diff_v'] = nrm((DEC_BATCH, PAST_LEN, DIFF_HEADS, 2 * DIFF_DH))
    out['cache_mla_ckv'] = nrm((DEC_BATCH, PAST_LEN, MLA_KV_RANK))
    out['cache_mla_krope'] = nrm((DEC_BATCH, PAST_LEN, MLA_ROPE))
    out['w_ada'] = nrm((DEPTH, D, 6 * D), 0.1 * D ** -0.5)
    out['b_ada'] = nrm((DEPTH, 6 * D), 0.01)
    out['g_mix'] = 1.0 + nrm((DEPTH, D), 0.05)
    out['g_ffn'] = 1.0 + nrm((DEPTH, D), 0.05)
    out['w_up'] = nrm((DEPTH, D, D_FF), D ** -0.5)
    out['w_down'] = nrm((DEPTH, D_FF, D), D_FF ** -0.5)
    out['g_final'] = 1.0 + nrm((D,), 0.05)
    out['s5_a_re'] = -0.5 + nrm((G, P), 0.01)
    out['s5_a_im'] = math.pi * n_idx + nrm((G, P), 0.01)
    out['s5_b_re'] = nrm((G, P, C), (2 * C) ** -0.5)
    out['s5_b_im'] = nrm((G, P, C), (2 * C) ** -0.5)
    out['s5_c_re'] = nrm((G, C, P), (2 * P) ** -0.5)
    out['s5_c_im'] = nrm((G, C, P), (2 * P) ** -0.5)
    out['s5_d'] = nrm((D,))
    out['s5_log_dt'] = jax.random.uniform(nxt(), (G,), f32, math.log(S5_DT_MIN), math.log(S5_DT_MAX))
    out['s5_w_glu_a'] = nrm((D, D), D ** -0.5)
    out['s5_w_glu_b'] = nrm((D, D), D ** -0.5)
    out['diff_w_qkv'] = nrm((D, 3 * D), D ** -0.5)
    out['diff_lambda_q1'] = nrm((DIFF_DH,), 0.1)
    out['diff_lambda_k1'] = nrm((DIFF_DH,), 0.1)
    out['diff_lambda_q2'] = nrm((DIFF_DH,), 0.1)
    out['diff_lambda_k2'] = nrm((DIFF_DH,), 0.1)
    out['diff_g_sub'] = 1.0 + nrm((2 * DIFF_DH,), 0.05)
    out['diff_w_o'] = nrm((D, D), D ** -0.5)
    out['mla_w_dq'] = nrm((D, MLA_Q_RANK), D ** -0.5)
    out['mla_g_q'] = 1.0 + nrm((MLA_Q_RANK,), 0.05)
    out['mla_w_uq'] = nrm((MLA_Q_RANK, MLA_HEADS * (MLA_NOPE + MLA_ROPE)), MLA_Q_RANK ** -0.5)
    out['mla_w_dkv'] = nrm((D, MLA_KV_RANK + MLA_ROPE), D ** -0.5)
    out['mla_g_kv'] = 1.0 + nrm((MLA_KV_RANK,), 0.05)
    out['mla_w_uk'] = nrm((MLA_KV_RANK, MLA_HEADS, MLA_NOPE), MLA_KV_RANK ** -0.5)
    out['mla_w_uv'] = nrm((MLA_KV_RANK, MLA_HEADS, MLA_V), MLA_KV_RANK ** -0.5)
    out['mla_w_o'] = nrm((MLA_HEADS * MLA_V, D), (MLA_HEADS * MLA_V) ** -0.5)
    out['sgu_w_in'] = nrm((D, 2 * SGU_WIDTH), D ** -0.5)
    out['sgu_g_v'] = 1.0 + nrm((SGU_WIDTH,), 0.05)
    out['sgu_w_s'] = nrm((SGU_GROUPS, SGU_CHUNK, SGU_CHUNK), SGU_CHUNK ** -0.5)
    out['sgu_b_s'] = 1.0 + nrm((SGU_GROUPS, SGU_CHUNK), 0.05)
    out['sgu_w_out'] = nrm((SGU_WIDTH, D), SGU_WIDTH ** -0.5)
    return out


def reference(x_prompt, x_sample, c_prompt, c_sample, state_s5_re, state_s5_im,
              cache_diff_k, cache_diff_v, cache_mla_ckv, cache_mla_krope,
              w_ada, b_ada, g_mix, g_ffn, w_up, w_down, g_final,
              s5_a_re, s5_a_im, s5_b_re, s5_b_im, s5_c_re, s5_c_im, s5_d, s5_log_dt,
              s5_w_glu_a, s5_w_glu_b,
              diff_w_qkv, diff_lambda_q1, diff_lambda_k1, diff_lambda_q2, diff_lambda_k2,
              diff_g_sub, diff_w_o,
              mla_w_dq, mla_g_q, mla_w_uq, mla_w_dkv, mla_g_kv, mla_w_uk, mla_w_uv, mla_w_o,
              sgu_w_in, sgu_g_v, sgu_w_s, sgu_b_s, sgu_w_out):
    params = {
        'w_ada': w_ada, 'b_ada': b_ada, 'g_mix': g_mix, 'g_ffn': g_ffn,
        'w_up': w_up, 'w_down': w_down, 'g_final': g_final,
        's5_a_re': s5_a_re, 's5_a_im': s5_a_im, 's5_b_re': s5_b_re, 's5_b_im': s5_b_im,
        's5_c_re': s5_c_re, 's5_c_im': s5_c_im, 's5_d': s5_d, 's5_log_dt': s5_log_dt,
        's5_w_glu_a': s5_w_glu_a, 's5_w_glu_b': s5_w_glu_b,
        'diff_w_qkv': diff_w_qkv, 'diff_lambda_q1': diff_lambda_q1, 'diff_lambda_k1': diff_lambda_k1,
        'diff_lambda_q2': diff_lambda_q2, 'diff_lambda_k2': diff_lambda_k2,
        'diff_g_sub': diff_g_sub, 'diff_w_o': diff_w_o,
        'mla_w_dq': mla_w_dq, 'mla_g_q': mla_g_q, 'mla_w_uq': mla_w_uq, 'mla_w_dkv': mla_w_dkv,
        'mla_g_kv': mla_g_kv, 'mla_w_uk': mla_w_uk, 'mla_w_uv': mla_w_uv, 'mla_w_o': mla_w_o,
        'sgu_w_in': sgu_w_in, 'sgu_g_v': sgu_g_v, 'sgu_w_s': sgu_w_s, 'sgu_b_s': sgu_b_s,
        'sgu_w_out': sgu_w_out,
    }
    past = {
        's5_re': state_s5_re, 's5_im': state_s5_im,
        'diff_k': cache_diff_k, 'diff_v': cache_diff_v,
        'mla_ckv': cache_mla_ckv, 'mla_krope': cache_mla_krope,
    }
    pos_p = jnp.arange(x_prompt.shape[1], dtype=jnp.int32)
    pos_s = cache_diff_k.shape[1] + jnp.arange(x_sample.shape[1], dtype=jnp.int32)
    y_prompt, sp = run_trunk(x_prompt, c_prompt, pos_p, None, params)
    y_sample, ss = run_trunk(x_sample, c_sample, pos_s, past, params)
    return (y_prompt, y_sample,
            sp['s5_re'], sp['s5_im'], ss['s5_re'], ss['s5_im'],
            sp['diff_k'], sp['diff_v'], ss['diff_k'], ss['diff_v'],
            sp['mla_ckv'], sp['mla_krope'], ss['mla_ckv'], ss['mla_krope'],
            ss['sgu_v'])
```

```python
import math
from contextlib import ExitStack

import numpy as np
import concourse.bass as bass
import concourse.mybir as mybir
from concourse.bass_utils import run_bass_kernel_spmd

F32 = mybir.dt.float32
BF16 = mybir.dt.bfloat16
I32 = mybir.dt.int32
AF = mybir.ActivationFunctionType
ALU = mybir.AluOpType

D = 1024
SEQ = 8192
NPT = SEQ // 128
NT = NPT + 1
NTOK = NT * 128
PAST = 4096
EPS = 1e-6
LAMBDA_INIT = 0.8 - 0.6 * math.exp(-0.3 * 1)
TWO_PI = 2.0 * math.pi
CW1 = 6.28125
CW2 = float(np.float32(TWO_PI - CW1))
CW3 = float(TWO_PI - CW1 - CW2)


class Buf:
    def __init__(self, ap, name=""):
        self.ap = ap
        self.name = name
        self.w = []
        self.r = {}
        self.pr = {}
        self.is_sb = False
        self.is_psum = False
        self.dsem = None


class Prog:
    ENGS = ["pe", "act", "dve", "pool", "sp"]

    def __init__(self, nc, es):
        self.nc = nc
        self.es = es
        self.sem = {e: es.enter_context(nc.semaphore("s_" + e)) for e in self.ENGS}
        self.cnt = {e: 0 for e in self.ENGS}
        self.waited = {e: {} for e in self.ENGS}
        self.stream = {e: [] for e in self.ENGS}
        self.dsems = []
        self.n_inst = 0
        self.n_dsem = 0
        self.free_dsems = []

    def take_dsem(self, key, stage, sw=False):
        if sw:
            return self.mk_dsem(key)
        if self.free_dsems:
            d = self.free_dsems.pop()
        else:
            d = self.mk_dsem(key)
        if stage is not None:
            stage.dsems.append(d)
        return d

    def no_dsem(self, key=None):
        return None

    def mk_dsem(self, key):
        self.n_dsem += 1
        s = self.es.enter_context(self.nc.semaphore("d_%s_%d" % (key, self.n_dsem)))
        d = {"sem": s, "cnt": 0, "key": "d_%s_%d" % (key, self.n_dsem)}
        self.dsems.append(d)
        return d

    def _need(self, e, reads, writes, pwrites):
        need = {}

        def add(dep):
            key, val, semobj = dep
            if key == e and e == "pe":
                return
            cur = need.get(key)
            if cur is None or cur[0] < val:
                need[key] = (val, semobj)

        for b in reads:
            for d, _ in b.w:
                add(d)
            if b.is_psum:
                for k, d in b.r.items():
                    if k != e:
                        add(d)
        for b in writes:
            for d, _ in b.w:
                add(d)
            for d in b.r.values():
                add(d)
        for b in pwrites:
            for d, part in b.w:
                if not part:
                    add(d)
            for d in b.r.values():
                add(d)
            for d in b.pr.values():
                add(d)
        out = []
        for key, (val, semobj) in need.items():
            if self.waited[e].get(key, 0) >= val:
                continue
            self.waited[e][key] = val
            out.append((semobj, val))
        return out

    def _commit(self, key, dep, reads, writes, pwrites):
        for b in reads:
            b.r[key] = dep
        for b in writes:
            b.w = [(dep, False)]
            b.r = {}
            b.pr = {}
        for b in pwrites:
            if b.r:
                b.pr = b.r
                b.w = [(dep, True)]
                b.r = {}
            else:
                b.w.append((dep, True))

    def op(self, e, fn, reads=(), writes=(), pwrites=()):
        waits = self._need(e, reads, writes, pwrites)
        self.cnt[e] += 1
        dep = (e, self.cnt[e], self.sem[e])
        self.stream[e].append((waits, fn, (self.sem[e], 1)))
        self._commit(e, dep, reads, writes, pwrites)
        self.n_inst += 1

    def dma(self, q, out_ap, in_ap, reads=(), writes=(), pwrites=(), sem=None, **kw):
        sbb = [b for b in list(writes) + list(pwrites) + list(reads) if b.is_sb]
        assert sbb, "dma without sbuf side"
        if sbb[0].dsem is None:
            sbb[0].dsem = self.take_dsem(sbb[0].name, getattr(sbb[0], "stage", None), sw=(q == "pool"))
            sbb[0].dq = q
        assert sbb[0].dq == q or (sbb[0].dq != "pool" and q != "pool"), "buffer mixes SW and HW DMA queues"
        sem = sbb[0].dsem
        waits = self._need(q, reads, writes, pwrites)
        sem["cnt"] += 16
        dep = (sem["key"], sem["cnt"], sem["sem"])

        def fn(eng, out_ap=out_ap, in_ap=in_ap, kw=kw):
            return eng.dma_start(out=out_ap, in_=in_ap, **kw)

        self.stream[q].append((waits, fn, (sem["sem"], 16)))
        self._commit(sem["key"], dep, reads, writes, pwrites)
        self.n_inst += 1

    def barrier(self):
        for e in self.ENGS:
            waits = []
            for x in self.ENGS:
                if x == e or self.cnt[x] == 0:
                    continue
                if self.waited[e].get(x, 0) < self.cnt[x]:
                    self.waited[e][x] = self.cnt[x]
                    waits.append((self.sem[x], self.cnt[x]))
            for d in self.dsems:
                if d["cnt"] and self.waited[e].get(d["key"], 0) < d["cnt"]:
                    self.waited[e][d["key"]] = d["cnt"]
                    waits.append((d["sem"], d["cnt"]))
            self.stream[e].append((waits, None, None))

    def emit(self):
        nc = self.nc
        streams = self.stream
        self.stream = {e: [] for e in self.ENGS}
        with nc.Block() as block:
            def mk(e):
                def body(eng):
                    for waits, fn, inc in streams[e]:
                        for s, v in waits:
                            eng.wait_ge(s, v)
                        if fn is not None:
                            fn(eng).then_inc(inc[0], inc[1])
                return body
            block.tensor(mk("pe"))
            block.scalar(mk("act"))
            block.vector(mk("dve"))
            block.gpsimd(mk("pool"))
            block.sync(mk("sp"))


class Stage:
    def __init__(self, P, name):
        self.P = P
        Stage.CNT = getattr(Stage, "CNT", 0) + 1
        self.name = "%s%d" % (name, Stage.CNT)
        self.es = ExitStack()
        self.n = 0
        self.dsems = []

    def __enter__(self):
        self.es.__enter__()
        return self

    def sb(self, shape, dt, name=None):
        self.n += 1
        t = self.es.enter_context(self.P.nc.sbuf_tensor("%s_%s_%d" % (self.name, name or "t", self.n), list(shape), dt))
        b = Buf(t, name or "t")
        b.is_sb = True
        b.stage = self
        return b

    def __exit__(self, *a):
        if a[0] is None:
            self.P.barrier()
            self.P.emit()
            self.P.free_dsems.extend(self.dsems)
            self.dsems = []
        return self.es.__exit__(*a)


def bc(ap, shape):
    return ap.to_broadcast(list(shape))


def build_program(upto=99):
    nc = bass.Bass("TRN2", target_bir_lowering=False)

    def din(name, shape):
        return nc.dram_tensor(name, list(shape), F32, kind="ExternalInput").ap()

    def dout(name, shape):
        return nc.dram_tensor(name, list(shape), F32, kind="ExternalOutput").ap()

    import os as _os
    _dbg = set(_os.environ.get("KD_DBGOUT", "").split(","))
    _only = _os.environ.get("KD_ONLY", "")
    _only = set(float(v) for v in _only.split(",")) if _only else None

    def en(k):
        return upto >= k and (_only is None or k in _only)

    def dscr(name, shape, dt):
        return nc.dram_tensor(name, list(shape), dt, kind=("ExternalOutput" if name in _dbg else "Internal")).ap()

    I = {}
    I["xp"] = din("xp", [SEQ, D])
    I["xs"] = din("xs", [128, D])
    I["c3"] = din("c3", [3, D])
    I["h0re"] = din("h0re", [2, 4096])
    I["h0im"] = din("h0im", [2, 4096])
    I["cdk"] = din("cdk", [2, PAST, D])
    I["cdv"] = din("cdv", [2, PAST, D])
    I["cck"] = din("cck", [2, PAST, 128])
    I["ckr"] = din("ckr", [2, PAST, 32])
    wshapes = dict(
        w_ada=[4, D, 6 * D], b_ada=[4, 6 * D], g_mix=[4, D], g_ffn=[4, D], w_up=[4, D, 4 * D], w_down=[4, 4 * D, D],
        g_final=[D], s5_a_re=[4096], s5_a_im=[4096], s5_b_re=[64, 64, 16], s5_b_im=[64, 64, 16],
        s5_c_re=[64, 16, 64], s5_c_im=[64, 16, 64], s5_d=[D], s5_log_dt=[64], s5_w_glu_a=[D, D], s5_w_glu_b=[D, D],
        diff_w_qkv=[D, 3 * D], diff_lambda_q1=[64], diff_lambda_k1=[64], diff_lambda_q2=[64], diff_lambda_k2=[64],
        diff_g_sub=[128], diff_w_o=[D, D], mla_w_dq=[D, 256], mla_g_q=[256], mla_w_uq=[256, 1536], mla_w_dkv=[D, 160],
        mla_g_kv=[128], mla_w_uk=[128, 1024], mla_w_uv=[128, 1024], mla_w_o=[D, D], sgu_w_in=[D, 4 * D],
        sgu_g_v=[2 * D], sgu_w_s=[8, 128, 128], sgu_b_s=[8, 128], sgu_w_out=[2 * D, D])
    W = {k: din(k, s) for k, s in wshapes.items()}

    O = {}
    O["yp"] = dout("yp", [SEQ, D])
    O["ys"] = dout("ys", [128, D])
    O["s5p_re"] = dout("s5p_re", [4096])
    O["s5p_im"] = dout("s5p_im", [4096])
    O["s5s_re"] = dout("s5s_re", [2, 4096])
    O["s5s_im"] = dout("s5s_im", [2, 4096])
    O["dkp"] = dout("dkp", [SEQ, D])
    O["dvp"] = dout("dvp", [SEQ, D])
    O["dks"] = dout("dks", [128, D])
    O["dvs"] = dout("dvs", [128, D])
    O["ckvp"] = dout("ckvp", [SEQ, 128])
    O["krp"] = dout("krp", [SEQ, 32])
    O["ckvs"] = dout("ckvs", [128, 128])
    O["krs"] = dout("krs", [128, 32])
    O["sguv"] = dout("sguv", [128, 2 * D])

    xa = dscr("xa", [NTOK, D], F32)
    xb = dscr("xb", [NTOK, D], F32)
    gates_d = dscr("gates_d", [4, 2, 2, 128, D], F32)
    zT_d = dscr("zT_d", [NT, 128, 8 * 128], BF16)

    with ExitStack() as es:
        P = Prog(nc, es)
        out_bufs = {k: Buf(v, k) for k, v in O.items()}
        xa_b = Buf(xa, "xa")
        xb_b = Buf(xb, "xb")
        gates_b = Buf(gates_d, "gates")
        zT_b = Buf(zT_d, "zT_d")
        osem = P.no_dsem("out")
        ssem = P.no_dsem("scr")

        def gsb(name, shape, dt):
            b = Buf(es.enter_context(nc.sbuf_tensor(name, list(shape), dt)), name)
            b.is_sb = True
            return b

        ident_f = gsb("ident_f", [128, 128], F32)
        ident_b = gsb("ident_b", [128, 128], BF16)
        ones_b = gsb("ones_b", [128, 128], BF16)
        ones_f = gsb("ones_f", [128, 128], F32)
        condT = gsb("condT", [128, 8, 4], BF16)
        GS = gsb("GS", [128, 4, 2, 2, 8, 4], F32)
        gfinT = gsb("gfinT", [128, 8], F32)
        PS = [Buf(es.enter_context(nc.psum_tensor("ps%d" % i, [128, 512], F32)), "ps%d" % i) for i in range(8)]
        for b in PS:
            b.is_psum = True

        def load_T(st, dst, dst_ap, rows_ap, R, psb):
            t = st.sb([128, 128], F32, "ldT")
            sem = P.no_dsem("ldT")
            P.dma("sp", t.ap[0:R, :], rows_ap, writes=[t], sem=sem)
            P.op("pe", lambda e: e.transpose(out=psb.ap[:, 0:R], in_=t.ap[0:R, :], identity=ident_f.ap[0:R, 0:R]),
                 reads=[t, ident_f], writes=[psb])
            P.op("dve", lambda e: e.tensor_copy(out=dst_ap, in_=psb.ap[:, 0:R]), reads=[psb], pwrites=[dst])

        with Stage(P, "s0") as st:
            ld = P.no_dsem("s0ld")
            condbc = [st.sb([128, 8, 128], BF16, "condbc") for i in range(2)]
            modT = st.sb([128, 4, 48, 4], F32, "modT")
            P.op("pool", lambda e: e.memset(ident_f.ap[:], 0.0), writes=[ident_f])
            P.op("pool", lambda e: e.affine_select(out=ident_f.ap[:], in_=ident_f.ap[:], pattern=[[-1, 128]],
                                                   compare_op=ALU.not_equal, fill=1.0, base=0, channel_multiplier=1),
                 reads=[ident_f], writes=[ident_f])
            P.op("dve", lambda e: e.tensor_copy(out=ident_b.ap[:], in_=ident_f.ap[:]), reads=[ident_f], writes=[ident_b])
            P.op("pool", lambda e: e.memset(ones_b.ap[:], 1.0), writes=[ones_b])
            P.op("pool", lambda e: e.memset(ones_f.ap[:], 1.0), writes=[ones_f])

            c3 = st.sb([4, D], F32, "c3")
            P.op("pool", lambda e: e.memset(c3.ap[:], 0.0), writes=[c3])
            P.dma("sp", c3.ap[0:3, :], I["c3"], pwrites=[c3], sem=ld)
            c3s = st.sb([4, D], F32, "c3s")
            P.op("act", lambda e: e.activation(out=c3s.ap[:], in_=c3.ap[:], func=AF.Silu), reads=[c3], writes=[c3s])
            for kc in range(8):
                P.op("pe", lambda e, kc=kc: e.transpose(out=PS[0].ap[:, kc * 4:kc * 4 + 4], in_=c3s.ap[0:4, kc * 128:(kc + 1) * 128],
                                                        identity=ident_f.ap[0:4, 0:4]), reads=[c3s, ident_f], writes=[PS[0]])
            P.op("dve", lambda e: e.tensor_copy(out=condT.ap[:], in_=PS[0].ap[:, 0:32].rearrange("p (k s) -> p k s", s=4)),
                 reads=[PS[0]], writes=[condT])
            P.op("dve", lambda e: e.tensor_copy(out=condbc[0].ap[:], in_=bc(condT.ap[:, :, 0:1], [128, 8, 128])),
                 reads=[condT], writes=[condbc[0]])
            P.op("dve", lambda e: e.tensor_copy(out=condbc[1].ap[:, :, 0:64], in_=bc(condT.ap[:, :, 1:2], [128, 8, 64])),
                 reads=[condT], pwrites=[condbc[1]])
            P.op("dve", lambda e: e.tensor_copy(out=condbc[1].ap[:, :, 64:128], in_=bc(condT.ap[:, :, 2:3], [128, 8, 64])),
                 reads=[condT], pwrites=[condbc[1]])

            badaT = st.sb([128, 192], F32, "badaT")
            brows = W["b_ada"].rearrange("l (c p) -> (l c) p", p=128)
            load_T(st, badaT, badaT.ap[:, 0:96], brows[0:96, :], 96, PS[1])
            load_T(st, badaT, badaT.ap[:, 96:192], brows[96:192, :], 96, PS[2])
            gmT = st.sb([128, 2, 32], F32, "gmT")
            load_T(st, gmT, gmT.ap[:, 0, :], W["g_mix"].rearrange("l (c p) -> (l c) p", p=128), 32, PS[3])
            load_T(st, gmT, gmT.ap[:, 1, :], W["g_ffn"].rearrange("l (c p) -> (l c) p", p=128), 32, PS[4])
            load_T(st, gfinT, gfinT.ap[:, :], W["g_final"].rearrange("(c p) -> c p", p=128), 8, PS[5])

            wblk = [st.sb([128, 8, 1024], BF16, "wblk") for _ in range(2)]
            wsem = [P.no_dsem("wblk") for _ in range(2)]
            bbc = [st.sb([128, 1024], F32, "bbc") for _ in range(2)]
            bsem = [P.no_dsem("bbc") for _ in range(2)]
            gtile = [st.sb([128, 1024], F32, "gtile") for _ in range(2)]
            it = 0
            for l in range(4):
                for cb in range(6):
                    wb = wblk[it % 2]
                    for kc in range(8):
                        P.dma("pool", wb.ap[:, kc, :], W["w_ada"][l, kc * 128:(kc + 1) * 128, cb * 1024:(cb + 1) * 1024],
                              pwrites=[wb], sem=wsem[it % 2])
                    pm = PS[it % 2]
                    for f in range(8):
                        for kc in range(8):
                            P.op("pe", lambda e, pm=pm, f=f, kc=kc, wb=wb: e.matmul(
                                out=pm.ap[:, f * 4:f * 4 + 4], lhsT=wb.ap[:, kc, f * 128:(f + 1) * 128], rhs=condT.ap[:, kc, :],
                                start=(kc == 0), stop=(kc == 7)), reads=[wb, condT], writes=[pm])
                    P.op("dve", lambda e, pm=pm, l=l, cb=cb: e.tensor_tensor(
                        out=modT.ap[:, l, cb * 8:(cb + 1) * 8, :], in0=pm.ap[:, 0:32].rearrange("p (k s) -> p k s", s=4),
                        in1=bc(badaT.ap[:, l * 48 + cb * 8:l * 48 + cb * 8 + 8].unsqueeze(2), [128, 8, 4]), op=ALU.add),
                        reads=[pm, badaT], pwrites=[modT])
                    if cb in (2, 5):
                        gi = 0 if cb == 2 else 1
                        bb = bbc[gi]
                        P.dma("sp", bb.ap[:], W["b_ada"][l, cb * 1024:(cb + 1) * 1024].partition_broadcast(128), writes=[bb], sem=bsem[gi])
                        for ps_i in range(2):
                            gt = gtile[ps_i]
                            for half in range(2):
                                pg = PS[2 + half]
                                for kc in range(8):
                                    P.op("pe", lambda e, pg=pg, kc=kc, half=half, ps_i=ps_i, wb=wb: e.matmul(
                                        out=pg.ap[:], lhsT=condbc[ps_i].ap[:, kc, :], rhs=wb.ap[:, kc, half * 512:(half + 1) * 512],
                                        start=(kc == 0), stop=(kc == 7)), reads=[wb, condbc[ps_i]], writes=[pg])
                                P.op("dve", lambda e, pg=pg, half=half, gt=gt, bb=bb: e.scalar_tensor_tensor(
                                    out=gt.ap[:, half * 512:(half + 1) * 512], in0=pg.ap[:], scalar=1.0, in1=bb.ap[:, half * 512:(half + 1) * 512],
                                    op0=ALU.add, op1=ALU.add), reads=[pg, bb], pwrites=[gt])
                            P.dma("sp", gates_d[l, gi, ps_i], gt.ap[:], reads=[gt], pwrites=[gates_b], sem=ssem)
                    it += 1
            for l in range(4):
                for sub in range(2):
                    base = 0 if sub == 0 else 24
                    P.op("dve", lambda e, l=l, sub=sub, base=base: e.scalar_tensor_tensor(
                        out=GS.ap[:, l, sub, 0, :, :], in0=modT.ap[:, l, base + 8:base + 16, :], scalar=1.0,
                        in1=bc(gmT.ap[:, sub, l * 8:(l + 1) * 8].unsqueeze(2), [128, 8, 4]), op0=ALU.add, op1=ALU.mult),
                        reads=[modT, gmT], pwrites=[GS])
                    P.op("dve", lambda e, l=l, sub=sub, base=base: e.tensor_copy(
                        out=GS.ap[:, l, sub, 1, :, :], in_=modT.ap[:, l, base:base + 8, :]), reads=[modT], pwrites=[GS])

        class Gate:
            def __init__(self, st, l, gi):
                self.buf = st.sb([128, D], F32, "gate")
                self.sem = P.no_dsem("gate")
                self.l, self.gi, self.cur = l, gi, None

            def get(self, is_sample):
                k = 1 if is_sample else 0
                if self.cur != k:
                    P.dma("sp", self.buf.ap[:], gates_d[self.l, self.gi, k], reads=[gates_b], writes=[self.buf], sem=self.sem)
                    self.cur = k
                return self.buf

        class NormCtx:
            def __init__(self, st, psb, psb2):
                self.junk = st.sb([128, D], BF16, "junk")
                self.stat = st.sb([128, 4], F32, "stat")
                self.xn = st.sb([128, D], BF16, "xn")
                self.psb = [psb, psb2]

            def run(self, xbuf, x_ap, hbuf, h_ap, l, sub, is_sample, mod=True):
                stat, xn, junk = self.stat, self.xn, self.junk
                P.op("act", lambda e: e.activation(out=junk.ap[:], in_=x_ap, func=AF.Square, accum_out=stat.ap[:, 0:1]),
                     reads=[xbuf], writes=[junk, stat])
                P.op("act", lambda e: e.activation(out=stat.ap[:, 1:2], in_=stat.ap[:, 0:1], func=AF.Sqrt, scale=1.0 / D, bias=EPS),
                     reads=[stat], writes=[stat])
                P.op("dve", lambda e: e.reciprocal(out=stat.ap[:, 2:3], in_=stat.ap[:, 1:2]), reads=[stat], writes=[stat])
                P.op("act", lambda e: e.activation(out=xn.ap[:], in_=x_ap, func=AF.Copy, scale=stat.ap[:, 2:3]),
                     reads=[xbuf, stat], writes=[xn])
                pvs = [pb.ap[:].bitcast(BF16).rearrange("p (k t) -> p k t", t=128) for pb in self.psb]
                for kc in range(8):
                    psb, pv = self.psb[kc // 4], pvs[kc // 4]
                    P.op("pe", lambda e, kc=kc, pv=pv: e.transpose(out=pv[:, kc % 4, :], in_=xn.ap[:, kc * 128:(kc + 1) * 128], identity=ident_b.ap[:]),
                         reads=[xn, ident_b], writes=[psb])
                segs = [(0, 128, 0)] if not is_sample else [(0, 64, 1), (64, 128, 2)]
                for kc in range(8):
                    psb, pv0 = self.psb[kc // 4], pvs[kc // 4]
                    pv = pv0[:, kc % 4:kc % 4 + 1, :]
                    for (c0, c1, s) in segs:
                        if kc < 4:
                            P.op("act", lambda e, kc=kc, c0=c0, c1=c1, s=s, pv=pv: e.activation(
                                out=h_ap[:, kc, c0:c1], in_=pv[:, 0, c0:c1], func=AF.Identity,
                                scale=GS.ap[:, l, sub, 0, kc, s:s + 1], bias=GS.ap[:, l, sub, 1, kc, s:s + 1]),
                                reads=[psb, GS], pwrites=[hbuf])
                        else:
                            P.op("dve", lambda e, kc=kc, c0=c0, c1=c1, s=s, pv=pv: e.tensor_scalar(
                                out=h_ap[:, kc, c0:c1], in0=pv[:, 0, c0:c1], scalar1=GS.ap[:, l, sub, 0, kc, s:s + 1],
                                scalar2=GS.ap[:, l, sub, 1, kc, s:s + 1], op0=ALU.mult, op1=ALU.add),
                                reads=[psb, GS], pwrites=[hbuf])

        def x_src(t, which):
            if which == "in":
                return (I["xp"][t * 128:(t + 1) * 128, :], None) if t < NPT else (I["xs"], None)
            d, b = (xa, xa_b) if which == "a" else (xb, xb_b)
            return d[t * 128:(t + 1) * 128, :], b

        def x_dst(t, which):
            d, b = (xa, xa_b) if which == "a" else (xb, xb_b)
            return d[t * 128:(t + 1) * 128, :], b

        def load_w_bf16(dst, dst_ap_fn, src, nk, sem, q="pool"):
            for kc in range(nk):
                P.dma(q, dst_ap_fn(kc), src[kc * 128:(kc + 1) * 128, :], pwrites=[dst], sem=sem)

        if en(0.5):
          with Stage(P, "l0a") as st:
            ld = P.no_dsem("l0ld")
            sm = st.sb([128, 16, 32], F32, "sm")
            A_RE, A_IM, DT, TH, R, C1, S1, FRE, FIM, TMP1, TMP2, TMP3 = range(12)
            NTB = 65
            Er = st.sb([128, 32, NTB], F32, "Er")
            Ei = st.sb([128, 32, NTB], F32, "Ei")
            Tr = st.sb([128, 32, 64], F32, "Tr")
            Ti = st.sb([128, 32, 64], F32, "Ti")
            Bbd = [st.sb([128, 32, 128], BF16, "Bbd") for _ in range(2)]
            Cbd = [st.sb([128, 32, 128], BF16, "Cbd") for _ in range(2)]
            dT = st.sb([128, 8], F32, "dT")
            h0 = st.sb([128, 2, 2, 32], F32, "h0")

            with Stage(P, "l0p") as sp:
                P.dma("sp", sm.ap[:, A_RE, :], W["s5_a_re"].rearrange("(j p) -> p j", p=128), pwrites=[sm], sem=ld, allow_slow_non_contiguous=True)
                P.dma("sp", sm.ap[:, A_IM, :], W["s5_a_im"].rearrange("(j p) -> p j", p=128), pwrites=[sm], sem=ld, allow_slow_non_contiguous=True)
                ldt2 = W["s5_log_dt"].rearrange("(j h) -> h j", h=2)
                for hh in range(2):
                    P.dma("sp", sm.ap[hh * 64:(hh + 1) * 64, DT, :], ldt2[hh, :].partition_broadcast(64),
                          pwrites=[sm], sem=ld, allow_slow_non_contiguous=True)
                for s in range(2):
                    P.dma("sp", h0.ap[:, 0, s, :], I["h0re"][s].rearrange("(j p) -> p j", p=128), pwrites=[h0], sem=ld, allow_slow_non_contiguous=True)
                    P.dma("sp", h0.ap[:, 1, s, :], I["h0im"][s].rearrange("(j p) -> p j", p=128), pwrites=[h0], sem=ld, allow_slow_non_contiguous=True)

                def sm_op(fn, eng="dve", extra=()):
                    P.op(eng, fn, reads=[sm] + list(extra), writes=[sm])

                sm_op(lambda e: e.activation(out=sm.ap[:, DT, :], in_=sm.ap[:, DT, :], func=AF.Exp), "act")
                sm_op(lambda e: e.tensor_tensor(out=sm.ap[:, TH, :], in0=sm.ap[:, A_IM, :], in1=sm.ap[:, DT, :], op=ALU.mult))
                sm_op(lambda e: e.tensor_tensor(out=sm.ap[:, R, :], in0=sm.ap[:, A_RE, :], in1=sm.ap[:, DT, :], op=ALU.mult))
                sm_op(lambda e: e.activation(out=sm.ap[:, R, :], in_=sm.ap[:, R, :], func=AF.Exp), "act")

                tpos = sp.sb([128, NTB], F32, "tpos")
                P.op("pool", lambda e: e.iota(tpos.ap[:], pattern=[[1, NTB]], base=0, channel_multiplier=0, allow_small_or_imprecise_dtypes=True), writes=[tpos])
                ang = sp.sb([128, 32, NTB], F32, "ang")
                kf = sp.sb([128, 32, NTB], F32, "kf")
                ki = sp.sb([128, 32, NTB], I32, "ki")

                def sincos(dbuf, shift):
                    dst, src, k_ap, ki_ap = dbuf.ap[:], ang.ap[:], kf.ap[:], ki.ap[:]
                    P.op("dve", lambda e: e.tensor_scalar(out=k_ap, in0=src, scalar1=shift, scalar2=1.0 / TWO_PI, op0=ALU.add, op1=ALU.mult),
                         reads=[ang], writes=[kf])
                    P.op("dve", lambda e: e.tensor_copy(out=ki_ap, in_=k_ap), reads=[kf], writes=[ki])
                    P.op("dve", lambda e: e.tensor_copy(out=k_ap, in_=ki_ap), reads=[ki], writes=[kf])
                    P.op("dve", lambda e: e.scalar_tensor_tensor(out=dst, in0=k_ap, scalar=-CW1, in1=src, op0=ALU.mult, op1=ALU.add),
                         reads=[kf, ang], writes=[dbuf])
                    P.op("dve", lambda e: e.scalar_tensor_tensor(out=dst, in0=k_ap, scalar=-CW2, in1=dst, op0=ALU.mult, op1=ALU.add),
                         reads=[kf, dbuf], writes=[dbuf])
                    P.op("dve", lambda e: e.scalar_tensor_tensor(out=dst, in0=k_ap, scalar=-CW3, in1=dst, op0=ALU.mult, op1=ALU.add),
                         reads=[kf, dbuf], writes=[dbuf])
                    P.op("dve", lambda e: e.tensor_scalar(out=dst, in0=dst, scalar1=shift, scalar2=None, op0=ALU.add), reads=[dbuf], writes=[dbuf])
                    P.op("dve", lambda e: e.tensor_scalar(out=dst, in0=dst, scalar1=math.pi, scalar2=-math.pi, op0=ALU.min, op1=ALU.max),
                         reads=[dbuf], writes=[dbuf])
                    P.op("act", lambda e: e.activation(out=dst, in_=dst, func=AF.Sin), reads=[dbuf], writes=[dbuf])

                P.op("dve", lambda e: e.tensor_tensor(out=ang.ap[:], in0=bc(sm.ap[:, TH, :].unsqueeze(2), [128, 32, NTB]),
                                                      in1=bc(tpos.ap[:].unsqueeze(1), [128, 32, NTB]), op=ALU.mult),
                     reads=[sm, tpos], writes=[ang])
                sincos(Ei, 0.0)
                sincos(Er, math.pi / 2)
                sm_op(lambda e: e.tensor_tensor(out=sm.ap[:, C1, :], in0=sm.ap[:, R, :], in1=Er.ap[:, :, 1], op=ALU.mult), extra=[Er])
                sm_op(lambda e: e.tensor_tensor(out=sm.ap[:, S1, :], in0=sm.ap[:, R, :], in1=Ei.ap[:, :, 1], op=ALU.mult), extra=[Ei])
                sm_op(lambda e: e.tensor_scalar(out=sm.ap[:, C1, :], in0=sm.ap[:, C1, :], scalar1=-1.0, scalar2=None, op0=ALU.add))
                sm_op(lambda e: e.tensor_tensor(out=sm.ap[:, TMP1, :], in0=sm.ap[:, A_RE, :], in1=sm.ap[:, A_RE, :], op=ALU.mult))
                sm_op(lambda e: e.tensor_tensor(out=sm.ap[:, TMP2, :], in0=sm.ap[:, A_IM, :], in1=sm.ap[:, A_IM, :], op=ALU.mult))
                sm_op(lambda e: e.tensor_tensor(out=sm.ap[:, TMP1, :], in0=sm.ap[:, TMP1, :], in1=sm.ap[:, TMP2, :], op=ALU.add))
                sm_op(lambda e: e.reciprocal(out=sm.ap[:, TMP1, :], in_=sm.ap[:, TMP1, :]))
                sm_op(lambda e: e.tensor_tensor(out=sm.ap[:, TMP2, :], in0=sm.ap[:, C1, :], in1=sm.ap[:, A_RE, :], op=ALU.mult))
                sm_op(lambda e: e.tensor_tensor(out=sm.ap[:, TMP3, :], in0=sm.ap[:, S1, :], in1=sm.ap[:, A_IM, :], op=ALU.mult))
                sm_op(lambda e: e.tensor_tensor(out=sm.ap[:, TMP2, :], in0=sm.ap[:, TMP2, :], in1=sm.ap[:, TMP3, :], op=ALU.add))
                sm_op(lambda e: e.tensor_tensor(out=sm.ap[:, FRE, :], in0=sm.ap[:, TMP2, :], in1=sm.ap[:, TMP1, :], op=ALU.mult))
                sm_op(lambda e: e.tensor_tensor(out=sm.ap[:, TMP2, :], in0=sm.ap[:, S1, :], in1=sm.ap[:, A_RE, :], op=ALU.mult))
                sm_op(lambda e: e.tensor_tensor(out=sm.ap[:, TMP3, :], in0=sm.ap[:, C1, :], in1=sm.ap[:, A_IM, :], op=ALU.mult))
                sm_op(lambda e: e.tensor_tensor(out=sm.ap[:, TMP2, :], in0=sm.ap[:, TMP2, :], in1=sm.ap[:, TMP3, :], op=ALU.subtract))
                sm_op(lambda e: e.tensor_tensor(out=sm.ap[:, FIM, :], in0=sm.ap[:, TMP2, :], in1=sm.ap[:, TMP1, :], op=ALU.mult))
                tt = sp.sb([128, 32, 64], F32, "tt")
                fre_b = bc(sm.ap[:, FRE, :].unsqueeze(2), [128, 32, 64])
                fim_b = bc(sm.ap[:, FIM, :].unsqueeze(2), [128, 32, 64])
                P.op("dve", lambda e: e.tensor_tensor(out=Tr.ap[:], in0=Er.ap[:, :, 0:64], in1=fre_b, op=ALU.mult), reads=[Er, sm], writes=[Tr])
                P.op("dve", lambda e: e.tensor_tensor(out=tt.ap[:], in0=Ei.ap[:, :, 0:64], in1=fim_b, op=ALU.mult), reads=[Ei, sm], writes=[tt])
                P.op("dve", lambda e: e.tensor_tensor(out=Tr.ap[:], in0=Tr.ap[:], in1=tt.ap[:], op=ALU.add), reads=[Tr, tt], writes=[Tr])
                P.op("dve", lambda e: e.tensor_tensor(out=Ti.ap[:], in0=Er.ap[:, :, 0:64], in1=fim_b, op=ALU.mult), reads=[Er, sm], writes=[Ti])
                P.op("dve", lambda e: e.tensor_tensor(out=tt.ap[:], in0=Ei.ap[:, :, 0:64], in1=fre_b, op=ALU.mult), reads=[Ei, sm, Ti], writes=[tt])
                P.op("dve", lambda e: e.tensor_tensor(out=Ti.ap[:], in0=Ti.ap[:], in1=tt.ap[:], op=ALU.subtract), reads=[Ti, tt], writes=[Ti])

                stg = [sp.sb([128, 8, 128], F32, "stg") for _ in range(2)]
                stg_sem = [P.no_dsem("stg") for _ in range(2)]
                it = 0
                for kind in range(4):
                    src = [W["s5_b_re"], W["s5_b_im"], W["s5_c_re"], W["s5_c_im"]][kind]
                    for jb in range(4):
                        sg = stg[it % 2]
                        P.op("pool", lambda e, sg=sg: e.memset(sg.ap[:], 0.0), writes=[sg])
                        for jj in range(8):
                            j = jb * 8 + jj
                            for gg in range(2):
                                g = 2 * j + gg
                                if kind < 2:
                                    P.dma("sp", sg.ap[gg * 64:(gg + 1) * 64, jj, (g % 8) * 16:(g % 8) * 16 + 16], src[g], pwrites=[sg], sem=stg_sem[it % 2])
                                else:
                                    P.dma("sp", sg.ap[(g % 8) * 16:(g % 8) * 16 + 16, jj, gg * 64:(gg + 1) * 64], src[g], pwrites=[sg], sem=stg_sem[it % 2])
                        for q4 in range(2):
                            psb = PS[(it * 2 + q4) % 4]
                            for u in range(4):
                                jj = q4 * 4 + u
                                P.op("pe", lambda e, psb=psb, u=u, jj=jj, sg=sg: e.transpose(out=psb.ap[:, u * 128:(u + 1) * 128], in_=sg.ap[:, jj, :], identity=ident_f.ap[:]),
                                     reads=[sg, ident_f], writes=[psb])
                            dst = (Bbd[kind] if kind < 2 else Cbd[kind - 2])
                            j0 = jb * 8 + q4 * 4
                            sc = -1.0 if kind == 3 else 1.0
                            P.op("act", lambda e, psb=psb, dst=dst, j0=j0, sc=sc: e.activation(out=dst.ap[:, j0:j0 + 4, :], in_=psb.ap[:].rearrange("p (u c) -> p u c", c=128), func=AF.Copy, scale=sc),
                                 reads=[psb], pwrites=[dst])
                        it += 1
                load_T(sp, dT, dT.ap[:, :], W["s5_d"].rearrange("(c p) -> c p", p=128), 8, PS[4])

            xt = [st.sb([128, D], F32, "x") for _ in range(2)]
            xsem = [P.no_dsem("x") for _ in range(2)]
            nctx = NormCtx(st, PS[0], PS[7])
            hTs = [st.sb([128, 8, 128], BF16, "hT") for _ in range(2)]
            wre = st.sb([128, 32, 128], BF16, "wre")
            wim = st.sb([128, 32, 128], BF16, "wim")
            gre = st.sb([128, 32, 128], F32, "gre")
            gim = st.sb([128, 32, 128], F32, "gim")
            hsr = st.sb([128, 32, 128], BF16, "hsr")
            hsi = st.sb([128, 32, 128], BF16, "hsi")
            tmp = [st.sb([128, 512], F32, "tmp") for _ in range(4)]
            init = st.sb([128, 2, 2, 32], F32, "init")
            hend = st.sb([128, 2, 2, 32], F32, "hend")
            ctmp = st.sb([128, 4, 32], F32, "ctmp")
            ysb = st.sb([128, 8, 128], F32, "ysb")
            zT = [st.sb([128, 8, 128], BF16, "zT") for _ in range(2)]
            P.op("pool", lambda e: e.memset(init.ap[:], 0.0), writes=[init])

            def cmul_small(dbuf, dst_re, dst_im, a_re, a_im, b_re, b_im, deps_r):
                P.op("dve", lambda e: e.tensor_tensor(out=ctmp.ap[:, 0, :], in0=a_re, in1=b_re, op=ALU.mult), reads=deps_r, pwrites=[ctmp])
                P.op("dve", lambda e: e.tensor_tensor(out=ctmp.ap[:, 1, :], in0=a_im, in1=b_im, op=ALU.mult), reads=deps_r, pwrites=[ctmp])
                P.op("dve", lambda e: e.tensor_tensor(out=ctmp.ap[:, 2, :], in0=a_re, in1=b_im, op=ALU.mult), reads=deps_r, pwrites=[ctmp])
                P.op("dve", lambda e: e.tensor_tensor(out=ctmp.ap[:, 3, :], in0=a_im, in1=b_re, op=ALU.mult), reads=deps_r, pwrites=[ctmp])
                P.op("dve", lambda e: e.tensor_tensor(out=dst_re, in0=ctmp.ap[:, 0, :], in1=ctmp.ap[:, 1, :], op=ALU.subtract), reads=[ctmp], pwrites=[dbuf])
                P.op("dve", lambda e: e.tensor_tensor(out=dst_im, in0=ctmp.ap[:, 2, :], in1=ctmp.ap[:, 3, :], op=ALU.add), reads=[ctmp], pwrites=[dbuf])

            tlist = list(range(NT)) if upto >= 1 else ([] if upto < 0.55 else ([0] if upto < 0.65 else [0, NPT]))
            for t in tlist:
                is_s = (t == NPT)
                i = t % 2
                x = xt[i]
                src, srcb = x_src(t, "in")
                P.dma("sp", x.ap[:], src, writes=[x], sem=xsem[i])
                hT = hTs[i]
                nctx.run(x, x.ap[:], hT, hT.ap[:], 0, 0, is_s)
                if is_s:
                    for s in range(2):
                        cmul_small(init, init.ap[:, 0, s, :], init.ap[:, 1, s, :], h0.ap[:, 0, s, :], h0.ap[:, 1, s, :],
                                   Er.ap[:, :, 1], Ei.ap[:, :, 1], [h0, Er, Ei])
                import os as _os
                CUT = int(_os.environ.get('KD_CUT', '99'))
                if CUT < 2:
                    continue
                for m in range(8):
                    pr, pi_ = PS[1 + (m % 2) * 2], PS[2 + (m % 2) * 2]
                    for u in range(4):
                        j = m * 4 + u
                        P.op("pe", lambda e, pr=pr, u=u, j=j, m=m, hT=hT: e.matmul(out=pr.ap[:, u * 128:(u + 1) * 128], lhsT=Bbd[0].ap[:, j, :], rhs=hT.ap[:, m, :], start=True, stop=True),
                             reads=[Bbd[0], hT], writes=[pr])
                    for u in range(4):
                        j = m * 4 + u
                        P.op("pe", lambda e, pi_=pi_, u=u, j=j, m=m, hT=hT: e.matmul(out=pi_.ap[:, u * 128:(u + 1) * 128], lhsT=Bbd[1].ap[:, j, :], rhs=hT.ap[:, m, :], start=True, stop=True),
                             reads=[Bbd[1], hT], writes=[pi_])
                    prv = pr.ap[:].rearrange("p (u h t) -> p u h t", u=4, h=2)
                    piv = pi_.ap[:].rearrange("p (u h t) -> p u h t", u=4, h=2)
                    Trv = bc(Tr.ap[:, m * 4:(m + 1) * 4, :].unsqueeze(2), [128, 4, 2, 64])
                    Tiv = bc(Ti.ap[:, m * 4:(m + 1) * 4, :].unsqueeze(2), [128, 4, 2, 64])
                    tv = [tb.ap[:].rearrange("p (u h t) -> p u h t", u=4, h=2) for tb in tmp]
                    wrv = wre.ap[:, m * 4:(m + 1) * 4, :].rearrange("p u (h t) -> p u h t", h=2)
                    wiv = wim.ap[:, m * 4:(m + 1) * 4, :].rearrange("p u (h t) -> p u h t", h=2)
                    P.op("dve", lambda e, prv=prv, Trv=Trv, tv=tv: e.tensor_tensor(out=tv[0], in0=prv, in1=Trv, op=ALU.mult), reads=[pr, Tr], writes=[tmp[0]])
                    P.op("dve", lambda e, piv=piv, Tiv=Tiv, tv=tv: e.tensor_tensor(out=tv[1], in0=piv, in1=Tiv, op=ALU.mult), reads=[pi_, Ti], writes=[tmp[1]])
                    P.op("dve", lambda e, prv=prv, Tiv=Tiv, tv=tv: e.tensor_tensor(out=tv[2], in0=prv, in1=Tiv, op=ALU.mult), reads=[pr, Ti], writes=[tmp[2]])
                    P.op("dve", lambda e, piv=piv, Trv=Trv, tv=tv: e.tensor_tensor(out=tv[3], in0=piv, in1=Trv, op=ALU.mult), reads=[pi_, Tr], writes=[tmp[3]])
                    P.op("pool", lambda e, wrv=wrv, tv=tv: e.tensor_tensor(out=wrv, in0=tv[0], in1=tv[1], op=ALU.subtract), reads=[tmp[0], tmp[1]], pwrites=[wre])
                    P.op("pool", lambda e, wiv=wiv, tv=tv: e.tensor_tensor(out=wiv, in0=tv[2], in1=tv[3], op=ALU.add), reads=[tmp[2], tmp[3]], pwrites=[wim])
                if CUT < 3:
                    continue
                for hf in range(2):
                    for j in range(32):
                        for ri, (wsrc, gdst) in enumerate(((wre, gre), (wim, gim))):
                            P.op("dve", lambda e, j=j, hf=hf, ri=ri, wsrc=wsrc, gdst=gdst: e.tensor_tensor_scan(
                                out=gdst.ap[:, j, hf * 64:(hf + 1) * 64], data0=bc(sm.ap[:, R, j:j + 1], [128, 64]),
                                data1=wsrc.ap[:, j, hf * 64:(hf + 1) * 64], initial=init.ap[:, ri, hf, j:j + 1], op0=ALU.mult, op1=ALU.add),
                                reads=[sm, wsrc, init], pwrites=[gdst])
                    ge_r = gre.ap[:, :, hf * 64 + 63]
                    ge_i = gim.ap[:, :, hf * 64 + 63]
                    if not is_s:
                        nh = 1 - hf
                        cmul_small(init, init.ap[:, 0, nh, :], init.ap[:, 1, nh, :], ge_r, ge_i, Er.ap[:, :, 64], Ei.ap[:, :, 64], [gre, gim, Er, Ei])
                    if is_s or (t == NPT - 1 and hf == 1):
                        cmul_small(hend, hend.ap[:, 0, hf, :], hend.ap[:, 1, hf, :], ge_r, ge_i, Er.ap[:, :, 63], Ei.ap[:, :, 63], [gre, gim, Er, Ei])
                        if is_s:
                            P.dma("sp", O["s5s_re"][hf].rearrange("(j p) -> p j", p=128), hend.ap[:, 0, hf, :], reads=[hend], pwrites=[out_bufs["s5s_re"]], sem=osem, allow_slow_non_contiguous=True)
                            P.dma("sp", O["s5s_im"][hf].rearrange("(j p) -> p j", p=128), hend.ap[:, 1, hf, :], reads=[hend], pwrites=[out_bufs["s5s_im"]], sem=osem, allow_slow_non_contiguous=True)
                        else:
                            P.dma("sp", O["s5p_re"].rearrange("(j p) -> p j", p=128), hend.ap[:, 0, hf, :], reads=[hend], pwrites=[out_bufs["s5p_re"]], sem=osem, allow_slow_non_contiguous=True)
                            P.dma("sp", O["s5p_im"].rearrange("(j p) -> p j", p=128), hend.ap[:, 1, hf, :], reads=[hend], pwrites=[out_bufs["s5p_im"]], sem=osem, allow_slow_non_contiguous=True)
                if CUT < 4:
                    continue
                z = zT[i]
                for m in range(8):
                    grv = gre.ap[:, m * 4:(m + 1) * 4, :].rearrange("p u (h t) -> p u h t", h=2)
                    giv = gim.ap[:, m * 4:(m + 1) * 4, :].rearrange("p u (h t) -> p u h t", h=2)
                    Erv = bc(Er.ap[:, m * 4:(m + 1) * 4, 0:64].unsqueeze(2), [128, 4, 2, 64])
                    Eiv = bc(Ei.ap[:, m * 4:(m + 1) * 4, 0:64].unsqueeze(2), [128, 4, 2, 64])
                    tv = [tb.ap[:].rearrange("p (u h t) -> p u h t", u=4, h=2) for tb in tmp]
                    hrv = hsr.ap[:, m * 4:(m + 1) * 4, :].rearrange("p u (h t) -> p u h t", h=2)
                    hiv = hsi.ap[:, m * 4:(m + 1) * 4, :].rearrange("p u (h t) -> p u h t", h=2)
                    P.op("dve", lambda e, grv=grv, Erv=Erv, tv=tv: e.tensor_tensor(out=tv[0], in0=grv, in1=Erv, op=ALU.mult), reads=[gre, Er], writes=[tmp[0]])
                    P.op("pool", lambda e, giv=giv, Eiv=Eiv, tv=tv: e.tensor_tensor(out=tv[1], in0=giv, in1=Eiv, op=ALU.mult), reads=[gim, Ei], writes=[tmp[1]])
                    P.op("pool", lambda e, grv=grv, Eiv=Eiv, tv=tv: e.tensor_tensor(out=tv[2], in0=grv, in1=Eiv, op=ALU.mult), reads=[gre, Ei], writes=[tmp[2]])
                    P.op("pool", lambda e, giv=giv, Erv=Erv, tv=tv: e.tensor_tensor(out=tv[3], in0=giv, in1=Erv, op=ALU.mult), reads=[gim, Er], writes=[tmp[3]])
                    P.op("dve", lambda e, hrv=hrv, tv=tv: e.tensor_tensor(out=hrv, in0=tv[0], in1=tv[1], op=ALU.subtract), reads=[tmp[0], tmp[1]], pwrites=[hsr])
                    P.op("pool", lambda e, hiv=hiv, tv=tv: e.tensor_tensor(out=hiv, in0=tv[2], in1=tv[3], op=ALU.add), reads=[tmp[2], tmp[3]], pwrites=[hsi])
                    py = PS[5 + (m // 4)]
                    for u in range(4):
                        j = m * 4 + u
                        P.op("pe", lambda e, py=py, m=m, j=j, u=u: e.matmul(out=py.ap[:, (m % 4) * 128:(m % 4 + 1) * 128], lhsT=Cbd[0].ap[:, j, :], rhs=hsr.ap[:, j, :], start=(u == 0), stop=False),
                             reads=[Cbd[0], hsr], writes=[py])
                    for u in range(4):
                        j = m * 4 + u
                        P.op("pe", lambda e, py=py, m=m, j=j, u=u: e.matmul(out=py.ap[:, (m % 4) * 128:(m % 4 + 1) * 128], lhsT=Cbd[1].ap[:, j, :], rhs=hsi.ap[:, j, :], start=False, stop=(u == 3)),
                             reads=[Cbd[1], hsi], writes=[py])
                    P.op("dve", lambda e, py=py, m=m, hT=hT: e.scalar_tensor_tensor(out=ysb.ap[:, m, :], in0=hT.ap[:, m, :], scalar=dT.ap[:, m:m + 1],
                                                                              in1=py.ap[:, (m % 4) * 128:(m % 4 + 1) * 128], op0=ALU.mult, op1=ALU.add),
                         reads=[hT, dT, py], pwrites=[ysb])
                if CUT < 5:
                    continue
                P.op("act", lambda e, z=z: e.activation(out=z.ap[:], in_=ysb.ap[:], func=AF.Gelu_apprx_tanh), reads=[ysb], writes=[z])
                P.dma("sp", zT_d[t], z.ap[:].rearrange("p k t -> p (k t)"), reads=[z], pwrites=[zT_b], sem=ssem)

        if en(2):
          with Stage(P, "l0g") as st:
            Wa = st.sb([128, 8, D], BF16, "Wa")
            Wb = st.sb([128, 8, D], BF16, "Wb")
            wsem = P.no_dsem("wglu")
            load_w_bf16(Wa, lambda kc: Wa.ap[:, kc, :], W["s5_w_glu_a"], 8, wsem)
            load_w_bf16(Wb, lambda kc: Wb.ap[:, kc, :], W["s5_w_glu_b"], 8, wsem)
            gate = Gate(st, 0, 0)
            xt = [st.sb([128, D], F32, "x") for _ in range(2)]
            xsem = [P.no_dsem("x") for _ in range(2)]
            zt = [st.sb([128, 8, 128], BF16, "z") for _ in range(2)]
            zsem = [P.no_dsem("z") for _ in range(2)]
            sig = [st.sb([128, D], F32, "sig") for _ in range(2)]
            for t in range(NT):
                is_s = (t == NPT)
                i = t % 2
                x, z, sg_ = xt[i], zt[i], sig[i]
                src, _ = x_src(t, "in")
                P.dma("sp", x.ap[:], src, writes=[x], sem=xsem[i])
                P.dma("sp", z.ap[:].rearrange("p k t -> p (k t)"), zT_d[t], reads=[zT_b], writes=[z], sem=zsem[i])
                g = gate.get(is_s)
                for cbk in range(2):
                    pa, pb = PS[cbk * 2], PS[1 + cbk * 2]
                    for kc in range(8):
                        P.op("pe", lambda e, pa=pa, kc=kc, cbk=cbk, z=z: e.matmul(out=pa.ap[:], lhsT=z.ap[:, kc, :], rhs=Wa.ap[:, kc, cbk * 512:(cbk + 1) * 512], start=(kc == 0), stop=(kc == 7)),
                             reads=[z, Wa], writes=[pa])
                    for kc in range(8):
                        P.op("pe", lambda e, pb=pb, kc=kc, cbk=cbk, z=z: e.matmul(out=pb.ap[:], lhsT=z.ap[:, kc, :], rhs=Wb.ap[:, kc, cbk * 512:(cbk + 1) * 512], start=(kc == 0), stop=(kc == 7)),
                             reads=[z, Wb], writes=[pb])
                    sl = slice(cbk * 512, (cbk + 1) * 512)
                    P.op("act", lambda e, pb=pb, sl=sl, sg_=sg_: e.activation(out=sg_.ap[:, sl], in_=pb.ap[:], func=AF.Sigmoid), reads=[pb], pwrites=[sg_])
                    P.op("dve", lambda e, pa=pa, sl=sl, sg_=sg_: e.tensor_tensor(out=sg_.ap[:, sl], in0=pa.ap[:], in1=sg_.ap[:, sl], op=ALU.mult), reads=[pa, sg_], pwrites=[sg_])
                    P.op("pool", lambda e, sl=sl, g=g, sg_=sg_: e.tensor_tensor(out=sg_.ap[:, sl], in0=sg_.ap[:, sl], in1=g.ap[:, sl], op=ALU.mult), reads=[sg_, g], pwrites=[sg_])
                    P.op("pool", lambda e, sl=sl, sg_=sg_, x=x: e.tensor_tensor(out=sg_.ap[:, sl], in0=sg_.ap[:, sl], in1=x.ap[:, sl], op=ALU.add), reads=[sg_, x], pwrites=[sg_])
                dst, dstb = x_dst(t, "a")
                P.dma("sp", dst, sg_.ap[:], reads=[sg_], pwrites=[dstb], sem=ssem)

        def mlp_stage(l, src_w, dst_w, final=False):
          with Stage(P, "mlp%d" % l) as st:
            wup = st.sb([128, 8, 4 * D], BF16, "wup")
            wdn = st.sb([128, 32, D], BF16, "wdn")
            wsem = P.no_dsem("wmlp")
            for kc in range(8):
                for c0 in range(0, 4 * D, 1024):
                    P.dma("pool", wup.ap[:, kc, c0:c0 + 1024], W["w_up"][l, kc * 128:(kc + 1) * 128, c0:c0 + 1024], pwrites=[wup], sem=wsem)
            for kc in range(32):
                P.dma("pool", wdn.ap[:, kc, :], W["w_down"][l, kc * 128:(kc + 1) * 128, :], pwrites=[wdn], sem=wsem)
            gate = Gate(st, l, 1)
            if final:
                gfin = st.sb([128, D], F32, "gfin")
                gsem = P.no_dsem("gfin")
                P.dma("sp", gfin.ap[:], W["g_final"].partition_broadcast(128), writes=[gfin], sem=gsem)
                fstat = st.sb([128, 4], F32, "fstat")
            xs_ = st.sb([128, 4, D], F32, "xs")
            xsem = P.no_dsem("x")
            nctx = NormCtx(st, PS[0], PS[3])
            hT = st.sb([128, 8, 512], BF16, "hT")
            actT = st.sb([128, 32, 512], BF16, "actT")
            rl = [st.sb([128, 512], BF16, "rl") for _ in range(2)]
            tiles = [(s * 4, 4) for s in range(NPT // 4)] + [(NPT, 1)]
            for (t0, nt) in tiles:
                is_s = (t0 == NPT)
                ntok = nt * 128
                for a in range(nt):
                    src, srcb = x_src(t0 + a, src_w)
                    P.dma("sp", xs_.ap[:, a, :], src, reads=([srcb] if srcb else []), pwrites=[xs_], sem=xsem)
                for a in range(nt):
                    hv = hT.ap[:, :, a * 128:(a + 1) * 128]
                    nctx.run(xs_, xs_.ap[:, a, :], hT, hv, l, 1, is_s)
                for oc in range(32):
                    pu = PS[(1, 2, 4, 5, 6, 7)[oc % 6]]
                    for kc in range(8):
                        P.op("pe", lambda e, pu=pu, kc=kc, oc=oc, ntok=ntok: e.matmul(out=pu.ap[:, 0:ntok], lhsT=wup.ap[:, kc, oc * 128:(oc + 1) * 128], rhs=hT.ap[:, kc, 0:ntok], start=(kc == 0), stop=(kc == 7)),
                             reads=[wup, hT], writes=[pu])
                    r = rl[oc % 2]
                    P.op("act", lambda e, pu=pu, r=r, ntok=ntok: e.activation(out=r.ap[:, 0:ntok], in_=pu.ap[:, 0:ntok], func=AF.Relu), reads=[pu], writes=[r])
                    eng = "dve" if oc % 2 == 0 else "pool"
                    P.op(eng, lambda e, r=r, oc=oc, ntok=ntok: e.tensor_tensor(out=actT.ap[:, oc, 0:ntok], in0=r.ap[:, 0:ntok], in1=r.ap[:, 0:ntok], op=ALU.mult), reads=[r], pwrites=[actT])
                g = gate.get(is_s)
                for a in range(nt):
                    for cbk in range(2):
                        pd = PS[4 + (a * 2 + cbk) % 4]
                        for kc in range(32):
                            P.op("pe", lambda e, pd=pd, kc=kc, a=a, cbk=cbk: e.matmul(out=pd.ap[:], lhsT=actT.ap[:, kc, a * 128:(a + 1) * 128], rhs=wdn.ap[:, kc, cbk * 512:(cbk + 1) * 512], start=(kc == 0), stop=(kc == 31)),
                                 reads=[actT, wdn], writes=[pd])
                        sl = slice(cbk * 512, (cbk + 1) * 512)
                        tmpb = rl
                        P.op("dve", lambda e, pd=pd, a=a, sl=sl, g=g: e.tensor_tensor(out=pd.ap[:], in0=pd.ap[:], in1=g.ap[:, sl], op=ALU.mult), reads=[pd, g], writes=[pd])
                        P.op("dve", lambda e, pd=pd, a=a, sl=sl: e.tensor_tensor(out=xs_.ap[:, a, sl], in0=pd.ap[:], in1=xs_.ap[:, a, sl], op=ALU.add), reads=[pd, xs_], pwrites=[xs_])
                    if final:
                        junk, stat = nctx.junk, fstat
                        P.op("act", lambda e, a=a: e.activation(out=junk.ap[:], in_=xs_.ap[:, a, :], func=AF.Square, accum_out=stat.ap[:, 0:1]), reads=[xs_], writes=[junk, stat])
                        P.op("act", lambda e: e.activation(out=stat.ap[:, 1:2], in_=stat.ap[:, 0:1], func=AF.Sqrt, scale=1.0 / D, bias=EPS), reads=[stat], writes=[stat])
                        P.op("dve", lambda e: e.reciprocal(out=stat.ap[:, 2:3], in_=stat.ap[:, 1:2]), reads=[stat], writes=[stat])
                        P.op("dve", lambda e, a=a: e.scalar_tensor_tensor(out=xs_.ap[:, a, :], in0=xs_.ap[:, a, :], scalar=stat.ap[:, 2:3], in1=gfin.ap[:], op0=ALU.mult, op1=ALU.mult),
                             reads=[xs_, stat, gfin], pwrites=[xs_])
                        t = t0 + a
                        if t < NPT:
                            P.dma("sp", O["yp"][t * 128:(t + 1) * 128, :], xs_.ap[:, a, :], reads=[xs_], pwrites=[out_bufs["yp"]], sem=osem)
                        else:
                            P.dma("sp", O["ys"], xs_.ap[:, a, :], reads=[xs_], pwrites=[out_bufs["ys"]], sem=osem)
                    else:
                        dst, dstb = x_dst(t0 + a, dst_w)
                        P.dma("sp", dst, xs_.ap[:, a, :], reads=[xs_], pwrites=[dstb], sem=ssem)

        if en(3):
            mlp_stage(0, "a", "b")

        def sin_reduce(st, dbuf, abuf, shift, shape):
            kf = st.sb(shape, F32, "kf")
            ki = st.sb(shape, I32, "ki")
            dst, src, k_ap, ki_ap = dbuf.ap[:], abuf.ap[:], kf.ap[:], ki.ap[:]
            P.op("dve", lambda e: e.tensor_scalar(out=k_ap, in0=src, scalar1=shift, scalar2=1.0 / TWO_PI, op0=ALU.add, op1=ALU.mult), reads=[abuf], writes=[kf])
            P.op("dve", lambda e: e.tensor_copy(out=ki_ap, in_=k_ap), reads=[kf], writes=[ki])
            P.op("dve", lambda e: e.tensor_copy(out=k_ap, in_=ki_ap), reads=[ki], writes=[kf])
            P.op("dve", lambda e: e.scalar_tensor_tensor(out=dst, in0=k_ap, scalar=-CW1, in1=src, op0=ALU.mult, op1=ALU.add), reads=[kf, abuf], writes=[dbuf])
            P.op("dve", lambda e: e.scalar_tensor_tensor(out=dst, in0=k_ap, scalar=-CW2, in1=dst, op0=ALU.mult, op1=ALU.add), reads=[kf, dbuf], writes=[dbuf])
            P.op("dve", lambda e: e.scalar_tensor_tensor(out=dst, in0=k_ap, scalar=-CW3, in1=dst, op0=ALU.mult, op1=ALU.add), reads=[kf, dbuf], writes=[dbuf])
            P.op("dve", lambda e: e.tensor_scalar(out=dst, in0=dst, scalar1=shift, scalar2=None, op0=ALU.add), reads=[dbuf], writes=[dbuf])
            P.op("dve", lambda e: e.tensor_scalar(out=dst, in0=dst, scalar1=math.pi, scalar2=-math.pi, op0=ALU.min, op1=ALU.max), reads=[dbuf], writes=[dbuf])
            P.op("act", lambda e: e.activation(out=dst, in_=dst, func=AF.Sin), reads=[dbuf], writes=[dbuf])

        def rope_tables(st, half):
            cosT = st.sb([128, NT, half], F32, "cosT")
            sinT = st.sb([128, NT, half], F32, "sinT")
            with Stage(P, "rp") as sp:
                pos = sp.sb([128, NT], F32, "pos")
                invf = sp.sb([128, half], F32, "invf")
                ang = sp.sb([128, NT, half], F32, "ang")
                P.op("pool", lambda e: e.iota(pos.ap[:], pattern=[[128, NT]], base=0, channel_multiplier=1, allow_small_or_imprecise_dtypes=True), writes=[pos])
                P.op("pool", lambda e: e.iota(pos.ap[0:64, NPT:NPT + 1], pattern=[[0, 1]], base=PAST, channel_multiplier=1, allow_small_or_imprecise_dtypes=True), pwrites=[pos])
                P.op("pool", lambda e: e.iota(pos.ap[64:128, NPT:NPT + 1], pattern=[[0, 1]], base=PAST, channel_multiplier=1, allow_small_or_imprecise_dtypes=True), pwrites=[pos])
                for i in range(half):
                    P.op("pool", lambda e, i=i: e.memset(invf.ap[:, i:i + 1], float(np.float32(10000.0) ** np.float32(-i / half))), pwrites=[invf])
                P.op("dve", lambda e: e.tensor_tensor(out=ang.ap[:], in0=bc(pos.ap[:].unsqueeze(2), [128, NT, half]), in1=bc(invf.ap[:].unsqueeze(1), [128, NT, half]), op=ALU.mult),
                     reads=[pos, invf], writes=[ang])
                sin_reduce(sp, sinT, ang, 0.0, [128, NT, half])
                sin_reduce(sp, cosT, ang, math.pi / 2, [128, NT, half])
            return cosT, sinT

        def rope_apply(src_ps, src_ap, dbuf, dst_ap, ngrp, half, cosT, sinT, t, tmps):
            c = bc(cosT.ap[:, t, :].unsqueeze(1), [128, ngrp, half])
            s = bc(sinT.ap[:, t, :].unsqueeze(1), [128, ngrp, half])
            x1, x2 = src_ap[:, :, 0:half], src_ap[:, :, half:2 * half]
            tv = [tb.ap[:, 0:ngrp * half].rearrange("p (g i) -> p g i", i=half) for tb in tmps]
            P.op("dve", lambda e: e.tensor_tensor(out=tv[0], in0=x1, in1=c, op=ALU.mult), reads=[src_ps, cosT], writes=[tmps[0]])
            P.op("dve", lambda e: e.tensor_tensor(out=tv[1], in0=x2, in1=s, op=ALU.mult), reads=[src_ps, sinT], writes=[tmps[1]])
            P.op("dve", lambda e: e.tensor_tensor(out=tv[2], in0=x1, in1=s, op=ALU.mult), reads=[src_ps, sinT], writes=[tmps[2]])
            P.op("dve", lambda e: e.tensor_tensor(out=tv[3], in0=x2, in1=c, op=ALU.mult), reads=[src_ps, cosT], writes=[tmps[3]])
            P.op("pool", lambda e: e.tensor_tensor(out=dst_ap[:, :, 0:half], in0=tv[0], in1=tv[1], op=ALU.subtract), reads=[tmps[0], tmps[1]], pwrites=[dbuf])
            P.op("pool", lambda e: e.tensor_tensor(out=dst_ap[:, :, half:2 * half], in0=tv[2], in1=tv[3], op=ALU.add), reads=[tmps[2], tmps[3]], pwrites=[dbuf])

        QT_d = dscr("QT_d", [8, 128, NTOK], BF16)
        KT_d = dscr("KT_d", [8, 128, NTOK], BF16)
        V_d = dscr("V_d", [NTOK, D], BF16)
        OT_d = dscr("OT_d", [8, 128, NTOK], BF16)
        QT_b, KT_b, V_b, OT_b = Buf(QT_d, "QT_d"), Buf(KT_d, "KT_d"), Buf(V_d, "V_d"), Buf(OT_d, "OT_d")

        if en(4):
          with Stage(P, "l1a") as st:
            cosT, sinT = rope_tables(st, 32)
            wqkv = st.sb([128, 8, 3 * D], BF16, "wqkv")
            for kc in range(8):
                for c0 in range(0, 3 * D, 1024):
                    P.dma("pool", wqkv.ap[:, kc, c0:c0 + 1024], W["diff_w_qkv"][kc * 128:(kc + 1) * 128, c0:c0 + 1024], pwrites=[wqkv])
            xt = [st.sb([128, D], F32, "x") for _ in range(2)]
            nctx = NormCtx(st, PS[0], PS[7])
            hTs = [st.sb([128, 8, 128], BF16, "hT") for _ in range(2)]
            qk = [st.sb([128, 2, D], F32, "qk") for _ in range(2)]
            qkb = [st.sb([128, 2, D], BF16, "qkb") for _ in range(2)]
            vf = [st.sb([128, D], F32, "vf") for _ in range(2)]
            vb = [st.sb([128, D], BF16, "vb") for _ in range(2)]
            tmps = [st.sb([128, 256], F32, "rt") for _ in range(4)]
            qkT = [st.sb([128, 16, 128], BF16, "qkT") for _ in range(2)]
            tlist = list(range(NT)) if upto >= 4.5 else [0, NPT]
            for t in tlist:
                is_s = (t == NPT)
                i = t % 2
                x, hT = xt[i], hTs[i]
                src, srcb = x_src(t, "b")
                P.dma("sp", x.ap[:], src, reads=[srcb], writes=[x])
                nctx.run(x, x.ap[:], hT, hT.ap[:], 1, 0, is_s)
                for cb in range(6):
                    pq = PS[1 + cb % 4]
                    for kc in range(8):
                        P.op("pe", lambda e, pq=pq, kc=kc, cb=cb, hT=hT: e.matmul(out=pq.ap[:], lhsT=hT.ap[:, kc, :], rhs=wqkv.ap[:, kc, cb * 512:(cb + 1) * 512], start=(kc == 0), stop=(kc == 7)),
                             reads=[hT, wqkv], writes=[pq])
                    if cb < 4:
                        which, half_ = cb // 2, cb % 2
                        dst = qk[i].ap[:, which, half_ * 512:(half_ + 1) * 512].rearrange("p (g d) -> p g d", d=64)
                        rope_apply(pq, pq.ap[:].rearrange("p (g d) -> p g d", d=64), qk[i], dst, 8, 32, cosT, sinT, t, tmps)
                    else:
                        sl = slice((cb - 4) * 512, (cb - 3) * 512)
                        P.op("act", lambda e, pq=pq, sl=sl, i=i: e.activation(out=vf[i].ap[:, sl], in_=pq.ap[:], func=AF.Copy), reads=[pq], pwrites=[vf[i]])
                        P.op("dve", lambda e, pq=pq, sl=sl, i=i: e.tensor_copy(out=vb[i].ap[:, sl], in_=vf[i].ap[:, sl]), reads=[vf[i]], pwrites=[vb[i]])
                if not is_s:
                    P.dma("sp", O["dkp"][t * 128:(t + 1) * 128, :], qk[i].ap[:, 1, :], reads=[qk[i]], pwrites=[out_bufs["dkp"]])
                    P.dma("sp", O["dvp"][t * 128:(t + 1) * 128, :], vf[i].ap[:], reads=[vf[i]], pwrites=[out_bufs["dvp"]])
                else:
                    P.dma("sp", O["dks"], qk[i].ap[:, 1, :], reads=[qk[i]], pwrites=[out_bufs["dks"]])
                    P.dma("sp", O["dvs"], vf[i].ap[:], reads=[vf[i]], pwrites=[out_bufs["dvs"]])
                P.dma("sp", V_d[t * 128:(t + 1) * 128, :], vb[i].ap[:], reads=[vb[i]], pwrites=[V_b])
                P.op("act", lambda e, i=i: e.activation(out=qkb[i].ap[:], in_=qk[i].ap[:], func=AF.Copy), reads=[qk[i]], writes=[qkb[i]])
                for which in range(2):
                    pt = PS[5 + which]
                    ptv = pt.ap[:].bitcast(BF16).rearrange("p (h t) -> p h t", t=128)
                    for h in range(8):
                        P.op("pe", lambda e, ptv=ptv, h=h, which=which, i=i: e.transpose(out=ptv[:, h, :], in_=qkb[i].ap[:, which, h * 128:(h + 1) * 128], identity=ident_b.ap[:]),
                             reads=[qkb[i], ident_b], writes=[pt])
                    eng = "act" if which == 0 else "dve"
                    if eng == "act":
                        P.op("act", lambda e, ptv=ptv, which=which, i=i: e.activation(out=qkT[i].ap[:, which * 8:(which + 1) * 8, :], in_=ptv, func=AF.Copy), reads=[pt], pwrites=[qkT[i]])
                    else:
                        P.op("dve", lambda e, ptv=ptv, which=which, i=i: e.tensor_copy(out=qkT[i].ap[:, which * 8:(which + 1) * 8, :], in_=ptv), reads=[pt], pwrites=[qkT[i]])
                P.dma("sp", QT_d[:, :, t * 128:(t + 1) * 128].rearrange("h p t -> p h t"), qkT[i].ap[:, 0:8, :], reads=[qkT[i]], pwrites=[QT_b])
                P.dma("sp", KT_d[:, :, t * 128:(t + 1) * 128].rearrange("h p t -> p h t"), qkT[i].ap[:, 8:16, :], reads=[qkT[i]], pwrites=[KT_b])

        if en(5):
          with Stage(P, "l1b") as st:
            lv = st.sb([128, 4, 64], F32, "lv")
            for k_i, nm in enumerate(["diff_lambda_q1", "diff_lambda_k1", "diff_lambda_q2", "diff_lambda_k2"]):
                P.dma("sp", lv.ap[:, k_i, :], W[nm].partition_broadcast(128), pwrites=[lv])
            lsc = st.sb([128, 8], F32, "lsc")
            lpr = st.sb([128, 2, 64], F32, "lpr")
            P.op("dve", lambda e: e.tensor_tensor(out=lpr.ap[:, 0, :], in0=lv.ap[:, 0, :], in1=lv.ap[:, 1, :], op=ALU.mult), reads=[lv], pwrites=[lpr])
            P.op("dve", lambda e: e.tensor_tensor(out=lpr.ap[:, 1, :], in0=lv.ap[:, 2, :], in1=lv.ap[:, 3, :], op=ALU.mult), reads=[lv], pwrites=[lpr])
            P.op("dve", lambda e: e.tensor_reduce(out=lsc.ap[:, 0:2], in_=lpr.ap[:], axis=mybir.AxisListType.X, op=ALU.add), reads=[lpr], writes=[lsc])
            P.op("act", lambda e: e.activation(out=lsc.ap[:, 2:4], in_=lsc.ap[:, 0:2], func=AF.Exp), reads=[lsc], writes=[lsc])
            P.op("dve", lambda e: e.tensor_tensor(out=lsc.ap[:, 4:5], in0=lsc.ap[:, 3:4], in1=lsc.ap[:, 2:3], op=ALU.subtract), reads=[lsc], writes=[lsc])
            P.op("dve", lambda e: e.tensor_scalar(out=lsc.ap[:, 5:6], in0=lsc.ap[:, 4:5], scalar1=-LAMBDA_INIT, scalar2=None, op0=ALU.add), reads=[lsc], writes=[lsc])
            neglam = lsc.ap[:, 5:6]
            SC = 64 ** -0.5

            KT = [st.sb([128, SEQ], BF16, "KT") for _ in range(2)]
            QT = [st.sb([128, SEQ], BF16, "QT") for _ in range(2)]
            Vh = [st.sb([128, NPT, 128], BF16, "Vh") for _ in range(2)]
            PT = [[st.sb([128, 512], BF16, "PT") for _ in range(3)] for _ in range(2)]
            R = [st.sb([128, 512], F32, "R") for _ in range(2)]
            o12 = [st.sb([128, 512], F32, "o12") for _ in range(2)]
            ob = [st.sb([128, 512], BF16, "ob") for _ in range(2)]
            Sb = [[PS[0], PS[1]], [PS[2], PS[3]]]
            Ob = [[PS[4], PS[5]], [PS[6], PS[7]]]
            acc = [[st.sb([128, 512], F32, "acc") for _ in range(2)] for _ in range(2)]
            oun = [st.sb([128, 512], F32, "oun") for _ in range(2)]
            aeng = ["dve", "pool"]
            nheads = 8 if upto >= 5.5 else 1
            nQ = 16 if upto >= 5.5 else 2
            if _os.environ.get('KD_SKIP_L1B'):
                nheads = 0
            for h in range(nheads):
                sl_ = h % 2
                kt, qt, vh = KT[sl_], QT[sl_], Vh[sl_]
                for c4 in range(4):
                    cs = slice(c4 * 2048, (c4 + 1) * 2048)
                    P.dma("sp", kt.ap[:, cs], KT_d[h, :, cs], reads=[KT_b], pwrites=[kt])
                    P.dma("sp", qt.ap[:, cs], QT_d[h, :, cs], reads=[QT_b], pwrites=[qt])
                    bs = slice(c4 * 16, (c4 + 1) * 16)
                    P.dma("sp", vh.ap[:, bs, :], V_d[c4 * 2048:(c4 + 1) * 2048, h * 128:(h + 1) * 128].rearrange("(b p) e -> p b e", p=128), reads=[V_b], pwrites=[vh])
                for Q in range(nQ):
                    blocks = [(kb, 0, False) for kb in range(4 * Q)] + [(4 * Q + i_, 128 * i_, True) for i_ in range(4)]
                    n = len(blocks)
                    qp = Q % 2
                    for c in range(2):
                        P.op(aeng[c], lambda e, c=c, qp=qp: e.memset(acc[c][qp].ap[:], 0.0), writes=[acc[c][qp]])

                    def issue_S(idx, Q=Q, kt=kt, qt=qt, blocks=blocks):
                        kb, col0, diag = blocks[idx]
                        for c in range(2):
                            sb_ = Sb[c][idx % 2]
                            rs = slice(c * 64, (c + 1) * 64)
                            P.op("pe", lambda e, sb_=sb_, rs=rs, kb=kb, col0=col0: e.matmul(
                                out=sb_.ap[:, col0:512], lhsT=kt.ap[rs, kb * 128:(kb + 1) * 128], rhs=qt.ap[rs, Q * 512 + col0:(Q + 1) * 512], start=True, stop=True),
                                reads=[kt, qt], writes=[sb_])

                    issue_S(0)
                    for idx in range(n):
                        kb, col0, diag = blocks[idx]
                        for c in range(2):
                            sb_, pt = Sb[c][idx % 2], PT[c][idx % 3]
                            if not diag:
                                P.op("act", lambda e, sb_=sb_, pt=pt: e.activation(out=pt.ap[:], in_=sb_.ap[:], func=AF.Exp, scale=SC), reads=[sb_], writes=[pt])
                            else:
                                P.op("act", lambda e, sb_=sb_, pt=pt, col0=col0: e.activation(out=pt.ap[0:64, col0:512], in_=sb_.ap[0:64, col0:512], func=AF.Exp, scale=SC), reads=[sb_], writes=[pt])
                                P.op("act", lambda e, sb_=sb_, pt=pt, col0=col0: e.activation(out=pt.ap[64:128, col0:col0 + 64], in_=sb_.ap[64:128, col0:col0 + 64], func=AF.Copy, scale=0.0), reads=[sb_], pwrites=[pt])
                                P.op("act", lambda e, sb_=sb_, pt=pt, col0=col0: e.activation(out=pt.ap[64:128, col0 + 64:512], in_=sb_.ap[64:128, col0 + 64:512], func=AF.Exp, scale=SC), reads=[sb_], pwrites=[pt])
                        if idx + 1 < n:
                            issue_S(idx + 1)
                        for c in range(2):
                            pt = PT[c][idx % 3]
                            ob_ = Ob[c][qp]
                            P.op("pe", lambda e, ob_=ob_, pt=pt, kb=kb, col0=col0, idx=idx, diag=diag, vh=vh: e.matmul(
                                out=ob_.ap[:, col0:512], lhsT=vh.ap[:, kb, :], rhs=pt.ap[:, col0:512], start=(idx == 0), stop=diag, skip_group_check=True),
                                reads=[vh, pt], writes=[ob_])
                            ac = acc[c][qp]
                            P.op(aeng[c], lambda e, ac=ac, pt=pt, col0=col0: e.tensor_tensor(out=ac.ap[:, col0:512], in0=ac.ap[:, col0:512], in1=pt.ap[:, col0:512], op=ALU.add),
                                 reads=[ac, pt], writes=[ac])
                    for c in range(2):
                        ob_, ac = Ob[c][qp], acc[c][qp]
                        P.op("dve", lambda e, c=c, ob_=ob_: e.tensor_copy(out=oun[c].ap[:], in_=ob_.ap[:]), reads=[ob_], writes=[oun[c]])
                        P.op("pe", lambda e, ob_=ob_, ac=ac: e.matmul(out=ob_.ap[:], lhsT=ones_f.ap[:], rhs=ac.ap[:], start=True, stop=True), reads=[ones_f, ac], writes=[ob_])
                        P.op("dve", lambda e, c=c, ob_=ob_: e.reciprocal(out=R[c].ap[:], in_=ob_.ap[:]), reads=[ob_], writes=[R[c]])
                        P.op("dve", lambda e, c=c: e.tensor_tensor(out=o12[c].ap[:], in0=oun[c].ap[:], in1=R[c].ap[:], op=ALU.mult), reads=[oun[c], R[c]], writes=[o12[c]])
                    obq = ob[Q % 2]
                    P.op("dve", lambda e, obq=obq: e.scalar_tensor_tensor(out=obq.ap[:], in0=o12[1].ap[:], scalar=neglam, in1=o12[0].ap[:], op0=ALU.mult, op1=ALU.add),
                         reads=[o12[0], o12[1], lsc], writes=[obq])
                    P.dma("sp", OT_d[h, :, Q * 512:(Q + 1) * 512], obq.ap[:], reads=[obq], pwrites=[OT_b])

          with Stage(P, "l1s") as st:
            lv = st.sb([128, 4, 64], F32, "lv")
            for k_i, nm in enumerate(["diff_lambda_q1", "diff_lambda_k1", "diff_lambda_q2", "diff_lambda_k2"]):
                P.dma("sp", lv.ap[:, k_i, :], W[nm].partition_broadcast(128), pwrites=[lv])
            lsc = st.sb([128, 8], F32, "lsc")
            lpr = st.sb([128, 2, 64], F32, "lpr")
            P.op("dve", lambda e: e.tensor_tensor(out=lpr.ap[:, 0, :], in0=lv.ap[:, 0, :], in1=lv.ap[:, 1, :], op=ALU.mult), reads=[lv], pwrites=[lpr])
            P.op("dve", lambda e: e.tensor_tensor(out=lpr.ap[:, 1, :], in0=lv.ap[:, 2, :], in1=lv.ap[:, 3, :], op=ALU.mult), reads=[lv], pwrites=[lpr])
            P.op("dve", lambda e: e.tensor_reduce(out=lsc.ap[:, 0:2], in_=lpr.ap[:], axis=mybir.AxisListType.X, op=ALU.add), reads=[lpr], writes=[lsc])
            P.op("act", lambda e: e.activation(out=lsc.ap[:, 2:4], in_=lsc.ap[:, 0:2], func=AF.Exp), reads=[lsc], writes=[lsc])
            P.op("dve", lambda e: e.tensor_tensor(out=lsc.ap[:, 4:5], in0=lsc.ap[:, 3:4], in1=lsc.ap[:, 2:3], op=ALU.subtract), reads=[lsc], writes=[lsc])
            P.op("dve", lambda e: e.tensor_scalar(out=lsc.ap[:, 5:6], in0=lsc.ap[:, 4:5], scalar1=-LAMBDA_INIT, scalar2=None, op0=ALU.add), reads=[lsc], writes=[lsc])
            neglam = lsc.ap[:, 5:6]
            SC = 64 ** -0.5
            zer = st.sb([128, 512], BF16, "zer")
            P.op("pool", lambda e: e.memset(zer.ap[:], 0.0), writes=[zer])
            Kt = [st.sb([128, D], BF16, "Kt") for _ in range(2)]
            Vt = [st.sb([128, D], BF16, "Vt") for _ in range(2)]
            KTb = [st.sb([128, 8, 128], BF16, "KTb") for _ in range(2)]
            QTs = st.sb([128, 8, 64], BF16, "QTs")
            PTs = [st.sb([128, 2, 512], BF16, "PTs") for _ in range(2)]
            Rr = st.sb([128, 1024], F32, "Rr")
            oo = st.sb([128, 1024], F32, "oo")
            obs = st.sb([128, 8, 64], BF16, "obs")
            Sbk, Obk, Lbk, Tbk = [PS[0], PS[1]], [PS[2], PS[3]], [PS[4], PS[5]], PS[6]
            NKB = PAST // 128
            for s in range(0 if _os.environ.get('KD_SKIP_L1S') else 2):
                tok0 = NPT * 128 + s * 64
                P.dma("sp", QTs.ap[:], QT_d[:, :, tok0:tok0 + 64].rearrange("h p t -> p h t"), reads=[QT_b], writes=[QTs])
                for b in Obk + Lbk:
                    P.op("pe", lambda e, b=b: e.matmul(out=b.ap[:], lhsT=zer.ap[:, 0:128], rhs=zer.ap[:], start=True, stop=False, skip_group_check=True), reads=[zer], writes=[b])
                _part = int(_os.environ.get('KD_L1S_PART', '9'))
                _kbs = [int(v) for v in _os.environ.get('KD_L1S_KBS', '').split(',')] if _os.environ.get('KD_L1S_KBS') else list(range(NKB + 1))
                for kb in _kbs:
                    i = kb % 2
                    last = (kb == NKB)
                    nk = 64 if last else 128
                    ktb = KTb[i]
                    if not last:
                        P.dma("pool", Kt[i].ap[:], I["cdk"][s, kb * 128:(kb + 1) * 128, :], writes=[Kt[i]])
                        P.dma("pool", Vt[i].ap[:], I["cdv"][s, kb * 128:(kb + 1) * 128, :], writes=[Vt[i]])
                        tv = Tbk.ap[:].bitcast(BF16).rearrange("p (h t) -> p h t", t=128)
                        for h in range(8):
                            P.op("pe", lambda e, tv=tv, h=h, i=i: e.transpose(out=tv[:, h, :], in_=Kt[i].ap[:, h * 128:(h + 1) * 128], identity=ident_b.ap[:]), reads=[Kt[i], ident_b], writes=[Tbk])
                        P.op("dve", lambda e, tv=tv, ktb=ktb: e.tensor_copy(out=ktb.ap[:], in_=tv), reads=[Tbk], writes=[ktb])
                    else:
                        P.dma("sp", ktb.ap[:, :, 0:64], KT_d[:, :, tok0:tok0 + 64].rearrange("h p t -> p h t"), reads=[KT_b], writes=[ktb])
                        P.dma("pool", Vt[i].ap[0:64, :], V_d[tok0:tok0 + 64, :], reads=[V_b], writes=[Vt[i]])
                    if _part < 2:
                        continue
                    for h in range(8):
                        for c in range(2):
                            sb_ = Sbk[c]
                            rs = slice(c * 64, (c + 1) * 64)
                            P.op("pe", lambda e, sb_=sb_, rs=rs, h=h, nk=nk, ktb=ktb: e.matmul(
                                out=sb_.ap[0:nk, h * 64:h * 64 + 64], lhsT=ktb.ap[rs, h, 0:nk], rhs=QTs.ap[rs, h, :], start=True, stop=True),
                                reads=[ktb, QTs], writes=[sb_])
                    pts = PTs[i]
                    for j in range(2):
                        P.op("act", lambda e, j=j, nk=nk, pts=pts: e.activation(out=pts.ap[0:nk, j, :], in_=Sbk[j].ap[0:nk, :], func=AF.Exp, scale=SC), reads=[Sbk[j]], pwrites=[pts])
                    if _part < 3:
                        continue
                    for h in range(8):
                        for c in range(2):
                            j, cs = c, slice(h * 64, h * 64 + 64)
                            P.op("pe", lambda e, j=j, cs=cs, h=h, nk=nk, i=i, pts=pts: e.matmul(
                                out=Obk[j].ap[:, cs], lhsT=Vt[i].ap[0:nk, h * 128:(h + 1) * 128], rhs=pts.ap[0:nk, j, cs], start=False, stop=True, skip_group_check=True),
                                reads=[Vt[i], pts], writes=[Obk[j]])
                            P.op("pe", lambda e, j=j, cs=cs, nk=nk, pts=pts: e.matmul(
                                out=Lbk[j].ap[:, cs], lhsT=ones_b.ap[0:nk, :], rhs=pts.ap[0:nk, j, cs], start=False, stop=True, skip_group_check=True),
                                reads=[ones_b, pts], writes=[Lbk[j]])
                if _part < 4:
                    continue
                for j in range(2):
                    js = slice(j * 512, (j + 1) * 512)
                    P.op("dve", lambda e, j=j, js=js: e.reciprocal(out=Rr.ap[:, js], in_=Lbk[j].ap[:]), reads=[Lbk[j]], pwrites=[Rr])
                    P.op("dve", lambda e, j=j, js=js: e.tensor_tensor(out=oo.ap[:, js], in0=Obk[j].ap[:], in1=Rr.ap[:, js], op=ALU.mult), reads=[Obk[j], Rr], pwrites=[oo])
                ov = oo.ap[:].rearrange("p (c h q) -> p c h q", c=2, q=64)
                P.op("dve", lambda e, ov=ov: e.scalar_tensor_tensor(out=obs.ap[:], in0=ov[:, 1, :, :], scalar=neglam, in1=ov[:, 0, :, :], op0=ALU.mult, op1=ALU.add),
                     reads=[oo, lsc], writes=[obs])
                P.dma("sp", OT_d[:, :, tok0:tok0 + 64].rearrange("h p t -> p h t"), obs.ap[:], reads=[obs], pwrites=[OT_b])

        def attn_out_stage(name, l, OT_src, OT_srcb, w_o_name, nh, e_dim, subnorm, gsub_name, src_w, dst_w, tile_ok):
          with Stage(P, name) as st:
            nk = nh * e_dim // 128
            wo = st.sb([128, nk, D], BF16, "wo")
            if subnorm:
                gsub = st.sb([128, 1], F32, "gsub")
                P.dma("sp", gsub.ap[:], W[gsub_name].rearrange("(p o) -> p o", o=1), writes=[gsub])
                wstg = [st.sb([128, D], F32, "wstg") for _ in range(2)]
                for kc in range(nk):
                    ws = wstg[kc % 2]
                    P.dma("sp", ws.ap[:], W[w_o_name][kc * 128:(kc + 1) * 128, :], writes=[ws])
                    P.op("dve", lambda e, ws=ws, kc=kc: e.tensor_scalar(out=wo.ap[:, kc, :], in0=ws.ap[:], scalar1=gsub.ap[:, 0:1], scalar2=(1.0 - LAMBDA_INIT), op0=ALU.mult, op1=ALU.mult),
                         reads=[ws, gsub], pwrites=[wo])
            else:
                for kc in range(nk):
                    P.dma("pool", wo.ap[:, kc, :], W[w_o_name][kc * 128:(kc + 1) * 128, :], pwrites=[wo])
            gate = Gate(st, l, 0)
            xt = [st.sb([128, D], F32, "x") for _ in range(2)]
            ot = [st.sb([128, nk, 128], BF16, "ot") for _ in range(2)]
            sq = st.sb([128, nk, 128], BF16, "sq")
            rs_ = st.sb([128, nk * 128], F32, "rs")
            on = [st.sb([128, nk, 128], BF16, "on") for _ in range(2)]
            for t in range(NT):
                if not tile_ok(t):
                    continue
                is_s = (t == NPT)
                i = t % 2
                x, o_ = xt[i], ot[i]
                src, srcb = x_src(t, src_w)
                P.dma("sp", x.ap[:], src, reads=[srcb], writes=[x])
                P.dma("sp", o_.ap[:], OT_src[:, :, t * 128:(t + 1) * 128].rearrange("h p t -> p h t"), reads=[OT_srcb], writes=[o_])
                if subnorm:
                    P.op("act", lambda e, o_=o_: e.activation(out=sq.ap[:], in_=o_.ap[:], func=AF.Square), reads=[o_], writes=[sq])
                    for j in range(2):
                        P.op("pe", lambda e, j=j: e.matmul(out=PS[j].ap[:], lhsT=ones_b.ap[:], rhs=sq.ap[:, j * 4:(j + 1) * 4, :].rearrange("p h t -> p (h t)"), start=True, stop=True),
                             reads=[ones_b, sq], writes=[PS[j]])
                        js = slice(j * 512, (j + 1) * 512)
                        P.op("act", lambda e, j=j, js=js: e.activation(out=rs_.ap[:, js], in_=PS[j].ap[:], func=AF.Sqrt, scale=1.0 / 128, bias=EPS), reads=[PS[j]], pwrites=[rs_])
                    P.op("dve", lambda e: e.reciprocal(out=rs_.ap[:], in_=rs_.ap[:]), reads=[rs_], writes=[rs_])
                    lhs = on[i]
                    P.op("dve", lambda e, o_=o_, lhs=lhs: e.tensor_tensor(out=lhs.ap[:].rearrange("p h t -> p (h t)"), in0=o_.ap[:].rearrange("p h t -> p (h t)"), in1=rs_.ap[:], op=ALU.mult),
                         reads=[o_, rs_], writes=[lhs])
                else:
                    lhs = o_
                g = gate.get(is_s)
                for cbk in range(2):
                    po = PS[2 + cbk + 2 * (t % 2)]
                    for kc in range(nk):
                        P.op("pe", lambda e, po=po, kc=kc, cbk=cbk, lhs=lhs: e.matmul(out=po.ap[:], lhsT=lhs.ap[:, kc, :], rhs=wo.ap[:, kc, cbk * 512:(cbk + 1) * 512], start=(kc == 0), stop=(kc == nk - 1)),
                             reads=[lhs, wo], writes=[po])
                    sl = slice(cbk * 512, (cbk + 1) * 512)
                    P.op("dve", lambda e, po=po, sl=sl, g=g: e.tensor_tensor(out=po.ap[:], in0=po.ap[:], in1=g.ap[:, sl], op=ALU.mult), reads=[po, g], writes=[po])
                    P.op("dve", lambda e, po=po, sl=sl, x=x: e.tensor_tensor(out=x.ap[:, sl], in0=po.ap[:], in1=x.ap[:, sl], op=ALU.add), reads=[po, x], pwrites=[x])
                dst, dstb = x_dst(t, dst_w)
                P.dma("sp", dst, x.ap[:], reads=[x], pwrites=[dstb])

        if en(6):
            attn_out_stage("l1c", 1, OT_d, OT_b, "diff_w_o", 8, 128, True, "diff_g_sub", "b", "a", lambda t: True)
        if en(7):
            mlp_stage(1, "a", "b")

        QA_d = dscr("QA_d", [16, 128, NTOK], BF16)
        QR_d = dscr("QR_d", [16, 32, NTOK], BF16)
        CK_d = dscr("CK_d", [NTOK, 128], BF16)
        CKT_d = dscr("CKT_d", [128, NTOK], BF16)
        KRT_d = dscr("KRT_d", [32, NTOK], BF16)
        OT2_d = dscr("OT2_d", [8, 128, NTOK], BF16)
        QA_b, QR_b, CK_b, CKT_b, KRT_b, OT2_b = (Buf(QA_d, "QA_d"), Buf(QR_d, "QR_d"), Buf(CK_d, "CK_d"), Buf(CKT_d, "CKT_d"),
                                                 Buf(KRT_d, "KRT_d"), Buf(OT2_d, "OT2_d"))
        MSC = 96 ** -0.5

        if en(8):
          with Stage(P, "l2a") as st:
            cos2, sin2 = rope_tables(st, 16)
            wdq = st.sb([128, 8, 416], BF16, "wdq")
            for kc in range(8):
                P.dma("pool", wdq.ap[:, kc, 0:256], W["mla_w_dq"][kc * 128:(kc + 1) * 128, :], pwrites=[wdq])
                P.dma("pool", wdq.ap[:, kc, 256:416], W["mla_w_dkv"][kc * 128:(kc + 1) * 128, :], pwrites=[wdq])
            gq = st.sb([128, 2], F32, "gq")
            P.dma("sp", gq.ap[:], W["mla_g_q"].rearrange("(c p) -> p c", p=128), writes=[gq], allow_slow_non_contiguous=True)
            wuq = st.sb([128, 2, 1536], BF16, "wuq")
            wst = [st.sb([128, 1536], F32, "wst") for _ in range(2)]
            for kc in range(2):
                P.dma("sp", wst[kc].ap[:], W["mla_w_uq"][kc * 128:(kc + 1) * 128, :], writes=[wst[kc]])
                P.op("dve", lambda e, kc=kc: e.tensor_scalar(out=wuq.ap[:, kc, :], in0=wst[kc].ap[:], scalar1=gq.ap[:, kc:kc + 1], scalar2=None, op0=ALU.mult), reads=[wst[kc], gq], pwrites=[wuq])
            wuk = st.sb([128, D], BF16, "wuk")
            P.dma("pool", wuk.ap[:], W["mla_w_uk"], writes=[wuk])
            wukT = st.sb([64, 16, 128], BF16, "wukT")
            for g4 in range(2):
                pb = PS[4 + g4]
                pv = pb.ap[:].bitcast(BF16).rearrange("p (h t) -> p h t", t=128)
                for u in range(8):
                    h = g4 * 8 + u
                    P.op("pe", lambda e, pv=pv, u=u, h=h: e.transpose(out=pv[0:64, u, :], in_=wuk.ap[:, h * 64:(h + 1) * 64], identity=ident_b.ap[:]), reads=[wuk, ident_b], writes=[pb])
                P.op("dve", lambda e, pv=pv, g4=g4: e.tensor_copy(out=wukT.ap[:, g4 * 8:(g4 + 1) * 8, :], in_=pv[0:64, :, :]), reads=[pb], pwrites=[wukT])
            gkv = st.sb([128, 128], F32, "gkv")
            P.dma("sp", gkv.ap[:], W["mla_g_kv"].partition_broadcast(128), writes=[gkv])
            xt = [st.sb([128, D], F32, "x") for _ in range(2)]
            nctx = NormCtx(st, PS[0], PS[7])
            hTs = [st.sb([128, 8, 128], BF16, "hT") for _ in range(2)]
            st4 = st.sb([128, 8], F32, "st4")
            jk = st.sb([128, 256], BF16, "jk")
            qn = st.sb([128, 256], BF16, "qn")
            qnT = st.sb([128, 2, 128], BF16, "qnT")
            qb = st.sb([128, 16, 96], BF16, "qb")
            qT = st.sb([96, 16, 128], BF16, "qT")
            qa = [st.sb([128, 16, 128], BF16, "qa") for _ in range(2)]
            ckf = [st.sb([128, 128], F32, "ckf") for _ in range(2)]
            ckb = [st.sb([128, 128], BF16, "ckb") for _ in range(2)]
            krf = [st.sb([128, 32], F32, "krf") for _ in range(2)]
            krb = [st.sb([128, 32], BF16, "krb") for _ in range(2)]
            ckT = [st.sb([128, 128], BF16, "ckT") for _ in range(2)]
            krT = [st.sb([32, 128], BF16, "krT") for _ in range(2)]
            tmps = [st.sb([128, 256], F32, "rt") for _ in range(4)]
            for t in range(NT):
                is_s = (t == NPT)
                i = t % 2
                x, hT = xt[i], hTs[i]
                src, srcb = x_src(t, "b")
                P.dma("sp", x.ap[:], src, reads=[srcb], writes=[x])
                nctx.run(x, x.ap[:], hT, hT.ap[:], 2, 0, is_s)
                pp = PS[1]
                for kc in range(8):
                    P.op("pe", lambda e, kc=kc, hT=hT: e.matmul(out=pp.ap[:, 0:416], lhsT=hT.ap[:, kc, :], rhs=wdq.ap[:, kc, :], start=(kc == 0), stop=(kc == 7)), reads=[hT, wdq], writes=[pp])
                P.op("act", lambda e: e.activation(out=jk.ap[:], in_=pp.ap[:, 0:256], func=AF.Square, accum_out=st4.ap[:, 0:1]), reads=[pp], writes=[jk, st4])
                P.op("act", lambda e: e.activation(out=st4.ap[:, 1:2], in_=st4.ap[:, 0:1], func=AF.Sqrt, scale=1.0 / 256, bias=EPS), reads=[st4], writes=[st4])
                P.op("dve", lambda e: e.reciprocal(out=st4.ap[:, 2:3], in_=st4.ap[:, 1:2]), reads=[st4], writes=[st4])
                P.op("act", lambda e: e.activation(out=qn.ap[:], in_=pp.ap[:, 0:256], func=AF.Copy, scale=st4.ap[:, 2:3]), reads=[pp, st4], writes=[qn])
                P.op("act", lambda e: e.activation(out=jk.ap[:, 0:128], in_=pp.ap[:, 256:384], func=AF.Square, accum_out=st4.ap[:, 4:5]), reads=[pp], writes=[jk, st4])
                P.op("act", lambda e: e.activation(out=st4.ap[:, 5:6], in_=st4.ap[:, 4:5], func=AF.Sqrt, scale=1.0 / 128, bias=EPS), reads=[st4], writes=[st4])
                P.op("dve", lambda e: e.reciprocal(out=st4.ap[:, 6:7], in_=st4.ap[:, 5:6]), reads=[st4], writes=[st4])
                P.op("dve", lambda e, i=i: e.scalar_tensor_tensor(out=ckf[i].ap[:], in0=pp.ap[:, 256:384], scalar=st4.ap[:, 6:7], in1=gkv.ap[:], op0=ALU.mult, op1=ALU.mult),
                     reads=[pp, st4, gkv], writes=[ckf[i]])
                P.op("dve", lambda e, i=i: e.tensor_copy(out=ckb[i].ap[:], in_=ckf[i].ap[:]), reads=[ckf[i]], writes=[ckb[i]])
                rope_apply(pp, pp.ap[:, 384:416].rearrange("p (g d) -> p g d", g=1), krf[i], krf[i].ap[:].rearrange("p (g d) -> p g d", g=1), 1, 16, cos2, sin2, t, tmps)
                P.op("dve", lambda e, i=i: e.tensor_copy(out=krb[i].ap[:], in_=krf[i].ap[:]), reads=[krf[i]], writes=[krb[i]])
                if not is_s:
                    P.dma("sp", O["ckvp"][t * 128:(t + 1) * 128, :], ckf[i].ap[:], reads=[ckf[i]], pwrites=[out_bufs["ckvp"]])
                    P.dma("sp", O["krp"][t * 128:(t + 1) * 128, :], krf[i].ap[:], reads=[krf[i]], pwrites=[out_bufs["krp"]])
                else:
                    P.dma("sp", O["ckvs"], ckf[i].ap[:], reads=[ckf[i]], pwrites=[out_bufs["ckvs"]])
                    P.dma("sp", O["krs"], krf[i].ap[:], reads=[krf[i]], pwrites=[out_bufs["krs"]])
                P.dma("sp", CK_d[t * 128:(t + 1) * 128, :], ckb[i].ap[:], reads=[ckb[i]], pwrites=[CK_b])
                p6 = PS[6]
                p6v = p6.ap[:].bitcast(BF16)
                P.op("pe", lambda e, i=i: e.transpose(out=p6v[:, 0:128], in_=ckb[i].ap[:], identity=ident_b.ap[:]), reads=[ckb[i], ident_b], writes=[p6])
                P.op("pe", lambda e, i=i: e.transpose(out=p6v[0:32, 128:256], in_=krb[i].ap[:], identity=ident_b.ap[:]), reads=[krb[i], ident_b], writes=[p6])
                for kc in range(2):
                    P.op("pe", lambda e, kc=kc: e.transpose(out=p6v[:, 256 + kc * 128:384 + kc * 128], in_=qn.ap[:, kc * 128:(kc + 1) * 128], identity=ident_b.ap[:]), reads=[qn, ident_b], writes=[p6])
                P.op("dve", lambda e, i=i: e.tensor_copy(out=ckT[i].ap[:], in_=p6v[:, 0:128]), reads=[p6], writes=[ckT[i]])
                P.op("dve", lambda e, i=i: e.tensor_copy(out=krT[i].ap[:], in_=p6v[0:32, 128:256]), reads=[p6], writes=[krT[i]])
                P.op("dve", lambda e: e.tensor_copy(out=qnT.ap[:], in_=p6v[:, 256:512].rearrange("p (k t) -> p k t", t=128)), reads=[p6], writes=[qnT])
                P.dma("sp", CKT_d[:, t * 128:(t + 1) * 128], ckT[i].ap[:], reads=[ckT[i]], pwrites=[CKT_b])
                P.dma("sp", KRT_d[:, t * 128:(t + 1) * 128], krT[i].ap[:], reads=[krT[i]], pwrites=[KRT_b])
                for qblk in range(4):
                    pq = PS[2 + qblk % 2]
                    for kc in range(2):
                        P.op("pe", lambda e, pq=pq, kc=kc, qblk=qblk: e.matmul(out=pq.ap[:, 0:384], lhsT=qnT.ap[:, kc, :], rhs=wuq.ap[:, kc, qblk * 384:(qblk + 1) * 384], start=(kc == 0), stop=(kc == 1)),
                             reads=[qnT, wuq], writes=[pq])
                    pqv = pq.ap[:, 0:384].rearrange("p (h d) -> p h d", d=96)
                    hs = slice(qblk * 4, (qblk + 1) * 4)
                    P.op("act", lambda e, pqv=pqv, hs=hs: e.activation(out=qb.ap[:, hs, 0:64], in_=pqv[:, :, 0:64], func=AF.Copy), reads=[pq], pwrites=[qb])
                    rope_apply(pq, pqv[:, :, 64:96], qb, qb.ap[:, hs, 64:96], 4, 16, cos2, sin2, t, tmps)
                for g4 in range(2):
                    pb = PS[4 + g4]
                    pv = pb.ap[:].bitcast(BF16).rearrange("p (h t) -> p h t", t=128)
                    for u in range(8):
                        h = g4 * 8 + u
                        P.op("pe", lambda e, pv=pv, u=u, h=h: e.transpose(out=pv[0:96, u, :], in_=qb.ap[:, h, :], identity=ident_b.ap[:]), reads=[qb, ident_b], writes=[pb])
                    if g4 == 0:
                        P.op("act", lambda e, pv=pv, g4=g4: e.activation(out=qT.ap[:, g4 * 8:(g4 + 1) * 8, :], in_=pv[0:96, :, :], func=AF.Copy), reads=[pb], pwrites=[qT])
                    else:
                        P.op("dve", lambda e, pv=pv, g4=g4: e.tensor_copy(out=qT.ap[:, g4 * 8:(g4 + 1) * 8, :], in_=pv[0:96, :, :]), reads=[pb], pwrites=[qT])
                P.dma("sp", QR_d[:, :, t * 128:(t + 1) * 128].rearrange("h p t -> p h t"), qT.ap[64:96, :, :], reads=[qT], pwrites=[QR_b])
                qa_ = qa[i]
                for g4 in range(4):
                    pa = PS[2 + g4 % 2]
                    for u in range(4):
                        h = g4 * 4 + u
                        P.op("pe", lambda e, pa=pa, u=u, h=h: e.matmul(out=pa.ap[:, u * 128:(u + 1) * 128], lhsT=wukT.ap[:, h, :], rhs=qT.ap[0:64, h, :], start=True, stop=True), reads=[wukT, qT], writes=[pa])
                    if g4 % 2 == 0:
                        P.op("act", lambda e, pa=pa, g4=g4, qa_=qa_: e.activation(out=qa_.ap[:, g4 * 4:(g4 + 1) * 4, :], in_=pa.ap[:].rearrange("p (u t) -> p u t", t=128), func=AF.Copy), reads=[pa], pwrites=[qa_])
                    else:
                        P.op("dve", lambda e, pa=pa, g4=g4, qa_=qa_: e.tensor_copy(out=qa_.ap[:, g4 * 4:(g4 + 1) * 4, :], in_=pa.ap[:].rearrange("p (u t) -> p u t", t=128)), reads=[pa], pwrites=[qa_])
                P.dma("sp", QA_d[:, :, t * 128:(t + 1) * 128].rearrange("h p t -> p h t"), qa_.ap[:], reads=[qa_], pwrites=[QA_b])

        if en(9):
          with Stage(P, "l2b") as st:
            wuv = st.sb([128, D], BF16, "wuv")
            P.dma("pool", wuv.ap[:], W["mla_w_uv"], writes=[wuv])
            CKT = st.sb([128, SEQ], BF16, "CKT")
            KRT = st.sb([128, SEQ], BF16, "KRT")
            CK = st.sb([128, NPT, 128], BF16, "CK")
            P.op("pool", lambda e: e.memset(KRT.ap[:], 0.0), writes=[KRT])
            for c4 in range(4):
                cs = slice(c4 * 2048, (c4 + 1) * 2048)
                P.dma("sp", CKT.ap[:, cs], CKT_d[:, cs], reads=[CKT_b], pwrites=[CKT])
                P.dma("sp", KRT.ap[0:32, cs], KRT_d[:, cs], reads=[KRT_b], pwrites=[KRT])
                P.dma("sp", CK.ap[:, c4 * 16:(c4 + 1) * 16, :], CK_d[c4 * 2048:(c4 + 1) * 2048, :].rearrange("(b p) e -> p b e", p=128), reads=[CK_b], pwrites=[CK])
            QA = [st.sb([128, SEQ], BF16, "QA") for _ in range(2)]
            QR = [st.sb([128, SEQ], BF16, "QR") for _ in range(2)]
            for b in QR:
                P.op("pool", lambda e, b=b: e.memset(b.ap[:], 0.0), writes=[b])
            PT = [st.sb([128, 512], BF16, "PT") for _ in range(4)]
            Rb = st.sb([128, 512], F32, "Rb")
            ol = st.sb([128, 512], BF16, "ol")
            ohb = [st.sb([64, 512], BF16, "ohb") for _ in range(2)]
            Sb, Obs, Lb, Eb = [PS[0], PS[1], PS[5], PS[6]], [PS[2], PS[3]], PS[7], PS[4]
            accs = [[st.sb([128, 512], F32, "acc") for _ in range(2)] for _ in range(2)]
            aeng = ["dve", "pool"]
            nheads = 16 if upto >= 9.5 else 1
            nQ = 16 if upto >= 9.5 else 2
            for h in range(nheads):
                qa_, qr_ = QA[h % 2], QR[h % 2]
                for c4 in range(4):
                    cs = slice(c4 * 2048, (c4 + 1) * 2048)
                    P.dma("sp", qa_.ap[:, cs], QA_d[h, :, cs], reads=[QA_b], pwrites=[qa_])
                    P.dma("sp", qr_.ap[0:32, cs], QR_d[h, :, cs], reads=[QR_b], pwrites=[qr_])
                for Q in range(nQ):
                    blocks = [(kb, 0, False) for kb in range(4 * Q)] + [(4 * Q + i_, 128 * i_, True) for i_ in range(4)]
                    n = len(blocks)
                    qp = Q % 2
                    Ob = Obs[qp]
                    for k_ in range(2):
                        P.op(aeng[k_], lambda e, k_=k_, qp=qp: e.memset(accs[k_][qp].ap[:], 0.0), writes=[accs[k_][qp]])

                    def issue_S(idx, Q=Q, qa_=qa_, qr_=qr_, blocks=blocks):
                        kb, col0, diag = blocks[idx]
                        sb_ = Sb[idx % 4]
                        P.op("pe", lambda e, sb_=sb_, kb=kb, col0=col0: e.matmul(out=sb_.ap[:, col0:512], lhsT=CKT.ap[:, kb * 128:(kb + 1) * 128], rhs=qa_.ap[:, Q * 512 + col0:(Q + 1) * 512], start=True, stop=False),
                             reads=[CKT, qa_], writes=[sb_])
                        P.op("pe", lambda e, sb_=sb_, kb=kb, col0=col0: e.matmul(out=sb_.ap[:, col0:512], lhsT=KRT.ap[:, kb * 128:(kb + 1) * 128], rhs=qr_.ap[:, Q * 512 + col0:(Q + 1) * 512], start=False, stop=True),
                             reads=[KRT, qr_], writes=[sb_])

                    issue_S(0)
                    if n > 1:
                        issue_S(1)
                    for idx in range(n):
                        kb, col0, diag = blocks[idx]
                        sb_, pt = Sb[idx % 4], PT[idx % 4]
                        if not diag:
                            P.op("act", lambda e, sb_=sb_, pt=pt: e.activation(out=pt.ap[:], in_=sb_.ap[:], func=AF.Exp, scale=MSC), reads=[sb_], writes=[pt])
                        else:
                            P.op("act", lambda e, sb_=sb_, pt=pt, col0=col0: e.activation(out=pt.ap[0:64, col0:512], in_=sb_.ap[0:64, col0:512], func=AF.Exp, scale=MSC), reads=[sb_], writes=[pt])
                            P.op("act", lambda e, sb_=sb_, pt=pt, col0=col0: e.activation(out=pt.ap[64:128, col0:col0 + 64], in_=sb_.ap[64:128, col0:col0 + 64], func=AF.Copy, scale=0.0), reads=[sb_], pwrites=[pt])
                            P.op("act", lambda e, sb_=sb_, pt=pt, col0=col0: e.activation(out=pt.ap[64:128, col0 + 64:512], in_=sb_.ap[64:128, col0 + 64:512], func=AF.Exp, scale=MSC), reads=[sb_], pwrites=[pt])
                        if idx + 2 < n:
                            issue_S(idx + 2)
                        P.op("pe", lambda e, Ob=Ob, pt=pt, kb=kb, col0=col0, idx=idx, diag=diag: e.matmul(out=Ob.ap[:, col0:512], lhsT=CK.ap[:, kb, :], rhs=pt.ap[:, col0:512], start=(idx == 0), stop=diag, skip_group_check=True),
                             reads=[CK, pt], writes=[Ob])
                        ac = accs[idx % 2][qp]
                        P.op(aeng[idx % 2], lambda e, ac=ac, pt=pt, col0=col0: e.tensor_tensor(out=ac.ap[:, col0:512], in0=ac.ap[:, col0:512], in1=pt.ap[:, col0:512], op=ALU.add),
                             reads=[ac, pt], writes=[ac])
                    a0, a1 = accs[0][qp], accs[1][qp]
                    P.op("dve", lambda e, a0=a0, a1=a1: e.tensor_tensor(out=a0.ap[:], in0=a0.ap[:], in1=a1.ap[:], op=ALU.add), reads=[a0, a1], writes=[a0])
                    P.op("pe", lambda e, a0=a0: e.matmul(out=Lb.ap[:], lhsT=ones_f.ap[:], rhs=a0.ap[:], start=True, stop=True), reads=[ones_f, a0], writes=[Lb])
                    P.op("dve", lambda e: e.reciprocal(out=Rb.ap[:], in_=Lb.ap[:]), reads=[Lb], writes=[Rb])
                    P.op("dve", lambda e, Ob=Ob: e.tensor_tensor(out=ol.ap[:], in0=Ob.ap[:], in1=Rb.ap[:], op=ALU.mult), reads=[Ob, Rb], writes=[ol])
                    P.op("pe", lambda e, h=h: e.matmul(out=Eb.ap[0:64, :], lhsT=wuv.ap[:, h * 64:(h + 1) * 64], rhs=ol.ap[:], start=True, stop=True), reads=[wuv, ol], writes=[Eb])
                    oh = ohb[Q % 2]
                    P.op("act", lambda e, oh=oh: e.activation(out=oh.ap[:], in_=Eb.ap[0:64, :], func=AF.Copy), reads=[Eb], writes=[oh])
                    P.dma("sp", OT2_d[h // 2, (h % 2) * 64:(h % 2) * 64 + 64, Q * 512:(Q + 1) * 512], oh.ap[:], reads=[oh], pwrites=[OT2_b])

          with Stage(P, "l2s") as st:
            wuv = st.sb([128, D], BF16, "wuv")
            P.dma("pool", wuv.ap[:], W["mla_w_uv"], writes=[wuv])
            zer = st.sb([128, 512], BF16, "zer")
            P.op("pool", lambda e: e.memset(zer.ap[:], 0.0), writes=[zer])
            ckr = [st.sb([128, 128], BF16, "ckr") for _ in range(2)]
            krr = [st.sb([128, 32], BF16, "krr") for _ in range(2)]
            ckT = [st.sb([128, 128], BF16, "ckT") for _ in range(2)]
            krT = [st.sb([128, 128], BF16, "krT") for _ in range(2)]
            for b in krT:
                P.op("pool", lambda e, b=b: e.memset(b.ap[:], 0.0), writes=[b])
            QAs = st.sb([128, 16, 64], BF16, "QAs")
            QRs = st.sb([128, 16, 64], BF16, "QRs")
            P.op("pool", lambda e: e.memset(QRs.ap[:], 0.0), writes=[QRs])
            PTs = [st.sb([128, 1024], BF16, "PTs") for _ in range(2)]
            Rr = st.sb([128, 1024], F32, "Rr")
            olb = st.sb([128, 1024], BF16, "olb")
            ohs = st.sb([64, 16, 64], BF16, "ohs")
            Sbk, Obk, Lbk, Tbk, Ebk = [PS[0], PS[1]], [PS[2], PS[3]], [PS[4], PS[5]], PS[6], PS[7]
            NKB = PAST // 128
            for s in range(2):
                tok0 = NPT * 128 + s * 64
                P.dma("sp", QAs.ap[:], QA_d[:, :, tok0:tok0 + 64].rearrange("h p t -> p h t"), reads=[QA_b], writes=[QAs])
                P.dma("sp", QRs.ap[0:32, :, :], QR_d[:, :, tok0:tok0 + 64].rearrange("h p t -> p h t"), reads=[QR_b], pwrites=[QRs])
                for b in Obk + Lbk:
                    P.op("pe", lambda e, b=b: e.matmul(out=b.ap[:], lhsT=zer.ap[:, 0:128], rhs=zer.ap[:], start=True, stop=False, skip_group_check=True), reads=[zer], writes=[b])
                for kb in range(NKB + 1):
                    i = kb % 2
                    last = (kb == NKB)
                    nk = 64 if last else 128
                    if not last:
                        P.dma("pool", ckr[i].ap[:], I["cck"][s, kb * 128:(kb + 1) * 128, :], writes=[ckr[i]])
                        P.dma("pool", krr[i].ap[:], I["ckr"][s, kb * 128:(kb + 1) * 128, :], writes=[krr[i]])
                        tv = Tbk.ap[:].bitcast(BF16)
                        P.op("pe", lambda e, tv=tv, i=i: e.transpose(out=tv[:, 0:128], in_=ckr[i].ap[:], identity=ident_b.ap[:]), reads=[ckr[i], ident_b], writes=[Tbk])
                        P.op("pe", lambda e, tv=tv, i=i: e.transpose(out=tv[0:32, 128:256], in_=krr[i].ap[:], identity=ident_b.ap[:]), reads=[krr[i], ident_b], writes=[Tbk])
                        P.op("dve", lambda e, tv=tv, i=i: e.tensor_copy(out=ckT[i].ap[:], in_=tv[:, 0:128]), reads=[Tbk], writes=[ckT[i]])
                        P.op("dve", lambda e, tv=tv, i=i: e.tensor_copy(out=krT[i].ap[0:32, :], in_=tv[0:32, 128:256]), reads=[Tbk], pwrites=[krT[i]])
                    else:
                        P.dma("pool", ckr[i].ap[0:64, :], CK_d[tok0:tok0 + 64, :], reads=[CK_b], writes=[ckr[i]])
                        P.dma("sp", ckT[i].ap[:, 0:64], CKT_d[:, tok0:tok0 + 64], reads=[CKT_b], writes=[ckT[i]])
                        P.dma("sp", krT[i].ap[0:32, 0:64], KRT_d[:, tok0:tok0 + 64], reads=[KRT_b], pwrites=[krT[i]])
                    for h in range(16):
                        sb_ = Sbk[h // 8]
                        cs = slice((h % 8) * 64, (h % 8) * 64 + 64)
                        P.op("pe", lambda e, sb_=sb_, cs=cs, h=h, nk=nk, i=i: e.matmul(out=sb_.ap[0:nk, cs], lhsT=ckT[i].ap[:, 0:nk], rhs=QAs.ap[:, h, :], start=True, stop=False), reads=[ckT[i], QAs], writes=[sb_])
                        P.op("pe", lambda e, sb_=sb_, cs=cs, h=h, nk=nk, i=i: e.matmul(out=sb_.ap[0:nk, cs], lhsT=krT[i].ap[:, 0:nk], rhs=QRs.ap[:, h, :], start=False, stop=True), reads=[krT[i], QRs], writes=[sb_])
                    pts = PTs[i]
                    for j in range(2):
                        P.op("act", lambda e, j=j, nk=nk, pts=pts: e.activation(out=pts.ap[0:nk, j * 512:(j + 1) * 512], in_=Sbk[j].ap[0:nk, :], func=AF.Exp, scale=MSC), reads=[Sbk[j]], pwrites=[pts])
                    for j in range(2):
                        js = slice(j * 512, (j + 1) * 512)
                        P.op("pe", lambda e, j=j, js=js, nk=nk, i=i, pts=pts: e.matmul(out=Obk[j].ap[:], lhsT=ckr[i].ap[0:nk, :], rhs=pts.ap[0:nk, js], start=False, stop=True, skip_group_check=True),
                             reads=[ckr[i], pts], writes=[Obk[j]])
                        P.op("pe", lambda e, j=j, js=js, nk=nk, pts=pts: e.matmul(out=Lbk[j].ap[:], lhsT=ones_b.ap[0:nk, :], rhs=pts.ap[0:nk, js], start=False, stop=True, skip_group_check=True),
                             reads=[ones_b, pts], writes=[Lbk[j]])
                for j in range(2):
                    js = slice(j * 512, (j + 1) * 512)
                    P.op("dve", lambda e, j=j, js=js: e.reciprocal(out=Rr.ap[:, js], in_=Lbk[j].ap[:]), reads=[Lbk[j]], pwrites=[Rr])
                    P.op("dve", lambda e, j=j, js=js: e.tensor_tensor(out=olb.ap[:, js], in0=Obk[j].ap[:], in1=Rr.ap[:, js], op=ALU.mult), reads=[Obk[j], Rr], pwrites=[olb])
                for g2 in range(2):
                    for u in range(8):
                        h = g2 * 8 + u
                        P.op("pe", lambda e, h=h, u=u: e.matmul(out=Ebk.ap[0:64, u * 64:(u + 1) * 64], lhsT=wuv.ap[:, h * 64:(h + 1) * 64], rhs=olb.ap[:, h * 64:(h + 1) * 64], start=True, stop=True), reads=[wuv, olb], writes=[Ebk])
                    P.op("act", lambda e, g2=g2: e.activation(out=ohs.ap[:, g2 * 8:(g2 + 1) * 8, :], in_=Ebk.ap[0:64, :].rearrange("p (u q) -> p u q", q=64), func=AF.Copy), reads=[Ebk], pwrites=[ohs])
                for h in range(16):
                    P.dma("sp", OT2_d[h // 2, (h % 2) * 64:(h % 2) * 64 + 64, tok0:tok0 + 64], ohs.ap[:, h, :], reads=[ohs], pwrites=[OT2_b])

        if en(10):
            attn_out_stage("l2c", 2, OT2_d, OT2_b, "mla_w_o", 16, 64, False, None, "b", "a", lambda t: True)
        if en(11):
            mlp_stage(2, "a", "b")

        if en(12):
          with Stage(P, "l3") as st:
            win = st.sb([128, 8, 4 * D], BF16, "win")
            for kc in range(8):
                for c0 in range(0, 4 * D, 1024):
                    P.dma("pool", win.ap[:, kc, c0:c0 + 1024], W["sgu_w_in"][kc * 128:(kc + 1) * 128, c0:c0 + 1024], pwrites=[win])
            wout = st.sb([128, 16, D], BF16, "wout")
            for kc in range(16):
                P.dma("pool", wout.ap[:, kc, :], W["sgu_w_out"][kc * 128:(kc + 1) * 128, :], pwrites=[wout])
            gvb = st.sb([128, 2 * D], F32, "gvb")
            P.dma("sp", gvb.ap[:], W["sgu_g_v"].partition_broadcast(128), writes=[gvb])
            Bs = [st.sb([128, 8, 128], F32, "Bs") for _ in range(2)]
            P.dma("sp", Bs[0].ap[:], W["sgu_b_s"].partition_broadcast(128), writes=[Bs[0]])
            for hh in range(2):
                P.dma("sp", Bs[1].ap[:, :, hh * 64:(hh + 1) * 64], W["sgu_b_s"][:, 0:64].partition_broadcast(128), pwrites=[Bs[1]])
            WgT = [st.sb([128, 8, 128], BF16, "WgT") for _ in range(2)]
            with Stage(P, "l3p") as sp:
                stg = [sp.sb([128, 128], F32, "stg") for _ in range(2)]
                k_ = 0
                for ps_i in range(2):
                    for g in range(8):
                        sg = stg[k_ % 2]
                        if ps_i == 0:
                            P.dma("sp", sg.ap[:], W["sgu_w_s"][g], writes=[sg])
                        else:
                            P.op("pool", lambda e, sg=sg: e.memset(sg.ap[:], 0.0), writes=[sg])
                            for hh in range(2):
                                P.dma("sp", sg.ap[hh * 64:(hh + 1) * 64, hh * 64:(hh + 1) * 64], W["sgu_w_s"][g, 0:64, 0:64], pwrites=[sg])
                        P.op("pool", lambda e, sg=sg: e.affine_select(out=sg.ap[:], in_=sg.ap[:], pattern=[[-1, 128]], compare_op=ALU.is_ge, fill=0.0, base=0, channel_multiplier=1),
                             reads=[sg], writes=[sg])
                        pb = PS[1 + k_ % 2]
                        P.op("pe", lambda e, sg=sg, pb=pb: e.transpose(out=pb.ap[:, 0:128], in_=sg.ap[:], identity=ident_f.ap[:]), reads=[sg, ident_f], writes=[pb])
                        P.op("dve", lambda e, pb=pb, ps_i=ps_i, g=g: e.tensor_copy(out=WgT[ps_i].ap[:, g, :], in_=pb.ap[:, 0:128]), reads=[pb], pwrites=[WgT[ps_i]])
                        k_ += 1
            gate = Gate(st, 3, 0)
            xt = [st.sb([128, D], F32, "x") for _ in range(2)]
            nctx = NormCtx(st, PS[0], PS[7])
            hTs = [st.sb([128, 8, 128], BF16, "hT") for _ in range(2)]
            uT = [st.sb([128, 16, 128], BF16, "uT") for _ in range(2)]
            vg = st.sb([128, 2 * D], F32, "vg")
            vn = st.sb([128, 2 * D], F32, "vn")
            vnb = st.sb([128, 2 * D], BF16, "vnb")
            jk2 = st.sb([128, 2 * D], BF16, "jk2")
            st5 = st.sb([128, 4], F32, "st5")
            svt = [st.sb([128, 512], F32, "svt") for _ in range(2)]
            pT = [st.sb([128, 16, 128], BF16, "pT") for _ in range(2)]
            tl3 = list(range(NT)) if upto >= 12.5 else [0, NPT]
            for t in tl3:
                is_s = (t == NPT)
                i = t % 2
                x, hT = xt[i], hTs[i]
                src, srcb = x_src(t, "b")
                P.dma("sp", x.ap[:], src, reads=[srcb], writes=[x])
                nctx.run(x, x.ap[:], hT, hT.ap[:], 3, 0, is_s)
                for q4 in range(4):
                    pu = PS[1 + q4 % 2]
                    for u in range(4):
                        oc = q4 * 4 + u
                        for kc in range(8):
                            P.op("pe", lambda e, pu=pu, u=u, oc=oc, kc=kc, hT=hT: e.matmul(out=pu.ap[:, u * 128:(u + 1) * 128], lhsT=win.ap[:, kc, oc * 128:(oc + 1) * 128], rhs=hT.ap[:, kc, :], start=(kc == 0), stop=(kc == 7)),
                                 reads=[win, hT], writes=[pu])
                    P.op("act", lambda e, pu=pu, q4=q4, i=i: e.activation(out=uT[i].ap[:, q4 * 4:(q4 + 1) * 4, :], in_=pu.ap[:].rearrange("p (u t) -> p u t", t=128), func=AF.Gelu_apprx_tanh), reads=[pu], pwrites=[uT[i]])
                for blk in range(4):
                    pv_ = PS[3 + blk % 2]
                    for kc in range(8):
                        P.op("pe", lambda e, pv_=pv_, blk=blk, kc=kc, hT=hT: e.matmul(out=pv_.ap[:], lhsT=hT.ap[:, kc, :], rhs=win.ap[:, kc, 2048 + blk * 512:2048 + (blk + 1) * 512], start=(kc == 0), stop=(kc == 7)),
                             reads=[hT, win], writes=[pv_])
                    P.op("act", lambda e, pv_=pv_, blk=blk: e.activation(out=vg.ap[:, blk * 512:(blk + 1) * 512], in_=pv_.ap[:], func=AF.Gelu_apprx_tanh), reads=[pv_], pwrites=[vg])
                P.op("act", lambda e: e.activation(out=jk2.ap[:], in_=vg.ap[:], func=AF.Square, accum_out=st5.ap[:, 0:1]), reads=[vg], writes=[jk2, st5])
                P.op("act", lambda e: e.activation(out=st5.ap[:, 1:2], in_=st5.ap[:, 0:1], func=AF.Sqrt, scale=1.0 / (2 * D), bias=EPS), reads=[st5], writes=[st5])
                P.op("dve", lambda e: e.reciprocal(out=st5.ap[:, 2:3], in_=st5.ap[:, 1:2]), reads=[st5], writes=[st5])
                P.op("dve", lambda e: e.scalar_tensor_tensor(out=vn.ap[:], in0=vg.ap[:], scalar=st5.ap[:, 2:3], in1=gvb.ap[:], op0=ALU.mult, op1=ALU.mult), reads=[vg, st5, gvb], writes=[vn])
                P.op("pool", lambda e: e.tensor_copy(out=vnb.ap[:], in_=vn.ap[:]), reads=[vn], writes=[vnb])
                if is_s:
                    P.dma("sp", O["sguv"], vn.ap[:], reads=[vn], pwrites=[out_bufs["sguv"]])
                wg, bs_ = WgT[1 if is_s else 0], Bs[1 if is_s else 0]
                p_ = pT[i]
                for q4 in range(4):
                    psv = PS[5 + q4 % 2]
                    for u in range(4):
                        fc = q4 * 4 + u
                        P.op("pe", lambda e, psv=psv, u=u, fc=fc, wg=wg: e.matmul(out=psv.ap[:, u * 128:(u + 1) * 128], lhsT=vnb.ap[:, fc * 128:(fc + 1) * 128], rhs=wg.ap[:, fc // 2, :], start=True, stop=True),
                             reads=[vnb, wg], writes=[psv])
                    sv_ = svt[q4 % 2]
                    P.op("dve", lambda e, psv=psv, sv_=sv_, q4=q4, bs_=bs_: e.tensor_tensor(out=sv_.ap[:].rearrange("p (g u t) -> p g u t", g=2, u=2), in0=psv.ap[:].rearrange("p (g u t) -> p g u t", g=2, u=2),
                                                                                  in1=bc(bs_.ap[:, q4 * 2:(q4 + 1) * 2, :].unsqueeze(2), [128, 2, 2, 128]), op=ALU.add), reads=[psv, bs_], writes=[sv_])
                    P.op("pool", lambda e, sv_=sv_, q4=q4, i=i, p_=p_: e.tensor_tensor(out=p_.ap[:, q4 * 4:(q4 + 1) * 4, :].rearrange("p u t -> p (u t)"), in0=sv_.ap[:], in1=uT[i].ap[:, q4 * 4:(q4 + 1) * 4, :].rearrange("p u t -> p (u t)"), op=ALU.mult),
                         reads=[sv_, uT[i]], pwrites=[p_])
                g = gate.get(is_s)
                for cbk in range(2):
                    po = PS[1 + cbk]
                    for kc in range(16):
                        P.op("pe", lambda e, po=po, kc=kc, cbk=cbk, p_=p_: e.matmul(out=po.ap[:], lhsT=p_.ap[:, kc, :], rhs=wout.ap[:, kc, cbk * 512:(cbk + 1) * 512], start=(kc == 0), stop=(kc == 15)),
                             reads=[p_, wout], writes=[po])
                    sl = slice(cbk * 512, (cbk + 1) * 512)
                    P.op("dve", lambda e, po=po, sl=sl, g=g: e.tensor_tensor(out=po.ap[:], in0=po.ap[:], in1=g.ap[:, sl], op=ALU.mult), reads=[po, g], writes=[po])
                    P.op("dve", lambda e, po=po, sl=sl, x=x: e.tensor_tensor(out=x.ap[:, sl], in0=po.ap[:], in1=x.ap[:, sl], op=ALU.add), reads=[po, x], pwrites=[x])
                dst, dstb = x_dst(t, "a")
                P.dma("sp", dst, x.ap[:], reads=[x], pwrites=[dstb])

        if en(13):
            mlp_stage(3, "a", None, final=True)

        P.barrier()
        P.emit()
        print("n_inst", P.n_inst, "n_dsem", P.n_dsem)
    return nc


_NC_CACHE = {}


def kernel(**inp):
    f = lambda a: np.ascontiguousarray(np.asarray(a, dtype=np.float32))
    upto = float(inp.get("_upto", 99))
    if upto not in _NC_CACHE:
        _NC_CACHE[upto] = build_program(upto)
    nc = _NC_CACHE[upto]
    wnames = ["w_ada", "b_ada", "g_mix", "g_ffn", "w_up", "w_down", "g_final", "s5_a_re", "s5_a_im", "s5_b_re", "s5_b_im",
              "s5_c_re", "s5_c_im", "s5_d", "s5_log_dt", "s5_w_glu_a", "s5_w_glu_b", "diff_w_qkv", "diff_lambda_q1",
              "diff_lambda_k1", "diff_lambda_q2", "diff_lambda_k2", "diff_g_sub", "diff_w_o", "mla_w_dq", "mla_g_q",
              "mla_w_uq", "mla_w_dkv", "mla_g_kv", "mla_w_uk", "mla_w_uv", "mla_w_o", "sgu_w_in", "sgu_g_v", "sgu_w_s",
              "sgu_b_s", "sgu_w_out"]
    wd = {k: f(inp[k]) for k in wnames}
    wd["s5_a_re"] = wd["s5_a_re"].reshape(4096)
    wd["s5_a_im"] = wd["s5_a_im"].reshape(4096)
    wd["mla_w_uk"] = wd["mla_w_uk"].reshape(128, 1024)
    wd["mla_w_uv"] = wd["mla_w_uv"].reshape(128, 1024)
    xp, xs = f(inp["x_prompt"]), f(inp["x_sample"])
    cp, cs = f(inp["c_prompt"]), f(inp["c_sample"])
    sre, sim = f(inp["state_s5_re"]), f(inp["state_s5_im"])
    cdk, cdv = f(inp["cache_diff_k"]), f(inp["cache_diff_v"])
    cck, ckr = f(inp["cache_mla_ckv"]), f(inp["cache_mla_krope"])
    in_maps = []
    for c in range(8):
        b = c % 4
        s0 = 2 * c
        m = dict(wd)
        m["xp"] = xp[b]
        m["xs"] = xs[s0:s0 + 2].reshape(128, D)
        m["c3"] = np.concatenate([cp[b:b + 1], cs[s0:s0 + 2]], axis=0)
        m["h0re"] = sre[s0:s0 + 2].reshape(2, 4096)
        m["h0im"] = sim[s0:s0 + 2].reshape(2, 4096)
        m["cdk"] = cdk[s0:s0 + 2].reshape(2, PAST, D)
        m["cdv"] = cdv[s0:s0 + 2].reshape(2, PAST, D)
        m["cck"] = cck[s0:s0 + 2]
        m["ckr"] = ckr[s0:s0 + 2]
        in_maps.append(m)
    ncores = int(inp.get("_ncores", 8))
    if ncores < 8:
        res = run_bass_kernel_spmd(nc, in_maps[:ncores], core_ids=list(range(ncores))).results
        res = [res[c % ncores] for c in range(8)]
    else:
        res = run_bass_kernel_spmd(nc, in_maps, core_ids=list(range(8))).results
    global _LAST_RES
    _LAST_RES = res
    R = lambda k, cores: [res[c][k] for c in cores]
    p4, a8 = range(4), range(8)
    y_prompt = np.stack(R("yp", p4)).reshape(4, SEQ, D)
    y_sample = np.concatenate(R("ys", a8)).reshape(16, 64, D)
    s5p_re = np.stack(R("s5p_re", p4)).reshape(4, 64, 64)
    s5p_im = np.stack(R("s5p_im", p4)).reshape(4, 64, 64)
    s5s_re = np.concatenate(R("s5s_re", a8)).reshape(16, 64, 64)
    s5s_im = np.concatenate(R("s5s_im", a8)).reshape(16, 64, 64)
    dkp = np.stack(R("dkp", p4)).reshape(4, SEQ, 8, 128)
    dvp = np.stack(R("dvp", p4)).reshape(4, SEQ, 8, 128)
    dks = np.concatenate(R("dks", a8)).reshape(16, 64, 8, 128)
    dvs = np.concatenate(R("dvs", a8)).reshape(16, 64, 8, 128)
    ckvp = np.stack(R("ckvp", p4)).reshape(4, SEQ, 128)
    krp = np.stack(R("krp", p4)).reshape(4, SEQ, 32)
    ckvs = np.concatenate(R("ckvs", a8)).reshape(16, 64, 128)
    krs = np.concatenate(R("krs", a8)).reshape(16, 64, 32)
    sguv = np.concatenate(R("sguv", a8)).reshape(16, 64, 2 * D)
    return (y_prompt, y_sample, s5p_re, s5p_im, s5s_re, s5s_im, dkp, dvp, dks, dvs, ckvp, krp, ckvs, krs, sguv)
```

```python
import math
from contextlib import ExitStack

import numpy as np
import concourse.bass as bass
import concourse.mybir as mybir
from concourse.bass_utils import run_bass_kernel_spmd

F32 = mybir.dt.float32
BF16 = mybir.dt.bfloat16
I32 = mybir.dt.int32
AF = mybir.ActivationFunctionType
ALU = mybir.AluOpType

D = 1024
SEQ = 8192
NPT = SEQ // 128
NT = NPT + 1
NTOK = NT * 128
PAST = 4096
EPS = 1e-6
LAMBDA_INIT = 0.8 - 0.6 * math.exp(-0.3 * 1)
TWO_PI = 2.0 * math.pi
CW1 = 6.28125
CW2 = float(np.float32(TWO_PI - CW1))
CW3 = float(TWO_PI - CW1 - CW2)


class Buf:
    def __init__(self, ap, name=""):
        self.ap = ap
        self.name = name
        self.w = []
        self.r = {}
        self.pr = {}
        self.is_sb = False
        self.is_psum = False
        self.dsem = None


class Prog:
    ENGS = ["pe", "act", "dve", "pool", "sp"]

    def __init__(self, nc, es):
        self.nc = nc
        self.es = es
        self.sem = {e: es.enter_context(nc.semaphore("s_" + e)) for e in self.ENGS}
        self.cnt = {e: 0 for e in self.ENGS}
        self.waited = {e: {} for e in self.ENGS}
        self.stream = {e: [] for e in self.ENGS}
        self.dsems = []
        self.n_inst = 0
        self.n_dsem = 0
        self.free_dsems = []

    def take_dsem(self, key, stage, sw=False):
        if sw:
            return self.mk_dsem(key)
        if self.free_dsems:
            d = self.free_dsems.pop()
        else:
            d = self.mk_dsem(key)
        if stage is not None:
            stage.dsems.append(d)
        return d

    def no_dsem(self, key=None):
        return None

    def mk_dsem(self, key):
        self.n_dsem += 1
        s = self.es.enter_context(self.nc.semaphore("d_%s_%d" % (key, self.n_dsem)))
        d = {"sem": s, "cnt": 0, "key": "d_%s_%d" % (key, self.n_dsem)}
        self.dsems.append(d)
        return d

    def _need(self, e, reads, writes, pwrites):
        need = {}

        def add(dep):
            key, val, semobj = dep
            if key == e and e == "pe":
                return
            cur = need.get(key)
            if cur is None or cur[0] < val:
                need[key] = (val, semobj)

        for b in reads:
            for d, _ in b.w:
                add(d)
            if b.is_psum:
                for k, d in b.r.items():
                    if k != e:
                        add(d)
        for b in writes:
            for d, _ in b.w:
                add(d)
            for d in b.r.values():
                add(d)
        for b in pwrites:
            for d, part in b.w:
                if not part:
                    add(d)
            for d in b.r.values():
                add(d)
            for d in b.pr.values():
                add(d)
        out = []
        for key, (val, semobj) in need.items():
            if self.waited[e].get(key, 0) >= val:
                continue
            self.waited[e][key] = val
            out.append((semobj, val))
        return out

    def _commit(self, key, dep, reads, writes, pwrites):
        for b in reads:
            b.r[key] = dep
        for b in writes:
            b.w = [(dep, False)]
            b.r = {}
            b.pr = {}
        for b in pwrites:
            if b.r:
                b.pr = b.r
                b.w = [(dep, True)]
                b.r = {}
            else:
                b.w.append((dep, True))

    def op(self, e, fn, reads=(), writes=(), pwrites=()):
        waits = self._need(e, reads, writes, pwrites)
        self.cnt[e] += 1
        dep = (e, self.cnt[e], self.sem[e])
        self.stream[e].append((waits, fn, (self.sem[e], 1)))
        self._commit(e, dep, reads, writes, pwrites)
        self.n_inst += 1

    def dma(self, q, out_ap, in_ap, reads=(), writes=(), pwrites=(), sem=None, **kw):
        sbb = [b for b in list(writes) + list(pwrites) + list(reads) if b.is_sb]
        assert sbb, "dma without sbuf side"
        if sbb[0].dsem is None:
            sbb[0].dsem = self.take_dsem(sbb[0].name, getattr(sbb[0], "stage", None), sw=(q == "pool"))
            sbb[0].dq = q
        assert sbb[0].dq == q or (sbb[0].dq != "pool" and q != "pool"), "buffer mixes SW and HW DMA queues"
        sem = sbb[0].dsem
        waits = self._need(q, reads, writes, pwrites)
        sem["cnt"] += 16
        dep = (sem["key"], sem["cnt"], sem["sem"])

        def fn(eng, out_ap=out_ap, in_ap=in_ap, kw=kw):
            return eng.dma_start(out=out_ap, in_=in_ap, **kw)

        self.stream[q].append((waits, fn, (sem["sem"], 16)))
        self._commit(sem["key"], dep, reads, writes, pwrites)
        self.n_inst += 1

    def barrier(self):
        for e in self.ENGS:
            waits = []
            for x in self.ENGS:
                if x == e or self.cnt[x] == 0:
                    continue
                if self.waited[e].get(x, 0) < self.cnt[x]:
                    self.waited[e][x] = self.cnt[x]
                    waits.append((self.sem[x], self.cnt[x]))
            for d in self.dsems:
                if d["cnt"] and self.waited[e].get(d["key"], 0) < d["cnt"]:
                    self.waited[e][d["key"]] = d["cnt"]
                    waits.append((d["sem"], d["cnt"]))
            self.stream[e].append((waits, None, None))

    def emit(self):
        nc = self.nc
        streams = self.stream
        self.stream = {e: [] for e in self.ENGS}
        with nc.Block() as block:
            def mk(e):
                def body(eng):
                    for waits, fn, inc in streams[e]:
                        for s, v in waits:
                            eng.wait_ge(s, v)
                        if fn is not None:
                            fn(eng).then_inc(inc[0], inc[1])
                return body
            block.tensor(mk("pe"))
            block.scalar(mk("act"))
            block.vector(mk("dve"))
            block.gpsimd(mk("pool"))
            block.sync(mk("sp"))


class Stage:
    def __init__(self, P, name):
        self.P = P
        Stage.CNT = getattr(Stage, "CNT", 0) + 1
        self.name = "%s%d" % (name, Stage.CNT)
        self.es = ExitStack()
        self.n = 0
        self.dsems = []

    def __enter__(self):
        self.es.__enter__()
        return self

    def sb(self, shape, dt, name=None):
        self.n += 1
        t = self.es.enter_context(self.P.nc.sbuf_tensor("%s_%s_%d" % (self.name, name or "t", self.n), list(shape), dt))
        b = Buf(t, name or "t")
        b.is_sb = True
        b.stage = self
        return b

    def __exit__(self, *a):
        if a[0] is None:
            self.P.barrier()
            self.P.emit()
            self.P.free_dsems.extend(self.dsems)
            self.dsems = []
        return self.es.__exit__(*a)


def bc(ap, shape):
    return ap.to_broadcast(list(shape))


def build_program(upto=99):
    nc = bass.Bass("TRN2", target_bir_lowering=False)

    def din(name, shape):
        return nc.dram_tensor(name, list(shape), F32, kind="ExternalInput").ap()

    def dout(name, shape):
        return nc.dram_tensor(name, list(shape), F32, kind="ExternalOutput").ap()

    import os as _os
    _dbg = set(_os.environ.get("KD_DBGOUT", "").split(","))
    _only = _os.environ.get("KD_ONLY", "")
    _only = set(float(v) for v in _only.split(",")) if _only else None

    def en(k):
        return upto >= k and (_only is None or k in _only)

    def dscr(name, shape, dt):
        return nc.dram_tensor(name, list(shape), dt, kind=("ExternalOutput" if name in _dbg else "Internal")).ap()

    I = {}
    I["xp"] = din("xp", [SEQ, D])
    I["xs"] = din("xs", [128, D])
    I["c3"] = din("c3", [3, D])
    I["h0re"] = din("h0re", [2, 4096])
    I["h0im"] = din("h0im", [2, 4096])
    I["cdk"] = din("cdk", [2, PAST, D])
    I["cdv"] = din("cdv", [2, PAST, D])
    I["cck"] = din("cck", [2, PAST, 128])
    I["ckr"] = din("ckr", [2, PAST, 32])
    wshapes = dict(
        w_ada=[4, D, 6 * D], b_ada=[4, 6 * D], g_mix=[4, D], g_ffn=[4, D], w_up=[4, D, 4 * D], w_down=[4, 4 * D, D],
        g_final=[D], s5_a_re=[4096], s5_a_im=[4096], s5_b_re=[64, 64, 16], s5_b_im=[64, 64, 16],
        s5_c_re=[64, 16, 64], s5_c_im=[64, 16, 64], s5_d=[D], s5_log_dt=[64], s5_w_glu_a=[D, D], s5_w_glu_b=[D, D],
        diff_w_qkv=[D, 3 * D], diff_lambda_q1=[64], diff_lambda_k1=[64], diff_lambda_q2=[64], diff_lambda_k2=[64],
        diff_g_sub=[128], diff_w_o=[D, D], mla_w_dq=[D, 256], mla_g_q=[256], mla_w_uq=[256, 1536], mla_w_dkv=[D, 160],
        mla_g_kv=[128], mla_w_uk=[128, 1024], mla_w_uv=[128, 1024], mla_w_o=[D, D], sgu_w_in=[D, 4 * D],
        sgu_g_v=[2 * D], sgu_w_s=[8, 128, 128], sgu_b_s=[8, 128], sgu_w_out=[2 * D, D])
    W = {k: din(k, s) for k, s in wshapes.items()}

    O = {}
    O["yp"] = dout("yp", [SEQ, D])
    O["ys"] = dout("ys", [128, D])
    O["s5p_re"] = dout("s5p_re", [4096])
    O["s5p_im"] = dout("s5p_im", [4096])
    O["s5s_re"] = dout("s5s_re", [2, 4096])
    O["s5s_im"] = dout("s5s_im", [2, 4096])
    O["dkp"] = dout("dkp", [SEQ, D])
    O["dvp"] = dout("dvp", [SEQ, D])
    O["dks"] = dout("dks", [128, D])
    O["dvs"] = dout("dvs", [128, D])
    O["ckvp"] = dout("ckvp", [SEQ, 128])
    O["krp"] = dout("krp", [SEQ, 32])
    O["ckvs"] = dout("ckvs", [128, 128])
    O["krs"] = dout("krs", [128, 32])
    O["sguv"] = dout("sguv", [128, 2 * D])

    xa = dscr("xa", [NTOK, D], F32)
    xb = dscr("xb", [NTOK, D], F32)
    gates_d = dscr("gates_d", [4, 2, 2, 128, D], F32)
    zT_d = dscr("zT_d", [NT, 128, 8 * 128], BF16)

    with ExitStack() as es:
        P = Prog(nc, es)
        out_bufs = {k: Buf(v, k) for k, v in O.items()}
        xa_b = Buf(xa, "xa")
        xb_b = Buf(xb, "xb")
        gates_b = Buf(gates_d, "gates")
        zT_b = Buf(zT_d, "zT_d")
        osem = P.no_dsem("out")
        ssem = P.no_dsem("scr")

        def gsb(name, shape, dt):
            b = Buf(es.enter_context(nc.sbuf_tensor(name, list(shape), dt)), name)
            b.is_sb = True
            return b

        ident_f = gsb("ident_f", [128, 128], F32)
        ident_b = gsb("ident_b", [128, 128], BF16)
        ones_b = gsb("ones_b", [128, 128], BF16)
        ones_f = gsb("ones_f", [128, 128], F32)
        condT = gsb("condT", [128, 8, 4], BF16)
        GS = gsb("GS", [128, 4, 2, 2, 8, 4], F32)
        gfinT = gsb("gfinT", [128, 8], F32)
        PS = [Buf(es.enter_context(nc.psum_tensor("ps%d" % i, [128, 512], F32)), "ps%d" % i) for i in range(8)]
        for b in PS:
            b.is_psum = True

        def load_T(st, dst, dst_ap, rows_ap, R, psb):
            t = st.sb([128, 128], F32, "ldT")
            sem = P.no_dsem("ldT")
            P.dma("sp", t.ap[0:R, :], rows_ap, writes=[t], sem=sem)
            P.op("pe", lambda e: e.transpose(out=psb.ap[:, 0:R], in_=t.ap[0:R, :], identity=ident_f.ap[0:R, 0:R]),
                 reads=[t, ident_f], writes=[psb])
            P.op("dve", lambda e: e.tensor_copy(out=dst_ap, in_=psb.ap[:, 0:R]), reads=[psb], pwrites=[dst])

        with Stage(P, "s0") as st:
            ld = P.no_dsem("s0ld")
            condbc = [st.sb([128, 8, 128], BF16, "condbc") for i in range(2)]
            modT = st.sb([128, 4, 48, 4], F32, "modT")
            P.op("pool", lambda e: e.memset(ident_f.ap[:], 0.0), writes=[ident_f])
            P.op("pool", lambda e: e.affine_select(out=ident_f.ap[:], in_=ident_f.ap[:], pattern=[[-1, 128]],
                                                   compare_op=ALU.not_equal, fill=1.0, base=0, channel_multiplier=1),
                 reads=[ident_f], writes=[ident_f])
            P.op("dve", lambda e: e.tensor_copy(out=ident_b.ap[:], in_=ident_f.ap[:]), reads=[ident_f], writes=[ident_b])
            P.op("pool", lambda e: e.memset(ones_b.ap[:], 1.0), writes=[ones_b])
            P.op("pool", lambda e: e.memset(ones_f.ap[:], 1.0), writes=[ones_f])

            c3 = st.sb([4, D], F32, "c3")
            P.op("pool", lambda e: e.memset(c3.ap[:], 0.0), writes=[c3])
            P.dma("sp", c3.ap[0:3, :], I["c3"], pwrites=[c3], sem=ld)
            c3s = st.sb([4, D], F32, "c3s")
            P.op("act", lambda e: e.activation(out=c3s.ap[:], in_=c3.ap[:], func=AF.Silu), reads=[c3], writes=[c3s])
            for kc in range(8):
                P.op("pe", lambda e, kc=kc: e.transpose(out=PS[0].ap[:, kc * 4:kc * 4 + 4], in_=c3s.ap[0:4, kc * 128:(kc + 1) * 128],
                                                        identity=ident_f.ap[0:4, 0:4]), reads=[c3s, ident_f], writes=[PS[0]])
            P.op("dve", lambda e: e.tensor_copy(out=condT.ap[:], in_=PS[0].ap[:, 0:32].rearrange("p (k s) -> p k s", s=4)),
                 reads=[PS[0]], writes=[condT])
            P.op("dve", lambda e: e.tensor_copy(out=condbc[0].ap[:], in_=bc(condT.ap[:, :, 0:1], [128, 8, 128])),
                 reads=[condT], writes=[condbc[0]])
            P.op("dve", lambda e: e.tensor_copy(out=condbc[1].ap[:, :, 0:64], in_=bc(condT.ap[:, :, 1:2], [128, 8, 64])),
                 reads=[condT], pwrites=[condbc[1]])
            P.op("dve", lambda e: e.tensor_copy(out=condbc[1].ap[:, :, 64:128], in_=bc(condT.ap[:, :, 2:3], [128, 8, 64])),
                 reads=[condT], pwrites=[condbc[1]])

            badaT = st.sb([128, 192], F32, "badaT")
            brows = W["b_ada"].rearrange("l (c p) -> (l c) p", p=128)
            load_T(st, badaT, badaT.ap[:, 0:96], brows[0:96, :], 96, PS[1])
            load_T(st, badaT, badaT.ap[:, 96:192], brows[96:192, :], 96, PS[2])
            gmT = st.sb([128, 2, 32], F32, "gmT")
            load_T(st, gmT, gmT.ap[:, 0, :], W["g_mix"].rearrange("l (c p) -> (l c) p", p=128), 32, PS[3])
            load_T(st, gmT, gmT.ap[:, 1, :], W["g_ffn"].rearrange("l (c p) -> (l c) p", p=128), 32, PS[4])
            load_T(st, gfinT, gfinT.ap[:, :], W["g_final"].rearrange("(c p) -> c p", p=128), 8, PS[5])

            wblk = [st.sb([128, 8, 1024], BF16, "wblk") for _ in range(2)]
            wsem = [P.no_dsem("wblk") for _ in range(2)]
            bbc = [st.sb([128, 1024], F32, "bbc") for _ in range(2)]
            bsem = [P.no_dsem("bbc") for _ in range(2)]
            gtile = [st.sb([128, 1024], F32, "gtile") for _ in range(2)]
            it = 0
            for l in range(4):
                for cb in range(6):
                    wb = wblk[it % 2]
                    for kc in range(8):
                        P.dma("pool", wb.ap[:, kc, :], W["w_ada"][l, kc * 128:(kc + 1) * 128, cb * 1024:(cb + 1) * 1024],
                              pwrites=[wb], sem=wsem[it % 2])
                    pm = PS[it % 2]
                    for f in range(8):
                        for kc in range(8):
                            P.op("pe", lambda e, pm=pm, f=f, kc=kc, wb=wb: e.matmul(
                                out=pm.ap[:, f * 4:f * 4 + 4], lhsT=wb.ap[:, kc, f * 128:(f + 1) * 128], rhs=condT.ap[:, kc, :],
                                start=(kc == 0), stop=(kc == 7)), reads=[wb, condT], writes=[pm])
                    P.op("dve", lambda e, pm=pm, l=l, cb=cb: e.tensor_tensor(
                        out=modT.ap[:, l, cb * 8:(cb + 1) * 8, :], in0=pm.ap[:, 0:32].rearrange("p (k s) -> p k s", s=4),
                        in1=bc(badaT.ap[:, l * 48 + cb * 8:l * 48 + cb * 8 + 8].unsqueeze(2), [128, 8, 4]), op=ALU.add),
                        reads=[pm, badaT], pwrites=[modT])
                    if cb in (2, 5):
                        gi = 0 if cb == 2 else 1
                        bb = bbc[gi]
                        P.dma("sp", bb.ap[:], W["b_ada"][l, cb * 1024:(cb + 1) * 1024].partition_broadcast(128), writes=[bb], sem=bsem[gi])
                        for ps_i in range(2):
                            gt = gtile[ps_i]
                            for half in range(2):
                                pg = PS[2 + half]
                                for kc in range(8):
                                    P.op("pe", lambda e, pg=pg, kc=kc, half=half, ps_i=ps_i, wb=wb: e.matmul(
                                        out=pg.ap[:], lhsT=condbc[ps_i].ap[:, kc, :], rhs=wb.ap[:, kc, half * 512:(half + 1) * 512],
                                        start=(kc == 0), stop=(kc == 7)), reads=[wb, condbc[ps_i]], writes=[pg])
                                P.op("dve", lambda e, pg=pg, half=half, gt=gt, bb=bb: e.scalar_tensor_tensor(
                                    out=gt.ap[:, half * 512:(half + 1) * 512], in0=pg.ap[:], scalar=1.0, in1=bb.ap[:, half * 512:(half + 1) * 512],
                                    op0=ALU.add, op1=ALU.add), reads=[pg, bb], pwrites=[gt])
                            P.dma("sp", gates_d[l, gi, ps_i], gt.ap[:], reads=[gt], pwrites=[gates_b], sem=ssem)
                    it += 1
            for l in range(4):
                for sub in range(2):
                    base = 0 if sub == 0 else 24
                    P.op("dve", lambda e, l=l, sub=sub, base=base: e.scalar_tensor_tensor(
                        out=GS.ap[:, l, sub, 0, :, :], in0=modT.ap[:, l, base + 8:base + 16, :], scalar=1.0,
                        in1=bc(gmT.ap[:, sub, l * 8:(l + 1) * 8].unsqueeze(2), [128, 8, 4]), op0=ALU.add, op1=ALU.mult),
                        reads=[modT, gmT], pwrites=[GS])
                    P.op("dve", lambda e, l=l, sub=sub, base=base: e.tensor_copy(
                        out=GS.ap[:, l, sub, 1, :, :], in_=modT.ap[:, l, base:base + 8, :]), reads=[modT], pwrites=[GS])

        class Gate:
            def __init__(self, st, l, gi):
                self.buf = st.sb([128, D], F32, "gate")
                self.sem = P.no_dsem("gate")
                self.l, self.gi, self.cur = l, gi, None

            def get(self, is_sample):
                k = 1 if is_sample else 0
                if self.cur != k:
                    P.dma("sp", self.buf.ap[:], gates_d[self.l, self.gi, k], reads=[gates_b], writes=[self.buf], sem=self.sem)
                    self.cur = k
                return self.buf

        class NormCtx:
            def __init__(self, st, psb, psb2):
                self.junk = st.sb([128, D], BF16, "junk")
                self.stat = st.sb([128, 4], F32, "stat")
                self.xn = st.sb([128, D], BF16, "xn")
                self.psb = [psb, psb2]

            def run(self, xbuf, x_ap, hbuf, h_ap, l, sub, is_sample, mod=True):
                stat, xn, junk = self.stat, self.xn, self.junk
                P.op("act", lambda e: e.activation(out=junk.ap[:], in_=x_ap, func=AF.Square, accum_out=stat.ap[:, 0:1]),
                     reads=[xbuf], writes=[junk, stat])
                P.op("act", lambda e: e.activation(out=stat.ap[:, 1:2], in_=stat.ap[:, 0:1], func=AF.Sqrt, scale=1.0 / D, bias=EPS),
                     reads=[stat], writes=[stat])
                P.op("dve", lambda e: e.reciprocal(out=stat.ap[:, 2:3], in_=stat.ap[:, 1:2]), reads=[stat], writes=[stat])
                P.op("act", lambda e: e.activation(out=xn.ap[:], in_=x_ap, func=AF.Copy, scale=stat.ap[:, 2:3]),
                     reads=[xbuf, stat], writes=[xn])
                pvs = [pb.ap[:].bitcast(BF16).rearrange("p (k t) -> p k t", t=128) for pb in self.psb]
                for kc in range(8):
                    psb, pv = self.psb[kc // 4], pvs[kc // 4]
                    P.op("pe", lambda e, kc=kc, pv=pv: e.transpose(out=pv[:, kc % 4, :], in_=xn.ap[:, kc * 128:(kc + 1) * 128], identity=ident_b.ap[:]),
                         reads=[xn, ident_b], writes=[psb])
                segs = [(0, 128, 0)] if not is_sample else [(0, 64, 1), (64, 128, 2)]
                for kc in range(8):
                    psb, pv0 = self.psb[kc // 4], pvs[kc // 4]
                    pv = pv0[:, kc % 4:kc % 4 + 1, :]
                    for (c0, c1, s) in segs:
                        if kc < 4:
                            P.op("act", lambda e, kc=kc, c0=c0, c1=c1, s=s, pv=pv: e.activation(
                                out=h_ap[:, kc, c0:c1], in_=pv[:, 0, c0:c1], func=AF.Identity,
                                scale=GS.ap[:, l, sub, 0, kc, s:s + 1], bias=GS.ap[:, l, sub, 1, kc, s:s + 1]),
                                reads=[psb, GS], pwrites=[hbuf])
                        else:
                            P.op("dve", lambda e, kc=kc, c0=c0, c1=c1, s=s, pv=pv: e.tensor_scalar(
                                out=h_ap[:, kc, c0:c1], in0=pv[:, 0, c0:c1], scalar1=GS.ap[:, l, sub, 0, kc, s:s + 1],
                                scalar2=GS.ap[:, l, sub, 1, kc, s:s + 1], op0=ALU.mult, op1=ALU.add),
                                reads=[psb, GS], pwrites=[hbuf])

        def x_src(t, which):
            if which == "in":
                return (I["xp"][t * 128:(t + 1) * 128, :], None) if t < NPT else (I["xs"], None)
            d, b = (xa, xa_b) if which == "a" else (xb, xb_b)
            return d[t * 128:(t + 1) * 128, :], b

        def x_dst(t, which):
            d, b = (xa, xa_b) if which == "a" else (xb, xb_b)
            return d[t * 128:(t + 1) * 128, :], b

        def load_w_bf16(dst, dst_ap_fn, src, nk, sem, q="pool"):
            for kc in range(nk):
                P.dma(q, dst_ap_fn(kc), src[kc * 128:(kc + 1) * 128, :], pwrites=[dst], sem=sem)

        if en(0.5):
          with Stage(P, "l0a") as st:
            ld = P.no_dsem("l0ld")
            sm = st.sb([128, 16, 32], F32, "sm")
            A_RE, A_IM, DT, TH, R, C1, S1, FRE, FIM, TMP1, TMP2, TMP3 = range(12)
            NTB = 65
            Er = st.sb([128, 32, NTB], F32, "Er")
            Ei = st.sb([128, 32, NTB], F32, "Ei")
            Tr = st.sb([128, 32, 64], F32, "Tr")
            Ti = st.sb([128, 32, 64], F32, "Ti")
            Bbd = [st.sb([128, 32, 128], BF16, "Bbd") for _ in range(2)]
            Cbd = [st.sb([128, 32, 128], BF16, "Cbd") for _ in range(2)]
            dT = st.sb([128, 8], F32, "dT")
            h0 = st.sb([128, 2, 2, 32], F32, "h0")

            with Stage(P, "l0p") as sp:
                P.dma("sp", sm.ap[:, A_RE, :], W["s5_a_re"].rearrange("(j p) -> p j", p=128), pwrites=[sm], sem=ld, allow_slow_non_contiguous=True)
                P.dma("sp", sm.ap[:, A_IM, :], W["s5_a_im"].rearrange("(j p) -> p j", p=128), pwrites=[sm], sem=ld, allow_slow_non_contiguous=True)
                ldt2 = W["s5_log_dt"].rearrange("(j h) -> h j", h=2)
                for hh in range(2):
                    P.dma("sp", sm.ap[hh * 64:(hh + 1) * 64, DT, :], ldt2[hh, :].partition_broadcast(64),
                          pwrites=[sm], sem=ld, allow_slow_non_contiguous=True)
                for s in range(2):
                    P.dma("sp", h0.ap[:, 0, s, :], I["h0re"][s].rearrange("(j p) -> p j", p=128), pwrites=[h0], sem=ld, allow_slow_non_contiguous=True)
                    P.dma("sp", h0.ap[:, 1, s, :], I["h0im"][s].rearrange("(j p) -> p j", p=128), pwrites=[h0], sem=ld, allow_slow_non_contiguous=True)

                def sm_op(fn, eng="dve", extra=()):
                    P.op(eng, fn, reads=[sm] + list(extra), writes=[sm])

                sm_op(lambda e: e.activation(out=sm.ap[:, DT, :], in_=sm.ap[:, DT, :], func=AF.Exp), "act")
                sm_op(lambda e: e.tensor_tensor(out=sm.ap[:, TH, :], in0=sm.ap[:, A_IM, :], in1=sm.ap[:, DT, :], op=ALU.mult))
                sm_op(lambda e: e.tensor_tensor(out=sm.ap[:, R, :], in0=sm.ap[:, A_RE, :], in1=sm.ap[:, DT, :], op=ALU.mult))
                sm_op(lambda e: e.activation(out=sm.ap[:, R, :], in_=sm.ap[:, R, :], func=AF.Exp), "act")

                tpos = sp.sb([128, NTB], F32, "tpos")
                P.op("pool", lambda e: e.iota(tpos.ap[:], pattern=[[1, NTB]], base=0, channel_multiplier=0, allow_small_or_imprecise_dtypes=True), writes=[tpos])
                ang = sp.sb([128, 32, NTB], F32, "ang")
                kf = sp.sb([128, 32, NTB], F32, "kf")
                ki = sp.sb([128, 32, NTB], I32, "ki")

                def sincos(dbuf, shift):
                    dst, src, k_ap, ki_ap = dbuf.ap[:], ang.ap[:], kf.ap[:], ki.ap[:]
                    P.op("dve", lambda e: e.tensor_scalar(out=k_ap, in0=src, scalar1=shift, scalar2=1.0 / TWO_PI, op0=ALU.add, op1=ALU.mult),
                         reads=[ang], writes=[kf])
                    P.op("dve", lambda e: e.tensor_copy(out=ki_ap, in_=k_ap), reads=[kf], writes=[ki])
                    P.op("dve", lambda e: e.tensor_copy(out=k_ap, in_=ki_ap), reads=[ki], writes=[kf])
                    P.op("dve", lambda e: e.scalar_tensor_tensor(out=dst, in0=k_ap, scalar=-CW1, in1=src, op0=ALU.mult, op1=ALU.add),
                         reads=[kf, ang], writes=[dbuf])
                    P.op("dve", lambda e: e.scalar_tensor_tensor(out=dst, in0=k_ap, scalar=-CW2, in1=dst, op0=ALU.mult, op1=ALU.add),
                         reads=[kf, dbuf], writes=[dbuf])
                    P.op("dve", lambda e: e.scalar_tensor_tensor(out=dst, in0=k_ap, scalar=-CW3, in1=dst, op0=ALU.mult, op1=ALU.add),
                         reads=[kf, dbuf], writes=[dbuf])
                    P.op("dve", lambda e: e.tensor_scalar(out=dst, in0=dst, scalar1=shift, scalar2=None, op0=ALU.add), reads=[dbuf], writes=[dbuf])
                    P.op("dve", lambda e: e.tensor_scalar(out=dst, in0=dst, scalar1=math.pi, scalar2=-math.pi, op0=ALU.min, op1=ALU.max),
                         reads=[dbuf], writes=[dbuf])
                    P.op("act", lambda e: e.activation(out=dst, in_=dst, func=AF.Sin), reads=[dbuf], writes=[dbuf])

                P.op("dve", lambda e: e.tensor_tensor(out=ang.ap[:], in0=bc(sm.ap[:, TH, :].unsqueeze(2), [128, 32, NTB]),
                                                      in1=bc(tpos.ap[:].unsqueeze(1), [128, 32, NTB]), op=ALU.mult),
                     reads=[sm, tpos], writes=[ang])
                sincos(Ei, 0.0)
                sincos(Er, math.pi / 2)
                sm_op(lambda e: e.tensor_tensor(out=sm.ap[:, C1, :], in0=sm.ap[:, R, :], in1=Er.ap[:, :, 1], op=ALU.mult), extra=[Er])
                sm_op(lambda e: e.tensor_tensor(out=sm.ap[:, S1, :], in0=sm.ap[:, R, :], in1=Ei.ap[:, :, 1], op=ALU.mult), extra=[Ei])
                sm_op(lambda e: e.tensor_scalar(out=sm.ap[:, C1, :], in0=sm.ap[:, C1, :], scalar1=-1.0, scalar2=None, op0=ALU.add))
                sm_op(lambda e: e.tensor_tensor(out=sm.ap[:, TMP1, :], in0=sm.ap[:, A_RE, :], in1=sm.ap[:, A_RE, :], op=ALU.mult))
                sm_op(lambda e: e.tensor_tensor(out=sm.ap[:, TMP2, :], in0=sm.ap[:, A_IM, :], in1=sm.ap[:, A_IM, :], op=ALU.mult))
                sm_op(lambda e: e.tensor_tensor(out=sm.ap[:, TMP1, :], in0=sm.ap[:, TMP1, :], in1=sm.ap[:, TMP2, :], op=ALU.add))
                sm_op(lambda e: e.reciprocal(out=sm.ap[:, TMP1, :], in_=sm.ap[:, TMP1, :]))
                sm_op(lambda e: e.tensor_tensor(out=sm.ap[:, TMP2, :], in0=sm.ap[:, C1, :], in1=sm.ap[:, A_RE, :], op=ALU.mult))
                sm_op(lambda e: e.tensor_tensor(out=sm.ap[:, TMP3, :], in0=sm.ap[:, S1, :], in1=sm.ap[:, A_IM, :], op=ALU.mult))
                sm_op(lambda e: e.tensor_tensor(out=sm.ap[:, TMP2, :], in0=sm.ap[:, TMP2, :], in1=sm.ap[:, TMP3, :], op=ALU.add))
                sm_op(lambda e: e.tensor_tensor(out=sm.ap[:, FRE, :], in0=sm.ap[:, TMP2, :], in1=sm.ap[:, TMP1, :], op=ALU.mult))
                sm_op(lambda e: e.tensor_tensor(out=sm.ap[:, TMP2, :], in0=sm.ap[:, S1, :], in1=sm.ap[:, A_RE, :], op=ALU.mult))
                sm_op(lambda e: e.tensor_tensor(out=sm.ap[:, TMP3, :], in0=sm.ap[:, C1, :], in1=sm.ap[:, A_IM, :], op=ALU.mult))
                sm_op(lambda e: e.tensor_tensor(out=sm.ap[:, TMP2, :], in0=sm.ap[:, TMP2, :], in1=sm.ap[:, TMP3, :], op=ALU.subtract))
                sm_op(lambda e: e.tensor_tensor(out=sm.ap[:, FIM, :], in0=sm.ap[:, TMP2, :], in1=sm.ap[:, TMP1, :], op=ALU.mult))
                tt = sp.sb([128, 32, 64], F32, "tt")
                fre_b = bc(sm.ap[:, FRE, :].unsqueeze(2), [128, 32, 64])
                fim_b = bc(sm.ap[:, FIM, :].unsqueeze(2), [128, 32, 64])
                P.op("dve", lambda e: e.tensor_tensor(out=Tr.ap[:], in0=Er.ap[:, :, 0:64], in1=fre_b, op=ALU.mult), reads=[Er, sm], writes=[Tr])
                P.op("dve", lambda e: e.tensor_tensor(out=tt.ap[:], in0=Ei.ap[:, :, 0:64], in1=fim_b, op=ALU.mult), reads=[Ei, sm], writes=[tt])
                P.op("dve", lambda e: e.tensor_tensor(out=Tr.ap[:], in0=Tr.ap[:], in1=tt.ap[:], op=ALU.add), reads=[Tr, tt], writes=[Tr])
                P.op("dve", lambda e: e.tensor_tensor(out=Ti.ap[:], in0=Er.ap[:, :, 0:64], in1=fim_b, op=ALU.mult), reads=[Er, sm], writes=[Ti])
                P.op("dve", lambda e: e.tensor_tensor(out=tt.ap[:], in0=Ei.ap[:, :, 0:64], in1=fre_b, op=ALU.mult), reads=[Ei, sm, Ti], writes=[tt])
                P.op("dve", lambda e: e.tensor_tensor(out=Ti.ap[:], in0=Ti.ap[:], in1=tt.ap[:], op=ALU.subtract), reads=[Ti, tt], writes=[Ti])

                stg = [sp.sb([128, 8, 128], F32, "stg") for _ in range(2)]
                stg_sem = [P.no_dsem("stg") for _ in range(2)]
                it = 0
                for kind in range(4):
                    src = [W["s5_b_re"], W["s5_b_im"], W["s5_c_re"], W["s5_c_im"]][kind]
                    for jb in range(4):
                        sg = stg[it % 2]
                        P.op("pool", lambda e, sg=sg: e.memset(sg.ap[:], 0.0), writes=[sg])
                        for jj in range(8):
                            j = jb * 8 + jj
                            for gg in range(2):
                                g = 2 * j + gg
                                if kind < 2:
                                    P.dma("sp", sg.ap[gg * 64:(gg + 1) * 64, jj, (g % 8) * 16:(g % 8) * 16 + 16], src[g], pwrites=[sg], sem=stg_sem[it % 2])
                                else:
                                    P.dma("sp", sg.ap[(g % 8) * 16:(g % 8) * 16 + 16, jj, gg * 64:(gg + 1) * 64], src[g], pwrites=[sg], sem=stg_sem[it % 2])
                        for q4 in range(2):
                            psb = PS[(it * 2 + q4) % 4]
                            for u in range(4):
                                jj = q4 * 4 + u
                                P.op("pe", lambda e, psb=psb, u=u, jj=jj, sg=sg: e.transpose(out=psb.ap[:, u * 128:(u + 1) * 128], in_=sg.ap[:, jj, :], identity=ident_f.ap[:]),
                                     reads=[sg, ident_f], writes=[psb])
                            dst = (Bbd[kind] if kind < 2 else Cbd[kind - 2])
                            j0 = jb * 8 + q4 * 4
                            sc = -1.0 if kind == 3 else 1.0
                            P.op("act", lambda e, psb=psb, dst=dst, j0=j0, sc=sc: e.activation(out=dst.ap[:, j0:j0 + 4, :], in_=psb.ap[:].rearrange("p (u c) -> p u c", c=128), func=AF.Copy, scale=sc),
                                 reads=[psb], pwrites=[dst])
                        it += 1
                load_T(sp, dT, dT.ap[:, :], W["s5_d"].rearrange("(c p) -> c p", p=128), 8, PS[4])

            xt = [st.sb([128, D], F32, "x") for _ in range(2)]
            xsem = [P.no_dsem("x") for _ in range(2)]
            nctx = NormCtx(st, PS[0], PS[7])
            hTs = [st.sb([128, 8, 128], BF16, "hT") for _ in range(2)]
            wre = st.sb([128, 32, 128], BF16, "wre")
            wim = st.sb([128, 32, 128], BF16, "wim")
            gre = st.sb([128, 32, 128], F32, "gre")
            gim = st.sb([128, 32, 128], F32, "gim")
            hsr = st.sb([128, 32, 128], BF16, "hsr")
            hsi = st.sb([128, 32, 128], BF16, "hsi")
            tmp = [st.sb([128, 512], F32, "tmp") for _ in range(4)]
            init = st.sb([128, 2, 2, 32], F32, "init")
            hend = st.sb([128, 2, 2, 32], F32, "hend")
            ctmp = st.sb([128, 4, 32], F32, "ctmp")
            ysb = st.sb([128, 8, 128], F32, "ysb")
            zT = [st.sb([128, 8, 128], BF16, "zT") for _ in range(2)]
            P.op("pool", lambda e: e.memset(init.ap[:], 0.0), writes=[init])

            def cmul_small(dbuf, dst_re, dst_im, a_re, a_im, b_re, b_im, deps_r):
                P.op("dve", lambda e: e.tensor_tensor(out=ctmp.ap[:, 0, :], in0=a_re, in1=b_re, op=ALU.mult), reads=deps_r, pwrites=[ctmp])
                P.op("dve", lambda e: e.tensor_tensor(out=ctmp.ap[:, 1, :], in0=a_im, in1=b_im, op=ALU.mult), reads=deps_r, pwrites=[ctmp])
                P.op("dve", lambda e: e.tensor_tensor(out=ctmp.ap[:, 2, :], in0=a_re, in1=b_im, op=ALU.mult), reads=deps_r, pwrites=[ctmp])
                P.op("dve", lambda e: e.tensor_tensor(out=ctmp.ap[:, 3, :], in0=a_im, in1=b_re, op=ALU.mult), reads=deps_r, pwrites=[ctmp])
                P.op("dve", lambda e: e.tensor_tensor(out=dst_re, in0=ctmp.ap[:, 0, :], in1=ctmp.ap[:, 1, :], op=ALU.subtract), reads=[ctmp], pwrites=[dbuf])
                P.op("dve", lambda e: e.tensor_tensor(out=dst_im, in0=ctmp.ap[:, 2, :], in1=ctmp.ap[:, 3, :], op=ALU.add), reads=[ctmp], pwrites=[dbuf])

            tlist = list(range(NT)) if upto >= 1 else ([] if upto < 0.55 else ([0] if upto < 0.65 else [0, NPT]))
            for t in tlist:
                is_s = (t == NPT)
                i = t % 2
                x = xt[i]
                src, srcb = x_src(t, "in")
                P.dma("sp", x.ap[:], src, writes=[x], sem=xsem[i])
                hT = hTs[i]
                nctx.run(x, x.ap[:], hT, hT.ap[:], 0, 0, is_s)
                if is_s:
                    for s in range(2):
                        cmul_small(init, init.ap[:, 0, s, :], init.ap[:, 1, s, :], h0.ap[:, 0, s, :], h0.ap[:, 1, s, :],
                                   Er.ap[:, :, 1], Ei.ap[:, :, 1], [h0, Er, Ei])
                import os as _os
                CUT = int(_os.environ.get('KD_CUT', '99'))
                if CUT < 2:
                    continue
                for m in range(8):
                    pr, pi_ = PS[1 + (m % 2) * 2], PS[2 + (m % 2) * 2]
                    for u in range(4):
                        j = m * 4 + u
                        P.op("pe", lambda e, pr=pr, u=u, j=j, m=m, hT=hT: e.matmul(out=pr.ap[:, u * 128:(u + 1) * 128], lhsT=Bbd[0].ap[:, j, :], rhs=hT.ap[:, m, :], start=True, stop=True),
                             reads=[Bbd[0], hT], writes=[pr])
                    for u in range(4):
                        j = m * 4 + u
                        P.op("pe", lambda e, pi_=pi_, u=u, j=j, m=m, hT=hT: e.matmul(out=pi_.ap[:, u * 128:(u + 1) * 128], lhsT=Bbd[1].ap[:, j, :], rhs=hT.ap[:, m, :], start=True, stop=True),
                             reads=[Bbd[1], hT], writes=[pi_])
                    prv = pr.ap[:].rearrange("p (u h t) -> p u h t", u=4, h=2)
                    piv = pi_.ap[:].rearrange("p (u h t) -> p u h t", u=4, h=2)
                    Trv = bc(Tr.ap[:, m * 4:(m + 1) * 4, :].unsqueeze(2), [128, 4, 2, 64])
                    Tiv = bc(Ti.ap[:, m * 4:(m + 1) * 4, :].unsqueeze(2), [128, 4, 2, 64])
                    tv = [tb.ap[:].rearrange("p (u h t) -> p u h t", u=4, h=2) for tb in tmp]
                    wrv = wre.ap[:, m * 4:(m + 1) * 4, :].rearrange("p u (h t) -> p u h t", h=2)
                    wiv = wim.ap[:, m * 4:(m + 1) * 4, :].rearrange("p u (h t) -> p u h t", h=2)
                    P.op("dve", lambda e, prv=prv, Trv=Trv, tv=tv: e.tensor_tensor(out=tv[0], in0=prv, in1=Trv, op=ALU.mult), reads=[pr, Tr], writes=[tmp[0]])
                    P.op("dve", lambda e, piv=piv, Tiv=Tiv, tv=tv: e.tensor_tensor(out=tv[1], in0=piv, in1=Tiv, op=ALU.mult), reads=[pi_, Ti], writes=[tmp[1]])
                    P.op("dve", lambda e, prv=prv, Tiv=Tiv, tv=tv: e.tensor_tensor(out=tv[2], in0=prv, in1=Tiv, op=ALU.mult), reads=[pr, Ti], writes=[tmp[2]])
                    P.op("dve", lambda e, piv=piv, Trv=Trv, tv=tv: e.tensor_tensor(out=tv[3], in0=piv, in1=Trv, op=ALU.mult), reads=[pi_, Tr], writes=[tmp[3]])
                    P.op("pool", lambda e, wrv=wrv, tv=tv: e.tensor_tensor(out=wrv, in0=tv[0], in1=tv[1], op=ALU.subtract), reads=[tmp[0], tmp[1]], pwrites=[wre])
                    P.op("pool", lambda e, wiv=wiv, tv=tv: e.tensor_tensor(out=wiv, in0=tv[2], in1=tv[3], op=ALU.add), reads=[tmp[2], tmp[3]], pwrites=[wim])
                if CUT < 3:
                    continue
                for hf in range(2):
                    for j in range(32):
                        for ri, (wsrc, gdst) in enumerate(((wre, gre), (wim, gim))):
                            P.op("dve", lambda e, j=j, hf=hf, ri=ri, wsrc=wsrc, gdst=gdst: e.tensor_tensor_scan(
                                out=gdst.ap[:, j, hf * 64:(hf + 1) * 64], data0=bc(sm.ap[:, R, j:j + 1], [128, 64]),
                                data1=wsrc.ap[:, j, hf * 64:(hf + 1) * 64], initial=init.ap[:, ri, hf, j:j + 1], op0=ALU.mult, op1=ALU.add),
                                reads=[sm, wsrc, init], pwrites=[gdst])
                    ge_r = gre.ap[:, :, hf * 64 + 63]
                    ge_i = gim.ap[:, :, hf * 64 + 63]
                    if not is_s:
                        nh = 1 - hf
                        cmul_small(init, init.ap[:, 0, nh, :], init.ap[:, 1, nh, :], ge_r, ge_i, Er.ap[:, :, 64], Ei.ap[:, :, 64], [gre, gim, Er, Ei])
                    if is_s or (t == NPT - 1 and hf == 1):
                        cmul_small(hend, hend.ap[:, 0, hf, :], hend.ap[:, 1, hf, :], ge_r, ge_i, Er.ap[:, :, 63], Ei.ap[:, :, 63], [gre, gim, Er, Ei])
                        if is_s:
                            P.dma("sp", O["s5s_re"][hf].rearrange("(j p) -> p j", p=128), hend.ap[:, 0, hf, :], reads=[hend], pwrites=[out_bufs["s5s_re"]], sem=osem, allow_slow_non_contiguous=True)
                            P.dma("sp", O["s5s_im"][hf].rearrange("(j p) -> p j", p=128), hend.ap[:, 1, hf, :], reads=[hend], pwrites=[out_bufs["s5s_im"]], sem=osem, allow_slow_non_contiguous=True)
                        else:
                            P.dma("sp", O["s5p_re"].rearrange("(j p) -> p j", p=128), hend.ap[:, 0, hf, :], reads=[hend], pwrites=[out_bufs["s5p_re"]], sem=osem, allow_slow_non_contiguous=True)
                            P.dma("sp", O["s5p_im"].rearrange("(j p) -> p j", p=128), hend.ap[:, 1, hf, :], reads=[hend], pwrites=[out_bufs["s5p_im"]], sem=osem, allow_slow_non_contiguous=True)
                if CUT < 4:
                    continue
                z = zT[i]
                for m in range(8):
                    grv = gre.ap[:, m * 4:(m + 1) * 4, :].rearrange("p u (h t) -> p u h t", h=2)
                    giv = gim.ap[:, m * 4:(m + 1) * 4, :].rearrange("p u (h t) -> p u h t", h=2)
                    Erv = bc(Er.ap[:, m * 4:(m + 1) * 4, 0:64].unsqueeze(2), [128, 4, 2, 64])
                    Eiv = bc(Ei.ap[:, m * 4:(m + 1) * 4, 0:64].unsqueeze(2), [128, 4, 2, 64])
                    tv = [tb.ap[:].rearrange("p (u h t) -> p u h t", u=4, h=2) for tb in tmp]
                    hrv = hsr.ap[:, m * 4:(m + 1) * 4, :].rearrange("p u (h t) -> p u h t", h=2)
                    hiv = hsi.ap[:, m * 4:(m + 1) * 4, :].rearrange("p u (h t) -> p u h t", h=2)
                    P.op("dve", lambda e, grv=grv, Erv=Erv, tv=tv: e.tensor_tensor(out=tv[0], in0=grv, in1=Erv, op=ALU.mult), reads=[gre, Er], writes=[tmp[0]])
                    P.op("pool", lambda e, giv=giv, Eiv=Eiv, tv=tv: e.tensor_tensor(out=tv[1], in0=giv, in1=Eiv, op=ALU.mult), reads=[gim, Ei], writes=[tmp[1]])
                    P.op("pool", lambda e, grv=grv, Eiv=Eiv, tv=tv: e.tensor_tensor(out=tv[2], in0=grv, in1=Eiv, op=ALU.mult), reads=[gre, Ei], writes=[tmp[2]])
                    P.op("pool", lambda e, giv=giv, Erv=Erv, tv=tv: e.tensor_tensor(out=tv[3], in0=giv, in1=Erv, op=ALU.mult), reads=[gim, Er], writes=[tmp[3]])
                    P.op("dve", lambda e, hrv=hrv, tv=tv: e.tensor_tensor(out=hrv, in0=tv[0], in1=tv[1], op=ALU.subtract), reads=[tmp[0], tmp[1]], pwrites=[hsr])
                    P.op("pool", lambda e, hiv=hiv, tv=tv: e.tensor_tensor(out=hiv, in0=tv[2], in1=tv[3], op=ALU.add), reads=[tmp[2], tmp[3]], pwrites=[hsi])
                    py = PS[5 + (m // 4)]
                    for u in range(4):
                        j = m * 4 + u
                        P.op("pe", lambda e, py=py, m=m, j=j, u=u: e.matmul(out=py.ap[:, (m % 4) * 128:(m % 4 + 1) * 128], lhsT=Cbd[0].ap[:, j, :], rhs=hsr.ap[:, j, :], start=(u == 0), stop=False),
                             reads=[Cbd[0], hsr], writes=[py])
                    for u in range(4):
                        j = m * 4 + u
                        P.op("pe", lambda e, py=py, m=m, j=j, u=u: e.matmul(out=py.ap[:, (m % 4) * 128:(m % 4 + 1) * 128], lhsT=Cbd[1].ap[:, j, :], rhs=hsi.ap[:, j, :], start=False, stop=(u == 3)),
                             reads=[Cbd[1], hsi], writes=[py])
                    P.op("dve", lambda e, py=py, m=m, hT=hT: e.scalar_tensor_tensor(out=ysb.ap[:, m, :], in0=hT.ap[:, m, :], scalar=dT.ap[:, m:m + 1],
                                                                              in1=py.ap[:, (m % 4) * 128:(m % 4 + 1) * 128], op0=ALU.mult, op1=ALU.add),
                         reads=[hT, dT, py], pwrites=[ysb])
                if CUT < 5:
                    continue
                P.op("act", lambda e, z=z: e.activation(out=z.ap[:], in_=ysb.ap[:], func=AF.Gelu_apprx_tanh), reads=[ysb], writes=[z])
                P.dma("sp", zT_d[t], z.ap[:].rearrange("p k t -> p (k t)"), reads=[z], pwrites=[zT_b], sem=ssem)

        if en(2):
          with Stage(P, "l0g") as st:
            Wa = st.sb([128, 8, D], BF16, "Wa")
            Wb = st.sb([128, 8, D], BF16, "Wb")
            wsem = P.no_dsem("wglu")
            load_w_bf16(Wa, lambda kc: Wa.ap[:, kc, :], W["s5_w_glu_a"], 8, wsem)
            load_w_bf16(Wb, lambda kc: Wb.ap[:, kc, :], W["s5_w_glu_b"], 8, wsem)
            gate = Gate(st, 0, 0)
            xt = [st.sb([128, D], F32, "x") for _ in range(2)]
            xsem = [P.no_dsem("x") for _ in range(2)]
            zt = [st.sb([128, 8, 128], BF16, "z") for _ in range(2)]
            zsem = [P.no_dsem("z") for _ in range(2)]
            sig = [st.sb([128, D], F32, "sig") for _ in range(2)]
            for t in range(NT):
                is_s = (t == NPT)
                i = t % 2
                x, z, sg_ = xt[i], zt[i], sig[i]
                src, _ = x_src(t, "in")
                P.dma("sp", x.ap[:], src, writes=[x], sem=xsem[i])
                P.dma("sp", z.ap[:].rearrange("p k t -> p (k t)"), zT_d[t], reads=[zT_b], writes=[z], sem=zsem[i])
                g = gate.get(is_s)
                for cbk in range(2):
                    pa, pb = PS[cbk * 2], PS[1 + cbk * 2]
                    for kc in range(8):
                        P.op("pe", lambda e, pa=pa, kc=kc, cbk=cbk, z=z: e.matmul(out=pa.ap[:], lhsT=z.ap[:, kc, :], rhs=Wa.ap[:, kc, cbk * 512:(cbk + 1) * 512], start=(kc == 0), stop=(kc == 7)),
                             reads=[z, Wa], writes=[pa])
                    for kc in range(8):
                        P.op("pe", lambda e, pb=pb, kc=kc, cbk=cbk, z=z: e.matmul(out=pb.ap[:], lhsT=z.ap[:, kc, :], rhs=Wb.ap[:, kc, cbk * 512:(cbk + 1) * 512], start=(kc == 0), stop=(kc == 7)),
                             reads=[z, Wb], writes=[pb])
                    sl = slice(cbk * 512, (cbk + 1) * 512)
                    P.op("act", lambda e, pb=pb, sl=sl, sg_=sg_: e.activation(out=sg_.ap[:, sl], in_=pb.ap[:], func=AF.Sigmoid), reads=[pb], pwrites=[sg_])
                    P.op("dve", lambda e, pa=pa, sl=sl, sg_=sg_: e.tensor_tensor(out=sg_.ap[:, sl], in0=pa.ap[:], in1=sg_.ap[:, sl], op=ALU.mult), reads=[pa, sg_], pwrites=[sg_])
                    P.op("pool", lambda e, sl=sl, g=g, sg_=sg_: e.tensor_tensor(out=sg_.ap[:, sl], in0=sg_.ap[:, sl], in1=g.ap[:, sl], op=ALU.mult), reads=[sg_, g], pwrites=[sg_])
                    P.op("pool", lambda e, sl=sl, sg_=sg_, x=x: e.tensor_tensor(out=sg_.ap[:, sl], in0=sg_.ap[:, sl], in1=x.ap[:, sl], op=ALU.add), reads=[sg_, x], pwrites=[sg_])
                dst, dstb = x_dst(t, "a")
                P.dma("sp", dst, sg_.ap[:], reads=[sg_], pwrites=[dstb], sem=ssem)

        def mlp_stage(l, src_w, dst_w, final=False):
          with Stage(P, "mlp%d" % l) as st:
            wup = st.sb([128, 8, 4 * D], BF16, "wup")
            wdn = st.sb([128, 32, D], BF16, "wdn")
            wsem = P.no_dsem("wmlp")
            for kc in range(8):
                for c0 in range(0, 4 * D, 1024):
                    P.dma("pool", wup.ap[:, kc, c0:c0 + 1024], W["w_up"][l, kc * 128:(kc + 1) * 128, c0:c0 + 1024], pwrites=[wup], sem=wsem)
            for kc in range(32):
                P.dma("pool", wdn.ap[:, kc, :], W["w_down"][l, kc * 128:(kc + 1) * 128, :], pwrites=[wdn], sem=wsem)
            gate = Gate(st, l, 1)
            if final:
                gfin = st.sb([128, D], F32, "gfin")
                gsem = P.no_dsem("gfin")
                P.dma("sp", gfin.ap[:], W["g_final"].partition_broadcast(128), writes=[gfin], sem=gsem)
                fstat = st.sb([128, 4], F32, "fstat")
            xs_ = st.sb([128, 4, D], F32, "xs")
            xsem = P.no_dsem("x")
            nctx = NormCtx(st, PS[0], PS[3])
            hT = st.sb([128, 8, 512], BF16, "hT")
            actT = st.sb([128, 32, 512], BF16, "actT")
            rl = [st.sb([128, 512], BF16, "rl") for _ in range(2)]
            tiles = [(s * 4, 4) for s in range(NPT // 4)] + [(NPT, 1)]
            for (t0, nt) in tiles:
                is_s = (t0 == NPT)
                ntok = nt * 128
                for a in range(nt):
                    src, srcb = x_src(t0 + a, src_w)
                    P.dma("sp", xs_.ap[:, a, :], src, reads=([srcb] if srcb else []), pwrites=[xs_], sem=xsem)
                for a in range(nt):
                    hv = hT.ap[:, :, a * 128:(a + 1) * 128]
                    nctx.run(xs_, xs_.ap[:, a, :], hT, hv, l, 1, is_s)
                for oc in range(32):
                    pu = PS[(1, 2, 4, 5, 6, 7)[oc % 6]]
                    for kc in range(8):
                        P.op("pe", lambda e, pu=pu, kc=kc, oc=oc, ntok=ntok: e.matmul(out=pu.ap[:, 0:ntok], lhsT=wup.ap[:, kc, oc * 128:(oc + 1) * 128], rhs=hT.ap[:, kc, 0:ntok], start=(kc == 0), stop=(kc == 7)),
                             reads=[wup, hT], writes=[pu])
                    r = rl[oc % 2]
                    P.op("act", lambda e, pu=pu, r=r, ntok=ntok: e.activation(out=r.ap[:, 0:ntok], in_=pu.ap[:, 0:ntok], func=AF.Relu), reads=[pu], writes=[r])
                    eng = "dve" if oc % 2 == 0 else "pool"
                    P.op(eng, lambda e, r=r, oc=oc, ntok=ntok: e.tensor_tensor(out=actT.ap[:, oc, 0:ntok], in0=r.ap[:, 0:ntok], in1=r.ap[:, 0:ntok], op=ALU.mult), reads=[r], pwrites=[actT])
                g = gate.get(is_s)
                for a in range(nt):
                    for cbk in range(2):
                        pd = PS[4 + (a * 2 + cbk) % 4]
                        for kc in range(32):
                            P.op("pe", lambda e, pd=pd, kc=kc, a=a, cbk=cbk: e.matmul(out=pd.ap[:], lhsT=actT.ap[:, kc, a * 128:(a + 1) * 128], rhs=wdn.ap[:, kc, cbk * 512:(cbk + 1) * 512], start=(kc == 0), stop=(kc == 31)),
                                 reads=[actT, wdn], writes=[pd])
                        sl = slice(cbk * 512, (cbk + 1) * 512)
                        tmpb = rl
                        P.op("dve", lambda e, pd=pd, a=a, sl=sl, g=g: e.tensor_tensor(out=pd.ap[:], in0=pd.ap[:], in1=g.ap[:, sl], op=ALU.mult), reads=[pd, g], writes=[pd])
                        P.op("dve", lambda e, pd=pd, a=a, sl=sl: e.tensor_tensor(out=xs_.ap[:, a, sl], in0=pd.ap[:], in1=xs_.ap[:, a, sl], op=ALU.add), reads=[pd, xs_], pwrites=[xs_])
                    if final:
                        junk, stat = nctx.junk, fstat
                        P.op("act", lambda e, a=a: e.activation(out=junk.ap[:], in_=xs_.ap[:, a, :], func=AF.Square, accum_out=stat.ap[:, 0:1]), reads=[xs_], writes=[junk, stat])
                        P.op("act", lambda e: e.activation(out=stat.ap[:, 1:2], in_=stat.ap[:, 0:1], func=AF.Sqrt, scale=1.0 / D, bias=EPS), reads=[stat], writes=[stat])
                        P.op("dve", lambda e: e.reciprocal(out=stat.ap[:, 2:3], in_=stat.ap[:, 1:2]), reads=[stat], writes=[stat])
                        P.op("dve", lambda e, a=a: e.scalar_tensor_tensor(out=xs_.ap[:, a, :], in0=xs_.ap[:, a, :], scalar=stat.ap[:, 2:3], in1=gfin.ap[:], op0=ALU.mult, op1=ALU.mult),
                             reads=[xs_, stat, gfin], pwrites=[xs_])
                        t = t0 + a
                        if t < NPT:
                            P.dma("sp", O["yp"][t * 128:(t + 1) * 128, :], xs_.ap[:, a, :], reads=[xs_], pwrites=[out_bufs["yp"]], sem=osem)
                        else:
                            P.dma("sp", O["ys"], xs_.ap[:, a, :], reads=[xs_], pwrites=[out_bufs["ys"]], sem=osem)
                    else:
                        dst, dstb = x_dst(t0 + a, dst_w)
                        P.dma("sp", dst, xs_.ap[:, a, :], reads=[xs_], pwrites=[dstb], sem=ssem)

        if en(3):
            mlp_stage(0, "a", "b")

        def sin_reduce(st, dbuf, abuf, shift, shape):
            kf = st.sb(shape, F32, "kf")
            ki = st.sb(shape, I32, "ki")
            dst, src, k_ap, ki_ap = dbuf.ap[:], abuf.ap[:], kf.ap[:], ki.ap[:]
            P.op("dve", lambda e: e.tensor_scalar(out=k_ap, in0=src, scalar1=shift, scalar2=1.0 / TWO_PI, op0=ALU.add, op1=ALU.mult), reads=[abuf], writes=[kf])
            P.op("dve", lambda e: e.tensor_copy(out=ki_ap, in_=k_ap), reads=[kf], writes=[ki])
            P.op("dve", lambda e: e.tensor_copy(out=k_ap, in_=ki_ap), reads=[ki], writes=[kf])
            P.op("dve", lambda e: e.scalar_tensor_tensor(out=dst, in0=k_ap, scalar=-CW1, in1=src, op0=ALU.mult, op1=ALU.add), reads=[kf, abuf], writes=[dbuf])
            P.op("dve", lambda e: e.scalar_tensor_tensor(out=dst, in0=k_ap, scalar=-CW2, in1=dst, op0=ALU.mult, op1=ALU.add), reads=[kf, dbuf], writes=[dbuf])
            P.op("dve", lambda e: e.scalar_tensor_tensor(out=dst, in0=k_ap, scalar=-CW3, in1=dst, op0=ALU.mult, op1=ALU.add), reads=[kf, dbuf], writes=[dbuf])
            P.op("dve", lambda e: e.tensor_scalar(out=dst, in0=dst, scalar1=shift, scalar2=None, op0=ALU.add), reads=[dbuf], writes=[dbuf])
            P.op("dve", lambda e: e.tensor_scalar(out=dst, in0=dst, scalar1=math.pi, scalar2=-math.pi, op0=ALU.min, op1=ALU.max), reads=[dbuf], writes=[dbuf])
            P.op("act", lambda e: e.activation(out=dst, in_=dst, func=AF.Sin), reads=[dbuf], writes=[dbuf])

        def rope_tables(st, half):
            cosT = st.sb([128, NT, half], F32, "cosT")
            sinT = st.sb([128, NT, half], F32, "sinT")
            with Stage(P, "rp") as sp:
                pos = sp.sb([128, NT], F32, "pos")
                invf = sp.sb([128, half], F32, "invf")
                ang = sp.sb([128, NT, half], F32, "ang")
                P.op("pool", lambda e: e.iota(pos.ap[:], pattern=[[128, NT]], base=0, channel_multiplier=1, allow_small_or_imprecise_dtypes=True), writes=[pos])
                P.op("pool", lambda e: e.iota(pos.ap[0:64, NPT:NPT + 1], pattern=[[0, 1]], base=PAST, channel_multiplier=1, allow_small_or_imprecise_dtypes=True), pwrites=[pos])
                P.op("pool", lambda e: e.iota(pos.ap[64:128, NPT:NPT + 1], pattern=[[0, 1]], base=PAST, channel_multiplier=1, allow_small_or_imprecise_dtypes=True), pwrites=[pos])
                for i in range(half):
                    P.op("pool", lambda e, i=i: e.memset(invf.ap[:, i:i + 1], float(np.float32(10000.0) ** np.float32(-i / half))), pwrites=[invf])
                P.op("dve", lambda e: e.tensor_tensor(out=ang.ap[:], in0=bc(pos.ap[:].unsqueeze(2), [128, NT, half]), in1=bc(invf.ap[:].unsqueeze(1), [128, NT, half]), op=ALU.mult),
                     reads=[pos, invf], writes=[ang])
                sin_reduce(sp, sinT, ang, 0.0, [128, NT, half])
                sin_reduce(sp, cosT, ang, math.pi / 2, [128, NT, half])
            return cosT, sinT

        def rope_apply(src_ps, src_ap, dbuf, dst_ap, ngrp, half, cosT, sinT, t, tmps):
            c = bc(cosT.ap[:, t, :].unsqueeze(1), [128, ngrp, half])
            s = bc(sinT.ap[:, t, :].unsqueeze(1), [128, ngrp, half])
            x1, x2 = src_ap[:, :, 0:half], src_ap[:, :, half:2 * half]
            tv = [tb.ap[:, 0:ngrp * half].rearrange("p (g i) -> p g i", i=half) for tb in tmps]
            P.op("dve", lambda e: e.tensor_tensor(out=tv[0], in0=x1, in1=c, op=ALU.mult), reads=[src_ps, cosT], writes=[tmps[0]])
            P.op("dve", lambda e: e.tensor_tensor(out=tv[1], in0=x2, in1=s, op=ALU.mult), reads=[src_ps, sinT], writes=[tmps[1]])
            P.op("dve", lambda e: e.tensor_tensor(out=tv[2], in0=x1, in1=s, op=ALU.mult), reads=[src_ps, sinT], writes=[tmps[2]])
            P.op("dve", lambda e: e.tensor_tensor(out=tv[3], in0=x2, in1=c, op=ALU.mult), reads=[src_ps, cosT], writes=[tmps[3]])
            P.op("pool", lambda e: e.tensor_tensor(out=dst_ap[:, :, 0:half], in0=tv[0], in1=tv[1], op=ALU.subtract), reads=[tmps[0], tmps[1]], pwrites=[dbuf])
            P.op("pool", lambda e: e.tensor_tensor(out=dst_ap[:, :, half:2 * half], in0=tv[2], in1=tv[3], op=ALU.add), reads=[tmps[2], tmps[3]], pwrites=[dbuf])

        QT_d = dscr("QT_d", [8, 128, NTOK], BF16)
        KT_d = dscr("KT_d", [8, 128, NTOK], BF16)
        V_d = dscr("V_d", [NTOK, D], BF16)
        OT_d = dscr("OT_d", [8, 128, NTOK], BF16)
        QT_b, KT_b, V_b, OT_b = Buf(QT_d, "QT_d"), Buf(KT_d, "KT_d"), Buf(V_d, "V_d"), Buf(OT_d, "OT_d")

        if en(4):
          with Stage(P, "l1a") as st:
            cosT, sinT = rope_tables(st, 32)
            wqkv = st.sb([128, 8, 3 * D], BF16, "wqkv")
            for kc in range(8):
                for c0 in range(0, 3 * D, 1024):
                    P.dma("pool", wqkv.ap[:, kc, c0:c0 + 1024], W["diff_w_qkv"][kc * 128:(kc + 1) * 128, c0:c0 + 1024], pwrites=[wqkv])
            xt = [st.sb([128, D], F32, "x") for _ in range(2)]
            nctx = NormCtx(st, PS[0], PS[7])
            hTs = [st.sb([128, 8, 128], BF16, "hT") for _ in range(2)]
            qk = [st.sb([128, 2, D], F32, "qk") for _ in range(2)]
            qkb = [st.sb([128, 2, D], BF16, "qkb") for _ in range(2)]
            vf = [st.sb([128, D], F32, "vf") for _ in range(2)]
            vb = [st.sb([128, D], BF16, "vb") for _ in range(2)]
            tmps = [st.sb([128, 256], F32, "rt") for _ in range(4)]
            qkT = [st.sb([128, 16, 128], BF16, "qkT") for _ in range(2)]
            tlist = list(range(NT)) if upto >= 4.5 else [0, NPT]
            for t in tlist:
                is_s = (t == NPT)
                i = t % 2
                x, hT = xt[i], hTs[i]
                src, srcb = x_src(t, "b")
                P.dma("sp", x.ap[:], src, reads=[srcb], writes=[x])
                nctx.run(x, x.ap[:], hT, hT.ap[:], 1, 0, is_s)
                for cb in range(6):
                    pq = PS[1 + cb % 4]
                    for kc in range(8):
                        P.op("pe", lambda e, pq=pq, kc=kc, cb=cb, hT=hT: e.matmul(out=pq.ap[:], lhsT=hT.ap[:, kc, :], rhs=wqkv.ap[:, kc, cb * 512:(cb + 1) * 512], start=(kc == 0), stop=(kc == 7)),
                             reads=[hT, wqkv], writes=[pq])
                    if cb < 4:
                        which, half_ = cb // 2, cb % 2
                        dst = qk[i].ap[:, which, half_ * 512:(half_ + 1) * 512].rearrange("p (g d) -> p g d", d=64)
                        rope_apply(pq, pq.ap[:].rearrange("p (g d) -> p g d", d=64), qk[i], dst, 8, 32, cosT, sinT, t, tmps)
                    else:
                        sl = slice((cb - 4) * 512, (cb - 3) * 512)
                        P.op("act", lambda e, pq=pq, sl=sl, i=i: e.activation(out=vf[i].ap[:, sl], in_=pq.ap[:], func=AF.Copy), reads=[pq], pwrites=[vf[i]])
                        P.op("dve", lambda e, pq=pq, sl=sl, i=i: e.tensor_copy(out=vb[i].ap[:, sl], in_=vf[i].ap[:, sl]), reads=[vf[i]], pwrites=[vb[i]])
                if not is_s:
                    P.dma("sp", O["dkp"][t * 128:(t + 1) * 128, :], qk[i].ap[:, 1, :], reads=[qk[i]], pwrites=[out_bufs["dkp"]])
                    P.dma("sp", O["dvp"][t * 128:(t + 1) * 128, :], vf[i].ap[:], reads=[vf[i]], pwrites=[out_bufs["dvp"]])
                else:
                    P.dma("sp", O["dks"], qk[i].ap[:, 1, :], reads=[qk[i]], pwrites=[out_bufs["dks"]])
                    P.dma("sp", O["dvs"], vf[i].ap[:], reads=[vf[i]], pwrites=[out_bufs["dvs"]])
                P.dma("sp", V_d[t * 128:(t + 1) * 128, :], vb[i].ap[:], reads=[vb[i]], pwrites=[V_b])
                P.op("act", lambda e, i=i: e.activation(out=qkb[i].ap[:], in_=qk[i].ap[:], func=AF.Copy), reads=[qk[i]], writes=[qkb[i]])
                for which in range(2):
                    pt = PS[5 + which]
                    ptv = pt.ap[:].bitcast(BF16).rearrange("p (h t) -> p h t", t=128)
                    for h in range(8):
                        P.op("pe", lambda e, ptv=ptv, h=h, which=which, i=i: e.transpose(out=ptv[:, h, :], in_=qkb[i].ap[:, which, h * 128:(h + 1) * 128], identity=ident_b.ap[:]),
                             reads=[qkb[i], ident_b], writes=[pt])
                    eng = "act" if which == 0 else "dve"
                    if eng == "act":
                        P.op("act", lambda e, ptv=ptv, which=which, i=i: e.activation(out=qkT[i].ap[:, which * 8:(which + 1) * 8, :], in_=ptv, func=AF.Copy), reads=[pt], pwrites=[qkT[i]])
                    else:
                        P.op("dve", lambda e, ptv=ptv, which=which, i=i: e.tensor_copy(out=qkT[i].ap[:, which * 8:(which + 1) * 8, :], in_=ptv), reads=[pt], pwrites=[qkT[i]])
                P.dma("sp", QT_d[:, :, t * 128:(t + 1) * 128].rearrange("h p t -> p h t"), qkT[i].ap[:, 0:8, :], reads=[qkT[i]], pwrites=[QT_b])
                P.dma("sp", KT_d[:, :, t * 128:(t + 1) * 128].rearrange("h p t -> p h t"), qkT[i].ap[:, 8:16, :], reads=[qkT[i]], pwrites=[KT_b])

        if en(5):
          with Stage(P, "l1b") as st:
            lv = st.sb([128, 4, 64], F32, "lv")
            for k_i, nm in enumerate(["diff_lambda_q1", "diff_lambda_k1", "diff_lambda_q2", "diff_lambda_k2"]):
                P.dma("sp", lv.ap[:, k_i, :], W[nm].partition_broadcast(128), pwrites=[lv])
            lsc = st.sb([128, 8], F32, "lsc")
            lpr = st.sb([128, 2, 64], F32, "lpr")
            P.op("dve", lambda e: e.tensor_tensor(out=lpr.ap[:, 0, :], in0=lv.ap[:, 0, :], in1=lv.ap[:, 1, :], op=ALU.mult), reads=[lv], pwrites=[lpr])
            P.op("dve", lambda e: e.tensor_tensor(out=lpr.ap[:, 1, :], in0=lv.ap[:, 2, :], in1=lv.ap[:, 3, :], op=ALU.mult), reads=[lv], pwrites=[lpr])
            P.op("dve", lambda e: e.tensor_reduce(out=lsc.ap[:, 0:2], in_=lpr.ap[:], axis=mybir.AxisListType.X, op=ALU.add), reads=[lpr], writes=[lsc])
            P.op("act", lambda e: e.activation(out=lsc.ap[:, 2:4], in_=lsc.ap[:, 0:2], func=AF.Exp), reads=[lsc], writes=[lsc])
            P.op("dve", lambda e: e.tensor_tensor(out=lsc.ap[:, 4:5], in0=lsc.ap[:, 3:4], in1=lsc.ap[:, 2:3], op=ALU.subtract), reads=[lsc], writes=[lsc])
            P.op("dve", lambda e: e.tensor_scalar(out=lsc.ap[:, 5:6], in0=lsc.ap[:, 4:5], scalar1=-LAMBDA_INIT, scalar2=None, op0=ALU.add), reads=[lsc], writes=[lsc])
            neglam = lsc.ap[:, 5:6]
            SC = 64 ** -0.5

            KT = [st.sb([128, SEQ], BF16, "KT") for _ in range(2)]
            QT = [st.sb([128, SEQ], BF16, "QT") for _ in range(2)]
            Vh = [st.sb([128, NPT, 128], BF16, "Vh") for _ in range(2)]
            PT = [[st.sb([128, 512], BF16, "PT") for _ in range(3)] for _ in range(2)]
            R = [st.sb([128, 512], F32, "R") for _ in range(2)]
            o12 = [st.sb([128, 512], F32, "o12") for _ in range(2)]
            ob = [st.sb([128, 512], BF16, "ob") for _ in range(2)]
            Sb = [[PS[0], PS[1]], [PS[2], PS[3]]]
            Ob = [[PS[4], PS[5]], [PS[6], PS[7]]]
            acc = [[st.sb([128, 512], F32, "acc") for _ in range(2)] for _ in range(2)]
            oun = [st.sb([128, 512], F32, "oun") for _ in range(2)]
            accb = [st.sb([128, 512], BF16, "accb") for _ in range(2)]
            aeng = ["dve", "pool"]
            nheads = 8 if upto >= 5.5 else 1
            nQ = 16 if upto >= 5.5 else 2
            if _os.environ.get('KD_SKIP_L1B'):
                nheads = 0
            for h in range(nheads):
                sl_ = h % 2
                kt, qt, vh = KT[sl_], QT[sl_], Vh[sl_]
                for c4 in range(4):
                    cs = slice(c4 * 2048, (c4 + 1) * 2048)
                    P.dma("sp", kt.ap[:, cs], KT_d[h, :, cs], reads=[KT_b], pwrites=[kt])
                    P.dma("sp", qt.ap[:, cs], QT_d[h, :, cs], reads=[QT_b], pwrites=[qt])
                    bs = slice(c4 * 16, (c4 + 1) * 16)
                    P.dma("sp", vh.ap[:, bs, :], V_d[c4 * 2048:(c4 + 1) * 2048, h * 128:(h + 1) * 128].rearrange("(b p) e -> p b e", p=128), reads=[V_b], pwrites=[vh])
                for Q in range(nQ):
                    blocks = [(kb, 0, False) for kb in range(4 * Q)] + [(4 * Q + i_, 128 * i_, True) for i_ in range(4)]
                    n = len(blocks)
                    qp = Q % 2
                    for c in range(2):
                        P.op(aeng[c], lambda e, c=c, qp=qp: e.memset(acc[c][qp].ap[:], 0.0), writes=[acc[c][qp]])

                    def issue_S(idx, Q=Q, kt=kt, qt=qt, blocks=blocks):
                        kb, col0, diag = blocks[idx]
                        for c in range(2):
                            sb_ = Sb[c][idx % 2]
                            rs = slice(c * 64, (c + 1) * 64)
                            P.op("pe", lambda e, sb_=sb_, rs=rs, kb=kb, col0=col0: e.matmul(
                                out=sb_.ap[:, col0:512], lhsT=kt.ap[rs, kb * 128:(kb + 1) * 128], rhs=qt.ap[rs, Q * 512 + col0:(Q + 1) * 512], start=True, stop=True),
                                reads=[kt, qt], writes=[sb_])

                    issue_S(0)
                    for idx in range(n):
                        kb, col0, diag = blocks[idx]
                        for c in range(2):
                            sb_, pt = Sb[c][idx % 2], PT[c][idx % 3]
                            if not diag:
                                P.op("act", lambda e, sb_=sb_, pt=pt: e.activation(out=pt.ap[:], in_=sb_.ap[:], func=AF.Exp, scale=SC), reads=[sb_], writes=[pt])
                            else:
                                P.op("act", lambda e, sb_=sb_, pt=pt, col0=col0: e.activation(out=pt.ap[0:64, col0:512], in_=sb_.ap[0:64, col0:512], func=AF.Exp, scale=SC), reads=[sb_], writes=[pt])
                                P.op("act", lambda e, sb_=sb_, pt=pt, col0=col0: e.activation(out=pt.ap[64:128, col0:col0 + 64], in_=sb_.ap[64:128, col0:col0 + 64], func=AF.Copy, scale=0.0), reads=[sb_], pwrites=[pt])
                                P.op("act", lambda e, sb_=sb_, pt=pt, col0=col0: e.activation(out=pt.ap[64:128, col0 + 64:512], in_=sb_.ap[64:128, col0 + 64:512], func=AF.Exp, scale=SC), reads=[sb_], pwrites=[pt])
                        if idx + 1 < n:
                            issue_S(idx + 1)
                        for c in range(2):
                            pt = PT[c][idx % 3]
                            ob_ = Ob[c][qp]
                            P.op("pe", lambda e, ob_=ob_, pt=pt, kb=kb, col0=col0, idx=idx, diag=diag, vh=vh: e.matmul(
                                out=ob_.ap[:, col0:512], lhsT=vh.ap[:, kb, :], rhs=pt.ap[:, col0:512], start=(idx == 0), stop=diag, skip_group_check=True),
                                reads=[vh, pt], writes=[ob_])
                            ac = acc[c][qp]
                            P.op(aeng[c], lambda e, ac=ac, pt=pt, col0=col0: e.tensor_tensor(out=ac.ap[:, col0:512], in0=ac.ap[:, col0:512], in1=pt.ap[:, col0:512], op=ALU.add),
                                 reads=[ac, pt], writes=[ac])
                    for c in range(2):
                        ob_, ac = Ob[c][qp], acc[c][qp]
                        P.op("dve", lambda e, c=c, ob_=ob_: e.tensor_copy(out=oun[c].ap[:], in_=ob_.ap[:]), reads=[ob_], writes=[oun[c]])
                        P.op("dve", lambda e, c=c, ac=ac: e.tensor_copy(out=accb[c].ap[:], in_=ac.ap[:]), reads=[ac], writes=[accb[c]])
                        P.op("pe", lambda e, ob_=ob_, c=c: e.matmul(out=ob_.ap[:], lhsT=ones_b.ap[:], rhs=accb[c].ap[:], start=True, stop=True), reads=[ones_b, accb[c]], writes=[ob_])
                        P.op("dve", lambda e, c=c, ob_=ob_: e.reciprocal(out=R[c].ap[:], in_=ob_.ap[:]), reads=[ob_], writes=[R[c]])
                        P.op("dve", lambda e, c=c: e.tensor_tensor(out=o12[c].ap[:], in0=oun[c].ap[:], in1=R[c].ap[:], op=ALU.mult), reads=[oun[c], R[c]], writes=[o12[c]])
                    obq = ob[Q % 2]
                    P.op("dve", lambda e, obq=obq: e.scalar_tensor_tensor(out=obq.ap[:], in0=o12[1].ap[:], scalar=neglam, in1=o12[0].ap[:], op0=ALU.mult, op1=ALU.add),
                         reads=[o12[0], o12[1], lsc], writes=[obq])
                    P.dma("sp", OT_d[h, :, Q * 512:(Q + 1) * 512], obq.ap[:], reads=[obq], pwrites=[OT_b])

          with Stage(P, "l1s") as st:
            lv = st.sb([128, 4, 64], F32, "lv")
            for k_i, nm in enumerate(["diff_lambda_q1", "diff_lambda_k1", "diff_lambda_q2", "diff_lambda_k2"]):
                P.dma("sp", lv.ap[:, k_i, :], W[nm].partition_broadcast(128), pwrites=[lv])
            lsc = st.sb([128, 8], F32, "lsc")
            lpr = st.sb([128, 2, 64], F32, "lpr")
            P.op("dve", lambda e: e.tensor_tensor(out=lpr.ap[:, 0, :], in0=lv.ap[:, 0, :], in1=lv.ap[:, 1, :], op=ALU.mult), reads=[lv], pwrites=[lpr])
            P.op("dve", lambda e: e.tensor_tensor(out=lpr.ap[:, 1, :], in0=lv.ap[:, 2, :], in1=lv.ap[:, 3, :], op=ALU.mult), reads=[lv], pwrites=[lpr])
            P.op("dve", lambda e: e.tensor_reduce(out=lsc.ap[:, 0:2], in_=lpr.ap[:], axis=mybir.AxisListType.X, op=ALU.add), reads=[lpr], writes=[lsc])
            P.op("act", lambda e: e.activation(out=lsc.ap[:, 2:4], in_=lsc.ap[:, 0:2], func=AF.Exp), reads=[lsc], writes=[lsc])
            P.op("dve", lambda e: e.tensor_tensor(out=lsc.ap[:, 4:5], in0=lsc.ap[:, 3:4], in1=lsc.ap[:, 2:3], op=ALU.subtract), reads=[lsc], writes=[lsc])
            P.op("dve", lambda e: e.tensor_scalar(out=lsc.ap[:, 5:6], in0=lsc.ap[:, 4:5], scalar1=-LAMBDA_INIT, scalar2=None, op0=ALU.add), reads=[lsc], writes=[lsc])
            neglam = lsc.ap[:, 5:6]
            SC = 64 ** -0.5
            zer = st.sb([128, 512], BF16, "zer")
            P.op("pool", lambda e: e.memset(zer.ap[:], 0.0), writes=[zer])
            Kt = [st.sb([128, D], BF16, "Kt") for _ in range(2)]
            Vt = [st.sb([128, D], BF16, "Vt") for _ in range(2)]
            KTb = [st.sb([128, 8, 128], BF16, "KTb") for _ in range(2)]
            QTs = st.sb([128, 8, 64], BF16, "QTs")
            PTs = [st.sb([128, 2, 512], BF16, "PTs") for _ in range(2)]
            Rr = st.sb([128, 1024], F32, "Rr")
            oo = st.sb([128, 1024], F32, "oo")
            obs = st.sb([128, 8, 64], BF16, "obs")
            Sbk, Obk, Lbk, Tbk = [PS[0], PS[1]], [PS[2], PS[3]], [PS[4], PS[5]], PS[6]
            NKB = PAST // 128
            for s in range(0 if _os.environ.get('KD_SKIP_L1S') else 2):
                tok0 = NPT * 128 + s * 64
                P.dma("sp", QTs.ap[:], QT_d[:, :, tok0:tok0 + 64].rearrange("h p t -> p h t"), reads=[QT_b], writes=[QTs])
                for b in Obk + Lbk:
                    P.op("pe", lambda e, b=b: e.matmul(out=b.ap[:], lhsT=zer.ap[:, 0:128], rhs=zer.ap[:], start=True, stop=False, skip_group_check=True), reads=[zer], writes=[b])
                _part = int(_os.environ.get('KD_L1S_PART', '9'))
                _kbs = [int(v) for v in _os.environ.get('KD_L1S_KBS', '').split(',')] if _os.environ.get('KD_L1S_KBS') else list(range(NKB + 1))
                for kb in _kbs:
                    i = kb % 2
                    last = (kb == NKB)
                    nk = 64 if last else 128
                    ktb = KTb[i]
                    if not last:
                        P.dma("pool", Kt[i].ap[:], I["cdk"][s, kb * 128:(kb + 1) * 128, :], writes=[Kt[i]])
                        P.dma("pool", Vt[i].ap[:], I["cdv"][s, kb * 128:(kb + 1) * 128, :], writes=[Vt[i]])
                        tv = Tbk.ap[:].bitcast(BF16).rearrange("p (h t) -> p h t", t=128)
                        for h in range(8):
                            P.op("pe", lambda e, tv=tv, h=h, i=i: e.transpose(out=tv[:, h, :], in_=Kt[i].ap[:, h * 128:(h + 1) * 128], identity=ident_b.ap[:]), reads=[Kt[i], ident_b], writes=[Tbk])
                        P.op("dve", lambda e, tv=tv, ktb=ktb: e.tensor_copy(out=ktb.ap[:], in_=tv), reads=[Tbk], writes=[ktb])
                    else:
                        P.dma("sp", ktb.ap[:, :, 0:64], KT_d[:, :, tok0:tok0 + 64].rearrange("h p t -> p h t"), reads=[KT_b], writes=[ktb])
                        P.dma("pool", Vt[i].ap[0:64, :], V_d[tok0:tok0 + 64, :], reads=[V_b], writes=[Vt[i]])
                    if _part < 2:
                        continue
                    for h in range(8):
                        for c in range(2):
                            sb_ = Sbk[c]
                            rs = slice(c * 64, (c + 1) * 64)
                            P.op("pe", lambda e, sb_=sb_, rs=rs, h=h, nk=nk, ktb=ktb: e.matmul(
                                out=sb_.ap[0:nk, h * 64:h * 64 + 64], lhsT=ktb.ap[rs, h, 0:nk], rhs=QTs.ap[rs, h, :], start=True, stop=True),
                                reads=[ktb, QTs], writes=[sb_])
                    pts = PTs[i]
                    for j in range(2):
                        P.op("act", lambda e, j=j, nk=nk, pts=pts: e.activation(out=pts.ap[0:nk, j, :], in_=Sbk[j].ap[0:nk, :], func=AF.Exp, scale=SC), reads=[Sbk[j]], pwrites=[pts])
                    if _part < 3:
                        continue
                    for h in range(8):
                        for c in range(2):
                            j, cs = c, slice(h * 64, h * 64 + 64)
                            P.op("pe", lambda e, j=j, cs=cs, h=h, nk=nk, i=i, pts=pts: e.matmul(
                                out=Obk[j].ap[:, cs], lhsT=Vt[i].ap[0:nk, h * 128:(h + 1) * 128], rhs=pts.ap[0:nk, j, cs], start=False, stop=True, skip_group_check=True),
                                reads=[Vt[i], pts], writes=[Obk[j]])
                            P.op("pe", lambda e, j=j, cs=cs, nk=nk, pts=pts: e.matmul(
                                out=Lbk[j].ap[:, cs], lhsT=ones_b.ap[0:nk, :], rhs=pts.ap[0:nk, j, cs], start=False, stop=True, skip_group_check=True),
                                reads=[ones_b, pts], writes=[Lbk[j]])
                if _part < 4:
                    continue
                for j in range(2):
                    js = slice(j * 512, (j + 1) * 512)
                    P.op("dve", lambda e, j=j, js=js: e.reciprocal(out=Rr.ap[:, js], in_=Lbk[j].ap[:]), reads=[Lbk[j]], pwrites=[Rr])
                    P.op("dve", lambda e, j=j, js=js: e.tensor_tensor(out=oo.ap[:, js], in0=Obk[j].ap[:], in1=Rr.ap[:, js], op=ALU.mult), reads=[Obk[j], Rr], pwrites=[oo])
                ov = oo.ap[:].rearrange("p (c h q) -> p c h q", c=2, q=64)
                P.op("dve", lambda e, ov=ov: e.scalar_tensor_tensor(out=obs.ap[:], in0=ov[:, 1, :, :], scalar=neglam, in1=ov[:, 0, :, :], op0=ALU.mult, op1=ALU.add),
                     reads=[oo, lsc], writes=[obs])
                P.dma("sp", OT_d[:, :, tok0:tok0 + 64].rearrange("h p t -> p h t"), obs.ap[:], reads=[obs], pwrites=[OT_b])

        def attn_out_stage(name, l, OT_src, OT_srcb, w_o_name, nh, e_dim, subnorm, gsub_name, src_w, dst_w, tile_ok):
          with Stage(P, name) as st:
            nk = nh * e_dim // 128
            wo = st.sb([128, nk, D], BF16, "wo")
            if subnorm:
                gsub = st.sb([128, 1], F32, "gsub")
                P.dma("sp", gsub.ap[:], W[gsub_name].rearrange("(p o) -> p o", o=1), writes=[gsub])
                wstg = [st.sb([128, D], F32, "wstg") for _ in range(2)]
                for kc in range(nk):
                    ws = wstg[kc % 2]
                    P.dma("sp", ws.ap[:], W[w_o_name][kc * 128:(kc + 1) * 128, :], writes=[ws])
                    P.op("dve", lambda e, ws=ws, kc=kc: e.tensor_scalar(out=wo.ap[:, kc, :], in0=ws.ap[:], scalar1=gsub.ap[:, 0:1], scalar2=(1.0 - LAMBDA_INIT), op0=ALU.mult, op1=ALU.mult),
                         reads=[ws, gsub], pwrites=[wo])
            else:
                for kc in range(nk):
                    P.dma("pool", wo.ap[:, kc, :], W[w_o_name][kc * 128:(kc + 1) * 128, :], pwrites=[wo])
            gate = Gate(st, l, 0)
            xt = [st.sb([128, D], F32, "x") for _ in range(2)]
            ot = [st.sb([128, nk, 128], BF16, "ot") for _ in range(2)]
            sq = st.sb([128, nk, 128], BF16, "sq")
            rs_ = st.sb([128, nk * 128], F32, "rs")
            on = [st.sb([128, nk, 128], BF16, "on") for _ in range(2)]
            for t in range(NT):
                if not tile_ok(t):
                    continue
                is_s = (t == NPT)
                i = t % 2
                x, o_ = xt[i], ot[i]
                src, srcb = x_src(t, src_w)
                P.dma("sp", x.ap[:], src, reads=[srcb], writes=[x])
                P.dma("sp", o_.ap[:], OT_src[:, :, t * 128:(t + 1) * 128].rearrange("h p t -> p h t"), reads=[OT_srcb], writes=[o_])
                if subnorm:
                    P.op("act", lambda e, o_=o_: e.activation(out=sq.ap[:], in_=o_.ap[:], func=AF.Square), reads=[o_], writes=[sq])
                    for j in range(2):
                        P.op("pe", lambda e, j=j: e.matmul(out=PS[j].ap[:], lhsT=ones_b.ap[:], rhs=sq.ap[:, j * 4:(j + 1) * 4, :].rearrange("p h t -> p (h t)"), start=True, stop=True),
                             reads=[ones_b, sq], writes=[PS[j]])
                        js = slice(j * 512, (j + 1) * 512)
                        P.op("act", lambda e, j=j, js=js: e.activation(out=rs_.ap[:, js], in_=PS[j].ap[:], func=AF.Sqrt, scale=1.0 / 128, bias=EPS), reads=[PS[j]], pwrites=[rs_])
                    P.op("dve", lambda e: e.reciprocal(out=rs_.ap[:], in_=rs_.ap[:]), reads=[rs_], writes=[rs_])
                    lhs = on[i]
                    P.op("dve", lambda e, o_=o_, lhs=lhs: e.tensor_tensor(out=lhs.ap[:].rearrange("p h t -> p (h t)"), in0=o_.ap[:].rearrange("p h t -> p (h t)"), in1=rs_.ap[:], op=ALU.mult),
                         reads=[o_, rs_], writes=[lhs])
                else:
                    lhs = o_
                g = gate.get(is_s)
                for cbk in range(2):
                    po = PS[2 + cbk + 2 * (t % 2)]
                    for kc in range(nk):
                        P.op("pe", lambda e, po=po, kc=kc, cbk=cbk, lhs=lhs: e.matmul(out=po.ap[:], lhsT=lhs.ap[:, kc, :], rhs=wo.ap[:, kc, cbk * 512:(cbk + 1) * 512], start=(kc == 0), stop=(kc == nk - 1)),
                             reads=[lhs, wo], writes=[po])
                    sl = slice(cbk * 512, (cbk + 1) * 512)
                    P.op("dve", lambda e, po=po, sl=sl, g=g: e.tensor_tensor(out=po.ap[:], in0=po.ap[:], in1=g.ap[:, sl], op=ALU.mult), reads=[po, g], writes=[po])
                    P.op("dve", lambda e, po=po, sl=sl, x=x: e.tensor_tensor(out=x.ap[:, sl], in0=po.ap[:], in1=x.ap[:, sl], op=ALU.add), reads=[po, x], pwrites=[x])
                dst, dstb = x_dst(t, dst_w)
                P.dma("sp", dst, x.ap[:], reads=[x], pwrites=[dstb])

        if en(6):
            attn_out_stage("l1c", 1, OT_d, OT_b, "diff_w_o", 8, 128, True, "diff_g_sub", "b", "a", lambda t: True)
        if en(7):
            mlp_stage(1, "a", "b")

        QA_d = dscr("QA_d", [16, 128, NTOK], BF16)
        QR_d = dscr("QR_d", [16, 32, NTOK], BF16)
        CK_d = dscr("CK_d", [NTOK, 128], BF16)
        CKT_d = dscr("CKT_d", [128, NTOK], BF16)
        KRT_d = dscr("KRT_d", [32, NTOK], BF16)
        OT2_d = dscr("OT2_d", [8, 128, NTOK], BF16)
        QA_b, QR_b, CK_b, CKT_b, KRT_b, OT2_b = (Buf(QA_d, "QA_d"), Buf(QR_d, "QR_d"), Buf(CK_d, "CK_d"), Buf(CKT_d, "CKT_d"),
                                                 Buf(KRT_d, "KRT_d"), Buf(OT2_d, "OT2_d"))
        MSC = 96 ** -0.5

        if en(8):
          with Stage(P, "l2a") as st:
            cos2, sin2 = rope_tables(st, 16)
            wdq = st.sb([128, 8, 416], BF16, "wdq")
            for kc in range(8):
                P.dma("pool", wdq.ap[:, kc, 0:256], W["mla_w_dq"][kc * 128:(kc + 1) * 128, :], pwrites=[wdq])
                P.dma("pool", wdq.ap[:, kc, 256:416], W["mla_w_dkv"][kc * 128:(kc + 1) * 128, :], pwrites=[wdq])
            gq = st.sb([128, 2], F32, "gq")
            P.dma("sp", gq.ap[:], W["mla_g_q"].rearrange("(c p) -> p c", p=128), writes=[gq], allow_slow_non_contiguous=True)
            wuq = st.sb([128, 2, 1536], BF16, "wuq")
            wst = [st.sb([128, 1536], F32, "wst") for _ in range(2)]
            for kc in range(2):
                P.dma("sp", wst[kc].ap[:], W["mla_w_uq"][kc * 128:(kc + 1) * 128, :], writes=[wst[kc]])
                P.op("dve", lambda e, kc=kc: e.tensor_scalar(out=wuq.ap[:, kc, :], in0=wst[kc].ap[:], scalar1=gq.ap[:, kc:kc + 1], scalar2=None, op0=ALU.mult), reads=[wst[kc], gq], pwrites=[wuq])
            wuk = st.sb([128, D], BF16, "wuk")
            P.dma("pool", wuk.ap[:], W["mla_w_uk"], writes=[wuk])
            wukT = st.sb([64, 16, 128], BF16, "wukT")
            for g4 in range(2):
                pb = PS[4 + g4]
                pv = pb.ap[:].bitcast(BF16).rearrange("p (h t) -> p h t", t=128)
                for u in range(8):
                    h = g4 * 8 + u
                    P.op("pe", lambda e, pv=pv, u=u, h=h: e.transpose(out=pv[0:64, u, :], in_=wuk.ap[:, h * 64:(h + 1) * 64], identity=ident_b.ap[:]), reads=[wuk, ident_b], writes=[pb])
                P.op("dve", lambda e, pv=pv, g4=g4: e.tensor_copy(out=wukT.ap[:, g4 * 8:(g4 + 1) * 8, :], in_=pv[0:64, :, :]), reads=[pb], pwrites=[wukT])
            gkv = st.sb([128, 128], F32, "gkv")
            P.dma("sp", gkv.ap[:], W["mla_g_kv"].partition_broadcast(128), writes=[gkv])
            xt = [st.sb([128, D], F32, "x") for _ in range(2)]
            nctx = NormCtx(st, PS[0], PS[7])
            hTs = [st.sb([128, 8, 128], BF16, "hT") for _ in range(2)]
            st4 = st.sb([128, 8], F32, "st4")
            jk = st.sb([128, 256], BF16, "jk")
            qn = st.sb([128, 256], BF16, "qn")
            qnT = st.sb([128, 2, 128], BF16, "qnT")
            qb = st.sb([128, 16, 96], BF16, "qb")
            qT = st.sb([96, 16, 128], BF16, "qT")
            qa = [st.sb([128, 16, 128], BF16, "qa") for _ in range(2)]
            ckf = [st.sb([128, 128], F32, "ckf") for _ in range(2)]
            ckb = [st.sb([128, 128], BF16, "ckb") for _ in range(2)]
            krf = [st.sb([128, 32], F32, "krf") for _ in range(2)]
            krb = [st.sb([128, 32], BF16, "krb") for _ in range(2)]
            ckT = [st.sb([128, 128], BF16, "ckT") for _ in range(2)]
            krT = [st.sb([32, 128], BF16, "krT") for _ in range(2)]
            tmps = [st.sb([128, 256], F32, "rt") for _ in range(4)]
            for t in range(NT):
                is_s = (t == NPT)
                i = t % 2
                x, hT = xt[i], hTs[i]
                src, srcb = x_src(t, "b")
                P.dma("sp", x.ap[:], src, reads=[srcb], writes=[x])
                nctx.run(x, x.ap[:], hT, hT.ap[:], 2, 0, is_s)
                pp = PS[1]
                for kc in range(8):
                    P.op("pe", lambda e, kc=kc, hT=hT: e.matmul(out=pp.ap[:, 0:416], lhsT=hT.ap[:, kc, :], rhs=wdq.ap[:, kc, :], start=(kc == 0), stop=(kc == 7)), reads=[hT, wdq], writes=[pp])
                P.op("act", lambda e: e.activation(out=jk.ap[:], in_=pp.ap[:, 0:256], func=AF.Square, accum_out=st4.ap[:, 0:1]), reads=[pp], writes=[jk, st4])
                P.op("act", lambda e: e.activation(out=st4.ap[:, 1:2], in_=st4.ap[:, 0:1], func=AF.Sqrt, scale=1.0 / 256, bias=EPS), reads=[st4], writes=[st4])
                P.op("dve", lambda e: e.reciprocal(out=st4.ap[:, 2:3], in_=st4.ap[:, 1:2]), reads=[st4], writes=[st4])
                P.op("act", lambda e: e.activation(out=qn.ap[:], in_=pp.ap[:, 0:256], func=AF.Copy, scale=st4.ap[:, 2:3]), reads=[pp, st4], writes=[qn])
                P.op("act", lambda e: e.activation(out=jk.ap[:, 0:128], in_=pp.ap[:, 256:384], func=AF.Square, accum_out=st4.ap[:, 4:5]), reads=[pp], writes=[jk, st4])
                P.op("act", lambda e: e.activation(out=st4.ap[:, 5:6], in_=st4.ap[:, 4:5], func=AF.Sqrt, scale=1.0 / 128, bias=EPS), reads=[st4], writes=[st4])
                P.op("dve", lambda e: e.reciprocal(out=st4.ap[:, 6:7], in_=st4.ap[:, 5:6]), reads=[st4], writes=[st4])
                P.op("dve", lambda e, i=i: e.scalar_tensor_tensor(out=ckf[i].ap[:], in0=pp.ap[:, 256:384], scalar=st4.ap[:, 6:7], in1=gkv.ap[:], op0=ALU.mult, op1=ALU.mult),
                     reads=[pp, st4, gkv], writes=[ckf[i]])
                P.op("dve", lambda e, i=i: e.tensor_copy(out=ckb[i].ap[:], in_=ckf[i].ap[:]), reads=[ckf[i]], writes=[ckb[i]])
                rope_apply(pp, pp.ap[:, 384:416].rearrange("p (g d) -> p g d", g=1), krf[i], krf[i].ap[:].rearrange("p (g d) -> p g d", g=1), 1, 16, cos2, sin2, t, tmps)
                P.op("dve", lambda e, i=i: e.tensor_copy(out=krb[i].ap[:], in_=krf[i].ap[:]), reads=[krf[i]], writes=[krb[i]])
                if not is_s:
                    P.dma("sp", O["ckvp"][t * 128:(t + 1) * 128, :], ckf[i].ap[:], reads=[ckf[i]], pwrites=[out_bufs["ckvp"]])
                    P.dma("sp", O["krp"][t * 128:(t + 1) * 128, :], krf[i].ap[:], reads=[krf[i]], pwrites=[out_bufs["krp"]])
                else:
                    P.dma("sp", O["ckvs"], ckf[i].ap[:], reads=[ckf[i]], pwrites=[out_bufs["ckvs"]])
                    P.dma("sp", O["krs"], krf[i].ap[:], reads=[krf[i]], pwrites=[out_bufs["krs"]])
                P.dma("sp", CK_d[t * 128:(t + 1) * 128, :], ckb[i].ap[:], reads=[ckb[i]], pwrites=[CK_b])
                p6 = PS[6]
                p6v = p6.ap[:].bitcast(BF16)
                P.op("pe", lambda e, i=i: e.transpose(out=p6v[:, 0:128], in_=ckb[i].ap[:], identity=ident_b.ap[:]), reads=[ckb[i], ident_b], writes=[p6])
                P.op("pe", lambda e, i=i: e.transpose(out=p6v[0:32, 128:256], in_=krb[i].ap[:], identity=ident_b.ap[:]), reads=[krb[i], ident_b], writes=[p6])
                for kc in range(2):
                    P.op("pe", lambda e, kc=kc: e.transpose(out=p6v[:, 256 + kc * 128:384 + kc * 128], in_=qn.ap[:, kc * 128:(kc + 1) * 128], identity=ident_b.ap[:]), reads=[qn, ident_b], writes=[p6])
                P.op("dve", lambda e, i=i: e.tensor_copy(out=ckT[i].ap[:], in_=p6v[:, 0:128]), reads=[p6], writes=[ckT[i]])
                P.op("dve", lambda e, i=i: e.tensor_copy(out=krT[i].ap[:], in_=p6v[0:32, 128:256]), reads=[p6], writes=[krT[i]])
                P.op("dve", lambda e: e.tensor_copy(out=qnT.ap[:], in_=p6v[:, 256:512].rearrange("p (k t) -> p k t", t=128)), reads=[p6], writes=[qnT])
                P.dma("sp", CKT_d[:, t * 128:(t + 1) * 128], ckT[i].ap[:], reads=[ckT[i]], pwrites=[CKT_b])
                P.dma("sp", KRT_d[:, t * 128:(t + 1) * 128], krT[i].ap[:], reads=[krT[i]], pwrites=[KRT_b])
                for qblk in range(4):
                    pq = PS[2 + qblk % 2]
                    for kc in range(2):
                        P.op("pe", lambda e, pq=pq, kc=kc, qblk=qblk: e.matmul(out=pq.ap[:, 0:384], lhsT=qnT.ap[:, kc, :], rhs=wuq.ap[:, kc, qblk * 384:(qblk + 1) * 384], start=(kc == 0), stop=(kc == 1)),
                             reads=[qnT, wuq], writes=[pq])
                    pqv = pq.ap[:, 0:384].rearrange("p (h d) -> p h d", d=96)
                    hs = slice(qblk * 4, (qblk + 1) * 4)
                    P.op("act", lambda e, pqv=pqv, hs=hs: e.activation(out=qb.ap[:, hs, 0:64], in_=pqv[:, :, 0:64], func=AF.Copy), reads=[pq], pwrites=[qb])
                    rope_apply(pq, pqv[:, :, 64:96], qb, qb.ap[:, hs, 64:96], 4, 16, cos2, sin2, t, tmps)
                for g4 in range(2):
                    pb = PS[4 + g4]
                    pv = pb.ap[:].bitcast(BF16).rearrange("p (h t) -> p h t", t=128)
                    for u in range(8):
                        h = g4 * 8 + u
                        P.op("pe", lambda e, pv=pv, u=u, h=h: e.transpose(out=pv[0:96, u, :], in_=qb.ap[:, h, :], identity=ident_b.ap[:]), reads=[qb, ident_b], writes=[pb])
                    if g4 == 0:
                        P.op("act", lambda e, pv=pv, g4=g4: e.activation(out=qT.ap[:, g4 * 8:(g4 + 1) * 8, :], in_=pv[0:96, :, :], func=AF.Copy), reads=[pb], pwrites=[qT])
                    else:
                        P.op("dve", lambda e, pv=pv, g4=g4: e.tensor_copy(out=qT.ap[:, g4 * 8:(g4 + 1) * 8, :], in_=pv[0:96, :, :]), reads=[pb], pwrites=[qT])
                P.dma("sp", QR_d[:, :, t * 128:(t + 1) * 128].rearrange("h p t -> p h t"), qT.ap[64:96, :, :], reads=[qT], pwrites=[QR_b])
                qa_ = qa[i]
                for g4 in range(4):
                    pa = PS[2 + g4 % 2]
                    for u in range(4):
                        h = g4 * 4 + u
                        P.op("pe", lambda e, pa=pa, u=u, h=h: e.matmul(out=pa.ap[:, u * 128:(u + 1) * 128], lhsT=wukT.ap[:, h, :], rhs=qT.ap[0:64, h, :], start=True, stop=True), reads=[wukT, qT], writes=[pa])
                    if g4 % 2 == 0:
                        P.op("act", lambda e, pa=pa, g4=g4, qa_=qa_: e.activation(out=qa_.ap[:, g4 * 4:(g4 + 1) * 4, :], in_=pa.ap[:].rearrange("p (u t) -> p u t", t=128), func=AF.Copy), reads=[pa], pwrites=[qa_])
                    else:
                        P.op("dve", lambda e, pa=pa, g4=g4, qa_=qa_: e.tensor_copy(out=qa_.ap[:, g4 * 4:(g4 + 1) * 4, :], in_=pa.ap[:].rearrange("p (u t) -> p u t", t=128)), reads=[pa], pwrites=[qa_])
                P.dma("sp", QA_d[:, :, t * 128:(t + 1) * 128].rearrange("h p t -> p h t"), qa_.ap[:], reads=[qa_], pwrites=[QA_b])

        if en(9):
          with Stage(P, "l2b") as st:
            wuv = st.sb([128, D], BF16, "wuv")
            P.dma("pool", wuv.ap[:], W["mla_w_uv"], writes=[wuv])
            CKT = st.sb([128, SEQ], BF16, "CKT")
            KRT = st.sb([128, SEQ], BF16, "KRT")
            CK = st.sb([128, NPT, 128], BF16, "CK")
            P.op("pool", lambda e: e.memset(KRT.ap[:], 0.0), writes=[KRT])
            for c4 in range(4):
                cs = slice(c4 * 2048, (c4 + 1) * 2048)
                P.dma("sp", CKT.ap[:, cs], CKT_d[:, cs], reads=[CKT_b], pwrites=[CKT])
                P.dma("sp", KRT.ap[0:32, cs], KRT_d[:, cs], reads=[KRT_b], pwrites=[KRT])
                P.dma("sp", CK.ap[:, c4 * 16:(c4 + 1) * 16, :], CK_d[c4 * 2048:(c4 + 1) * 2048, :].rearrange("(b p) e -> p b e", p=128), reads=[CK_b], pwrites=[CK])
            QA = [st.sb([128, SEQ], BF16, "QA") for _ in range(2)]
            QR = [st.sb([128, SEQ], BF16, "QR") for _ in range(2)]
            for b in QR:
                P.op("pool", lambda e, b=b: e.memset(b.ap[:], 0.0), writes=[b])
            PT = [st.sb([128, 512], BF16, "PT") for _ in range(4)]
            Rb = st.sb([128, 512], F32, "Rb")
            ol = st.sb([128, 512], BF16, "ol")
            ohb = [st.sb([64, 512], BF16, "ohb") for _ in range(2)]
            Sb, Obs, Lb, Eb = [PS[0], PS[1], PS[5], PS[6]], [PS[2], PS[3]], PS[7], PS[4]
            accs = [[st.sb([128, 512], F32, "acc") for _ in range(2)] for _ in range(2)]
            accb = st.sb([128, 512], BF16, "accb")
            aeng = ["dve", "pool"]
            nheads = 16 if upto >= 9.5 else 1
            nQ = 16 if upto >= 9.5 else 2
            for h in range(nheads):
                qa_, qr_ = QA[h % 2], QR[h % 2]
                for c4 in range(4):
                    cs = slice(c4 * 2048, (c4 + 1) * 2048)
                    P.dma("sp", qa_.ap[:, cs], QA_d[h, :, cs], reads=[QA_b], pwrites=[qa_])
                    P.dma("sp", qr_.ap[0:32, cs], QR_d[h, :, cs], reads=[QR_b], pwrites=[qr_])
                for Q in range(nQ):
                    blocks = [(kb, 0, False) for kb in range(4 * Q)] + [(4 * Q + i_, 128 * i_, True) for i_ in range(4)]
                    n = len(blocks)
                    qp = Q % 2
                    Ob = Obs[qp]
                    for k_ in range(2):
                        P.op(aeng[k_], lambda e, k_=k_, qp=qp: e.memset(accs[k_][qp].ap[:], 0.0), writes=[accs[k_][qp]])

                    def issue_S(idx, Q=Q, qa_=qa_, qr_=qr_, blocks=blocks):
                        kb, col0, diag = blocks[idx]
                        sb_ = Sb[idx % 4]
                        P.op("pe", lambda e, sb_=sb_, kb=kb, col0=col0: e.matmul(out=sb_.ap[:, col0:512], lhsT=CKT.ap[:, kb * 128:(kb + 1) * 128], rhs=qa_.ap[:, Q * 512 + col0:(Q + 1) * 512], start=True, stop=False),
                             reads=[CKT, qa_], writes=[sb_])
                        P.op("pe", lambda e, sb_=sb_, kb=kb, col0=col0: e.matmul(out=sb_.ap[:, col0:512], lhsT=KRT.ap[:, kb * 128:(kb + 1) * 128], rhs=qr_.ap[:, Q * 512 + col0:(Q + 1) * 512], start=False, stop=True),
                             reads=[KRT, qr_], writes=[sb_])

                    issue_S(0)
                    if n > 1:
                        issue_S(1)
                    for idx in range(n):
                        kb, col0, diag = blocks[idx]
                        sb_, pt = Sb[idx % 4], PT[idx % 4]
                        if not diag:
                            P.op("act", lambda e, sb_=sb_, pt=pt: e.activation(out=pt.ap[:], in_=sb_.ap[:], func=AF.Exp, scale=MSC), reads=[sb_], writes=[pt])
                        else:
                            P.op("act", lambda e, sb_=sb_, pt=pt, col0=col0: e.activation(out=pt.ap[0:64, col0:512], in_=sb_.ap[0:64, col0:512], func=AF.Exp, scale=MSC), reads=[sb_], writes=[pt])
                            P.op("act", lambda e, sb_=sb_, pt=pt, col0=col0: e.activation(out=pt.ap[64:128, col0:col0 + 64], in_=sb_.ap[64:128, col0:col0 + 64], func=AF.Copy, scale=0.0), reads=[sb_], pwrites=[pt])
                            P.op("act", lambda e, sb_=sb_, pt=pt, col0=col0: e.activation(out=pt.ap[64:128, col0 + 64:512], in_=sb_.ap[64:128, col0 + 64:512], func=AF.Exp, scale=MSC), reads=[sb_], pwrites=[pt])
                        if idx + 2 < n:
                            issue_S(idx + 2)
                        P.op("pe", lambda e, Ob=Ob, pt=pt, kb=kb, col0=col0, idx=idx, diag=diag: e.matmul(out=Ob.ap[:, col0:512], lhsT=CK.ap[:, kb, :], rhs=pt.ap[:, col0:512], start=(idx == 0), stop=diag, skip_group_check=True),
                             reads=[CK, pt], writes=[Ob])
                        ac = accs[idx % 2][qp]
                        P.op(aeng[idx % 2], lambda e, ac=ac, pt=pt, col0=col0: e.tensor_tensor(out=ac.ap[:, col0:512], in0=ac.ap[:, col0:512], in1=pt.ap[:, col0:512], op=ALU.add),
                             reads=[ac, pt], writes=[ac])
                    a0, a1 = accs[0][qp], accs[1][qp]
                    P.op("dve", lambda e, a0=a0, a1=a1: e.tensor_tensor(out=accb.ap[:], in0=a0.ap[:], in1=a1.ap[:], op=ALU.add), reads=[a0, a1], writes=[accb])
                    P.op("pe", lambda e: e.matmul(out=Lb.ap[:], lhsT=ones_b.ap[:], rhs=accb.ap[:], start=True, stop=True), reads=[ones_b, accb], writes=[Lb])
                    P.op("dve", lambda e: e.reciprocal(out=Rb.ap[:], in_=Lb.ap[:]), reads=[Lb], writes=[Rb])
                    P.op("dve", lambda e, Ob=Ob: e.tensor_tensor(out=ol.ap[:], in0=Ob.ap[:], in1=Rb.ap[:], op=ALU.mult), reads=[Ob, Rb], writes=[ol])
                    P.op("pe", lambda e, h=h: e.matmul(out=Eb.ap[0:64, :], lhsT=wuv.ap[:, h * 64:(h + 1) * 64], rhs=ol.ap[:], start=True, stop=True), reads=[wuv, ol], writes=[Eb])
                    oh = ohb[Q % 2]
                    P.op("act", lambda e, oh=oh: e.activation(out=oh.ap[:], in_=Eb.ap[0:64, :], func=AF.Copy), reads=[Eb], writes=[oh])
                    P.dma("sp", OT2_d[h // 2, (h % 2) * 64:(h % 2) * 64 + 64, Q * 512:(Q + 1) * 512], oh.ap[:], reads=[oh], pwrites=[OT2_b])

          with Stage(P, "l2s") as st:
            wuv = st.sb([128, D], BF16, "wuv")
            P.dma("pool", wuv.ap[:], W["mla_w_uv"], writes=[wuv])
            zer = st.sb([128, 512], BF16, "zer")
            P.op("pool", lambda e: e.memset(zer.ap[:], 0.0), writes=[zer])
            ckr = [st.sb([128, 128], BF16, "ckr") for _ in range(2)]
            krr = [st.sb([128, 32], BF16, "krr") for _ in range(2)]
            ckT = [st.sb([128, 128], BF16, "ckT") for _ in range(2)]
            krT = [st.sb([128, 128], BF16, "krT") for _ in range(2)]
            for b in krT:
                P.op("pool", lambda e, b=b: e.memset(b.ap[:], 0.0), writes=[b])
            QAs = st.sb([128, 16, 64], BF16, "QAs")
            QRs = st.sb([128, 16, 64], BF16, "QRs")
            P.op("pool", lambda e: e.memset(QRs.ap[:], 0.0), writes=[QRs])
            PTs = [st.sb([128, 1024], BF16, "PTs") for _ in range(2)]
            Rr = st.sb([128, 1024], F32, "Rr")
            olb = st.sb([128, 1024], BF16, "olb")
            ohs = st.sb([64, 16, 64], BF16, "ohs")
            Sbk, Obk, Lbk, Tbk, Ebk = [PS[0], PS[1]], [PS[2], PS[3]], [PS[4], PS[5]], PS[6], PS[7]
            NKB = PAST // 128
            for s in range(2):
                tok0 = NPT * 128 + s * 64
                P.dma("sp", QAs.ap[:], QA_d[:, :, tok0:tok0 + 64].rearrange("h p t -> p h t"), reads=[QA_b], writes=[QAs])
                P.dma("sp", QRs.ap[0:32, :, :], QR_d[:, :, tok0:tok0 + 64].rearrange("h p t -> p h t"), reads=[QR_b], pwrites=[QRs])
                for b in Obk + Lbk:
                    P.op("pe", lambda e, b=b: e.matmul(out=b.ap[:], lhsT=zer.ap[:, 0:128], rhs=zer.ap[:], start=True, stop=False, skip_group_check=True), reads=[zer], writes=[b])
                for kb in range(NKB + 1):
                    i = kb % 2
                    last = (kb == NKB)
                    nk = 64 if last else 128
                    if not last:
                        P.dma("pool", ckr[i].ap[:], I["cck"][s, kb * 128:(kb + 1) * 128, :], writes=[ckr[i]])
                        P.dma("pool", krr[i].ap[:], I["ckr"][s, kb * 128:(kb + 1) * 128, :], writes=[krr[i]])
                        tv = Tbk.ap[:].bitcast(BF16)
                        P.op("pe", lambda e, tv=tv, i=i: e.transpose(out=tv[:, 0:128], in_=ckr[i].ap[:], identity=ident_b.ap[:]), reads=[ckr[i], ident_b], writes=[Tbk])
                        P.op("pe", lambda e, tv=tv, i=i: e.transpose(out=tv[0:32, 128:256], in_=krr[i].ap[:], identity=ident_b.ap[:]), reads=[krr[i], ident_b], writes=[Tbk])
                        P.op("dve", lambda e, tv=tv, i=i: e.tensor_copy(out=ckT[i].ap[:], in_=tv[:, 0:128]), reads=[Tbk], writes=[ckT[i]])
                        P.op("dve", lambda e, tv=tv, i=i: e.tensor_copy(out=krT[i].ap[0:32, :], in_=tv[0:32, 128:256]), reads=[Tbk], pwrites=[krT[i]])
                    else:
                        P.dma("pool", ckr[i].ap[0:64, :], CK_d[tok0:tok0 + 64, :], reads=[CK_b], writes=[ckr[i]])
                        P.dma("sp", ckT[i].ap[:, 0:64], CKT_d[:, tok0:tok0 + 64], reads=[CKT_b], writes=[ckT[i]])
                        P.dma("sp", krT[i].ap[0:32, 0:64], KRT_d[:, tok0:tok0 + 64], reads=[KRT_b], pwrites=[krT[i]])
                    for h in range(16):
                        sb_ = Sbk[h // 8]
                        cs = slice((h % 8) * 64, (h % 8) * 64 + 64)
                        P.op("pe", lambda e, sb_=sb_, cs=cs, h=h, nk=nk, i=i: e.matmul(out=sb_.ap[0:nk, cs], lhsT=ckT[i].ap[:, 0:nk], rhs=QAs.ap[:, h, :], start=True, stop=False), reads=[ckT[i], QAs], writes=[sb_])
                        P.op("pe", lambda e, sb_=sb_, cs=cs, h=h, nk=nk, i=i: e.matmul(out=sb_.ap[0:nk, cs], lhsT=krT[i].ap[:, 0:nk], rhs=QRs.ap[:, h, :], start=False, stop=True), reads=[krT[i], QRs], writes=[sb_])
                    pts = PTs[i]
                    for j in range(2):
                        P.op("act", lambda e, j=j, nk=nk, pts=pts: e.activation(out=pts.ap[0:nk, j * 512:(j + 1) * 512], in_=Sbk[j].ap[0:nk, :], func=AF.Exp, scale=MSC), reads=[Sbk[j]], pwrites=[pts])
                    for j in range(2):
                        js = slice(j * 512, (j + 1) * 512)
                        P.op("pe", lambda e, j=j, js=js, nk=nk, i=i, pts=pts: e.matmul(out=Obk[j].ap[:], lhsT=ckr[i].ap[0:nk, :], rhs=pts.ap[0:nk, js], start=False, stop=True, skip_group_check=True),
                             reads=[ckr[i], pts], writes=[Obk[j]])
                        P.op("pe", lambda e, j=j, js=js, nk=nk, pts=pts: e.matmul(out=Lbk[j].ap[:], lhsT=ones_b.ap[0:nk, :], rhs=pts.ap[0:nk, js], start=False, stop=True, skip_group_check=True),
                             reads=[ones_b, pts], writes=[Lbk[j]])
                for j in range(2):
                    js = slice(j * 512, (j + 1) * 512)
                    P.op("dve", lambda e, j=j, js=js: e.reciprocal(out=Rr.ap[:, js], in_=Lbk[j].ap[:]), reads=[Lbk[j]], pwrites=[Rr])
                    P.op("dve", lambda e, j=j, js=js: e.tensor_tensor(out=olb.ap[:, js], in0=Obk[j].ap[:], in1=Rr.ap[:, js], op=ALU.mult), reads=[Obk[j], Rr], pwrites=[olb])
                for g2 in range(2):
                    for u in range(8):
                        h = g2 * 8 + u
                        P.op("pe", lambda e, h=h, u=u: e.matmul(out=Ebk.ap[0:64, u * 64:(u + 1) * 64], lhsT=wuv.ap[:, h * 64:(h + 1) * 64], rhs=olb.ap[:, h * 64:(h + 1) * 64], start=True, stop=True), reads=[wuv, olb], writes=[Ebk])
                    P.op("act", lambda e, g2=g2: e.activation(out=ohs.ap[:, g2 * 8:(g2 + 1) * 8, :], in_=Ebk.ap[0:64, :].rearrange("p (u q) -> p u q", q=64), func=AF.Copy), reads=[Ebk], pwrites=[ohs])
                for h in range(16):
                    P.dma("sp", OT2_d[h // 2, (h % 2) * 64:(h % 2) * 64 + 64, tok0:tok0 + 64], ohs.ap[:, h, :], reads=[ohs], pwrites=[OT2_b])

        if en(10):
            attn_out_stage("l2c", 2, OT2_d, OT2_b, "mla_w_o", 16, 64, False, None, "b", "a", lambda t: True)
        if en(11):
            mlp_stage(2, "a", "b")

        if en(12):
          with Stage(P, "l3") as st:
            win = st.sb([128, 8, 4 * D], BF16, "win")
            for kc in range(8):
                for c0 in range(0, 4 * D, 1024):
                    P.dma("pool", win.ap[:, kc, c0:c0 + 1024], W["sgu_w_in"][kc * 128:(kc + 1) * 128, c0:c0 + 1024], pwrites=[win])
            wout = st.sb([128, 16, D], BF16, "wout")
            for kc in range(16):
                P.dma("pool", wout.ap[:, kc, :], W["sgu_w_out"][kc * 128:(kc + 1) * 128, :], pwrites=[wout])
            gvb = st.sb([128, 2 * D], F32, "gvb")
            P.dma("sp", gvb.ap[:], W["sgu_g_v"].partition_broadcast(128), writes=[gvb])
            Bs = [st.sb([128, 8, 128], F32, "Bs") for _ in range(2)]
            P.dma("sp", Bs[0].ap[:], W["sgu_b_s"].partition_broadcast(128), writes=[Bs[0]])
            for hh in range(2):
                P.dma("sp", Bs[1].ap[:, :, hh * 64:(hh + 1) * 64], W["sgu_b_s"][:, 0:64].partition_broadcast(128), pwrites=[Bs[1]])
            WgT = [st.sb([128, 8, 128], BF16, "WgT") for _ in range(2)]
            with Stage(P, "l3p") as sp:
                stg = [sp.sb([128, 128], F32, "stg") for _ in range(2)]
                k_ = 0
                for ps_i in range(2):
                    for g in range(8):
                        sg = stg[k_ % 2]
                        if ps_i == 0:
                            P.dma("sp", sg.ap[:], W["sgu_w_s"][g], writes=[sg])
                        else:
                            P.op("pool", lambda e, sg=sg: e.memset(sg.ap[:], 0.0), writes=[sg])
                            for hh in range(2):
                                P.dma("sp", sg.ap[hh * 64:(hh + 1) * 64, hh * 64:(hh + 1) * 64], W["sgu_w_s"][g, 0:64, 0:64], pwrites=[sg])
                        P.op("pool", lambda e, sg=sg: e.affine_select(out=sg.ap[:], in_=sg.ap[:], pattern=[[-1, 128]], compare_op=ALU.is_ge, fill=0.0, base=0, channel_multiplier=1),
                             reads=[sg], writes=[sg])
                        pb = PS[1 + k_ % 2]
                        P.op("pe", lambda e, sg=sg, pb=pb: e.transpose(out=pb.ap[:, 0:128], in_=sg.ap[:], identity=ident_f.ap[:]), reads=[sg, ident_f], writes=[pb])
                        P.op("dve", lambda e, pb=pb, ps_i=ps_i, g=g: e.tensor_copy(out=WgT[ps_i].ap[:, g, :], in_=pb.ap[:, 0:128]), reads=[pb], pwrites=[WgT[ps_i]])
                        k_ += 1
            gate = Gate(st, 3, 0)
            xt = [st.sb([128, D], F32, "x") for _ in range(2)]
            nctx = NormCtx(st, PS[0], PS[7])
            hTs = [st.sb([128, 8, 128], BF16, "hT") for _ in range(2)]
            uT = [st.sb([128, 16, 128], BF16, "uT") for _ in range(2)]
            vg = st.sb([128, 2 * D], F32, "vg")
            vn = st.sb([128, 2 * D], F32, "vn")
            vnb = st.sb([128, 2 * D], BF16, "vnb")
            jk2 = st.sb([128, 2 * D], BF16, "jk2")
            st5 = st.sb([128, 4], F32, "st5")
            svt = [st.sb([128, 512], F32, "svt") for _ in range(2)]
            pT = [st.sb([128, 16, 128], BF16, "pT") for _ in range(2)]
            tl3 = list(range(NT)) if upto >= 12.5 else [0, NPT]
            for t in tl3:
                is_s = (t == NPT)
                i = t % 2
                x, hT = xt[i], hTs[i]
                src, srcb = x_src(t, "b")
                P.dma("sp", x.ap[:], src, reads=[srcb], writes=[x])
                nctx.run(x, x.ap[:], hT, hT.ap[:], 3, 0, is_s)
                for q4 in range(4):
                    pu = PS[1 + q4 % 2]
                    for u in range(4):
                        oc = q4 * 4 + u
                        for kc in range(8):
                            P.op("pe", lambda e, pu=pu, u=u, oc=oc, kc=kc, hT=hT: e.matmul(out=pu.ap[:, u * 128:(u + 1) * 128], lhsT=win.ap[:, kc, oc * 128:(oc + 1) * 128], rhs=hT.ap[:, kc, :], start=(kc == 0), stop=(kc == 7)),
                                 reads=[win, hT], writes=[pu])
                    P.op("act", lambda e, pu=pu, q4=q4, i=i: e.activation(out=uT[i].ap[:, q4 * 4:(q4 + 1) * 4, :], in_=pu.ap[:].rearrange("p (u t) -> p u t", t=128), func=AF.Gelu_apprx_tanh), reads=[pu], pwrites=[uT[i]])
                for blk in range(4):
                    pv_ = PS[3 + blk % 2]
                    for kc in range(8):
                        P.op("pe", lambda e, pv_=pv_, blk=blk, kc=kc, hT=hT: e.matmul(out=pv_.ap[:], lhsT=hT.ap[:, kc, :], rhs=win.ap[:, kc, 2048 + blk * 512:2048 + (blk + 1) * 512], start=(kc == 0), stop=(kc == 7)),
                             reads=[hT, win], writes=[pv_])
                    P.op("act", lambda e, pv_=pv_, blk=blk: e.activation(out=vg.ap[:, blk * 512:(blk + 1) * 512], in_=pv_.ap[:], func=AF.Gelu_apprx_tanh), reads=[pv_], pwrites=[vg])
                P.op("act", lambda e: e.activation(out=jk2.ap[:], in_=vg.ap[:], func=AF.Square, accum_out=st5.ap[:, 0:1]), reads=[vg], writes=[jk2, st5])
                P.op("act", lambda e: e.activation(out=st5.ap[:, 1:2], in_=st5.ap[:, 0:1], func=AF.Sqrt, scale=1.0 / (2 * D), bias=EPS), reads=[st5], writes=[st5])
                P.op("dve", lambda e: e.reciprocal(out=st5.ap[:, 2:3], in_=st5.ap[:, 1:2]), reads=[st5], writes=[st5])
                P.op("dve", lambda e: e.scalar_tensor_tensor(out=vn.ap[:], in0=vg.ap[:], scalar=st5.ap[:, 2:3], in1=gvb.ap[:], op0=ALU.mult, op1=ALU.mult), reads=[vg, st5, gvb], writes=[vn])
                P.op("pool", lambda e: e.tensor_copy(out=vnb.ap[:], in_=vn.ap[:]), reads=[vn], writes=[vnb])
                if is_s:
                    P.dma("sp", O["sguv"], vn.ap[:], reads=[vn], pwrites=[out_bufs["sguv"]])
                wg, bs_ = WgT[1 if is_s else 0], Bs[1 if is_s else 0]
                p_ = pT[i]
                for q4 in range(4):
                    psv = PS[5 + q4 % 2]
                    for u in range(4):
                        fc = q4 * 4 + u
                        P.op("pe", lambda e, psv=psv, u=u, fc=fc, wg=wg: e.matmul(out=psv.ap[:, u * 128:(u + 1) * 128], lhsT=vnb.ap[:, fc * 128:(fc + 1) * 128], rhs=wg.ap[:, fc // 2, :], start=True, stop=True),
                             reads=[vnb, wg], writes=[psv])
                    sv_ = svt[q4 % 2]
                    P.op("dve", lambda e, psv=psv, sv_=sv_, q4=q4, bs_=bs_: e.tensor_tensor(out=sv_.ap[:].rearrange("p (g u t) -> p g u t", g=2, u=2), in0=psv.ap[:].rearrange("p (g u t) -> p g u t", g=2, u=2),
                                                                                  in1=bc(bs_.ap[:, q4 * 2:(q4 + 1) * 2, :].unsqueeze(2), [128, 2, 2, 128]), op=ALU.add), reads=[psv, bs_], writes=[sv_])
                    P.op("pool", lambda e, sv_=sv_, q4=q4, i=i, p_=p_: e.tensor_tensor(out=p_.ap[:, q4 * 4:(q4 + 1) * 4, :].rearrange("p u t -> p (u t)"), in0=sv_.ap[:], in1=uT[i].ap[:, q4 * 4:(q4 + 1) * 4, :].rearrange("p u t -> p (u t)"), op=ALU.mult),
                         reads=[sv_, uT[i]], pwrites=[p_])
                g = gate.get(is_s)
                for cbk in range(2):
                    po = PS[1 + cbk]
                    for kc in range(16):
                        P.op("pe", lambda e, po=po, kc=kc, cbk=cbk, p_=p_: e.matmul(out=po.ap[:], lhsT=p_.ap[:, kc, :], rhs=wout.ap[:, kc, cbk * 512:(cbk + 1) * 512], start=(kc == 0), stop=(kc == 15)),
                             reads=[p_, wout], writes=[po])
                    sl = slice(cbk * 512, (cbk + 1) * 512)
                    P.op("dve", lambda e, po=po, sl=sl, g=g: e.tensor_tensor(out=po.ap[:], in0=po.ap[:], in1=g.ap[:, sl], op=ALU.mult), reads=[po, g], writes=[po])
                    P.op("dve", lambda e, po=po, sl=sl, x=x: e.tensor_tensor(out=x.ap[:, sl], in0=po.ap[:], in1=x.ap[:, sl], op=ALU.add), reads=[po, x], pwrites=[x])
                dst, dstb = x_dst(t, "a")
                P.dma("sp", dst, x.ap[:], reads=[x], pwrites=[dstb])

        if en(13):
            mlp_stage(3, "a", None, final=True)

        P.barrier()
        P.emit()
        print("n_inst", P.n_inst, "n_dsem", P.n_dsem)
    return nc


_NC_CACHE = {}


def kernel(**inp):
    f = lambda a: np.ascontiguousarray(np.asarray(a, dtype=np.float32))
    upto = float(inp.get("_upto", 99))
    if upto not in _NC_CACHE:
        _NC_CACHE[upto] = build_program(upto)
    nc = _NC_CACHE[upto]
    wnames = ["w_ada", "b_ada", "g_mix", "g_ffn", "w_up", "w_down", "g_final", "s5_a_re", "s5_a_im", "s5_b_re", "s5_b_im",
              "s5_c_re", "s5_c_im", "s5_d", "s5_log_dt", "s5_w_glu_a", "s5_w_glu_b", "diff_w_qkv", "diff_lambda_q1",
              "diff_lambda_k1", "diff_lambda_q2", "diff_lambda_k2", "diff_g_sub", "diff_w_o", "mla_w_dq", "mla_g_q",
              "mla_w_uq", "mla_w_dkv", "mla_g_kv", "mla_w_uk", "mla_w_uv", "mla_w_o", "sgu_w_in", "sgu_g_v", "sgu_w_s",
              "sgu_b_s", "sgu_w_out"]
    wd = {k: f(inp[k]) for k in wnames}
    wd["s5_a_re"] = wd["s5_a_re"].reshape(4096)
    wd["s5_a_im"] = wd["s5_a_im"].reshape(4096)
    wd["mla_w_uk"] = wd["mla_w_uk"].reshape(128, 1024)
    wd["mla_w_uv"] = wd["mla_w_uv"].reshape(128, 1024)
    xp, xs = f(inp["x_prompt"]), f(inp["x_sample"])
    cp, cs = f(inp["c_prompt"]), f(inp["c_sample"])
    sre, sim = f(inp["state_s5_re"]), f(inp["state_s5_im"])
    cdk, cdv = f(inp["cache_diff_k"]), f(inp["cache_diff_v"])
    cck, ckr = f(inp["cache_mla_ckv"]), f(inp["cache_mla_krope"])
    in_maps = []
    for c in range(8):
        b = c % 4
        s0 = 2 * c
        m = dict(wd)
        m["xp"] = xp[b]
        m["xs"] = xs[s0:s0 + 2].reshape(128, D)
        m["c3"] = np.concatenate([cp[b:b + 1], cs[s0:s0 + 2]], axis=0)
        m["h0re"] = sre[s0:s0 + 2].reshape(2, 4096)
        m["h0im"] = sim[s0:s0 + 2].reshape(2, 4096)
        m["cdk"] = cdk[s0:s0 + 2].reshape(2, PAST, D)
        m["cdv"] = cdv[s0:s0 + 2].reshape(2, PAST, D)
        m["cck"] = cck[s0:s0 + 2]
        m["ckr"] = ckr[s0:s0 + 2]
        in_maps.append(m)
    ncores = int(inp.get("_ncores", 8))
    if ncores < 8:
        res = run_bass_kernel_spmd(nc, in_maps[:ncores], core_ids=list(range(ncores))).results
        res = [res[c % ncores] for c in range(8)]
    else:
        res = run_bass_kernel_spmd(nc, in_maps, core_ids=list(range(8))).results
    global _LAST_RES
    _LAST_RES = res
    R = lambda k, cores: [res[c][k] for c in cores]
    p4, a8 = range(4), range(8)
    y_prompt = np.stack(R("yp", p4)).reshape(4, SEQ, D)
    y_sample = np.concatenate(R("ys", a8)).reshape(16, 64, D)
    s5p_re = np.stack(R("s5p_re", p4)).reshape(4, 64, 64)
    s5p_im = np.stack(R("s5p_im", p4)).reshape(4, 64, 64)
    s5s_re = np.concatenate(R("s5s_re", a8)).reshape(16, 64, 64)
    s5s_im = np.concatenate(R("s5s_im", a8)).reshape(16, 64, 64)
    dkp = np.stack(R("dkp", p4)).reshape(4, SEQ, 8, 128)
    dvp = np.stack(R("dvp", p4)).reshape(4, SEQ, 8, 128)
    dks = np.concatenate(R("dks", a8)).reshape(16, 64, 8, 128)
    dvs = np.concatenate(R("dvs", a8)).reshape(16, 64, 8, 128)
    ckvp = np.stack(R("ckvp", p4)).reshape(4, SEQ, 128)
    krp = np.stack(R("krp", p4)).reshape(4, SEQ, 32)
    ckvs = np.concatenate(R("ckvs", a8)).reshape(16, 64, 128)
    krs = np.concatenate(R("krs", a8)).reshape(16, 64, 32)
    sguv = np.concatenate(R("sguv", a8)).reshape(16, 64, 2 * D)
    return (y_prompt, y_sample, s5p_re, s5p_im, s5s_re, s5s_im, dkp, dvp, dks, dvs, ckvp, krp, ckvs, krs, sguv)
```

```python
import math
from contextlib import ExitStack

import numpy as np
import concourse.bass as bass
import concourse.mybir as mybir
from concourse.bass_utils import run_bass_kernel_spmd

F32 = mybir.dt.float32
BF16 = mybir.dt.bfloat16
I32 = mybir.dt.int32
AF = mybir.ActivationFunctionType
ALU = mybir.AluOpType

D = 1024
SEQ = 8192
NPT = SEQ // 128
NT = NPT + 1
NTOK = NT * 128
PAST = 4096
EPS = 1e-6
LAMBDA_INIT = 0.8 - 0.6 * math.exp(-0.3 * 1)
TWO_PI = 2.0 * math.pi
CW1 = 6.28125
CW2 = float(np.float32(TWO_PI - CW1))
CW3 = float(TWO_PI - CW1 - CW2)


class Buf:
    def __init__(self, ap, name=""):
        self.ap = ap
        self.name = name
        self.w = []
        self.r = {}
        self.pr = {}
        self.is_sb = False
        self.is_psum = False
        self.dsem = None


class Prog:
    ENGS = ["pe", "act", "dve", "pool", "sp"]

    def __init__(self, nc, es):
        self.nc = nc
        self.es = es
        self.sem = {e: es.enter_context(nc.semaphore("s_" + e)) for e in self.ENGS}
        self.cnt = {e: 0 for e in self.ENGS}
        self.waited = {e: {} for e in self.ENGS}
        self.stream = {e: [] for e in self.ENGS}
        self.dsems = []
        self.n_inst = 0
        self.n_dsem = 0
        self.free_dsems = []

    def take_dsem(self, key, stage, sw=False):
        if sw:
            return self.mk_dsem(key)
        if self.free_dsems:
            d = self.free_dsems.pop()
        else:
            d = self.mk_dsem(key)
        if stage is not None:
            stage.dsems.append(d)
        return d

    def no_dsem(self, key=None):
        return None

    def mk_dsem(self, key):
        self.n_dsem += 1
        s = self.es.enter_context(self.nc.semaphore("d_%s_%d" % (key, self.n_dsem)))
        d = {"sem": s, "cnt": 0, "key": "d_%s_%d" % (key, self.n_dsem)}
        self.dsems.append(d)
        return d

    def _need(self, e, reads, writes, pwrites):
        need = {}

        def add(dep):
            key, val, semobj = dep
            if key == e and e == "pe":
                return
            cur = need.get(key)
            if cur is None or cur[0] < val:
                need[key] = (val, semobj)

        for b in reads:
            for d, _ in b.w:
                add(d)
            if b.is_psum:
                for k, d in b.r.items():
                    if k != e:
                        add(d)
        for b in writes:
            for d, _ in b.w:
                add(d)
            for d in b.r.values():
                add(d)
        for b in pwrites:
            for d, part in b.w:
                if not part:
                    add(d)
            for d in b.r.values():
                add(d)
            for d in b.pr.values():
                add(d)
        out = []
        for key, (val, semobj) in need.items():
            if self.waited[e].get(key, 0) >= val:
                continue
            self.waited[e][key] = val
            out.append((semobj, val))
        return out

    def _commit(self, key, dep, reads, writes, pwrites):
        for b in reads:
            b.r[key] = dep
        for b in writes:
            b.w = [(dep, False)]
            b.r = {}
            b.pr = {}
        for b in pwrites:
            if b.r:
                b.pr = b.r
                b.w = [(dep, True)]
                b.r = {}
            else:
                b.w.append((dep, True))

    def op(self, e, fn, reads=(), writes=(), pwrites=()):
        waits = self._need(e, reads, writes, pwrites)
        self.cnt[e] += 1
        dep = (e, self.cnt[e], self.sem[e])
        self.stream[e].append((waits, fn, (self.sem[e], 1)))
        self._commit(e, dep, reads, writes, pwrites)
        self.n_inst += 1

    def dma(self, q, out_ap, in_ap, reads=(), writes=(), pwrites=(), sem=None, **kw):
        sbb = [b for b in list(writes) + list(pwrites) + list(reads) if b.is_sb]
        assert sbb, "dma without sbuf side"
        if sbb[0].dsem is None:
            sbb[0].dsem = self.take_dsem(sbb[0].name, getattr(sbb[0], "stage", None), sw=(q == "pool"))
            sbb[0].dq = q
        assert sbb[0].dq == q or (sbb[0].dq != "pool" and q != "pool"), "buffer mixes SW and HW DMA queues"
        sem = sbb[0].dsem
        waits = self._need(q, reads, writes, pwrites)
        sem["cnt"] += 16
        dep = (sem["key"], sem["cnt"], sem["sem"])

        def fn(eng, out_ap=out_ap, in_ap=in_ap, kw=kw):
            return eng.dma_start(out=out_ap, in_=in_ap, **kw)

        self.stream[q].append((waits, fn, (sem["sem"], 16)))
        self._commit(sem["key"], dep, reads, writes, pwrites)
        self.n_inst += 1

    def barrier(self):
        for e in self.ENGS:
            waits = []
            for x in self.ENGS:
                if x == e or self.cnt[x] == 0:
                    continue
                if self.waited[e].get(x, 0) < self.cnt[x]:
                    self.waited[e][x] = self.cnt[x]
                    waits.append((self.sem[x], self.cnt[x]))
            for d in self.dsems:
                if d["cnt"] and self.waited[e].get(d["key"], 0) < d["cnt"]:
                    self.waited[e][d["key"]] = d["cnt"]
                    waits.append((d["sem"], d["cnt"]))
            self.stream[e].append((waits, None, None))

    def emit(self):
        nc = self.nc
        streams = self.stream
        self.stream = {e: [] for e in self.ENGS}
        with nc.Block() as block:
            def mk(e):
                def body(eng):
                    for waits, fn, inc in streams[e]:
                        for s, v in waits:
                            eng.wait_ge(s, v)
                        if fn is not None:
                            fn(eng).then_inc(inc[0], inc[1])
                return body
            block.tensor(mk("pe"))
            block.scalar(mk("act"))
            block.vector(mk("dve"))
            block.gpsimd(mk("pool"))
            block.sync(mk("sp"))


class Stage:
    def __init__(self, P, name):
        self.P = P
        Stage.CNT = getattr(Stage, "CNT", 0) + 1
        self.name = "%s%d" % (name, Stage.CNT)
        self.es = ExitStack()
        self.n = 0
        self.dsems = []

    def __enter__(self):
        self.es.__enter__()
        return self

    def sb(self, shape, dt, name=None):
        self.n += 1
        t = self.es.enter_context(self.P.nc.sbuf_tensor("%s_%s_%d" % (self.name, name or "t", self.n), list(shape), dt))
        b = Buf(t, name or "t")
        b.is_sb = True
        b.stage = self
        return b

    def __exit__(self, *a):
        if a[0] is None:
            self.P.barrier()
            self.P.emit()
            self.P.free_dsems.extend(self.dsems)
            self.dsems = []
        return self.es.__exit__(*a)


def bc(ap, shape):
    return ap.to_broadcast(list(shape))


def build_program(upto=99):
    nc = bass.Bass("TRN2", target_bir_lowering=False)

    def din(name, shape):
        return nc.dram_tensor(name, list(shape), F32, kind="ExternalInput").ap()

    def dout(name, shape):
        return nc.dram_tensor(name, list(shape), F32, kind="ExternalOutput").ap()

    import os as _os
    _dbg = set(_os.environ.get("KD_DBGOUT", "").split(","))
    _only = _os.environ.get("KD_ONLY", "")
    _only = set(float(v) for v in _only.split(",")) if _only else None

    def en(k):
        return upto >= k and (_only is None or k in _only)

    def dscr(name, shape, dt):
        return nc.dram_tensor(name, list(shape), dt, kind=("ExternalOutput" if name in _dbg else "Internal")).ap()

    I = {}
    I["xp"] = din("xp", [SEQ, D])
    I["xs"] = din("xs", [128, D])
    I["c3"] = din("c3", [3, D])
    I["h0re"] = din("h0re", [2, 4096])
    I["h0im"] = din("h0im", [2, 4096])
    I["cdk"] = din("cdk", [2, PAST, D])
    I["cdv"] = din("cdv", [2, PAST, D])
    I["cck"] = din("cck", [2, PAST, 128])
    I["ckr"] = din("ckr", [2, PAST, 32])
    wshapes = dict(
        w_ada=[4, D, 6 * D], b_ada=[4, 6 * D], g_mix=[4, D], g_ffn=[4, D], w_up=[4, D, 4 * D], w_down=[4, 4 * D, D],
        g_final=[D], s5_a_re=[4096], s5_a_im=[4096], s5_b_re=[64, 64, 16], s5_b_im=[64, 64, 16],
        s5_c_re=[64, 16, 64], s5_c_im=[64, 16, 64], s5_d=[D], s5_log_dt=[64], s5_w_glu_a=[D, D], s5_w_glu_b=[D, D],
        diff_w_qkv=[D, 3 * D], diff_lambda_q1=[64], diff_lambda_k1=[64], diff_lambda_q2=[64], diff_lambda_k2=[64],
        diff_g_sub=[128], diff_w_o=[D, D], mla_w_dq=[D, 256], mla_g_q=[256], mla_w_uq=[256, 1536], mla_w_dkv=[D, 160],
        mla_g_kv=[128], mla_w_uk=[128, 1024], mla_w_uv=[128, 1024], mla_w_o=[D, D], sgu_w_in=[D, 4 * D],
        sgu_g_v=[2 * D], sgu_w_s=[8, 128, 128], sgu_b_s=[8, 128], sgu_w_out=[2 * D, D])
    W = {k: din(k, s) for k, s in wshapes.items()}

    O = {}
    O["yp"] = dout("yp", [SEQ, D])
    O["ys"] = dout("ys", [128, D])
    O["s5p_re"] = dout("s5p_re", [4096])
    O["s5p_im"] = dout("s5p_im", [4096])
    O["s5s_re"] = dout("s5s_re", [2, 4096])
    O["s5s_im"] = dout("s5s_im", [2, 4096])
    O["dkp"] = dout("dkp", [SEQ, D])
    O["dvp"] = dout("dvp", [SEQ, D])
    O["dks"] = dout("dks", [128, D])
    O["dvs"] = dout("dvs", [128, D])
    O["ckvp"] = dout("ckvp", [SEQ, 128])
    O["krp"] = dout("krp", [SEQ, 32])
    O["ckvs"] = dout("ckvs", [128, 128])
    O["krs"] = dout("krs", [128, 32])
    O["sguv"] = dout("sguv", [128, 2 * D])

    xa = dscr("xa", [NTOK, D], F32)
    xb = dscr("xb", [NTOK, D], F32)
    gates_d = dscr("gates_d", [4, 2, 2, 128, D], F32)
    zT_d = dscr("zT_d", [NT, 128, 8 * 128], BF16)

    with ExitStack() as es:
        P = Prog(nc, es)
        out_bufs = {k: Buf(v, k) for k, v in O.items()}
        xa_b = Buf(xa, "xa")
        xb_b = Buf(xb, "xb")
        gates_b = Buf(gates_d, "gates")
        zT_b = Buf(zT_d, "zT_d")
        osem = P.no_dsem("out")
        ssem = P.no_dsem("scr")

        def gsb(name, shape, dt):
            b = Buf(es.enter_context(nc.sbuf_tensor(name, list(shape), dt)), name)
            b.is_sb = True
            return b

        ident_f = gsb("ident_f", [128, 128], F32)
        ident_b = gsb("ident_b", [128, 128], BF16)
        ones_b = gsb("ones_b", [128, 128], BF16)
        ones_f = gsb("ones_f", [128, 128], F32)
        condT = gsb("condT", [128, 8, 4], BF16)
        GS = gsb("GS", [128, 4, 2, 2, 8, 4], F32)
        gfinT = gsb("gfinT", [128, 8], F32)
        PS = [Buf(es.enter_context(nc.psum_tensor("ps%d" % i, [128, 512], F32)), "ps%d" % i) for i in range(8)]
        for b in PS:
            b.is_psum = True

        def load_T(st, dst, dst_ap, rows_ap, R, psb):
            t = st.sb([128, 128], F32, "ldT")
            sem = P.no_dsem("ldT")
            P.dma("sp", t.ap[0:R, :], rows_ap, writes=[t], sem=sem)
            P.op("pe", lambda e: e.transpose(out=psb.ap[:, 0:R], in_=t.ap[0:R, :], identity=ident_f.ap[0:R, 0:R]),
                 reads=[t, ident_f], writes=[psb])
            P.op("dve", lambda e: e.tensor_copy(out=dst_ap, in_=psb.ap[:, 0:R]), reads=[psb], pwrites=[dst])

        with Stage(P, "s0") as st:
            ld = P.no_dsem("s0ld")
            condbc = [st.sb([128, 8, 128], BF16, "condbc") for i in range(2)]
            modT = st.sb([128, 4, 48, 4], F32, "modT")
            P.op("pool", lambda e: e.memset(ident_f.ap[:], 0.0), writes=[ident_f])
            P.op("pool", lambda e: e.affine_select(out=ident_f.ap[:], in_=ident_f.ap[:], pattern=[[-1, 128]],
                                                   compare_op=ALU.not_equal, fill=1.0, base=0, channel_multiplier=1),
                 reads=[ident_f], writes=[ident_f])
            P.op("dve", lambda e: e.tensor_copy(out=ident_b.ap[:], in_=ident_f.ap[:]), reads=[ident_f], writes=[ident_b])
            P.op("pool", lambda e: e.memset(ones_b.ap[:], 1.0), writes=[ones_b])
            P.op("pool", lambda e: e.memset(ones_f.ap[:], 1.0), writes=[ones_f])

            c3 = st.sb([4, D], F32, "c3")
            P.op("pool", lambda e: e.memset(c3.ap[:], 0.0), writes=[c3])
            P.dma("sp", c3.ap[0:3, :], I["c3"], pwrites=[c3], sem=ld)
            c3s = st.sb([4, D], F32, "c3s")
            P.op("act", lambda e: e.activation(out=c3s.ap[:], in_=c3.ap[:], func=AF.Silu), reads=[c3], writes=[c3s])
            for kc in range(8):
                P.op("pe", lambda e, kc=kc: e.transpose(out=PS[0].ap[:, kc * 4:kc * 4 + 4], in_=c3s.ap[0:4, kc * 128:(kc + 1) * 128],
                                                        identity=ident_f.ap[0:4, 0:4]), reads=[c3s, ident_f], writes=[PS[0]])
            P.op("dve", lambda e: e.tensor_copy(out=condT.ap[:], in_=PS[0].ap[:, 0:32].rearrange("p (k s) -> p k s", s=4)),
                 reads=[PS[0]], writes=[condT])
            P.op("dve", lambda e: e.tensor_copy(out=condbc[0].ap[:], in_=bc(condT.ap[:, :, 0:1], [128, 8, 128])),
                 reads=[condT], writes=[condbc[0]])
            P.op("dve", lambda e: e.tensor_copy(out=condbc[1].ap[:, :, 0:64], in_=bc(condT.ap[:, :, 1:2], [128, 8, 64])),
                 reads=[condT], pwrites=[condbc[1]])
            P.op("dve", lambda e: e.tensor_copy(out=condbc[1].ap[:, :, 64:128], in_=bc(condT.ap[:, :, 2:3], [128, 8, 64])),
                 reads=[condT], pwrites=[condbc[1]])

            badaT = st.sb([128, 192], F32, "badaT")
            brows = W["b_ada"].rearrange("l (c p) -> (l c) p", p=128)
            load_T(st, badaT, badaT.ap[:, 0:96], brows[0:96, :], 96, PS[1])
            load_T(st, badaT, badaT.ap[:, 96:192], brows[96:192, :], 96, PS[2])
            gmT = st.sb([128, 2, 32], F32, "gmT")
            load_T(st, gmT, gmT.ap[:, 0, :], W["g_mix"].rearrange("l (c p) -> (l c) p", p=128), 32, PS[3])
            load_T(st, gmT, gmT.ap[:, 1, :], W["g_ffn"].rearrange("l (c p) -> (l c) p", p=128), 32, PS[4])
            load_T(st, gfinT, gfinT.ap[:, :], W["g_final"].rearrange("(c p) -> c p", p=128), 8, PS[5])

            wblk = [st.sb([128, 8, 1024], BF16, "wblk") for _ in range(2)]
            wsem = [P.no_dsem("wblk") for _ in range(2)]
            bbc = [st.sb([128, 1024], F32, "bbc") for _ in range(2)]
            bsem = [P.no_dsem("bbc") for _ in range(2)]
            gtile = [st.sb([128, 1024], F32, "gtile") for _ in range(2)]
            it = 0
            for l in range(4):
                for cb in range(6):
                    wb = wblk[it % 2]
                    for kc in range(8):
                        P.dma("pool", wb.ap[:, kc, :], W["w_ada"][l, kc * 128:(kc + 1) * 128, cb * 1024:(cb + 1) * 1024],
                              pwrites=[wb], sem=wsem[it % 2])
                    pm = PS[it % 2]
                    for f in range(8):
                        for kc in range(8):
                            P.op("pe", lambda e, pm=pm, f=f, kc=kc, wb=wb: e.matmul(
                                out=pm.ap[:, f * 4:f * 4 + 4], lhsT=wb.ap[:, kc, f * 128:(f + 1) * 128], rhs=condT.ap[:, kc, :],
                                start=(kc == 0), stop=(kc == 7)), reads=[wb, condT], writes=[pm])
                    P.op("dve", lambda e, pm=pm, l=l, cb=cb: e.tensor_tensor(
                        out=modT.ap[:, l, cb * 8:(cb + 1) * 8, :], in0=pm.ap[:, 0:32].rearrange("p (k s) -> p k s", s=4),
                        in1=bc(badaT.ap[:, l * 48 + cb * 8:l * 48 + cb * 8 + 8].unsqueeze(2), [128, 8, 4]), op=ALU.add),
                        reads=[pm, badaT], pwrites=[modT])
                    if cb in (2, 5):
                        gi = 0 if cb == 2 else 1
                        bb = bbc[gi]
                        P.dma("sp", bb.ap[:], W["b_ada"][l, cb * 1024:(cb + 1) * 1024].partition_broadcast(128), writes=[bb], sem=bsem[gi])
                        for ps_i in range(2):
                            gt = gtile[ps_i]
                            for half in range(2):
                                pg = PS[2 + half]
                                for kc in range(8):
                                    P.op("pe", lambda e, pg=pg, kc=kc, half=half, ps_i=ps_i, wb=wb: e.matmul(
                                        out=pg.ap[:], lhsT=condbc[ps_i].ap[:, kc, :], rhs=wb.ap[:, kc, half * 512:(half + 1) * 512],
                                        start=(kc == 0), stop=(kc == 7)), reads=[wb, condbc[ps_i]], writes=[pg])
                                P.op("dve", lambda e, pg=pg, half=half, gt=gt, bb=bb: e.scalar_tensor_tensor(
                                    out=gt.ap[:, half * 512:(half + 1) * 512], in0=pg.ap[:], scalar=1.0, in1=bb.ap[:, half * 512:(half + 1) * 512],
                                    op0=ALU.add, op1=ALU.add), reads=[pg, bb], pwrites=[gt])
                            P.dma("sp", gates_d[l, gi, ps_i], gt.ap[:], reads=[gt], pwrites=[gates_b], sem=ssem)
                    it += 1
            for l in range(4):
                for sub in range(2):
                    base = 0 if sub == 0 else 24
                    P.op("dve", lambda e, l=l, sub=sub, base=base: e.scalar_tensor_tensor(
                        out=GS.ap[:, l, sub, 0, :, :], in0=modT.ap[:, l, base + 8:base + 16, :], scalar=1.0,
                        in1=bc(gmT.ap[:, sub, l * 8:(l + 1) * 8].unsqueeze(2), [128, 8, 4]), op0=ALU.add, op1=ALU.mult),
                        reads=[modT, gmT], pwrites=[GS])
                    P.op("dve", lambda e, l=l, sub=sub, base=base: e.tensor_copy(
                        out=GS.ap[:, l, sub, 1, :, :], in_=modT.ap[:, l, base:base + 8, :]), reads=[modT], pwrites=[GS])

        class Gate:
            def __init__(self, st, l, gi):
                self.buf = st.sb([128, D], F32, "gate")
                self.sem = P.no_dsem("gate")
                self.l, self.gi, self.cur = l, gi, None

            def get(self, is_sample):
                k = 1 if is_sample else 0
                if self.cur != k:
                    P.dma("sp", self.buf.ap[:], gates_d[self.l, self.gi, k], reads=[gates_b], writes=[self.buf], sem=self.sem)
                    self.cur = k
                return self.buf

        class NormCtx:
            def __init__(self, st, psb, psb2):
                self.junk = st.sb([128, D], BF16, "junk")
                self.stat = st.sb([128, 4], F32, "stat")
                self.xn = st.sb([128, D], BF16, "xn")
                self.psb = [psb, psb2]

            def run(self, xbuf, x_ap, hbuf, h_ap, l, sub, is_sample, mod=True):
                stat, xn, junk = self.stat, self.xn, self.junk
                P.op("act", lambda e: e.activation(out=junk.ap[:], in_=x_ap, func=AF.Square, accum_out=stat.ap[:, 0:1]),
                     reads=[xbuf], writes=[junk, stat])
                P.op("act", lambda e: e.activation(out=stat.ap[:, 1:2], in_=stat.ap[:, 0:1], func=AF.Sqrt, scale=1.0 / D, bias=EPS),
                     reads=[stat], writes=[stat])
                P.op("dve", lambda e: e.reciprocal(out=stat.ap[:, 2:3], in_=stat.ap[:, 1:2]), reads=[stat], writes=[stat])
                P.op("act", lambda e: e.activation(out=xn.ap[:], in_=x_ap, func=AF.Copy, scale=stat.ap[:, 2:3]),
                     reads=[xbuf, stat], writes=[xn])
                pvs = [pb.ap[:].bitcast(BF16).rearrange("p (k t) -> p k t", t=128) for pb in self.psb]
                for kc in range(8):
                    psb, pv = self.psb[kc // 4], pvs[kc // 4]
                    P.op("pe", lambda e, kc=kc, pv=pv: e.transpose(out=pv[:, kc % 4, :], in_=xn.ap[:, kc * 128:(kc + 1) * 128], identity=ident_b.ap[:]),
                         reads=[xn, ident_b], writes=[psb])
                segs = [(0, 128, 0)] if not is_sample else [(0, 64, 1), (64, 128, 2)]
                for kc in range(8):
                    psb, pv0 = self.psb[kc // 4], pvs[kc // 4]
                    pv = pv0[:, kc % 4:kc % 4 + 1, :]
                    for (c0, c1, s) in segs:
                        if kc < 4:
                            P.op("act", lambda e, kc=kc, c0=c0, c1=c1, s=s, pv=pv: e.activation(
                                out=h_ap[:, kc, c0:c1], in_=pv[:, 0, c0:c1], func=AF.Identity,
                                scale=GS.ap[:, l, sub, 0, kc, s:s + 1], bias=GS.ap[:, l, sub, 1, kc, s:s + 1]),
                                reads=[psb, GS], pwrites=[hbuf])
                        else:
                            P.op("dve", lambda e, kc=kc, c0=c0, c1=c1, s=s, pv=pv: e.tensor_scalar(
                                out=h_ap[:, kc, c0:c1], in0=pv[:, 0, c0:c1], scalar1=GS.ap[:, l, sub, 0, kc, s:s + 1],
                                scalar2=GS.ap[:, l, sub, 1, kc, s:s + 1], op0=ALU.mult, op1=ALU.add),
                                reads=[psb, GS], pwrites=[hbuf])

        def x_src(t, which):
            if which == "in":
                return (I["xp"][t * 128:(t + 1) * 128, :], None) if t < NPT else (I["xs"], None)
            d, b = (xa, xa_b) if which == "a" else (xb, xb_b)
            return d[t * 128:(t + 1) * 128, :], b

        def x_dst(t, which):
            d, b = (xa, xa_b) if which == "a" else (xb, xb_b)
            return d[t * 128:(t + 1) * 128, :], b

        def load_w_bf16(dst, dst_ap_fn, src, nk, sem, q="pool"):
            for kc in range(nk):
                P.dma(q, dst_ap_fn(kc), src[kc * 128:(kc + 1) * 128, :], pwrites=[dst], sem=sem)

        if en(0.5):
          with Stage(P, "l0a") as st:
            ld = P.no_dsem("l0ld")
            sm = st.sb([128, 16, 32], F32, "sm")
            A_RE, A_IM, DT, TH, R, C1, S1, FRE, FIM, TMP1, TMP2, TMP3 = range(12)
            NTB = 65
            Er = st.sb([128, 32, NTB], F32, "Er")
            Ei = st.sb([128, 32, NTB], F32, "Ei")
            Tr = st.sb([128, 32, 64], F32, "Tr")
            Ti = st.sb([128, 32, 64], F32, "Ti")
            Bbd = [st.sb([128, 32, 128], BF16, "Bbd") for _ in range(2)]
            Cbd = [st.sb([128, 32, 128], BF16, "Cbd") for _ in range(2)]
            dT = st.sb([128, 8], F32, "dT")
            h0 = st.sb([128, 2, 2, 32], F32, "h0")

            with Stage(P, "l0p") as sp:
                P.dma("sp", sm.ap[:, A_RE, :], W["s5_a_re"].rearrange("(j p) -> p j", p=128), pwrites=[sm], sem=ld, allow_slow_non_contiguous=True)
                P.dma("sp", sm.ap[:, A_IM, :], W["s5_a_im"].rearrange("(j p) -> p j", p=128), pwrites=[sm], sem=ld, allow_slow_non_contiguous=True)
                ldt2 = W["s5_log_dt"].rearrange("(j h) -> h j", h=2)
                for hh in range(2):
                    P.dma("sp", sm.ap[hh * 64:(hh + 1) * 64, DT, :], ldt2[hh, :].partition_broadcast(64),
                          pwrites=[sm], sem=ld, allow_slow_non_contiguous=True)
                for s in range(2):
                    P.dma("sp", h0.ap[:, 0, s, :], I["h0re"][s].rearrange("(j p) -> p j", p=128), pwrites=[h0], sem=ld, allow_slow_non_contiguous=True)
                    P.dma("sp", h0.ap[:, 1, s, :], I["h0im"][s].rearrange("(j p) -> p j", p=128), pwrites=[h0], sem=ld, allow_slow_non_contiguous=True)

                def sm_op(fn, eng="dve", extra=()):
                    P.op(eng, fn, reads=[sm] + list(extra), writes=[sm])

                sm_op(lambda e: e.activation(out=sm.ap[:, DT, :], in_=sm.ap[:, DT, :], func=AF.Exp), "act")
                sm_op(lambda e: e.tensor_tensor(out=sm.ap[:, TH, :], in0=sm.ap[:, A_IM, :], in1=sm.ap[:, DT, :], op=ALU.mult))
                sm_op(lambda e: e.tensor_tensor(out=sm.ap[:, R, :], in0=sm.ap[:, A_RE, :], in1=sm.ap[:, DT, :], op=ALU.mult))
                sm_op(lambda e: e.activation(out=sm.ap[:, R, :], in_=sm.ap[:, R, :], func=AF.Exp), "act")

                tpos = sp.sb([128, NTB], F32, "tpos")
                P.op("pool", lambda e: e.iota(tpos.ap[:], pattern=[[1, NTB]], base=0, channel_multiplier=0, allow_small_or_imprecise_dtypes=True), writes=[tpos])
                ang = sp.sb([128, 32, NTB], F32, "ang")
                kf = sp.sb([128, 32, NTB], F32, "kf")
                ki = sp.sb([128, 32, NTB], I32, "ki")

                def sincos(dbuf, shift):
                    dst, src, k_ap, ki_ap = dbuf.ap[:], ang.ap[:], kf.ap[:], ki.ap[:]
                    P.op("dve", lambda e: e.tensor_scalar(out=k_ap, in0=src, scalar1=shift, scalar2=1.0 / TWO_PI, op0=ALU.add, op1=ALU.mult),
                         reads=[ang], writes=[kf])
                    P.op("dve", lambda e: e.tensor_copy(out=ki_ap, in_=k_ap), reads=[kf], writes=[ki])
                    P.op("dve", lambda e: e.tensor_copy(out=k_ap, in_=ki_ap), reads=[ki], writes=[kf])
                    P.op("dve", lambda e: e.scalar_tensor_tensor(out=dst, in0=k_ap, scalar=-CW1, in1=src, op0=ALU.mult, op1=ALU.add),
                         reads=[kf, ang], writes=[dbuf])
                    P.op("dve", lambda e: e.scalar_tensor_tensor(out=dst, in0=k_ap, scalar=-CW2, in1=dst, op0=ALU.mult, op1=ALU.add),
                         reads=[kf, dbuf], writes=[dbuf])
                    P.op("dve", lambda e: e.scalar_tensor_tensor(out=dst, in0=k_ap, scalar=-CW3, in1=dst, op0=ALU.mult, op1=ALU.add),
                         reads=[kf, dbuf], writes=[dbuf])
                    P.op("dve", lambda e: e.tensor_scalar(out=dst, in0=dst, scalar1=shift, scalar2=None, op0=ALU.add), reads=[dbuf], writes=[dbuf])
                    P.op("dve", lambda e: e.tensor_scalar(out=dst, in0=dst, scalar1=math.pi, scalar2=-math.pi, op0=ALU.min, op1=ALU.max),
                         reads=[dbuf], writes=[dbuf])
                    P.op("act", lambda e: e.activation(out=dst, in_=dst, func=AF.Sin), reads=[dbuf], writes=[dbuf])

                P.op("dve", lambda e: e.tensor_tensor(out=ang.ap[:], in0=bc(sm.ap[:, TH, :].unsqueeze(2), [128, 32, NTB]),
                                                      in1=bc(tpos.ap[:].unsqueeze(1), [128, 32, NTB]), op=ALU.mult),
                     reads=[sm, tpos], writes=[ang])
                sincos(Ei, 0.0)
                sincos(Er, math.pi / 2)
                sm_op(lambda e: e.tensor_tensor(out=sm.ap[:, C1, :], in0=sm.ap[:, R, :], in1=Er.ap[:, :, 1], op=ALU.mult), extra=[Er])
                sm_op(lambda e: e.tensor_tensor(out=sm.ap[:, S1, :], in0=sm.ap[:, R, :], in1=Ei.ap[:, :, 1], op=ALU.mult), extra=[Ei])
                sm_op(lambda e: e.tensor_scalar(out=sm.ap[:, C1, :], in0=sm.ap[:, C1, :], scalar1=-1.0, scalar2=None, op0=ALU.add))
                sm_op(lambda e: e.tensor_tensor(out=sm.ap[:, TMP1, :], in0=sm.ap[:, A_RE, :], in1=sm.ap[:, A_RE, :], op=ALU.mult))
                sm_op(lambda e: e.tensor_tensor(out=sm.ap[:, TMP2, :], in0=sm.ap[:, A_IM, :], in1=sm.ap[:, A_IM, :], op=ALU.mult))
                sm_op(lambda e: e.tensor_tensor(out=sm.ap[:, TMP1, :], in0=sm.ap[:, TMP1, :], in1=sm.ap[:, TMP2, :], op=ALU.add))
                sm_op(lambda e: e.reciprocal(out=sm.ap[:, TMP1, :], in_=sm.ap[:, TMP1, :]))
                sm_op(lambda e: e.tensor_tensor(out=sm.ap[:, TMP2, :], in0=sm.ap[:, C1, :], in1=sm.ap[:, A_RE, :], op=ALU.mult))
                sm_op(lambda e: e.tensor_tensor(out=sm.ap[:, TMP3, :], in0=sm.ap[:, S1, :], in1=sm.ap[:, A_IM, :], op=ALU.mult))
                sm_op(lambda e: e.tensor_tensor(out=sm.ap[:, TMP2, :], in0=sm.ap[:, TMP2, :], in1=sm.ap[:, TMP3, :], op=ALU.add))
                sm_op(lambda e: e.tensor_tensor(out=sm.ap[:, FRE, :], in0=sm.ap[:, TMP2, :], in1=sm.ap[:, TMP1, :], op=ALU.mult))
                sm_op(lambda e: e.tensor_tensor(out=sm.ap[:, TMP2, :], in0=sm.ap[:, S1, :], in1=sm.ap[:, A_RE, :], op=ALU.mult))
                sm_op(lambda e: e.tensor_tensor(out=sm.ap[:, TMP3, :], in0=sm.ap[:, C1, :], in1=sm.ap[:, A_IM, :], op=ALU.mult))
                sm_op(lambda e: e.tensor_tensor(out=sm.ap[:, TMP2, :], in0=sm.ap[:, TMP2, :], in1=sm.ap[:, TMP3, :], op=ALU.subtract))
                sm_op(lambda e: e.tensor_tensor(out=sm.ap[:, FIM, :], in0=sm.ap[:, TMP2, :], in1=sm.ap[:, TMP1, :], op=ALU.mult))
                tt = sp.sb([128, 32, 64], F32, "tt")
                fre_b = bc(sm.ap[:, FRE, :].unsqueeze(2), [128, 32, 64])
                fim_b = bc(sm.ap[:, FIM, :].unsqueeze(2), [128, 32, 64])
                P.op("dve", lambda e: e.tensor_tensor(out=Tr.ap[:], in0=Er.ap[:, :, 0:64], in1=fre_b, op=ALU.mult), reads=[Er, sm], writes=[Tr])
                P.op("dve", lambda e: e.tensor_tensor(out=tt.ap[:], in0=Ei.ap[:, :, 0:64], in1=fim_b, op=ALU.mult), reads=[Ei, sm], writes=[tt])
                P.op("dve", lambda e: e.tensor_tensor(out=Tr.ap[:], in0=Tr.ap[:], in1=tt.ap[:], op=ALU.add), reads=[Tr, tt], writes=[Tr])
                P.op("dve", lambda e: e.tensor_tensor(out=Ti.ap[:], in0=Er.ap[:, :, 0:64], in1=fim_b, op=ALU.mult), reads=[Er, sm], writes=[Ti])
                P.op("dve", lambda e: e.tensor_tensor(out=tt.ap[:], in0=Ei.ap[:, :, 0:64], in1=fre_b, op=ALU.mult), reads=[Ei, sm, Ti], writes=[tt])
                P.op("dve", lambda e: e.tensor_tensor(out=Ti.ap[:], in0=Ti.ap[:], in1=tt.ap[:], op=ALU.subtract), reads=[Ti, tt], writes=[Ti])

                stg = [sp.sb([128, 8, 128], F32, "stg") for _ in range(2)]
                stg_sem = [P.no_dsem("stg") for _ in range(2)]
                it = 0
                for kind in range(4):
                    src = [W["s5_b_re"], W["s5_b_im"], W["s5_c_re"], W["s5_c_im"]][kind]
                    for jb in range(4):
                        sg = stg[it % 2]
                        P.op("pool", lambda e, sg=sg: e.memset(sg.ap[:], 0.0), writes=[sg])
                        for jj in range(8):
                            j = jb * 8 + jj
                            for gg in range(2):
                                g = 2 * j + gg
                                if kind < 2:
                                    P.dma("sp", sg.ap[gg * 64:(gg + 1) * 64, jj, (g % 8) * 16:(g % 8) * 16 + 16], src[g], pwrites=[sg], sem=stg_sem[it % 2])
                                else:
                                    P.dma("sp", sg.ap[(g % 8) * 16:(g % 8) * 16 + 16, jj, gg * 64:(gg + 1) * 64], src[g], pwrites=[sg], sem=stg_sem[it % 2])
                        for q4 in range(2):
                            psb = PS[(it * 2 + q4) % 4]
                            for u in range(4):
                                jj = q4 * 4 + u
                                P.op("pe", lambda e, psb=psb, u=u, jj=jj, sg=sg: e.transpose(out=psb.ap[:, u * 128:(u + 1) * 128], in_=sg.ap[:, jj, :], identity=ident_f.ap[:]),
                                     reads=[sg, ident_f], writes=[psb])
                            dst = (Bbd[kind] if kind < 2 else Cbd[kind - 2])
                            j0 = jb * 8 + q4 * 4
                            sc = -1.0 if kind == 3 else 1.0
                            P.op("act", lambda e, psb=psb, dst=dst, j0=j0, sc=sc: e.activation(out=dst.ap[:, j0:j0 + 4, :], in_=psb.ap[:].rearrange("p (u c) -> p u c", c=128), func=AF.Copy, scale=sc),
                                 reads=[psb], pwrites=[dst])
                        it += 1
                load_T(sp, dT, dT.ap[:, :], W["s5_d"].rearrange("(c p) -> c p", p=128), 8, PS[4])

            xt = [st.sb([128, D], F32, "x") for _ in range(2)]
            xsem = [P.no_dsem("x") for _ in range(2)]
            nctx = NormCtx(st, PS[0], PS[7])
            hTs = [st.sb([128, 8, 128], BF16, "hT") for _ in range(2)]
            wre = st.sb([128, 32, 128], BF16, "wre")
            wim = st.sb([128, 32, 128], BF16, "wim")
            gre = st.sb([128, 32, 128], F32, "gre")
            gim = st.sb([128, 32, 128], F32, "gim")
            hsr = st.sb([128, 32, 128], BF16, "hsr")
            hsi = st.sb([128, 32, 128], BF16, "hsi")
            tmp = [st.sb([128, 512], F32, "tmp") for _ in range(4)]
            init = st.sb([128, 2, 2, 32], F32, "init")
            hend = st.sb([128, 2, 2, 32], F32, "hend")
            ctmp = st.sb([128, 4, 32], F32, "ctmp")
            ysb = st.sb([128, 8, 128], F32, "ysb")
            zT = [st.sb([128, 8, 128], BF16, "zT") for _ in range(2)]
            P.op("pool", lambda e: e.memset(init.ap[:], 0.0), writes=[init])

            def cmul_small(dbuf, dst_re, dst_im, a_re, a_im, b_re, b_im, deps_r):
                P.op("dve", lambda e: e.tensor_tensor(out=ctmp.ap[:, 0, :], in0=a_re, in1=b_re, op=ALU.mult), reads=deps_r, pwrites=[ctmp])
                P.op("dve", lambda e: e.tensor_tensor(out=ctmp.ap[:, 1, :], in0=a_im, in1=b_im, op=ALU.mult), reads=deps_r, pwrites=[ctmp])
                P.op("dve", lambda e: e.tensor_tensor(out=ctmp.ap[:, 2, :], in0=a_re, in1=b_im, op=ALU.mult), reads=deps_r, pwrites=[ctmp])
                P.op("dve", lambda e: e.tensor_tensor(out=ctmp.ap[:, 3, :], in0=a_im, in1=b_re, op=ALU.mult), reads=deps_r, pwrites=[ctmp])
                P.op("dve", lambda e: e.tensor_tensor(out=dst_re, in0=ctmp.ap[:, 0, :], in1=ctmp.ap[:, 1, :], op=ALU.subtract), reads=[ctmp], pwrites=[dbuf])
                P.op("dve", lambda e: e.tensor_tensor(out=dst_im, in0=ctmp.ap[:, 2, :], in1=ctmp.ap[:, 3, :], op=ALU.add), reads=[ctmp], pwrites=[dbuf])

            tlist = list(range(NT)) if upto >= 1 else ([] if upto < 0.55 else ([0] if upto < 0.65 else [0, NPT]))
            for t in tlist:
                is_s = (t == NPT)
                i = t % 2
                x = xt[i]
                src, srcb = x_src(t, "in")
                P.dma("sp", x.ap[:], src, writes=[x], sem=xsem[i])
                hT = hTs[i]
                nctx.run(x, x.ap[:], hT, hT.ap[:], 0, 0, is_s)
                if is_s:
                    for s in range(2):
                        cmul_small(init, init.ap[:, 0, s, :], init.ap[:, 1, s, :], h0.ap[:, 0, s, :], h0.ap[:, 1, s, :],
                                   Er.ap[:, :, 1], Ei.ap[:, :, 1], [h0, Er, Ei])
                import os as _os
                CUT = int(_os.environ.get('KD_CUT', '99'))
                if CUT < 2:
                    continue
                for m in range(8):
                    pr, pi_ = PS[1 + (m % 2) * 2], PS[2 + (m % 2) * 2]
                    for u in range(4):
                        j = m * 4 + u
                        P.op("pe", lambda e, pr=pr, u=u, j=j, m=m, hT=hT: e.matmul(out=pr.ap[:, u * 128:(u + 1) * 128], lhsT=Bbd[0].ap[:, j, :], rhs=hT.ap[:, m, :], start=True, stop=True),
                             reads=[Bbd[0], hT], writes=[pr])
                    for u in range(4):
                        j = m * 4 + u
                        P.op("pe", lambda e, pi_=pi_, u=u, j=j, m=m, hT=hT: e.matmul(out=pi_.ap[:, u * 128:(u + 1) * 128], lhsT=Bbd[1].ap[:, j, :], rhs=hT.ap[:, m, :], start=True, stop=True),
                             reads=[Bbd[1], hT], writes=[pi_])
                    prv = pr.ap[:].rearrange("p (u h t) -> p u h t", u=4, h=2)
                    piv = pi_.ap[:].rearrange("p (u h t) -> p u h t", u=4, h=2)
                    Trv = bc(Tr.ap[:, m * 4:(m + 1) * 4, :].unsqueeze(2), [128, 4, 2, 64])
                    Tiv = bc(Ti.ap[:, m * 4:(m + 1) * 4, :].unsqueeze(2), [128, 4, 2, 64])
                    tv = [tb.ap[:].rearrange("p (u h t) -> p u h t", u=4, h=2) for tb in tmp]
                    wrv = wre.ap[:, m * 4:(m + 1) * 4, :].rearrange("p u (h t) -> p u h t", h=2)
                    wiv = wim.ap[:, m * 4:(m + 1) * 4, :].rearrange("p u (h t) -> p u h t", h=2)
                    P.op("dve", lambda e, prv=prv, Trv=Trv, tv=tv: e.tensor_tensor(out=tv[0], in0=prv, in1=Trv, op=ALU.mult), reads=[pr, Tr], writes=[tmp[0]])
                    P.op("dve", lambda e, piv=piv, Tiv=Tiv, tv=tv: e.tensor_tensor(out=tv[1], in0=piv, in1=Tiv, op=ALU.mult), reads=[pi_, Ti], writes=[tmp[1]])
                    P.op("dve", lambda e, prv=prv, Tiv=Tiv, tv=tv: e.tensor_tensor(out=tv[2], in0=prv, in1=Tiv, op=ALU.mult), reads=[pr, Ti], writes=[tmp[2]])
                    P.op("dve", lambda e, piv=piv, Trv=Trv, tv=tv: e.tensor_tensor(out=tv[3], in0=piv, in1=Trv, op=ALU.mult), reads=[pi_, Tr], writes=[tmp[3]])
                    P.op("pool", lambda e, wrv=wrv, tv=tv: e.tensor_tensor(out=wrv, in0=tv[0], in1=tv[1], op=ALU.subtract), reads=[tmp[0], tmp[1]], pwrites=[wre])
                    P.op("pool", lambda e, wiv=wiv, tv=tv: e.tensor_tensor(out=wiv, in0=tv[2], in1=tv[3], op=ALU.add), reads=[tmp[2], tmp[3]], pwrites=[wim])
                if CUT < 3:
                    continue
                for hf in range(2):
                    for j in range(32):
                        for ri, (wsrc, gdst) in enumerate(((wre, gre), (wim, gim))):
                            P.op("dve", lambda e, j=j, hf=hf, ri=ri, wsrc=wsrc, gdst=gdst: e.tensor_tensor_scan(
                                out=gdst.ap[:, j, hf * 64:(hf + 1) * 64], data0=bc(sm.ap[:, R, j:j + 1], [128, 64]),
                                data1=wsrc.ap[:, j, hf * 64:(hf + 1) * 64], initial=init.ap[:, ri, hf, j:j + 1], op0=ALU.mult, op1=ALU.add),
                                reads=[sm, wsrc, init], pwrites=[gdst])
                    ge_r = gre.ap[:, :, hf * 64 + 63]
                    ge_i = gim.ap[:, :, hf * 64 + 63]
                    if not is_s:
                        nh = 1 - hf
                        cmul_small(init, init.ap[:, 0, nh, :], init.ap[:, 1, nh, :], ge_r, ge_i, Er.ap[:, :, 64], Ei.ap[:, :, 64], [gre, gim, Er, Ei])
                    if is_s or (t == NPT - 1 and hf == 1):
                        cmul_small(hend, hend.ap[:, 0, hf, :], hend.ap[:, 1, hf, :], ge_r, ge_i, Er.ap[:, :, 63], Ei.ap[:, :, 63], [gre, gim, Er, Ei])
                        if is_s:
                            P.dma("sp", O["s5s_re"][hf].rearrange("(j p) -> p j", p=128), hend.ap[:, 0, hf, :], reads=[hend], pwrites=[out_bufs["s5s_re"]], sem=osem, allow_slow_non_contiguous=True)
                            P.dma("sp", O["s5s_im"][hf].rearrange("(j p) -> p j", p=128), hend.ap[:, 1, hf, :], reads=[hend], pwrites=[out_bufs["s5s_im"]], sem=osem, allow_slow_non_contiguous=True)
                        else:
                            P.dma("sp", O["s5p_re"].rearrange("(j p) -> p j", p=128), hend.ap[:, 0, hf, :], reads=[hend], pwrites=[out_bufs["s5p_re"]], sem=osem, allow_slow_non_contiguous=True)
                            P.dma("sp", O["s5p_im"].rearrange("(j p) -> p j", p=128), hend.ap[:, 1, hf, :], reads=[hend], pwrites=[out_bufs["s5p_im"]], sem=osem, allow_slow_non_contiguous=True)
                if CUT < 4:
                    continue
                z = zT[i]
                for m in range(8):
                    grv = gre.ap[:, m * 4:(m + 1) * 4, :].rearrange("p u (h t) -> p u h t", h=2)
                    giv = gim.ap[:, m * 4:(m + 1) * 4, :].rearrange("p u (h t) -> p u h t", h=2)
                    Erv = bc(Er.ap[:, m * 4:(m + 1) * 4, 0:64].unsqueeze(2), [128, 4, 2, 64])
                    Eiv = bc(Ei.ap[:, m * 4:(m + 1) * 4, 0:64].unsqueeze(2), [128, 4, 2, 64])
                    tv = [tb.ap[:].rearrange("p (u h t) -> p u h t", u=4, h=2) for tb in tmp]
                    hrv = hsr.ap[:, m * 4:(m + 1) * 4, :].rearrange("p u (h t) -> p u h t", h=2)
                    hiv = hsi.ap[:, m * 4:(m + 1) * 4, :].rearrange("p u (h t) -> p u h t", h=2)
                    P.op("dve", lambda e, grv=grv, Erv=Erv, tv=tv: e.tensor_tensor(out=tv[0], in0=grv, in1=Erv, op=ALU.mult), reads=[gre, Er], writes=[tmp[0]])
                    P.op("pool", lambda e, giv=giv, Eiv=Eiv, tv=tv: e.tensor_tensor(out=tv[1], in0=giv, in1=Eiv, op=ALU.mult), reads=[gim, Ei], writes=[tmp[1]])
                    P.op("pool", lambda e, grv=grv, Eiv=Eiv, tv=tv: e.tensor_tensor(out=tv[2], in0=grv, in1=Eiv, op=ALU.mult), reads=[gre, Ei], writes=[tmp[2]])
                    P.op("pool", lambda e, giv=giv, Erv=Erv, tv=tv: e.tensor_tensor(out=tv[3], in0=giv, in1=Erv, op=ALU.mult), reads=[gim, Er], writes=[tmp[3]])
                    P.op("dve", lambda e, hrv=hrv, tv=tv: e.tensor_tensor(out=hrv, in0=tv[0], in1=tv[1], op=ALU.subtract), reads=[tmp[0], tmp[1]], pwrites=[hsr])
                    P.op("pool", lambda e, hiv=hiv, tv=tv: e.tensor_tensor(out=hiv, in0=tv[2], in1=tv[3], op=ALU.add), reads=[tmp[2], tmp[3]], pwrites=[hsi])
                    py = PS[5 + (m // 4)]
                    for u in range(4):
                        j = m * 4 + u
                        P.op("pe", lambda e, py=py, m=m, j=j, u=u: e.matmul(out=py.ap[:, (m % 4) * 128:(m % 4 + 1) * 128], lhsT=Cbd[0].ap[:, j, :], rhs=hsr.ap[:, j, :], start=(u == 0), stop=False),
                             reads=[Cbd[0], hsr], writes=[py])
                    for u in range(4):
                        j = m * 4 + u
                        P.op("pe", lambda e, py=py, m=m, j=j, u=u: e.matmul(out=py.ap[:, (m % 4) * 128:(m % 4 + 1) * 128], lhsT=Cbd[1].ap[:, j, :], rhs=hsi.ap[:, j, :], start=False, stop=(u == 3)),
                             reads=[Cbd[1], hsi], writes=[py])
                    P.op("dve", lambda e, py=py, m=m, hT=hT: e.scalar_tensor_tensor(out=ysb.ap[:, m, :], in0=hT.ap[:, m, :], scalar=dT.ap[:, m:m + 1],
                                                                              in1=py.ap[:, (m % 4) * 128:(m % 4 + 1) * 128], op0=ALU.mult, op1=ALU.add),
                         reads=[hT, dT, py], pwrites=[ysb])
                if CUT < 5:
                    continue
                P.op("act", lambda e, z=z: e.activation(out=z.ap[:], in_=ysb.ap[:], func=AF.Gelu_apprx_tanh), reads=[ysb], writes=[z])
                P.dma("sp", zT_d[t], z.ap[:].rearrange("p k t -> p (k t)"), reads=[z], pwrites=[zT_b], sem=ssem)

        if en(2):
          with Stage(P, "l0g") as st:
            Wa = st.sb([128, 8, D], BF16, "Wa")
            Wb = st.sb([128, 8, D], BF16, "Wb")
            wsem = P.no_dsem("wglu")
            load_w_bf16(Wa, lambda kc: Wa.ap[:, kc, :], W["s5_w_glu_a"], 8, wsem)
            load_w_bf16(Wb, lambda kc: Wb.ap[:, kc, :], W["s5_w_glu_b"], 8, wsem)
            gate = Gate(st, 0, 0)
            xt = [st.sb([128, D], F32, "x") for _ in range(2)]
            xsem = [P.no_dsem("x") for _ in range(2)]
            zt = [st.sb([128, 8, 128], BF16, "z") for _ in range(2)]
            zsem = [P.no_dsem("z") for _ in range(2)]
            sig = [st.sb([128, D], F32, "sig") for _ in range(2)]
            for t in range(NT):
                is_s = (t == NPT)
                i = t % 2
                x, z, sg_ = xt[i], zt[i], sig[i]
                src, _ = x_src(t, "in")
                P.dma("sp", x.ap[:], src, writes=[x], sem=xsem[i])
                P.dma("sp", z.ap[:].rearrange("p k t -> p (k t)"), zT_d[t], reads=[zT_b], writes=[z], sem=zsem[i])
                g = gate.get(is_s)
                for cbk in range(2):
                    pa, pb = PS[cbk * 2], PS[1 + cbk * 2]
                    for kc in range(8):
                        P.op("pe", lambda e, pa=pa, kc=kc, cbk=cbk, z=z: e.matmul(out=pa.ap[:], lhsT=z.ap[:, kc, :], rhs=Wa.ap[:, kc, cbk * 512:(cbk + 1) * 512], start=(kc == 0), stop=(kc == 7)),
                             reads=[z, Wa], writes=[pa])
                    for kc in range(8):
                        P.op("pe", lambda e, pb=pb, kc=kc, cbk=cbk, z=z: e.matmul(out=pb.ap[:], lhsT=z.ap[:, kc, :], rhs=Wb.ap[:, kc, cbk * 512:(cbk + 1) * 512], start=(kc == 0), stop=(kc == 7)),
                             reads=[z, Wb], writes=[pb])
                    sl = slice(cbk * 512, (cbk + 1) * 512)
                    P.op("act", lambda e, pb=pb, sl=sl, sg_=sg_: e.activation(out=sg_.ap[:, sl], in_=pb.ap[:], func=AF.Sigmoid), reads=[pb], pwrites=[sg_])
                    P.op("dve", lambda e, pa=pa, sl=sl, sg_=sg_: e.tensor_tensor(out=sg_.ap[:, sl], in0=pa.ap[:], in1=sg_.ap[:, sl], op=ALU.mult), reads=[pa, sg_], pwrites=[sg_])
                    P.op("pool", lambda e, sl=sl, g=g, sg_=sg_: e.tensor_tensor(out=sg_.ap[:, sl], in0=sg_.ap[:, sl], in1=g.ap[:, sl], op=ALU.mult), reads=[sg_, g], pwrites=[sg_])
                    P.op("pool", lambda e, sl=sl, sg_=sg_, x=x: e.tensor_tensor(out=sg_.ap[:, sl], in0=sg_.ap[:, sl], in1=x.ap[:, sl], op=ALU.add), reads=[sg_, x], pwrites=[sg_])
                dst, dstb = x_dst(t, "a")
                P.dma("sp", dst, sg_.ap[:], reads=[sg_], pwrites=[dstb], sem=ssem)

        def mlp_stage(l, src_w, dst_w, final=False):
          with Stage(P, "mlp%d" % l) as st:
            wup = st.sb([128, 8, 4 * D], BF16, "wup")
            wdn = st.sb([128, 32, D], BF16, "wdn")
            wsem = P.no_dsem("wmlp")
            for kc in range(8):
                for c0 in range(0, 4 * D, 1024):
                    P.dma("pool", wup.ap[:, kc, c0:c0 + 1024], W["w_up"][l, kc * 128:(kc + 1) * 128, c0:c0 + 1024], pwrites=[wup], sem=wsem)
            for kc in range(32):
                P.dma("pool", wdn.ap[:, kc, :], W["w_down"][l, kc * 128:(kc + 1) * 128, :], pwrites=[wdn], sem=wsem)
            gate = Gate(st, l, 1)
            if final:
                gfin = st.sb([128, D], F32, "gfin")
                gsem = P.no_dsem("gfin")
                P.dma("sp", gfin.ap[:], W["g_final"].partition_broadcast(128), writes=[gfin], sem=gsem)
                fstat = st.sb([128, 4], F32, "fstat")
            xs_ = st.sb([128, 4, D], F32, "xs")
            xsem = P.no_dsem("x")
            nctx = NormCtx(st, PS[0], PS[3])
            hT = st.sb([128, 8, 512], BF16, "hT")
            actT = st.sb([128, 32, 512], BF16, "actT")
            rl = [st.sb([128, 512], BF16, "rl") for _ in range(2)]
            tiles = [(s * 4, 4) for s in range(NPT // 4)] + [(NPT, 1)]
            for (t0, nt) in tiles:
                is_s = (t0 == NPT)
                ntok = nt * 128
                for a in range(nt):
                    src, srcb = x_src(t0 + a, src_w)
                    P.dma("sp", xs_.ap[:, a, :], src, reads=([srcb] if srcb else []), pwrites=[xs_], sem=xsem)
                for a in range(nt):
                    hv = hT.ap[:, :, a * 128:(a + 1) * 128]
                    nctx.run(xs_, xs_.ap[:, a, :], hT, hv, l, 1, is_s)
                for oc in range(32):
                    pu = PS[(1, 2, 4, 5, 6, 7)[oc % 6]]
                    for kc in range(8):
                        P.op("pe", lambda e, pu=pu, kc=kc, oc=oc, ntok=ntok: e.matmul(out=pu.ap[:, 0:ntok], lhsT=wup.ap[:, kc, oc * 128:(oc + 1) * 128], rhs=hT.ap[:, kc, 0:ntok], start=(kc == 0), stop=(kc == 7)),
                             reads=[wup, hT], writes=[pu])
                    r = rl[oc % 2]
                    P.op("act", lambda e, pu=pu, r=r, ntok=ntok: e.activation(out=r.ap[:, 0:ntok], in_=pu.ap[:, 0:ntok], func=AF.Relu), reads=[pu], writes=[r])
                    eng = "dve" if oc % 2 == 0 else "pool"
                    P.op(eng, lambda e, r=r, oc=oc, ntok=ntok: e.tensor_tensor(out=actT.ap[:, oc, 0:ntok], in0=r.ap[:, 0:ntok], in1=r.ap[:, 0:ntok], op=ALU.mult), reads=[r], pwrites=[actT])
                g = gate.get(is_s)
                for a in range(nt):
                    for cbk in range(2):
                        pd = PS[4 + (a * 2 + cbk) % 4]
                        for kc in range(32):
                            P.op("pe", lambda e, pd=pd, kc=kc, a=a, cbk=cbk: e.matmul(out=pd.ap[:], lhsT=actT.ap[:, kc, a * 128:(a + 1) * 128], rhs=wdn.ap[:, kc, cbk * 512:(cbk + 1) * 512], start=(kc == 0), stop=(kc == 31)),
                                 reads=[actT, wdn], writes=[pd])
                        sl = slice(cbk * 512, (cbk + 1) * 512)
                        tmpb = rl
                        P.op("dve", lambda e, pd=pd, a=a, sl=sl, g=g: e.tensor_tensor(out=pd.ap[:], in0=pd.ap[:], in1=g.ap[:, sl], op=ALU.mult), reads=[pd, g], writes=[pd])
                        P.op("dve", lambda e, pd=pd, a=a, sl=sl: e.tensor_tensor(out=xs_.ap[:, a, sl], in0=pd.ap[:], in1=xs_.ap[:, a, sl], op=ALU.add), reads=[pd, xs_], pwrites=[xs_])
                    if final:
                        junk, stat = nctx.junk, fstat
                        P.op("act", lambda e, a=a: e.activation(out=junk.ap[:], in_=xs_.ap[:, a, :], func=AF.Square, accum_out=stat.ap[:, 0:1]), reads=[xs_], writes=[junk, stat])
                        P.op("act", lambda e: e.activation(out=stat.ap[:, 1:2], in_=stat.ap[:, 0:1], func=AF.Sqrt, scale=1.0 / D, bias=EPS), reads=[stat], writes=[stat])
                        P.op("dve", lambda e: e.reciprocal(out=stat.ap[:, 2:3], in_=stat.ap[:, 1:2]), reads=[stat], writes=[stat])
                        P.op("dve", lambda e, a=a: e.scalar_tensor_tensor(out=xs_.ap[:, a, :], in0=xs_.ap[:, a, :], scalar=stat.ap[:, 2:3], in1=gfin.ap[:], op0=ALU.mult, op1=ALU.mult),
                             reads=[xs_, stat, gfin], pwrites=[xs_])
                        t = t0 + a
                        if t < NPT:
                            P.dma("sp", O["yp"][t * 128:(t + 1) * 128, :], xs_.ap[:, a, :], reads=[xs_], pwrites=[out_bufs["yp"]], sem=osem)
                        else:
                            P.dma("sp", O["ys"], xs_.ap[:, a, :], reads=[xs_], pwrites=[out_bufs["ys"]], sem=osem)
                    else:
                        dst, dstb = x_dst(t0 + a, dst_w)
                        P.dma("sp", dst, xs_.ap[:, a, :], reads=[xs_], pwrites=[dstb], sem=ssem)

        if en(3):
            mlp_stage(0, "a", "b")

        def sin_reduce(st, dbuf, abuf, shift, shape):
            kf = st.sb(shape, F32, "kf")
            ki = st.sb(shape, I32, "ki")
            dst, src, k_ap, ki_ap = dbuf.ap[:], abuf.ap[:], kf.ap[:], ki.ap[:]
            P.op("dve", lambda e: e.tensor_scalar(out=k_ap, in0=src, scalar1=shift, scalar2=1.0 / TWO_PI, op0=ALU.add, op1=ALU.mult), reads=[abuf], writes=[kf])
            P.op("dve", lambda e: e.tensor_copy(out=ki_ap, in_=k_ap), reads=[kf], writes=[ki])
            P.op("dve", lambda e: e.tensor_copy(out=k_ap, in_=ki_ap), reads=[ki], writes=[kf])
            P.op("dve", lambda e: e.scalar_tensor_tensor(out=dst, in0=k_ap, scalar=-CW1, in1=src, op0=ALU.mult, op1=ALU.add), reads=[kf, abuf], writes=[dbuf])
            P.op("dve", lambda e: e.scalar_tensor_tensor(out=dst, in0=k_ap, scalar=-CW2, in1=dst, op0=ALU.mult, op1=ALU.add), reads=[kf, dbuf], writes=[dbuf])
            P.op("dve", lambda e: e.scalar_tensor_tensor(out=dst, in0=k_ap, scalar=-CW3, in1=dst, op0=ALU.mult, op1=ALU.add), reads=[kf, dbuf], writes=[dbuf])
            P.op("dve", lambda e: e.tensor_scalar(out=dst, in0=dst, scalar1=shift, scalar2=None, op0=ALU.add), reads=[dbuf], writes=[dbuf])
            P.op("dve", lambda e: e.tensor_scalar(out=dst, in0=dst, scalar1=math.pi, scalar2=-math.pi, op0=ALU.min, op1=ALU.max), reads=[dbuf], writes=[dbuf])
            P.op("act", lambda e: e.activation(out=dst, in_=dst, func=AF.Sin), reads=[dbuf], writes=[dbuf])

        def rope_tables(st, half):
            cosT = st.sb([128, NT, half], F32, "cosT")
            sinT = st.sb([128, NT, half], F32, "sinT")
            with Stage(P, "rp") as sp:
                pos = sp.sb([128, NT], F32, "pos")
                invf = sp.sb([128, half], F32, "invf")
                ang = sp.sb([128, NT, half], F32, "ang")
                P.op("pool", lambda e: e.iota(pos.ap[:], pattern=[[128, NT]], base=0, channel_multiplier=1, allow_small_or_imprecise_dtypes=True), writes=[pos])
                P.op("pool", lambda e: e.iota(pos.ap[0:64, NPT:NPT + 1], pattern=[[0, 1]], base=PAST, channel_multiplier=1, allow_small_or_imprecise_dtypes=True), pwrites=[pos])
                P.op("pool", lambda e: e.iota(pos.ap[64:128, NPT:NPT + 1], pattern=[[0, 1]], base=PAST, channel_multiplier=1, allow_small_or_imprecise_dtypes=True), pwrites=[pos])
                for i in range(half):
                    P.op("pool", lambda e, i=i: e.memset(invf.ap[:, i:i + 1], float(np.float32(10000.0) ** np.float32(-i / half))), pwrites=[invf])
                P.op("dve", lambda e: e.tensor_tensor(out=ang.ap[:], in0=bc(pos.ap[:].unsqueeze(2), [128, NT, half]), in1=bc(invf.ap[:].unsqueeze(1), [128, NT, half]), op=ALU.mult),
                     reads=[pos, invf], writes=[ang])
                sin_reduce(sp, sinT, ang, 0.0, [128, NT, half])
                sin_reduce(sp, cosT, ang, math.pi / 2, [128, NT, half])
            return cosT, sinT

        def rope_apply(src_ps, src_ap, dbuf, dst_ap, ngrp, half, cosT, sinT, t, tmps):
            c = bc(cosT.ap[:, t, :].unsqueeze(1), [128, ngrp, half])
            s = bc(sinT.ap[:, t, :].unsqueeze(1), [128, ngrp, half])
            x1, x2 = src_ap[:, :, 0:half], src_ap[:, :, half:2 * half]
            tv = [tb.ap[:, 0:ngrp * half].rearrange("p (g i) -> p g i", i=half) for tb in tmps]
            P.op("dve", lambda e: e.tensor_tensor(out=tv[0], in0=x1, in1=c, op=ALU.mult), reads=[src_ps, cosT], writes=[tmps[0]])
            P.op("dve", lambda e: e.tensor_tensor(out=tv[1], in0=x2, in1=s, op=ALU.mult), reads=[src_ps, sinT], writes=[tmps[1]])
            P.op("dve", lambda e: e.tensor_tensor(out=tv[2], in0=x1, in1=s, op=ALU.mult), reads=[src_ps, sinT], writes=[tmps[2]])
            P.op("dve", lambda e: e.tensor_tensor(out=tv[3], in0=x2, in1=c, op=ALU.mult), reads=[src_ps, cosT], writes=[tmps[3]])
            P.op("pool", lambda e: e.tensor_tensor(out=dst_ap[:, :, 0:half], in0=tv[0], in1=tv[1], op=ALU.subtract), reads=[tmps[0], tmps[1]], pwrites=[dbuf])
            P.op("pool", lambda e: e.tensor_tensor(out=dst_ap[:, :, half:2 * half], in0=tv[2], in1=tv[3], op=ALU.add), reads=[tmps[2], tmps[3]], pwrites=[dbuf])

        QT_d = dscr("QT_d", [8, 128, NTOK], BF16)
        KT_d = dscr("KT_d", [8, 128, NTOK], BF16)
        V_d = dscr("V_d", [NTOK, D], BF16)
        OT_d = dscr("OT_d", [8, 128, NTOK], BF16)
        QT_b, KT_b, V_b, OT_b = Buf(QT_d, "QT_d"), Buf(KT_d, "KT_d"), Buf(V_d, "V_d"), Buf(OT_d, "OT_d")

        if en(4):
          with Stage(P, "l1a") as st:
            cosT, sinT = rope_tables(st, 32)
            wqkv = st.sb([128, 8, 3 * D], BF16, "wqkv")
            for kc in range(8):
                for c0 in range(0, 3 * D, 1024):
                    P.dma("pool", wqkv.ap[:, kc, c0:c0 + 1024], W["diff_w_qkv"][kc * 128:(kc + 1) * 128, c0:c0 + 1024], pwrites=[wqkv])
            xt = [st.sb([128, D], F32, "x") for _ in range(2)]
            nctx = NormCtx(st, PS[0], PS[7])
            hTs = [st.sb([128, 8, 128], BF16, "hT") for _ in range(2)]
            qk = [st.sb([128, 2, D], F32, "qk") for _ in range(2)]
            qkb = [st.sb([128, 2, D], BF16, "qkb") for _ in range(2)]
            vf = [st.sb([128, D], F32, "vf") for _ in range(2)]
            vb = [st.sb([128, D], BF16, "vb") for _ in range(2)]
            tmps = [st.sb([128, 256], F32, "rt") for _ in range(4)]
            qkT = [st.sb([128, 16, 128], BF16, "qkT") for _ in range(2)]
            tlist = list(range(NT)) if upto >= 4.5 else [0, NPT]
            for t in tlist:
                is_s = (t == NPT)
                i = t % 2
                x, hT = xt[i], hTs[i]
                src, srcb = x_src(t, "b")
                P.dma("sp", x.ap[:], src, reads=[srcb], writes=[x])
                nctx.run(x, x.ap[:], hT, hT.ap[:], 1, 0, is_s)
                for cb in range(6):
                    pq = PS[1 + cb % 4]
                    for kc in range(8):
                        P.op("pe", lambda e, pq=pq, kc=kc, cb=cb, hT=hT: e.matmul(out=pq.ap[:], lhsT=hT.ap[:, kc, :], rhs=wqkv.ap[:, kc, cb * 512:(cb + 1) * 512], start=(kc == 0), stop=(kc == 7)),
                             reads=[hT, wqkv], writes=[pq])
                    if cb < 4:
                        which, half_ = cb // 2, cb % 2
                        dst = qk[i].ap[:, which, half_ * 512:(half_ + 1) * 512].rearrange("p (g d) -> p g d", d=64)
                        rope_apply(pq, pq.ap[:].rearrange("p (g d) -> p g d", d=64), qk[i], dst, 8, 32, cosT, sinT, t, tmps)
                    else:
                        sl = slice((cb - 4) * 512, (cb - 3) * 512)
                        P.op("act", lambda e, pq=pq, sl=sl, i=i: e.activation(out=vf[i].ap[:, sl], in_=pq.ap[:], func=AF.Copy), reads=[pq], pwrites=[vf[i]])
                        P.op("dve", lambda e, pq=pq, sl=sl, i=i: e.tensor_copy(out=vb[i].ap[:, sl], in_=vf[i].ap[:, sl]), reads=[vf[i]], pwrites=[vb[i]])
                if not is_s:
                    P.dma("sp", O["dkp"][t * 128:(t + 1) * 128, :], qk[i].ap[:, 1, :], reads=[qk[i]], pwrites=[out_bufs["dkp"]])
                    P.dma("sp", O["dvp"][t * 128:(t + 1) * 128, :], vf[i].ap[:], reads=[vf[i]], pwrites=[out_bufs["dvp"]])
                else:
                    P.dma("sp", O["dks"], qk[i].ap[:, 1, :], reads=[qk[i]], pwrites=[out_bufs["dks"]])
                    P.dma("sp", O["dvs"], vf[i].ap[:], reads=[vf[i]], pwrites=[out_bufs["dvs"]])
                P.dma("sp", V_d[t * 128:(t + 1) * 128, :], vb[i].ap[:], reads=[vb[i]], pwrites=[V_b])
                P.op("act", lambda e, i=i: e.activation(out=qkb[i].ap[:], in_=qk[i].ap[:], func=AF.Copy), reads=[qk[i]], writes=[qkb[i]])
                for which in range(2):
                    pt = PS[5 + which]
                    ptv = pt.ap[:].bitcast(BF16).rearrange("p (h t) -> p h t", t=128)
                    for h in range(8):
                        P.op("pe", lambda e, ptv=ptv, h=h, which=which, i=i: e.transpose(out=ptv[:, h, :], in_=qkb[i].ap[:, which, h * 128:(h + 1) * 128], identity=ident_b.ap[:]),
                             reads=[qkb[i], ident_b], writes=[pt])
                    eng = "act" if which == 0 else "dve"
                    if eng == "act":
                        P.op("act", lambda e, ptv=ptv, which=which, i=i: e.activation(out=qkT[i].ap[:, which * 8:(which + 1) * 8, :], in_=ptv, func=AF.Copy), reads=[pt], pwrites=[qkT[i]])
                    else:
                        P.op("dve", lambda e, ptv=ptv, which=which, i=i: e.tensor_copy(out=qkT[i].ap[:, which * 8:(which + 1) * 8, :], in_=ptv), reads=[pt], pwrites=[qkT[i]])
                P.dma("sp", QT_d[:, :, t * 128:(t + 1) * 128].rearrange("h p t -> p h t"), qkT[i].ap[:, 0:8, :], reads=[qkT[i]], pwrites=[QT_b])
                P.dma("sp", KT_d[:, :, t * 128:(t + 1) * 128].rearrange("h p t -> p h t"), qkT[i].ap[:, 8:16, :], reads=[qkT[i]], pwrites=[KT_b])

        if en(5):
          with Stage(P, "l1b") as st:
            lv = st.sb([128, 4, 64], F32, "lv")
            for k_i, nm in enumerate(["diff_lambda_q1", "diff_lambda_k1", "diff_lambda_q2", "diff_lambda_k2"]):
                P.dma("sp", lv.ap[:, k_i, :], W[nm].partition_broadcast(128), pwrites=[lv])
            lsc = st.sb([128, 8], F32, "lsc")
            lpr = st.sb([128, 2, 64], F32, "lpr")
            P.op("dve", lambda e: e.tensor_tensor(out=lpr.ap[:, 0, :], in0=lv.ap[:, 0, :], in1=lv.ap[:, 1, :], op=ALU.mult), reads=[lv], pwrites=[lpr])
            P.op("dve", lambda e: e.tensor_tensor(out=lpr.ap[:, 1, :], in0=lv.ap[:, 2, :], in1=lv.ap[:, 3, :], op=ALU.mult), reads=[lv], pwrites=[lpr])
            P.op("dve", lambda e: e.tensor_reduce(out=lsc.ap[:, 0:2], in_=lpr.ap[:], axis=mybir.AxisListType.X, op=ALU.add), reads=[lpr], writes=[lsc])
            P.op("act", lambda e: e.activation(out=lsc.ap[:, 2:4], in_=lsc.ap[:, 0:2], func=AF.Exp), reads=[lsc], writes=[lsc])
            P.op("dve", lambda e: e.tensor_tensor(out=lsc.ap[:, 4:5], in0=lsc.ap[:, 3:4], in1=lsc.ap[:, 2:3], op=ALU.subtract), reads=[lsc], writes=[lsc])
            P.op("dve", lambda e: e.tensor_scalar(out=lsc.ap[:, 5:6], in0=lsc.ap[:, 4:5], scalar1=-LAMBDA_INIT, scalar2=None, op0=ALU.add), reads=[lsc], writes=[lsc])
            neglam = lsc.ap[:, 5:6]
            SC = 64 ** -0.5

            KT = [st.sb([128, SEQ], BF16, "KT") for _ in range(2)]
            QT = [st.sb([128, SEQ], BF16, "QT") for _ in range(2)]
            Vh = [st.sb([128, NPT, 128], BF16, "Vh") for _ in range(2)]
            PT = [[st.sb([128, 512], BF16, "PT") for _ in range(3)] for _ in range(2)]
            R = [st.sb([128, 512], F32, "R") for _ in range(2)]
            o12 = [st.sb([128, 512], F32, "o12") for _ in range(2)]
            ob = [st.sb([128, 512], BF16, "ob") for _ in range(2)]
            Sb = [[PS[0], PS[1]], [PS[2], PS[3]]]
            Ob = [PS[4], PS[5]]
            Lb = [PS[6], PS[7]]
            nheads = 8 if upto >= 5.5 else 1
            if _os.environ.get('KD_SKIP_L1B'):
                nheads = 0
            nQ = 16 if upto >= 5.5 else 2
            for h in range(nheads):
                sl_ = h % 2
                kt, qt, vh = KT[sl_], QT[sl_], Vh[sl_]
                for c4 in range(4):
                    cs = slice(c4 * 2048, (c4 + 1) * 2048)
                    P.dma("sp", kt.ap[:, cs], KT_d[h, :, cs], reads=[KT_b], pwrites=[kt])
                    P.dma("sp", qt.ap[:, cs], QT_d[h, :, cs], reads=[QT_b], pwrites=[qt])
                    bs = slice(c4 * 16, (c4 + 1) * 16)
                    P.dma("sp", vh.ap[:, bs, :], V_d[c4 * 2048:(c4 + 1) * 2048, h * 128:(h + 1) * 128].rearrange("(b p) e -> p b e", p=128), reads=[V_b], pwrites=[vh])
                for Q in range(nQ):
                    blocks = [(kb, 0, False) for kb in range(4 * Q)] + [(4 * Q + i_, 128 * i_, True) for i_ in range(4)]
                    n = len(blocks)

                    def issue_S(idx):
                        kb, col0, diag = blocks[idx]
                        for c in range(2):
                            sb_ = Sb[c][idx % 2]
                            rs = slice(c * 64, (c + 1) * 64)
                            P.op("pe", lambda e, sb_=sb_, rs=rs, kb=kb, col0=col0, kt=kt, qt=qt, Q=Q: e.matmul(
                                out=sb_.ap[:, col0:512], lhsT=kt.ap[rs, kb * 128:(kb + 1) * 128], rhs=qt.ap[rs, Q * 512 + col0:(Q + 1) * 512], start=True, stop=True),
                                reads=[kt, qt], writes=[sb_])

                    issue_S(0)
                    for idx in range(n):
                        kb, col0, diag = blocks[idx]
                        for c in range(2):
                            sb_, pt = Sb[c][idx % 2], PT[c][idx % 3]
                            if not diag:
                                P.op("act", lambda e, sb_=sb_, pt=pt: e.activation(out=pt.ap[:], in_=sb_.ap[:], func=AF.Exp, scale=SC), reads=[sb_], writes=[pt])
                            else:
                                P.op("act", lambda e, sb_=sb_, pt=pt, col0=col0: e.activation(out=pt.ap[0:64, col0:512], in_=sb_.ap[0:64, col0:512], func=AF.Exp, scale=SC), reads=[sb_], writes=[pt])
                                P.op("act", lambda e, sb_=sb_, pt=pt, col0=col0: e.activation(out=pt.ap[64:128, col0:col0 + 64], in_=sb_.ap[64:128, col0:col0 + 64], func=AF.Copy, scale=0.0), reads=[sb_], pwrites=[pt])
                                P.op("act", lambda e, sb_=sb_, pt=pt, col0=col0: e.activation(out=pt.ap[64:128, col0 + 64:512], in_=sb_.ap[64:128, col0 + 64:512], func=AF.Exp, scale=SC), reads=[sb_], pwrites=[pt])
                        if idx + 1 < n:
                            issue_S(idx + 1)
                        for c in range(2):
                            pt = PT[c][idx % 3]
                            P.op("pe", lambda e, c=c, pt=pt, kb=kb, col0=col0, idx=idx, diag=diag, vh=vh: e.matmul(
                                out=Ob[c].ap[:, col0:512], lhsT=vh.ap[:, kb, :], rhs=pt.ap[:, col0:512], start=(idx == 0), stop=diag, skip_group_check=True),
                                reads=[vh, pt], writes=[Ob[c]])
                            P.op("pe", lambda e, c=c, pt=pt, col0=col0, idx=idx, diag=diag: e.matmul(
                                out=Lb[c].ap[:, col0:512], lhsT=ones_b.ap[:], rhs=pt.ap[:, col0:512], start=(idx == 0), stop=diag, skip_group_check=True),
                                reads=[ones_b, pt], writes=[Lb[c]])
                    for c in range(2):
                        P.op("dve", lambda e, c=c: e.reciprocal(out=R[c].ap[:], in_=Lb[c].ap[:]), reads=[Lb[c]], writes=[R[c]])
                        P.op("dve", lambda e, c=c: e.tensor_tensor(out=o12[c].ap[:], in0=Ob[c].ap[:], in1=R[c].ap[:], op=ALU.mult), reads=[Ob[c], R[c]], writes=[o12[c]])
                    obq = ob[Q % 2]
                    P.op("dve", lambda e, obq=obq: e.scalar_tensor_tensor(out=obq.ap[:], in0=o12[1].ap[:], scalar=neglam, in1=o12[0].ap[:], op0=ALU.mult, op1=ALU.add),
                         reads=[o12[0], o12[1], lsc], writes=[obq])
                    P.dma("sp", OT_d[h, :, Q * 512:(Q + 1) * 512], obq.ap[:], reads=[obq], pwrites=[OT_b])

          with Stage(P, "l1s") as st:
            lv = st.sb([128, 4, 64], F32, "lv")
            for k_i, nm in enumerate(["diff_lambda_q1", "diff_lambda_k1", "diff_lambda_q2", "diff_lambda_k2"]):
                P.dma("sp", lv.ap[:, k_i, :], W[nm].partition_broadcast(128), pwrites=[lv])
            lsc = st.sb([128, 8], F32, "lsc")
            lpr = st.sb([128, 2, 64], F32, "lpr")
            P.op("dve", lambda e: e.tensor_tensor(out=lpr.ap[:, 0, :], in0=lv.ap[:, 0, :], in1=lv.ap[:, 1, :], op=ALU.mult), reads=[lv], pwrites=[lpr])
            P.op("dve", lambda e: e.tensor_tensor(out=lpr.ap[:, 1, :], in0=lv.ap[:, 2, :], in1=lv.ap[:, 3, :], op=ALU.mult), reads=[lv], pwrites=[lpr])
            P.op("dve", lambda e: e.tensor_reduce(out=lsc.ap[:, 0:2], in_=lpr.ap[:], axis=mybir.AxisListType.X, op=ALU.add), reads=[lpr], writes=[lsc])
            P.op("act", lambda e: e.activation(out=lsc.ap[:, 2:4], in_=lsc.ap[:, 0:2], func=AF.Exp), reads=[lsc], writes=[lsc])
            P.op("dve", lambda e: e.tensor_tensor(out=lsc.ap[:, 4:5], in0=lsc.ap[:, 3:4], in1=lsc.ap[:, 2:3], op=ALU.subtract), reads=[lsc], writes=[lsc])
            P.op("dve", lambda e: e.tensor_scalar(out=lsc.ap[:, 5:6], in0=lsc.ap[:, 4:5], scalar1=-LAMBDA_INIT, scalar2=None, op0=ALU.add), reads=[lsc], writes=[lsc])
            neglam = lsc.ap[:, 5:6]
            SC = 64 ** -0.5
            zer = st.sb([128, 512], BF16, "zer")
            P.op("pool", lambda e: e.memset(zer.ap[:], 0.0), writes=[zer])
            Kt = [st.sb([128, D], BF16, "Kt") for _ in range(2)]
            Vt = [st.sb([128, D], BF16, "Vt") for _ in range(2)]
            KTb = [st.sb([128, 8, 128], BF16, "KTb") for _ in range(2)]
            QTs = st.sb([128, 8, 64], BF16, "QTs")
            PTs = [st.sb([128, 2, 512], BF16, "PTs") for _ in range(2)]
            Rr = st.sb([128, 1024], F32, "Rr")
            oo = st.sb([128, 1024], F32, "oo")
            obs = st.sb([128, 8, 64], BF16, "obs")
            Sbk, Obk, Lbk, Tbk = [PS[0], PS[1]], [PS[2], PS[3]], [PS[4], PS[5]], PS[6]
            NKB = PAST // 128
            for s in range(0 if _os.environ.get('KD_SKIP_L1S') else 2):
                tok0 = NPT * 128 + s * 64
                P.dma("sp", QTs.ap[:], QT_d[:, :, tok0:tok0 + 64].rearrange("h p t -> p h t"), reads=[QT_b], writes=[QTs])
                for b in Obk + Lbk:
                    P.op("pe", lambda e, b=b: e.matmul(out=b.ap[:], lhsT=zer.ap[:, 0:128], rhs=zer.ap[:], start=True, stop=False, skip_group_check=True), reads=[zer], writes=[b])
                _part = int(_os.environ.get('KD_L1S_PART', '9'))
                _kbs = [int(v) for v in _os.environ.get('KD_L1S_KBS', '').split(',')] if _os.environ.get('KD_L1S_KBS') else list(range(NKB + 1))
                for kb in _kbs:
                    i = kb % 2
                    last = (kb == NKB)
                    nk = 64 if last else 128
                    ktb = KTb[i]
                    if not last:
                        P.dma("pool", Kt[i].ap[:], I["cdk"][s, kb * 128:(kb + 1) * 128, :], writes=[Kt[i]])
                        P.dma("pool", Vt[i].ap[:], I["cdv"][s, kb * 128:(kb + 1) * 128, :], writes=[Vt[i]])
                        tv = Tbk.ap[:].bitcast(BF16).rearrange("p (h t) -> p h t", t=128)
                        for h in range(8):
                            P.op("pe", lambda e, tv=tv, h=h, i=i: e.transpose(out=tv[:, h, :], in_=Kt[i].ap[:, h * 128:(h + 1) * 128], identity=ident_b.ap[:]), reads=[Kt[i], ident_b], writes=[Tbk])
                        P.op("dve", lambda e, tv=tv, ktb=ktb: e.tensor_copy(out=ktb.ap[:], in_=tv), reads=[Tbk], writes=[ktb])
                    else:
                        P.dma("sp", ktb.ap[:, :, 0:64], KT_d[:, :, tok0:tok0 + 64].rearrange("h p t -> p h t"), reads=[KT_b], writes=[ktb])
                        P.dma("pool", Vt[i].ap[0:64, :], V_d[tok0:tok0 + 64, :], reads=[V_b], writes=[Vt[i]])
                    if _part < 2:
                        continue
                    for h in range(8):
                        for c in range(2):
                            sb_ = Sbk[c]
                            rs = slice(c * 64, (c + 1) * 64)
                            P.op("pe", lambda e, sb_=sb_, rs=rs, h=h, nk=nk, ktb=ktb: e.matmul(
                                out=sb_.ap[0:nk, h * 64:h * 64 + 64], lhsT=ktb.ap[rs, h, 0:nk], rhs=QTs.ap[rs, h, :], start=True, stop=True),
                                reads=[ktb, QTs], writes=[sb_])
                    pts = PTs[i]
                    for j in range(2):
                        P.op("act", lambda e, j=j, nk=nk, pts=pts: e.activation(out=pts.ap[0:nk, j, :], in_=Sbk[j].ap[0:nk, :], func=AF.Exp, scale=SC), reads=[Sbk[j]], pwrites=[pts])
                    if _part < 3:
                        continue
                    for h in range(8):
                        for c in range(2):
                            j, cs = c, slice(h * 64, h * 64 + 64)
                            P.op("pe", lambda e, j=j, cs=cs, h=h, nk=nk, i=i, pts=pts: e.matmul(
                                out=Obk[j].ap[:, cs], lhsT=Vt[i].ap[0:nk, h * 128:(h + 1) * 128], rhs=pts.ap[0:nk, j, cs], start=False, stop=True, skip_group_check=True),
                                reads=[Vt[i], pts], writes=[Obk[j]])
                            P.op("pe", lambda e, j=j, cs=cs, nk=nk, pts=pts: e.matmul(
                                out=Lbk[j].ap[:, cs], lhsT=ones_b.ap[0:nk, :], rhs=pts.ap[0:nk, j, cs], start=False, stop=True, skip_group_check=True),
                                reads=[ones_b, pts], writes=[Lbk[j]])
                if _part < 4:
                    continue
                for j in range(2):
                    js = slice(j * 512, (j + 1) * 512)
                    P.op("dve", lambda e, j=j, js=js: e.reciprocal(out=Rr.ap[:, js], in_=Lbk[j].ap[:]), reads=[Lbk[j]], pwrites=[Rr])
                    P.op("dve", lambda e, j=j, js=js: e.tensor_tensor(out=oo.ap[:, js], in0=Obk[j].ap[:], in1=Rr.ap[:, js], op=ALU.mult), reads=[Obk[j], Rr], pwrites=[oo])
                ov = oo.ap[:].rearrange("p (c h q) -> p c h q", c=2, q=64)
                P.op("dve", lambda e, ov=ov: e.scalar_tensor_tensor(out=obs.ap[:], in0=ov[:, 1, :, :], scalar=neglam, in1=ov[:, 0, :, :], op0=ALU.mult, op1=ALU.add),
                     reads=[oo, lsc], writes=[obs])
                P.dma("sp", OT_d[:, :, tok0:tok0 + 64].rearrange("h p t -> p h t"), obs.ap[:], reads=[obs], pwrites=[OT_b])

        def attn_out_stage(name, l, OT_src, OT_srcb, w_o_name, nh, e_dim, subnorm, gsub_name, src_w, dst_w, tile_ok):
          with Stage(P, name) as st:
            nk = nh * e_dim // 128
            wo = st.sb([128, nk, D], BF16, "wo")
            if subnorm:
                gsub = st.sb([128, 1], F32, "gsub")
                P.dma("sp", gsub.ap[:], W[gsub_name].rearrange("(p o) -> p o", o=1), writes=[gsub])
                wstg = [st.sb([128, D], F32, "wstg") for _ in range(2)]
                for kc in range(nk):
                    ws = wstg[kc % 2]
                    P.dma("sp", ws.ap[:], W[w_o_name][kc * 128:(kc + 1) * 128, :], writes=[ws])
                    P.op("dve", lambda e, ws=ws, kc=kc: e.tensor_scalar(out=wo.ap[:, kc, :], in0=ws.ap[:], scalar1=gsub.ap[:, 0:1], scalar2=(1.0 - LAMBDA_INIT), op0=ALU.mult, op1=ALU.mult),
                         reads=[ws, gsub], pwrites=[wo])
            else:
                for kc in range(nk):
                    P.dma("pool", wo.ap[:, kc, :], W[w_o_name][kc * 128:(kc + 1) * 128, :], pwrites=[wo])
            gate = Gate(st, l, 0)
            xt = [st.sb([128, D], F32, "x") for _ in range(2)]
            ot = [st.sb([128, nk, 128], BF16, "ot") for _ in range(2)]
            sq = st.sb([128, nk, 128], BF16, "sq")
            rs_ = st.sb([128, nk * 128], F32, "rs")
            on = [st.sb([128, nk, 128], BF16, "on") for _ in range(2)]
            for t in range(NT):
                if not tile_ok(t):
                    continue
                is_s = (t == NPT)
                i = t % 2
                x, o_ = xt[i], ot[i]
                src, srcb = x_src(t, src_w)
                P.dma("sp", x.ap[:], src, reads=[srcb], writes=[x])
                P.dma("sp", o_.ap[:], OT_src[:, :, t * 128:(t + 1) * 128].rearrange("h p t -> p h t"), reads=[OT_srcb], writes=[o_])
                if subnorm:
                    P.op("act", lambda e, o_=o_: e.activation(out=sq.ap[:], in_=o_.ap[:], func=AF.Square), reads=[o_], writes=[sq])
                    for j in range(2):
                        P.op("pe", lambda e, j=j: e.matmul(out=PS[j].ap[:], lhsT=ones_b.ap[:], rhs=sq.ap[:, j * 4:(j + 1) * 4, :].rearrange("p h t -> p (h t)"), start=True, stop=True),
                             reads=[ones_b, sq], writes=[PS[j]])
                        js = slice(j * 512, (j + 1) * 512)
                        P.op("act", lambda e, j=j, js=js: e.activation(out=rs_.ap[:, js], in_=PS[j].ap[:], func=AF.Sqrt, scale=1.0 / 128, bias=EPS), reads=[PS[j]], pwrites=[rs_])
                    P.op("dve", lambda e: e.reciprocal(out=rs_.ap[:], in_=rs_.ap[:]), reads=[rs_], writes=[rs_])
                    lhs = on[i]
                    P.op("dve", lambda e, o_=o_, lhs=lhs: e.tensor_tensor(out=lhs.ap[:].rearrange("p h t -> p (h t)"), in0=o_.ap[:].rearrange("p h t -> p (h t)"), in1=rs_.ap[:], op=ALU.mult),
                         reads=[o_, rs_], writes=[lhs])
                else:
                    lhs = o_
                g = gate.get(is_s)
                for cbk in range(2):
                    po = PS[2 + cbk + 2 * (t % 2)]
                    for kc in range(nk):
                        P.op("pe", lambda e, po=po, kc=kc, cbk=cbk, lhs=lhs: e.matmul(out=po.ap[:], lhsT=lhs.ap[:, kc, :], rhs=wo.ap[:, kc, cbk * 512:(cbk + 1) * 512], start=(kc == 0), stop=(kc == nk - 1)),
                             reads=[lhs, wo], writes=[po])
                    sl = slice(cbk * 512, (cbk + 1) * 512)
                    P.op("dve", lambda e, po=po, sl=sl, g=g: e.tensor_tensor(out=po.ap[:], in0=po.ap[:], in1=g.ap[:, sl], op=ALU.mult), reads=[po, g], writes=[po])
                    P.op("dve", lambda e, po=po, sl=sl, x=x: e.tensor_tensor(out=x.ap[:, sl], in0=po.ap[:], in1=x.ap[:, sl], op=ALU.add), reads=[po, x], pwrites=[x])
                dst, dstb = x_dst(t, dst_w)
                P.dma("sp", dst, x.ap[:], reads=[x], pwrites=[dstb])

        if en(6):
            attn_out_stage("l1c", 1, OT_d, OT_b, "diff_w_o", 8, 128, True, "diff_g_sub", "b", "a", lambda t: True)
        if en(7):
            mlp_stage(1, "a", "b")

        QA_d = dscr("QA_d", [16, 128, NTOK], BF16)
        QR_d = dscr("QR_d", [16, 32, NTOK], BF16)
        CK_d = dscr("CK_d", [NTOK, 128], BF16)
        CKT_d = dscr("CKT_d", [128, NTOK], BF16)
        KRT_d = dscr("KRT_d", [32, NTOK], BF16)
        OT2_d = dscr("OT2_d", [8, 128, NTOK], BF16)
        QA_b, QR_b, CK_b, CKT_b, KRT_b, OT2_b = (Buf(QA_d, "QA_d"), Buf(QR_d, "QR_d"), Buf(CK_d, "CK_d"), Buf(CKT_d, "CKT_d"),
                                                 Buf(KRT_d, "KRT_d"), Buf(OT2_d, "OT2_d"))
        MSC = 96 ** -0.5

        if en(8):
          with Stage(P, "l2a") as st:
            cos2, sin2 = rope_tables(st, 16)
            wdq = st.sb([128, 8, 416], BF16, "wdq")
            for kc in range(8):
                P.dma("pool", wdq.ap[:, kc, 0:256], W["mla_w_dq"][kc * 128:(kc + 1) * 128, :], pwrites=[wdq])
                P.dma("pool", wdq.ap[:, kc, 256:416], W["mla_w_dkv"][kc * 128:(kc + 1) * 128, :], pwrites=[wdq])
            gq = st.sb([128, 2], F32, "gq")
            P.dma("sp", gq.ap[:], W["mla_g_q"].rearrange("(c p) -> p c", p=128), writes=[gq], allow_slow_non_contiguous=True)
            wuq = st.sb([128, 2, 1536], BF16, "wuq")
            wst = [st.sb([128, 1536], F32, "wst") for _ in range(2)]
            for kc in range(2):
                P.dma("sp", wst[kc].ap[:], W["mla_w_uq"][kc * 128:(kc + 1) * 128, :], writes=[wst[kc]])
                P.op("dve", lambda e, kc=kc: e.tensor_scalar(out=wuq.ap[:, kc, :], in0=wst[kc].ap[:], scalar1=gq.ap[:, kc:kc + 1], scalar2=None, op0=ALU.mult), reads=[wst[kc], gq], pwrites=[wuq])
            wuk = st.sb([128, D], BF16, "wuk")
            P.dma("pool", wuk.ap[:], W["mla_w_uk"], writes=[wuk])
            wukT = st.sb([64, 16, 128], BF16, "wukT")
            for g4 in range(2):
                pb = PS[4 + g4]
                pv = pb.ap[:].bitcast(BF16).rearrange("p (h t) -> p h t", t=128)
                for u in range(8):
                    h = g4 * 8 + u
                    P.op("pe", lambda e, pv=pv, u=u, h=h: e.transpose(out=pv[0:64, u, :], in_=wuk.ap[:, h * 64:(h + 1) * 64], identity=ident_b.ap[:]), reads=[wuk, ident_b], writes=[pb])
                P.op("dve", lambda e, pv=pv, g4=g4: e.tensor_copy(out=wukT.ap[:, g4 * 8:(g4 + 1) * 8, :], in_=pv[0:64, :, :]), reads=[pb], pwrites=[wukT])
            gkv = st.sb([128, 128], F32, "gkv")
            P.dma("sp", gkv.ap[:], W["mla_g_kv"].partition_broadcast(128), writes=[gkv])
            xt = [st.sb([128, D], F32, "x") for _ in range(2)]
            nctx = NormCtx(st, PS[0], PS[7])
            hTs = [st.sb([128, 8, 128], BF16, "hT") for _ in range(2)]
            st4 = st.sb([128, 8], F32, "st4")
            jk = st.sb([128, 256], BF16, "jk")
            qn = st.sb([128, 256], BF16, "qn")
            qnT = st.sb([128, 2, 128], BF16, "qnT")
            qb = st.sb([128, 16, 96], BF16, "qb")
            qT = st.sb([96, 16, 128], BF16, "qT")
            qa = [st.sb([128, 16, 128], BF16, "qa") for _ in range(2)]
            ckf = [st.sb([128, 128], F32, "ckf") for _ in range(2)]
            ckb = [st.sb([128, 128], BF16, "ckb") for _ in range(2)]
            krf = [st.sb([128, 32], F32, "krf") for _ in range(2)]
            krb = [st.sb([128, 32], BF16, "krb") for _ in range(2)]
            ckT = [st.sb([128, 128], BF16, "ckT") for _ in range(2)]
            krT = [st.sb([32, 128], BF16, "krT") for _ in range(2)]
            tmps = [st.sb([128, 256], F32, "rt") for _ in range(4)]
            for t in range(NT):
                is_s = (t == NPT)
                i = t % 2
                x, hT = xt[i], hTs[i]
                src, srcb = x_src(t, "b")
                P.dma("sp", x.ap[:], src, reads=[srcb], writes=[x])
                nctx.run(x, x.ap[:], hT, hT.ap[:], 2, 0, is_s)
                pp = PS[1]
                for kc in range(8):
                    P.op("pe", lambda e, kc=kc, hT=hT: e.matmul(out=pp.ap[:, 0:416], lhsT=hT.ap[:, kc, :], rhs=wdq.ap[:, kc, :], start=(kc == 0), stop=(kc == 7)), reads=[hT, wdq], writes=[pp])
                P.op("act", lambda e: e.activation(out=jk.ap[:], in_=pp.ap[:, 0:256], func=AF.Square, accum_out=st4.ap[:, 0:1]), reads=[pp], writes=[jk, st4])
                P.op("act", lambda e: e.activation(out=st4.ap[:, 1:2], in_=st4.ap[:, 0:1], func=AF.Sqrt, scale=1.0 / 256, bias=EPS), reads=[st4], writes=[st4])
                P.op("dve", lambda e: e.reciprocal(out=st4.ap[:, 2:3], in_=st4.ap[:, 1:2]), reads=[st4], writes=[st4])
                P.op("act", lambda e: e.activation(out=qn.ap[:], in_=pp.ap[:, 0:256], func=AF.Copy, scale=st4.ap[:, 2:3]), reads=[pp, st4], writes=[qn])
                P.op("act", lambda e: e.activation(out=jk.ap[:, 0:128], in_=pp.ap[:, 256:384], func=AF.Square, accum_out=st4.ap[:, 4:5]), reads=[pp], writes=[jk, st4])
                P.op("act", lambda e: e.activation(out=st4.ap[:, 5:6], in_=st4.ap[:, 4:5], func=AF.Sqrt, scale=1.0 / 128, bias=EPS), reads=[st4], writes=[st4])
                P.op("dve", lambda e: e.reciprocal(out=st4.ap[:, 6:7], in_=st4.ap[:, 5:6]), reads=[st4], writes=[st4])
                P.op("dve", lambda e, i=i: e.scalar_tensor_tensor(out=ckf[i].ap[:], in0=pp.ap[:, 256:384], scalar=st4.ap[:, 6:7], in1=gkv.ap[:], op0=ALU.mult, op1=ALU.mult),
                     reads=[pp, st4, gkv], writes=[ckf[i]])
                P.op("dve", lambda e, i=i: e.tensor_copy(out=ckb[i].ap[:], in_=ckf[i].ap[:]), reads=[ckf[i]], writes=[ckb[i]])
                rope_apply(pp, pp.ap[:, 384:416].rearrange("p (g d) -> p g d", g=1), krf[i], krf[i].ap[:].rearrange("p (g d) -> p g d", g=1), 1, 16, cos2, sin2, t, tmps)
                P.op("dve", lambda e, i=i: e.tensor_copy(out=krb[i].ap[:], in_=krf[i].ap[:]), reads=[krf[i]], writes=[krb[i]])
                if not is_s:
                    P.dma("sp", O["ckvp"][t * 128:(t + 1) * 128, :], ckf[i].ap[:], reads=[ckf[i]], pwrites=[out_bufs["ckvp"]])
                    P.dma("sp", O["krp"][t * 128:(t + 1) * 128, :], krf[i].ap[:], reads=[krf[i]], pwrites=[out_bufs["krp"]])
                else:
                    P.dma("sp", O["ckvs"], ckf[i].ap[:], reads=[ckf[i]], pwrites=[out_bufs["ckvs"]])
                    P.dma("sp", O["krs"], krf[i].ap[:], reads=[krf[i]], pwrites=[out_bufs["krs"]])
                P.dma("sp", CK_d[t * 128:(t + 1) * 128, :], ckb[i].ap[:], reads=[ckb[i]], pwrites=[CK_b])
                p6 = PS[6]
                p6v = p6.ap[:].bitcast(BF16)
                P.op("pe", lambda e, i=i: e.transpose(out=p6v[:, 0:128], in_=ckb[i].ap[:], identity=ident_b.ap[:]), reads=[ckb[i], ident_b], writes=[p6])
                P.op("pe", lambda e, i=i: e.transpose(out=p6v[0:32, 128:256], in_=krb[i].ap[:], identity=ident_b.ap[:]), reads=[krb[i], ident_b], writes=[p6])
                for kc in range(2):
                    P.op("pe", lambda e, kc=kc: e.transpose(out=p6v[:, 256 + kc * 128:384 + kc * 128], in_=qn.ap[:, kc * 128:(kc + 1) * 128], identity=ident_b.ap[:]), reads=[qn, ident_b], writes=[p6])
                P.op("dve", lambda e, i=i: e.tensor_copy(out=ckT[i].ap[:], in_=p6v[:, 0:128]), reads=[p6], writes=[ckT[i]])
                P.op("dve", lambda e, i=i: e.tensor_copy(out=krT[i].ap[:], in_=p6v[0:32, 128:256]), reads=[p6], writes=[krT[i]])
                P.op("dve", lambda e: e.tensor_copy(out=qnT.ap[:], in_=p6v[:, 256:512].rearrange("p (k t) -> p k t", t=128)), reads=[p6], writes=[qnT])
                P.dma("sp", CKT_d[:, t * 128:(t + 1) * 128], ckT[i].ap[:], reads=[ckT[i]], pwrites=[CKT_b])
                P.dma("sp", KRT_d[:, t * 128:(t + 1) * 128], krT[i].ap[:], reads=[krT[i]], pwrites=[KRT_b])
                for qblk in range(4):
                    pq = PS[2 + qblk % 2]
                    for kc in range(2):
                        P.op("pe", lambda e, pq=pq, kc=kc, qblk=qblk: e.matmul(out=pq.ap[:, 0:384], lhsT=qnT.ap[:, kc, :], rhs=wuq.ap[:, kc, qblk * 384:(qblk + 1) * 384], start=(kc == 0), stop=(kc == 1)),
                             reads=[qnT, wuq], writes=[pq])
                    pqv = pq.ap[:, 0:384].rearrange("p (h d) -> p h d", d=96)
                    hs = slice(qblk * 4, (qblk + 1) * 4)
                    P.op("act", lambda e, pqv=pqv, hs=hs: e.activation(out=qb.ap[:, hs, 0:64], in_=pqv[:, :, 0:64], func=AF.Copy), reads=[pq], pwrites=[qb])
                    rope_apply(pq, pqv[:, :, 64:96], qb, qb.ap[:, hs, 64:96], 4, 16, cos2, sin2, t, tmps)
                for g4 in range(2):
                    pb = PS[4 + g4]
                    pv = pb.ap[:].bitcast(BF16).rearrange("p (h t) -> p h t", t=128)
                    for u in range(8):
                        h = g4 * 8 + u
                        P.op("pe", lambda e, pv=pv, u=u, h=h: e.transpose(out=pv[0:96, u, :], in_=qb.ap[:, h, :], identity=ident_b.ap[:]), reads=[qb, ident_b], writes=[pb])
                    if g4 == 0:
                        P.op("act", lambda e, pv=pv, g4=g4: e.activation(out=qT.ap[:, g4 * 8:(g4 + 1) * 8, :], in_=pv[0:96, :, :], func=AF.Copy), reads=[pb], pwrites=[qT])
                    else:
                        P.op("dve", lambda e, pv=pv, g4=g4: e.tensor_copy(out=qT.ap[:, g4 * 8:(g4 + 1) * 8, :], in_=pv[0:96, :, :]), reads=[pb], pwrites=[qT])
                P.dma("sp", QR_d[:, :, t * 128:(t + 1) * 128].rearrange("h p t -> p h t"), qT.ap[64:96, :, :], reads=[qT], pwrites=[QR_b])
                qa_ = qa[i]
                for g4 in range(4):
                    pa = PS[2 + g4 % 2]
                    for u in range(4):
                        h = g4 * 4 + u
                        P.op("pe", lambda e, pa=pa, u=u, h=h: e.matmul(out=pa.ap[:, u * 128:(u + 1) * 128], lhsT=wukT.ap[:, h, :], rhs=qT.ap[0:64, h, :], start=True, stop=True), reads=[wukT, qT], writes=[pa])
                    if g4 % 2 == 0:
                        P.op("act", lambda e, pa=pa, g4=g4, qa_=qa_: e.activation(out=qa_.ap[:, g4 * 4:(g4 + 1) * 4, :], in_=pa.ap[:].rearrange("p (u t) -> p u t", t=128), func=AF.Copy), reads=[pa], pwrites=[qa_])
                    else:
                        P.op("dve", lambda e, pa=pa, g4=g4, qa_=qa_: e.tensor_copy(out=qa_.ap[:, g4 * 4:(g4 + 1) * 4, :], in_=pa.ap[:].rearrange("p (u t) -> p u t", t=128)), reads=[pa], pwrites=[qa_])
                P.dma("sp", QA_d[:, :, t * 128:(t + 1) * 128].rearrange("h p t -> p h t"), qa_.ap[:], reads=[qa_], pwrites=[QA_b])

        if en(9):
          with Stage(P, "l2b") as st:
            wuv = st.sb([128, D], BF16, "wuv")
            P.dma("pool", wuv.ap[:], W["mla_w_uv"], writes=[wuv])
            CKT = st.sb([128, SEQ], BF16, "CKT")
            KRT = st.sb([128, SEQ], BF16, "KRT")
            CK = st.sb([128, NPT, 128], BF16, "CK")
            P.op("pool", lambda e: e.memset(KRT.ap[:], 0.0), writes=[KRT])
            for c4 in range(4):
                cs = slice(c4 * 2048, (c4 + 1) * 2048)
                P.dma("sp", CKT.ap[:, cs], CKT_d[:, cs], reads=[CKT_b], pwrites=[CKT])
                P.dma("sp", KRT.ap[0:32, cs], KRT_d[:, cs], reads=[KRT_b], pwrites=[KRT])
                P.dma("sp", CK.ap[:, c4 * 16:(c4 + 1) * 16, :], CK_d[c4 * 2048:(c4 + 1) * 2048, :].rearrange("(b p) e -> p b e", p=128), reads=[CK_b], pwrites=[CK])
            QA = [st.sb([128, SEQ], BF16, "QA") for _ in range(2)]
            QR = [st.sb([128, SEQ], BF16, "QR") for _ in range(2)]
            for b in QR:
                P.op("pool", lambda e, b=b: e.memset(b.ap[:], 0.0), writes=[b])
            PT = [st.sb([128, 512], BF16, "PT") for _ in range(4)]
            Rb = st.sb([128, 512], F32, "Rb")
            ol = st.sb([128, 512], BF16, "ol")
            ohb = [st.sb([64, 512], BF16, "ohb") for _ in range(2)]
            Sb, Obs, Lb, Eb = [PS[0], PS[1], PS[5], PS[6]], [PS[2], PS[3]], PS[7], PS[4]
            accs = [[st.sb([128, 512], F32, "acc") for _ in range(2)] for _ in range(2)]
            accb = st.sb([128, 512], BF16, "accb")
            aeng = ["dve", "dve"]
            nheads = 16 if upto >= 9.5 else 1
            nQ = 16 if upto >= 9.5 else 2
            for h in range(nheads):
                qa_, qr_ = QA[h % 2], QR[h % 2]
                for c4 in range(4):
                    cs = slice(c4 * 2048, (c4 + 1) * 2048)
                    P.dma("sp", qa_.ap[:, cs], QA_d[h, :, cs], reads=[QA_b], pwrites=[qa_])
                    P.dma("sp", qr_.ap[0:32, cs], QR_d[h, :, cs], reads=[QR_b], pwrites=[qr_])
                for Q in range(nQ):
                    blocks = [(kb, 0, False) for kb in range(4 * Q)] + [(4 * Q + i_, 128 * i_, True) for i_ in range(4)]
                    n = len(blocks)
                    qp = Q % 2
                    Ob = Obs[qp]
                    for k_ in range(2):
                        P.op(aeng[k_], lambda e, k_=k_, qp=qp: e.memset(accs[k_][qp].ap[:], 0.0), writes=[accs[k_][qp]])

                    def issue_S(idx, Q=Q, qa_=qa_, qr_=qr_, blocks=blocks):
                        kb, col0, diag = blocks[idx]
                        sb_ = Sb[idx % 4]
                        P.op("pe", lambda e, sb_=sb_, kb=kb, col0=col0: e.matmul(out=sb_.ap[:, col0:512], lhsT=CKT.ap[:, kb * 128:(kb + 1) * 128], rhs=qa_.ap[:, Q * 512 + col0:(Q + 1) * 512], start=True, stop=False),
                             reads=[CKT, qa_], writes=[sb_])
                        P.op("pe", lambda e, sb_=sb_, kb=kb, col0=col0: e.matmul(out=sb_.ap[:, col0:512], lhsT=KRT.ap[:, kb * 128:(kb + 1) * 128], rhs=qr_.ap[:, Q * 512 + col0:(Q + 1) * 512], start=False, stop=True),
                             reads=[KRT, qr_], writes=[sb_])

                    issue_S(0)
                    if n > 1:
                        issue_S(1)
                    for idx in range(n):
                        kb, col0, diag = blocks[idx]
                        sb_, pt = Sb[idx % 4], PT[idx % 4]
                        if not diag:
                            P.op("act", lambda e, sb_=sb_, pt=pt: e.activation(out=pt.ap[:], in_=sb_.ap[:], func=AF.Exp, scale=MSC), reads=[sb_], writes=[pt])
                        else:
                            P.op("act", lambda e, sb_=sb_, pt=pt, col0=col0: e.activation(out=pt.ap[0:64, col0:512], in_=sb_.ap[0:64, col0:512], func=AF.Exp, scale=MSC), reads=[sb_], writes=[pt])
                            P.op("act", lambda e, sb_=sb_, pt=pt, col0=col0: e.activation(out=pt.ap[64:128, col0:col0 + 64], in_=sb_.ap[64:128, col0:col0 + 64], func=AF.Copy, scale=0.0), reads=[sb_], pwrites=[pt])
                            P.op("act", lambda e, sb_=sb_, pt=pt, col0=col0: e.activation(out=pt.ap[64:128, col0 + 64:512], in_=sb_.ap[64:128, col0 + 64:512], func=AF.Exp, scale=MSC), reads=[sb_], pwrites=[pt])
                        if idx + 2 < n:
                            issue_S(idx + 2)
                        P.op("pe", lambda e, Ob=Ob, pt=pt, kb=kb, col0=col0, idx=idx, diag=diag: e.matmul(out=Ob.ap[:, col0:512], lhsT=CK.ap[:, kb, :], rhs=pt.ap[:, col0:512], start=(idx == 0), stop=diag, skip_group_check=True),
                             reads=[CK, pt], writes=[Ob])
                        ac = accs[idx % 2][qp]
                        P.op(aeng[idx % 2], lambda e, ac=ac, pt=pt, col0=col0: e.tensor_tensor(out=ac.ap[:, col0:512], in0=ac.ap[:, col0:512], in1=pt.ap[:, col0:512], op=ALU.add),
                             reads=[ac, pt], writes=[ac])
                    a0, a1 = accs[0][qp], accs[1][qp]
                    P.op("dve", lambda e, a0=a0, a1=a1: e.tensor_tensor(out=accb.ap[:], in0=a0.ap[:], in1=a1.ap[:], op=ALU.add), reads=[a0, a1], writes=[accb])
                    P.op("pe", lambda e: e.matmul(out=Lb.ap[:], lhsT=ones_b.ap[:], rhs=accb.ap[:], start=True, stop=True), reads=[ones_b, accb], writes=[Lb])
                    P.op("dve", lambda e: e.reciprocal(out=Rb.ap[:], in_=Lb.ap[:]), reads=[Lb], writes=[Rb])
                    P.op("dve", lambda e, Ob=Ob: e.tensor_tensor(out=ol.ap[:], in0=Ob.ap[:], in1=Rb.ap[:], op=ALU.mult), reads=[Ob, Rb], writes=[ol])
                    P.op("pe", lambda e, h=h: e.matmul(out=Eb.ap[0:64, :], lhsT=wuv.ap[:, h * 64:(h + 1) * 64], rhs=ol.ap[:], start=True, stop=True), reads=[wuv, ol], writes=[Eb])
                    oh = ohb[Q % 2]
                    P.op("act", lambda e, oh=oh: e.activation(out=oh.ap[:], in_=Eb.ap[0:64, :], func=AF.Copy), reads=[Eb], writes=[oh])
                    P.dma("sp", OT2_d[h // 2, (h % 2) * 64:(h % 2) * 64 + 64, Q * 512:(Q + 1) * 512], oh.ap[:], reads=[oh], pwrites=[OT2_b])

          with Stage(P, "l2s") as st:
            wuv = st.sb([128, D], BF16, "wuv")
            P.dma("pool", wuv.ap[:], W["mla_w_uv"], writes=[wuv])
            zer = st.sb([128, 512], BF16, "zer")
            P.op("pool", lambda e: e.memset(zer.ap[:], 0.0), writes=[zer])
            ckr = [st.sb([128, 128], BF16, "ckr") for _ in range(2)]
            krr = [st.sb([128, 32], BF16, "krr") for _ in range(2)]
            ckT = [st.sb([128, 128], BF16, "ckT") for _ in range(2)]
            krT = [st.sb([128, 128], BF16, "krT") for _ in range(2)]
            for b in krT:
                P.op("pool", lambda e, b=b: e.memset(b.ap[:], 0.0), writes=[b])
            QAs = st.sb([128, 16, 64], BF16, "QAs")
            QRs = st.sb([128, 16, 64], BF16, "QRs")
            P.op("pool", lambda e: e.memset(QRs.ap[:], 0.0), writes=[QRs])
            PTs = [st.sb([128, 1024], BF16, "PTs") for _ in range(2)]
            Rr = st.sb([128, 1024], F32, "Rr")
            olb = st.sb([128, 1024], BF16, "olb")
            ohs = st.sb([64, 16, 64], BF16, "ohs")
            Sbk, Obk, Lbk, Tbk, Ebk = [PS[0], PS[1]], [PS[2], PS[3]], [PS[4], PS[5]], PS[6], PS[7]
            NKB = PAST // 128
            for s in range(2):
                tok0 = NPT * 128 + s * 64
                P.dma("sp", QAs.ap[:], QA_d[:, :, tok0:tok0 + 64].rearrange("h p t -> p h t"), reads=[QA_b], writes=[QAs])
                P.dma("sp", QRs.ap[0:32, :, :], QR_d[:, :, tok0:tok0 + 64].rearrange("h p t -> p h t"), reads=[QR_b], pwrites=[QRs])
                for b in Obk + Lbk:
                    P.op("pe", lambda e, b=b: e.matmul(out=b.ap[:], lhsT=zer.ap[:, 0:128], rhs=zer.ap[:], start=True, stop=False, skip_group_check=True), reads=[zer], writes=[b])
                for kb in range(NKB + 1):
                    i = kb % 2
                    last = (kb == NKB)
                    nk = 64 if last else 128
                    if not last:
                        P.dma("pool", ckr[i].ap[:], I["cck"][s, kb * 128:(kb + 1) * 128, :], writes=[ckr[i]])
                        P.dma("pool", krr[i].ap[:], I["ckr"][s, kb * 128:(kb + 1) * 128, :], writes=[krr[i]])
                        tv = Tbk.ap[:].bitcast(BF16)
                        P.op("pe", lambda e, tv=tv, i=i: e.transpose(out=tv[:, 0:128], in_=ckr[i].ap[:], identity=ident_b.ap[:]), reads=[ckr[i], ident_b], writes=[Tbk])
                        P.op("pe", lambda e, tv=tv, i=i: e.transpose(out=tv[0:32, 128:256], in_=krr[i].ap[:], identity=ident_b.ap[:]), reads=[krr[i], ident_b], writes=[Tbk])
                        P.op("dve", lambda e, tv=tv, i=i: e.tensor_copy(out=ckT[i].ap[:], in_=tv[:, 0:128]), reads=[Tbk], writes=[ckT[i]])
                        P.op("dve", lambda e, tv=tv, i=i: e.tensor_copy(out=krT[i].ap[0:32, :], in_=tv[0:32, 128:256]), reads=[Tbk], pwrites=[krT[i]])
                    else:
                        P.dma("pool", ckr[i].ap[0:64, :], CK_d[tok0:tok0 + 64, :], reads=[CK_b], writes=[ckr[i]])
                        P.dma("sp", ckT[i].ap[:, 0:64], CKT_d[:, tok0:tok0 + 64], reads=[CKT_b], writes=[ckT[i]])
                        P.dma("sp", krT[i].ap[0:32, 0:64], KRT_d[:, tok0:tok0 + 64], reads=[KRT_b], pwrites=[krT[i]])
                    for h in range(16):
                        sb_ = Sbk[h // 8]
                        cs = slice((h % 8) * 64, (h % 8) * 64 + 64)
                        P.op("pe", lambda e, sb_=sb_, cs=cs, h=h, nk=nk, i=i: e.matmul(out=sb_.ap[0:nk, cs], lhsT=ckT[i].ap[:, 0:nk], rhs=QAs.ap[:, h, :], start=True, stop=False), reads=[ckT[i], QAs], writes=[sb_])
                        P.op("pe", lambda e, sb_=sb_, cs=cs, h=h, nk=nk, i=i: e.matmul(out=sb_.ap[0:nk, cs], lhsT=krT[i].ap[:, 0:nk], rhs=QRs.ap[:, h, :], start=False, stop=True), reads=[krT[i], QRs], writes=[sb_])
                    pts = PTs[i]
                    for j in range(2):
                        P.op("act", lambda e, j=j, nk=nk, pts=pts: e.activation(out=pts.ap[0:nk, j * 512:(j + 1) * 512], in_=Sbk[j].ap[0:nk, :], func=AF.Exp, scale=MSC), reads=[Sbk[j]], pwrites=[pts])
                    for j in range(2):
                        js = slice(j * 512, (j + 1) * 512)
                        P.op("pe", lambda e, j=j, js=js, nk=nk, i=i, pts=pts: e.matmul(out=Obk[j].ap[:], lhsT=ckr[i].ap[0:nk, :], rhs=pts.ap[0:nk, js], start=False, stop=True, skip_group_check=True),
                             reads=[ckr[i], pts], writes=[Obk[j]])
                        P.op("pe", lambda e, j=j, js=js, nk=nk, pts=pts: e.matmul(out=Lbk[j].ap[:], lhsT=ones_b.ap[0:nk, :], rhs=pts.ap[0:nk, js], start=False, stop=True, skip_group_check=True),
                             reads=[ones_b, pts], writes=[Lbk[j]])
                for j in range(2):
                    js = slice(j * 512, (j + 1) * 512)
                    P.op("dve", lambda e, j=j, js=js: e.reciprocal(out=Rr.ap[:, js], in_=Lbk[j].ap[:]), reads=[Lbk[j]], pwrites=[Rr])
                    P.op("dve", lambda e, j=j, js=js: e.tensor_tensor(out=olb.ap[:, js], in0=Obk[j].ap[:], in1=Rr.ap[:, js], op=ALU.mult), reads=[Obk[j], Rr], pwrites=[olb])
                for g2 in range(2):
                    for u in range(8):
                        h = g2 * 8 + u
                        P.op("pe", lambda e, h=h, u=u: e.matmul(out=Ebk.ap[0:64, u * 64:(u + 1) * 64], lhsT=wuv.ap[:, h * 64:(h + 1) * 64], rhs=olb.ap[:, h * 64:(h + 1) * 64], start=True, stop=True), reads=[wuv, olb], writes=[Ebk])
                    P.op("act", lambda e, g2=g2: e.activation(out=ohs.ap[:, g2 * 8:(g2 + 1) * 8, :], in_=Ebk.ap[0:64, :].rearrange("p (u q) -> p u q", q=64), func=AF.Copy), reads=[Ebk], pwrites=[ohs])
                for h in range(16):
                    P.dma("sp", OT2_d[h // 2, (h % 2) * 64:(h % 2) * 64 + 64, tok0:tok0 + 64], ohs.ap[:, h, :], reads=[ohs], pwrites=[OT2_b])

        if en(10):
            attn_out_stage("l2c", 2, OT2_d, OT2_b, "mla_w_o", 16, 64, False, None, "b", "a", lambda t: True)
        if en(11):
            mlp_stage(2, "a", "b")

        if en(12):
          with Stage(P, "l3") as st:
            win = st.sb([128, 8, 4 * D], BF16, "win")
            for kc in range(8):
                for c0 in range(0, 4 * D, 1024):
                    P.dma("pool", win.ap[:, kc, c0:c0 + 1024], W["sgu_w_in"][kc * 128:(kc + 1) * 128, c0:c0 + 1024], pwrites=[win])
            wout = st.sb([128, 16, D], BF16, "wout")
            for kc in range(16):
                P.dma("pool", wout.ap[:, kc, :], W["sgu_w_out"][kc * 128:(kc + 1) * 128, :], pwrites=[wout])
            gvb = st.sb([128, 2 * D], F32, "gvb")
            P.dma("sp", gvb.ap[:], W["sgu_g_v"].partition_broadcast(128), writes=[gvb])
            Bs = [st.sb([128, 8, 128], F32, "Bs") for _ in range(2)]
            P.dma("sp", Bs[0].ap[:], W["sgu_b_s"].partition_broadcast(128), writes=[Bs[0]])
            for hh in range(2):
                P.dma("sp", Bs[1].ap[:, :, hh * 64:(hh + 1) * 64], W["sgu_b_s"][:, 0:64].partition_broadcast(128), pwrites=[Bs[1]])
            WgT = [st.sb([128, 8, 128], BF16, "WgT") for _ in range(2)]
            with Stage(P, "l3p") as sp:
                stg = [sp.sb([128, 128], F32, "stg") for _ in range(2)]
                k_ = 0
                for ps_i in range(2):
                    for g in range(8):
                        sg = stg[k_ % 2]
                        if ps_i == 0:
                            P.dma("sp", sg.ap[:], W["sgu_w_s"][g], writes=[sg])
                        else:
                            P.op("pool", lambda e, sg=sg: e.memset(sg.ap[:], 0.0), writes=[sg])
                            for hh in range(2):
                                P.dma("sp", sg.ap[hh * 64:(hh + 1) * 64, hh * 64:(hh + 1) * 64], W["sgu_w_s"][g, 0:64, 0:64], pwrites=[sg])
                        P.op("pool", lambda e, sg=sg: e.affine_select(out=sg.ap[:], in_=sg.ap[:], pattern=[[-1, 128]], compare_op=ALU.is_ge, fill=0.0, base=0, channel_multiplier=1),
                             reads=[sg], writes=[sg])
                        pb = PS[1 + k_ % 2]
                        P.op("pe", lambda e, sg=sg, pb=pb: e.transpose(out=pb.ap[:, 0:128], in_=sg.ap[:], identity=ident_f.ap[:]), reads=[sg, ident_f], writes=[pb])
                        P.op("dve", lambda e, pb=pb, ps_i=ps_i, g=g: e.tensor_copy(out=WgT[ps_i].ap[:, g, :], in_=pb.ap[:, 0:128]), reads=[pb], pwrites=[WgT[ps_i]])
                        k_ += 1
            gate = Gate(st, 3, 0)
            xt = [st.sb([128, D], F32, "x") for _ in range(2)]
            nctx = NormCtx(st, PS[0], PS[7])
            hTs = [st.sb([128, 8, 128], BF16, "hT") for _ in range(2)]
            uT = [st.sb([128, 16, 128], BF16, "uT") for _ in range(2)]
            vg = st.sb([128, 2 * D], F32, "vg")
            vn = st.sb([128, 2 * D], F32, "vn")
            vnb = st.sb([128, 2 * D], BF16, "vnb")
            jk2 = st.sb([128, 2 * D], BF16, "jk2")
            st5 = st.sb([128, 4], F32, "st5")
            svt = [st.sb([128, 512], F32, "svt") for _ in range(2)]
            pT = [st.sb([128, 16, 128], BF16, "pT") for _ in range(2)]
            tl3 = list(range(NT)) if upto >= 12.5 else [0, NPT]
            for t in tl3:
                is_s = (t == NPT)
                i = t % 2
                x, hT = xt[i], hTs[i]
                src, srcb = x_src(t, "b")
                P.dma("sp", x.ap[:], src, reads=[srcb], writes=[x])
                nctx.run(x, x.ap[:], hT, hT.ap[:], 3, 0, is_s)
                for q4 in range(4):
                    pu = PS[1 + q4 % 2]
                    for u in range(4):
                        oc = q4 * 4 + u
                        for kc in range(8):
                            P.op("pe", lambda e, pu=pu, u=u, oc=oc, kc=kc, hT=hT: e.matmul(out=pu.ap[:, u * 128:(u + 1) * 128], lhsT=win.ap[:, kc, oc * 128:(oc + 1) * 128], rhs=hT.ap[:, kc, :], start=(kc == 0), stop=(kc == 7)),
                                 reads=[win, hT], writes=[pu])
                    P.op("act", lambda e, pu=pu, q4=q4, i=i: e.activation(out=uT[i].ap[:, q4 * 4:(q4 + 1) * 4, :], in_=pu.ap[:].rearrange("p (u t) -> p u t", t=128), func=AF.Gelu_apprx_tanh), reads=[pu], pwrites=[uT[i]])
                for blk in range(4):
                    pv_ = PS[3 + blk % 2]
                    for kc in range(8):
                        P.op("pe", lambda e, pv_=pv_, blk=blk, kc=kc, hT=hT: e.matmul(out=pv_.ap[:], lhsT=hT.ap[:, kc, :], rhs=win.ap[:, kc, 2048 + blk * 512:2048 + (blk + 1) * 512], start=(kc == 0), stop=(kc == 7)),
                             reads=[hT, win], writes=[pv_])
                    P.op("act", lambda e, pv_=pv_, blk=blk: e.activation(out=vg.ap[:, blk * 512:(blk + 1) * 512], in_=pv_.ap[:], func=AF.Gelu_apprx_tanh), reads=[pv_], pwrites=[vg])
                P.op("act", lambda e: e.activation(out=jk2.ap[:], in_=vg.ap[:], func=AF.Square, accum_out=st5.ap[:, 0:1]), reads=[vg], writes=[jk2, st5])
                P.op("act", lambda e: e.activation(out=st5.ap[:, 1:2], in_=st5.ap[:, 0:1], func=AF.Sqrt, scale=1.0 / (2 * D), bias=EPS), reads=[st5], writes=[st5])
                P.op("dve", lambda e: e.reciprocal(out=st5.ap[:, 2:3], in_=st5.ap[:, 1:2]), reads=[st5], writes=[st5])
                P.op("dve", lambda e: e.scalar_tensor_tensor(out=vn.ap[:], in0=vg.ap[:], scalar=st5.ap[:, 2:3], in1=gvb.ap[:], op0=ALU.mult, op1=ALU.mult), reads=[vg, st5, gvb], writes=[vn])
                P.op("pool", lambda e: e.tensor_copy(out=vnb.ap[:], in_=vn.ap[:]), reads=[vn], writes=[vnb])
                if is_s:
                    P.dma("sp", O["sguv"], vn.ap[:], reads=[vn], pwrites=[out_bufs["sguv"]])
                wg, bs_ = WgT[1 if is_s else 0], Bs[1 if is_s else 0]
                p_ = pT[i]
                for q4 in range(4):
                    psv = PS[5 + q4 % 2]
                    for u in range(4):
                        fc = q4 * 4 + u
                        P.op("pe", lambda e, psv=psv, u=u, fc=fc, wg=wg: e.matmul(out=psv.ap[:, u * 128:(u + 1) * 128], lhsT=vnb.ap[:, fc * 128:(fc + 1) * 128], rhs=wg.ap[:, fc // 2, :], start=True, stop=True),
                             reads=[vnb, wg], writes=[psv])
                    sv_ = svt[q4 % 2]
                    P.op("dve", lambda e, psv=psv, sv_=sv_, q4=q4, bs_=bs_: e.tensor_tensor(out=sv_.ap[:].rearrange("p (g u t) -> p g u t", g=2, u=2), in0=psv.ap[:].rearrange("p (g u t) -> p g u t", g=2, u=2),
                                                                                  in1=bc(bs_.ap[:, q4 * 2:(q4 + 1) * 2, :].unsqueeze(2), [128, 2, 2, 128]), op=ALU.add), reads=[psv, bs_], writes=[sv_])
                    P.op("pool", lambda e, sv_=sv_, q4=q4, i=i, p_=p_: e.tensor_tensor(out=p_.ap[:, q4 * 4:(q4 + 1) * 4, :].rearrange("p u t -> p (u t)"), in0=sv_.ap[:], in1=uT[i].ap[:, q4 * 4:(q4 + 1) * 4, :].rearrange("p u t -> p (u t)"), op=ALU.mult),
                         reads=[sv_, uT[i]], pwrites=[p_])
                g = gate.get(is_s)
                for cbk in range(2):
                    po = PS[1 + cbk]
                    for kc in range(16):
                        P.op("pe", lambda e, po=po, kc=kc, cbk=cbk, p_=p_: e.matmul(out=po.ap[:], lhsT=p_.ap[:, kc, :], rhs=wout.ap[:, kc, cbk * 512:(cbk + 1) * 512], start=(kc == 0), stop=(kc == 15)),
                             reads=[p_, wout], writes=[po])
                    sl = slice(cbk * 512, (cbk + 1) * 512)
                    P.op("dve", lambda e, po=po, sl=sl, g=g: e.tensor_tensor(out=po.ap[:], in0=po.ap[:], in1=g.ap[:, sl], op=ALU.mult), reads=[po, g], writes=[po])
                    P.op("dve", lambda e, po=po, sl=sl, x=x: e.tensor_tensor(out=x.ap[:, sl], in0=po.ap[:], in1=x.ap[:, sl], op=ALU.add), reads=[po, x], pwrites=[x])
                dst, dstb = x_dst(t, "a")
                P.dma("sp", dst, x.ap[:], reads=[x], pwrites=[dstb])

        if en(13):
            mlp_stage(3, "a", None, final=True)

        P.barrier()
        P.emit()
        print("n_inst", P.n_inst, "n_dsem", P.n_dsem)
    return nc


_NC_CACHE = {}


def kernel(**inp):
    f = lambda a: np.ascontiguousarray(np.asarray(a, dtype=np.float32))
    upto = float(inp.get("_upto", 99))
    if upto not in _NC_CACHE:
        _NC_CACHE[upto] = build_program(upto)
    nc = _NC_CACHE[upto]
    wnames = ["w_ada", "b_ada", "g_mix", "g_ffn", "w_up", "w_down", "g_final", "s5_a_re", "s5_a_im", "s5_b_re", "s5_b_im",
              "s5_c_re", "s5_c_im", "s5_d", "s5_log_dt", "s5_w_glu_a", "s5_w_glu_b", "diff_w_qkv", "diff_lambda_q1",
              "diff_lambda_k1", "diff_lambda_q2", "diff_lambda_k2", "diff_g_sub", "diff_w_o", "mla_w_dq", "mla_g_q",
              "mla_w_uq", "mla_w_dkv", "mla_g_kv", "mla_w_uk", "mla_w_uv", "mla_w_o", "sgu_w_in", "sgu_g_v", "sgu_w_s",
              "sgu_b_s", "sgu_w_out"]
    wd = {k: f(inp[k]) for k in wnames}
    wd["s5_a_re"] = wd["s5_a_re"].reshape(4096)
    wd["s5_a_im"] = wd["s5_a_im"].reshape(4096)
    wd["mla_w_uk"] = wd["mla_w_uk"].reshape(128, 1024)
    wd["mla_w_uv"] = wd["mla_w_uv"].reshape(128, 1024)
    xp, xs = f(inp["x_prompt"]), f(inp["x_sample"])
    cp, cs = f(inp["c_prompt"]), f(inp["c_sample"])
    sre, sim = f(inp["state_s5_re"]), f(inp["state_s5_im"])
    cdk, cdv = f(inp["cache_diff_k"]), f(inp["cache_diff_v"])
    cck, ckr = f(inp["cache_mla_ckv"]), f(inp["cache_mla_krope"])
    in_maps = []
    for c in range(8):
        b = c % 4
        s0 = 2 * c
        m = dict(wd)
        m["xp"] = xp[b]
        m["xs"] = xs[s0:s0 + 2].reshape(128, D)
        m["c3"] = np.concatenate([cp[b:b + 1], cs[s0:s0 + 2]], axis=0)
        m["h0re"] = sre[s0:s0 + 2].reshape(2, 4096)
        m["h0im"] = sim[s0:s0 + 2].reshape(2, 4096)
        m["cdk"] = cdk[s0:s0 + 2].reshape(2, PAST, D)
        m["cdv"] = cdv[s0:s0 + 2].reshape(2, PAST, D)
        m["cck"] = cck[s0:s0 + 2]
        m["ckr"] = ckr[s0:s0 + 2]
        in_maps.append(m)
    ncores = int(inp.get("_ncores", 8))
    if ncores < 8:
        res = run_bass_kernel_spmd(nc, in_maps[:ncores], core_ids=list(range(ncores))).results
        res = [res[c % ncores] for c in range(8)]
    else:
        res = run_bass_kernel_spmd(nc, in_maps, core_ids=list(range(8))).results
    global _LAST_RES
    _LAST_RES = res
    R = lambda k, cores: [res[c][k] for c in cores]
    p4, a8 = range(4), range(8)
    y_prompt = np.stack(R("yp", p4)).reshape(4, SEQ, D)
    y_sample = np.concatenate(R("ys", a8)).reshape(16, 64, D)
    s5p_re = np.stack(R("s5p_re", p4)).reshape(4, 64, 64)
    s5p_im = np.stack(R("s5p_im", p4)).reshape(4, 64, 64)
    s5s_re = np.concatenate(R("s5s_re", a8)).reshape(16, 64, 64)
    s5s_im = np.concatenate(R("s5s_im", a8)).reshape(16, 64, 64)
    dkp = np.stack(R("dkp", p4)).reshape(4, SEQ, 8, 128)
    dvp = np.stack(R("dvp", p4)).reshape(4, SEQ, 8, 128)
    dks = np.concatenate(R("dks", a8)).reshape(16, 64, 8, 128)
    dvs = np.concatenate(R("dvs", a8)).reshape(16, 64, 8, 128)
    ckvp = np.stack(R("ckvp", p4)).reshape(4, SEQ, 128)
    krp = np.stack(R("krp", p4)).reshape(4, SEQ, 32)
    ckvs = np.concatenate(R("ckvs", a8)).reshape(16, 64, 128)
    krs = np.concatenate(R("krs", a8)).reshape(16, 64, 32)
    sguv = np.concatenate(R("sguv", a8)).reshape(16, 64, 2 * D)
    return (y_prompt, y_sample, s5p_re, s5p_im, s5s_re, s5s_im, dkp, dvp, dks, dvs, ckvp, krp, ckvs, krs, sguv)
```

```python
import math
from contextlib import ExitStack

import numpy as np
import concourse.bass as bass
import concourse.mybir as mybir
from concourse.bass_utils import run_bass_kernel_spmd

F32 = mybir.dt.float32
BF16 = mybir.dt.bfloat16
I32 = mybir.dt.int32
AF = mybir.ActivationFunctionType
ALU = mybir.AluOpType

D = 1024
SEQ = 8192
NPT = SEQ // 128
NT = NPT + 1
NTOK = NT * 128
PAST = 4096
EPS = 1e-6
LAMBDA_INIT = 0.8 - 0.6 * math.exp(-0.3 * 1)
TWO_PI = 2.0 * math.pi
CW1 = 6.28125
CW2 = float(np.float32(TWO_PI - CW1))
CW3 = float(TWO_PI - CW1 - CW2)


class Buf:
    def __init__(self, ap, name=""):
        self.ap = ap
        self.name = name
        self.w = []
        self.r = {}
        self.pr = {}
        self.is_sb = False
        self.is_psum = False
        self.dsem = None


class Prog:
    ENGS = ["pe", "act", "dve", "pool", "sp"]

    def __init__(self, nc, es):
        self.nc = nc
        self.es = es
        self.sem = {e: es.enter_context(nc.semaphore("s_" + e)) for e in self.ENGS}
        self.cnt = {e: 0 for e in self.ENGS}
        self.waited = {e: {} for e in self.ENGS}
        self.stream = {e: [] for e in self.ENGS}
        self.dsems = []
        self.n_inst = 0
        self.n_dsem = 0
        self.free_dsems = []

    def take_dsem(self, key, stage, sw=False):
        if sw:
            return self.mk_dsem(key)
        if self.free_dsems:
            d = self.free_dsems.pop()
        else:
            d = self.mk_dsem(key)
        if stage is not None:
            stage.dsems.append(d)
        return d

    def no_dsem(self, key=None):
        return None

    def mk_dsem(self, key):
        self.n_dsem += 1
        s = self.es.enter_context(self.nc.semaphore("d_%s_%d" % (key, self.n_dsem)))
        d = {"sem": s, "cnt": 0, "key": "d_%s_%d" % (key, self.n_dsem)}
        self.dsems.append(d)
        return d

    def _need(self, e, reads, writes, pwrites):
        need = {}

        def add(dep):
            key, val, semobj = dep
            if key == e and e == "pe":
                return
            cur = need.get(key)
            if cur is None or cur[0] < val:
                need[key] = (val, semobj)

        for b in reads:
            for d, _ in b.w:
                add(d)
            if b.is_psum:
                for k, d in b.r.items():
                    if k != e:
                        add(d)
        for b in writes:
            for d, _ in b.w:
                add(d)
            for d in b.r.values():
                add(d)
        for b in pwrites:
            for d, part in b.w:
                if not part:
                    add(d)
            for d in b.r.values():
                add(d)
            for d in b.pr.values():
                add(d)
        out = []
        for key, (val, semobj) in need.items():
            if self.waited[e].get(key, 0) >= val:
                continue
            self.waited[e][key] = val
            out.append((semobj, val))
        return out

    def _commit(self, key, dep, reads, writes, pwrites):
        for b in reads:
            b.r[key] = dep
        for b in writes:
            b.w = [(dep, False)]
            b.r = {}
            b.pr = {}
        for b in pwrites:
            if b.r:
                b.pr = b.r
                b.w = [(dep, True)]
                b.r = {}
            else:
                b.w.append((dep, True))

    def op(self, e, fn, reads=(), writes=(), pwrites=()):
        waits = self._need(e, reads, writes, pwrites)
        self.cnt[e] += 1
        dep = (e, self.cnt[e], self.sem[e])
        self.stream[e].append((waits, fn, (self.sem[e], 1)))
        self._commit(e, dep, reads, writes, pwrites)
        self.n_inst += 1

    def dma(self, q, out_ap, in_ap, reads=(), writes=(), pwrites=(), sem=None, **kw):
        sbb = [b for b in list(writes) + list(pwrites) + list(reads) if b.is_sb]
        assert sbb, "dma without sbuf side"
        if sbb[0].dsem is None:
            sbb[0].dsem = self.take_dsem(sbb[0].name, getattr(sbb[0], "stage", None), sw=(q == "pool"))
            sbb[0].dq = q
        assert sbb[0].dq == q or (sbb[0].dq != "pool" and q != "pool"), "buffer mixes SW and HW DMA queues"
        sem = sbb[0].dsem
        waits = self._need(q, reads, writes, pwrites)
        sem["cnt"] += 16
        dep = (sem["key"], sem["cnt"], sem["sem"])

        def fn(eng, out_ap=out_ap, in_ap=in_ap, kw=kw):
            return eng.dma_start(out=out_ap, in_=in_ap, **kw)

        self.stream[q].append((waits, fn, (sem["sem"], 16)))
        self._commit(sem["key"], dep, reads, writes, pwrites)
        self.n_inst += 1

    def barrier(self):
        for e in self.ENGS:
            waits = []
            for x in self.ENGS:
                if x == e or self.cnt[x] == 0:
                    continue
                if self.waited[e].get(x, 0) < self.cnt[x]:
                    self.waited[e][x] = self.cnt[x]
                    waits.append((self.sem[x], self.cnt[x]))
            for d in self.dsems:
                if d["cnt"] and self.waited[e].get(d["key"], 0) < d["cnt"]:
                    self.waited[e][d["key"]] = d["cnt"]
                    waits.append((d["sem"], d["cnt"]))
            self.stream[e].append((waits, None, None))

    def emit(self):
        nc = self.nc
        streams = self.stream
        self.stream = {e: [] for e in self.ENGS}
        with nc.Block() as block:
            def mk(e):
                def body(eng):
                    for waits, fn, inc in streams[e]:
                        for s, v in waits:
                            eng.wait_ge(s, v)
                        if fn is not None:
                            fn(eng).then_inc(inc[0], inc[1])
                return body
            block.tensor(mk("pe"))
            block.scalar(mk("act"))
            block.vector(mk("dve"))
            block.gpsimd(mk("pool"))
            block.sync(mk("sp"))


class Stage:
    def __init__(self, P, name):
        self.P = P
        Stage.CNT = getattr(Stage, "CNT", 0) + 1
        self.name = "%s%d" % (name, Stage.CNT)
        self.es = ExitStack()
        self.n = 0
        self.dsems = []

    def __enter__(self):
        self.es.__enter__()
        return self

    def sb(self, shape, dt, name=None):
        self.n += 1
        t = self.es.enter_context(self.P.nc.sbuf_tensor("%s_%s_%d" % (self.name, name or "t", self.n), list(shape), dt))
        b = Buf(t, name or "t")
        b.is_sb = True
        b.stage = self
        return b

    def __exit__(self, *a):
        if a[0] is None:
            self.P.barrier()
            self.P.emit()
            self.P.free_dsems.extend(self.dsems)
            self.dsems = []
        return self.es.__exit__(*a)


def bc(ap, shape):
    return ap.to_broadcast(list(shape))


def build_program(upto=99):
    nc = bass.Bass("TRN2", target_bir_lowering=False)

    def din(name, shape):
        return nc.dram_tensor(name, list(shape), F32, kind="ExternalInput").ap()

    def dout(name, shape):
        return nc.dram_tensor(name, list(shape), F32, kind="ExternalOutput").ap()

    import os as _os
    _dbg = set(_os.environ.get("KD_DBGOUT", "").split(","))
    _only = _os.environ.get("KD_ONLY", "")
    _only = set(float(v) for v in _only.split(",")) if _only else None

    def en(k):
        return upto >= k and (_only is None or k in _only)

    def dscr(name, shape, dt):
        return nc.dram_tensor(name, list(shape), dt, kind=("ExternalOutput" if name in _dbg else "Internal")).ap()

    I = {}
    I["xp"] = din("xp", [SEQ, D])
    I["xs"] = din("xs", [128, D])
    I["c3"] = din("c3", [3, D])
    I["h0re"] = din("h0re", [2, 4096])
    I["h0im"] = din("h0im", [2, 4096])
    I["cdk"] = din("cdk", [2, PAST, D])
    I["cdv"] = din("cdv", [2, PAST, D])
    I["cck"] = din("cck", [2, PAST, 128])
    I["ckr"] = din("ckr", [2, PAST, 32])
    wshapes = dict(
        w_ada=[4, D, 6 * D], b_ada=[4, 6 * D], g_mix=[4, D], g_ffn=[4, D], w_up=[4, D, 4 * D], w_down=[4, 4 * D, D],
        g_final=[D], s5_a_re=[4096], s5_a_im=[4096], s5_b_re=[64, 64, 16], s5_b_im=[64, 64, 16],
        s5_c_re=[64, 16, 64], s5_c_im=[64, 16, 64], s5_d=[D], s5_log_dt=[64], s5_w_glu_a=[D, D], s5_w_glu_b=[D, D],
        diff_w_qkv=[D, 3 * D], diff_lambda_q1=[64], diff_lambda_k1=[64], diff_lambda_q2=[64], diff_lambda_k2=[64],
        diff_g_sub=[128], diff_w_o=[D, D], mla_w_dq=[D, 256], mla_g_q=[256], mla_w_uq=[256, 1536], mla_w_dkv=[D, 160],
        mla_g_kv=[128], mla_w_uk=[128, 1024], mla_w_uv=[128, 1024], mla_w_o=[D, D], sgu_w_in=[D, 4 * D],
        sgu_g_v=[2 * D], sgu_w_s=[8, 128, 128], sgu_b_s=[8, 128], sgu_w_out=[2 * D, D])
    W = {k: din(k, s) for k, s in wshapes.items()}

    O = {}
    O["yp"] = dout("yp", [SEQ, D])
    O["ys"] = dout("ys", [128, D])
    O["s5p_re"] = dout("s5p_re", [4096])
    O["s5p_im"] = dout("s5p_im", [4096])
    O["s5s_re"] = dout("s5s_re", [2, 4096])
    O["s5s_im"] = dout("s5s_im", [2, 4096])
    O["dkp"] = dout("dkp", [SEQ, D])
    O["dvp"] = dout("dvp", [SEQ, D])
    O["dks"] = dout("dks", [128, D])
    O["dvs"] = dout("dvs", [128, D])
    O["ckvp"] = dout("ckvp", [SEQ, 128])
    O["krp"] = dout("krp", [SEQ, 32])
    O["ckvs"] = dout("ckvs", [128, 128])
    O["krs"] = dout("krs", [128, 32])
    O["sguv"] = dout("sguv", [128, 2 * D])

    xa = dscr("xa", [NTOK, D], F32)
    xb = dscr("xb", [NTOK, D], F32)
    gates_d = dscr("gates_d", [4, 2, 2, 128, D], F32)
    zT_d = dscr("zT_d", [NT, 128, 8 * 128], BF16)

    with ExitStack() as es:
        P = Prog(nc, es)
        out_bufs = {k: Buf(v, k) for k, v in O.items()}
        xa_b = Buf(xa, "xa")
        xb_b = Buf(xb, "xb")
        gates_b = Buf(gates_d, "gates")
        zT_b = Buf(zT_d, "zT_d")
        osem = P.no_dsem("out")
        ssem = P.no_dsem("scr")

        def gsb(name, shape, dt):
            b = Buf(es.enter_context(nc.sbuf_tensor(name, list(shape), dt)), name)
            b.is_sb = True
            return b

        ident_f = gsb("ident_f", [128, 128], F32)
        ident_b = gsb("ident_b", [128, 128], BF16)
        ones_b = gsb("ones_b", [128, 128], BF16)
        ones_f = gsb("ones_f", [128, 128], F32)
        condT = gsb("condT", [128, 8, 4], BF16)
        GS = gsb("GS", [128, 4, 2, 2, 8, 4], F32)
        gfinT = gsb("gfinT", [128, 8], F32)
        PS = [Buf(es.enter_context(nc.psum_tensor("ps%d" % i, [128, 512], F32)), "ps%d" % i) for i in range(8)]
        for b in PS:
            b.is_psum = True

        def load_T(st, dst, dst_ap, rows_ap, R, psb):
            t = st.sb([128, 128], F32, "ldT")
            sem = P.no_dsem("ldT")
            P.dma("sp", t.ap[0:R, :], rows_ap, writes=[t], sem=sem)
            P.op("pe", lambda e: e.transpose(out=psb.ap[:, 0:R], in_=t.ap[0:R, :], identity=ident_f.ap[0:R, 0:R]),
                 reads=[t, ident_f], writes=[psb])
            P.op("dve", lambda e: e.tensor_copy(out=dst_ap, in_=psb.ap[:, 0:R]), reads=[psb], pwrites=[dst])

        with Stage(P, "s0") as st:
            ld = P.no_dsem("s0ld")
            condbc = [st.sb([128, 8, 128], BF16, "condbc") for i in range(2)]
            modT = st.sb([128, 4, 48, 4], F32, "modT")
            P.op("pool", lambda e: e.memset(ident_f.ap[:], 0.0), writes=[ident_f])
            P.op("pool", lambda e: e.affine_select(out=ident_f.ap[:], in_=ident_f.ap[:], pattern=[[-1, 128]],
                                                   compare_op=ALU.not_equal, fill=1.0, base=0, channel_multiplier=1),
                 reads=[ident_f], writes=[ident_f])
            P.op("dve", lambda e: e.tensor_copy(out=ident_b.ap[:], in_=ident_f.ap[:]), reads=[ident_f], writes=[ident_b])
            P.op("pool", lambda e: e.memset(ones_b.ap[:], 1.0), writes=[ones_b])
            P.op("pool", lambda e: e.memset(ones_f.ap[:], 1.0), writes=[ones_f])

            c3 = st.sb([4, D], F32, "c3")
            P.op("pool", lambda e: e.memset(c3.ap[:], 0.0), writes=[c3])
            P.dma("sp", c3.ap[0:3, :], I["c3"], pwrites=[c3], sem=ld)
            c3s = st.sb([4, D], F32, "c3s")
            P.op("act", lambda e: e.activation(out=c3s.ap[:], in_=c3.ap[:], func=AF.Silu), reads=[c3], writes=[c3s])
            for kc in range(8):
                P.op("pe", lambda e, kc=kc: e.transpose(out=PS[0].ap[:, kc * 4:kc * 4 + 4], in_=c3s.ap[0:4, kc * 128:(kc + 1) * 128],
                                                        identity=ident_f.ap[0:4, 0:4]), reads=[c3s, ident_f], writes=[PS[0]])
            P.op("dve", lambda e: e.tensor_copy(out=condT.ap[:], in_=PS[0].ap[:, 0:32].rearrange("p (k s) -> p k s", s=4)),
                 reads=[PS[0]], writes=[condT])
            P.op("dve", lambda e: e.tensor_copy(out=condbc[0].ap[:], in_=bc(condT.ap[:, :, 0:1], [128, 8, 128])),
                 reads=[condT], writes=[condbc[0]])
            P.op("dve", lambda e: e.tensor_copy(out=condbc[1].ap[:, :, 0:64], in_=bc(condT.ap[:, :, 1:2], [128, 8, 64])),
                 reads=[condT], pwrites=[condbc[1]])
            P.op("dve", lambda e: e.tensor_copy(out=condbc[1].ap[:, :, 64:128], in_=bc(condT.ap[:, :, 2:3], [128, 8, 64])),
                 reads=[condT], pwrites=[condbc[1]])

            badaT = st.sb([128, 192], F32, "badaT")
            brows = W["b_ada"].rearrange("l (c p) -> (l c) p", p=128)
            load_T(st, badaT, badaT.ap[:, 0:96], brows[0:96, :], 96, PS[1])
            load_T(st, badaT, badaT.ap[:, 96:192], brows[96:192, :], 96, PS[2])
            gmT = st.sb([128, 2, 32], F32, "gmT")
            load_T(st, gmT, gmT.ap[:, 0, :], W["g_mix"].rearrange("l (c p) -> (l c) p", p=128), 32, PS[3])
            load_T(st, gmT, gmT.ap[:, 1, :], W["g_ffn"].rearrange("l (c p) -> (l c) p", p=128), 32, PS[4])
            load_T(st, gfinT, gfinT.ap[:, :], W["g_final"].rearrange("(c p) -> c p", p=128), 8, PS[5])

            wblk = [st.sb([128, 8, 1024], BF16, "wblk") for _ in range(2)]
            wsem = [P.no_dsem("wblk") for _ in range(2)]
            bbc = [st.sb([128, 1024], F32, "bbc") for _ in range(2)]
            bsem = [P.no_dsem("bbc") for _ in range(2)]
            gtile = [st.sb([128, 1024], F32, "gtile") for _ in range(2)]
            it = 0
            for l in range(4):
                for cb in range(6):
                    wb = wblk[it % 2]
                    for kc in range(8):
                        P.dma("pool", wb.ap[:, kc, :], W["w_ada"][l, kc * 128:(kc + 1) * 128, cb * 1024:(cb + 1) * 1024],
                              pwrites=[wb], sem=wsem[it % 2])
                    pm = PS[it % 2]
                    for f in range(8):
                        for kc in range(8):
                            P.op("pe", lambda e, pm=pm, f=f, kc=kc, wb=wb: e.matmul(
                                out=pm.ap[:, f * 4:f * 4 + 4], lhsT=wb.ap[:, kc, f * 128:(f + 1) * 128], rhs=condT.ap[:, kc, :],
                                start=(kc == 0), stop=(kc == 7)), reads=[wb, condT], writes=[pm])
                    P.op("dve", lambda e, pm=pm, l=l, cb=cb: e.tensor_tensor(
                        out=modT.ap[:, l, cb * 8:(cb + 1) * 8, :], in0=pm.ap[:, 0:32].rearrange("p (k s) -> p k s", s=4),
                        in1=bc(badaT.ap[:, l * 48 + cb * 8:l * 48 + cb * 8 + 8].unsqueeze(2), [128, 8, 4]), op=ALU.add),
                        reads=[pm, badaT], pwrites=[modT])
                    if cb in (2, 5):
                        gi = 0 if cb == 2 else 1
                        bb = bbc[gi]
                        P.dma("sp", bb.ap[:], W["b_ada"][l, cb * 1024:(cb + 1) * 1024].partition_broadcast(128), writes=[bb], sem=bsem[gi])
                        for ps_i in range(2):
                            gt = gtile[ps_i]
                            for half in range(2):
                                pg = PS[2 + half]
                                for kc in range(8):
                                    P.op("pe", lambda e, pg=pg, kc=kc, half=half, ps_i=ps_i, wb=wb: e.matmul(
                                        out=pg.ap[:], lhsT=condbc[ps_i].ap[:, kc, :], rhs=wb.ap[:, kc, half * 512:(half + 1) * 512],
                                        start=(kc == 0), stop=(kc == 7)), reads=[wb, condbc[ps_i]], writes=[pg])
                                P.op("dve", lambda e, pg=pg, half=half, gt=gt, bb=bb: e.scalar_tensor_tensor(
                                    out=gt.ap[:, half * 512:(half + 1) * 512], in0=pg.ap[:], scalar=1.0, in1=bb.ap[:, half * 512:(half + 1) * 512],
                                    op0=ALU.add, op1=ALU.add), reads=[pg, bb], pwrites=[gt])
                            P.dma("sp", gates_d[l, gi, ps_i], gt.ap[:], reads=[gt], pwrites=[gates_b], sem=ssem)
                    it += 1
            for l in range(4):
                for sub in range(2):
                    base = 0 if sub == 0 else 24
                    P.op("dve", lambda e, l=l, sub=sub, base=base: e.scalar_tensor_tensor(
                        out=GS.ap[:, l, sub, 0, :, :], in0=modT.ap[:, l, base + 8:base + 16, :], scalar=1.0,
                        in1=bc(gmT.ap[:, sub, l * 8:(l + 1) * 8].unsqueeze(2), [128, 8, 4]), op0=ALU.add, op1=ALU.mult),
                        reads=[modT, gmT], pwrites=[GS])
                    P.op("dve", lambda e, l=l, sub=sub, base=base: e.tensor_copy(
                        out=GS.ap[:, l, sub, 1, :, :], in_=modT.ap[:, l, base:base + 8, :]), reads=[modT], pwrites=[GS])

        class Gate:
            def __init__(self, st, l, gi):
                self.buf = st.sb([128, D], F32, "gate")
                self.sem = P.no_dsem("gate")
                self.l, self.gi, self.cur = l, gi, None

            def get(self, is_sample):
                k = 1 if is_sample else 0
                if self.cur != k:
                    P.dma("sp", self.buf.ap[:], gates_d[self.l, self.gi, k], reads=[gates_b], writes=[self.buf], sem=self.sem)
                    self.cur = k
                return self.buf

        class NormCtx:
            def __init__(self, st, psb, psb2):
                self.junk = st.sb([128, D], BF16, "junk")
                self.stat = st.sb([128, 4], F32, "stat")
                self.xn = st.sb([128, D], BF16, "xn")
                self.psb = [psb, psb2]

            def run(self, xbuf, x_ap, hbuf, h_ap, l, sub, is_sample, mod=True):
                stat, xn, junk = self.stat, self.xn, self.junk
                P.op("act", lambda e: e.activation(out=junk.ap[:], in_=x_ap, func=AF.Square, accum_out=stat.ap[:, 0:1]),
                     reads=[xbuf], writes=[junk, stat])
                P.op("act", lambda e: e.activation(out=stat.ap[:, 1:2], in_=stat.ap[:, 0:1], func=AF.Sqrt, scale=1.0 / D, bias=EPS),
                     reads=[stat], writes=[stat])
                P.op("dve", lambda e: e.reciprocal(out=stat.ap[:, 2:3], in_=stat.ap[:, 1:2]), reads=[stat], writes=[stat])
                P.op("act", lambda e: e.activation(out=xn.ap[:], in_=x_ap, func=AF.Copy, scale=stat.ap[:, 2:3]),
                     reads=[xbuf, stat], writes=[xn])
                pvs = [pb.ap[:].bitcast(BF16).rearrange("p (k t) -> p k t", t=128) for pb in self.psb]
                for kc in range(8):
                    psb, pv = self.psb[kc // 4], pvs[kc // 4]
                    P.op("pe", lambda e, kc=kc, pv=pv: e.transpose(out=pv[:, kc % 4, :], in_=xn.ap[:, kc * 128:(kc + 1) * 128], identity=ident_b.ap[:]),
                         reads=[xn, ident_b], writes=[psb])
                segs = [(0, 128, 0)] if not is_sample else [(0, 64, 1), (64, 128, 2)]
                for kc in range(8):
                    psb, pv0 = self.psb[kc // 4], pvs[kc // 4]
                    pv = pv0[:, kc % 4:kc % 4 + 1, :]
                    for (c0, c1, s) in segs:
                        if kc < 4:
                            P.op("act", lambda e, kc=kc, c0=c0, c1=c1, s=s, pv=pv: e.activation(
                                out=h_ap[:, kc, c0:c1], in_=pv[:, 0, c0:c1], func=AF.Identity,
                                scale=GS.ap[:, l, sub, 0, kc, s:s + 1], bias=GS.ap[:, l, sub, 1, kc, s:s + 1]),
                                reads=[psb, GS], pwrites=[hbuf])
                        else:
                            P.op("dve", lambda e, kc=kc, c0=c0, c1=c1, s=s, pv=pv: e.tensor_scalar(
                                out=h_ap[:, kc, c0:c1], in0=pv[:, 0, c0:c1], scalar1=GS.ap[:, l, sub, 0, kc, s:s + 1],
                                scalar2=GS.ap[:, l, sub, 1, kc, s:s + 1], op0=ALU.mult, op1=ALU.add),
                                reads=[psb, GS], pwrites=[hbuf])

        def x_src(t, which):
            if which == "in":
                return (I["xp"][t * 128:(t + 1) * 128, :], None) if t < NPT else (I["xs"], None)
            d, b = (xa, xa_b) if which == "a" else (xb, xb_b)
            return d[t * 128:(t + 1) * 128, :], b

        def x_dst(t, which):
            d, b = (xa, xa_b) if which == "a" else (xb, xb_b)
            return d[t * 128:(t + 1) * 128, :], b

        def load_w_bf16(dst, dst_ap_fn, src, nk, sem, q="pool"):
            for kc in range(nk):
                P.dma(q, dst_ap_fn(kc), src[kc * 128:(kc + 1) * 128, :], pwrites=[dst], sem=sem)

        if en(0.5):
          with Stage(P, "l0a") as st:
            ld = P.no_dsem("l0ld")
            sm = st.sb([128, 16, 32], F32, "sm")
            A_RE, A_IM, DT, TH, R, C1, S1, FRE, FIM, TMP1, TMP2, TMP3 = range(12)
            NTB = 65
            Er = st.sb([128, 32, NTB], F32, "Er")
            Ei = st.sb([128, 32, NTB], F32, "Ei")
            Tr = st.sb([128, 32, 64], F32, "Tr")
            Ti = st.sb([128, 32, 64], F32, "Ti")
            Bbd = [st.sb([128, 32, 128], BF16, "Bbd") for _ in range(2)]
            Cbd = [st.sb([128, 32, 128], BF16, "Cbd") for _ in range(2)]
            dT = st.sb([128, 8], F32, "dT")
            h0 = st.sb([128, 2, 2, 32], F32, "h0")

            with Stage(P, "l0p") as sp:
                P.dma("sp", sm.ap[:, A_RE, :], W["s5_a_re"].rearrange("(j p) -> p j", p=128), pwrites=[sm], sem=ld, allow_slow_non_contiguous=True)
                P.dma("sp", sm.ap[:, A_IM, :], W["s5_a_im"].rearrange("(j p) -> p j", p=128), pwrites=[sm], sem=ld, allow_slow_non_contiguous=True)
                ldt2 = W["s5_log_dt"].rearrange("(j h) -> h j", h=2)
                for hh in range(2):
                    P.dma("sp", sm.ap[hh * 64:(hh + 1) * 64, DT, :], ldt2[hh, :].partition_broadcast(64),
                          pwrites=[sm], sem=ld, allow_slow_non_contiguous=True)
                for s in range(2):
                    P.dma("sp", h0.ap[:, 0, s, :], I["h0re"][s].rearrange("(j p) -> p j", p=128), pwrites=[h0], sem=ld, allow_slow_non_contiguous=True)
                    P.dma("sp", h0.ap[:, 1, s, :], I["h0im"][s].rearrange("(j p) -> p j", p=128), pwrites=[h0], sem=ld, allow_slow_non_contiguous=True)

                def sm_op(fn, eng="dve", extra=()):
                    P.op(eng, fn, reads=[sm] + list(extra), writes=[sm])

                sm_op(lambda e: e.activation(out=sm.ap[:, DT, :], in_=sm.ap[:, DT, :], func=AF.Exp), "act")
                sm_op(lambda e: e.tensor_tensor(out=sm.ap[:, TH, :], in0=sm.ap[:, A_IM, :], in1=sm.ap[:, DT, :], op=ALU.mult))
                sm_op(lambda e: e.tensor_tensor(out=sm.ap[:, R, :], in0=sm.ap[:, A_RE, :], in1=sm.ap[:, DT, :], op=ALU.mult))
                sm_op(lambda e: e.activation(out=sm.ap[:, R, :], in_=sm.ap[:, R, :], func=AF.Exp), "act")

                tpos = sp.sb([128, NTB], F32, "tpos")
                P.op("pool", lambda e: e.iota(tpos.ap[:], pattern=[[1, NTB]], base=0, channel_multiplier=0, allow_small_or_imprecise_dtypes=True), writes=[tpos])
                ang = sp.sb([128, 32, NTB], F32, "ang")
                kf = sp.sb([128, 32, NTB], F32, "kf")
                ki = sp.sb([128, 32, NTB], I32, "ki")

                def sincos(dbuf, shift):
                    dst, src, k_ap, ki_ap = dbuf.ap[:], ang.ap[:], kf.ap[:], ki.ap[:]
                    P.op("dve", lambda e: e.tensor_scalar(out=k_ap, in0=src, scalar1=shift, scalar2=1.0 / TWO_PI, op0=ALU.add, op1=ALU.mult),
                         reads=[ang], writes=[kf])
                    P.op("dve", lambda e: e.tensor_copy(out=ki_ap, in_=k_ap), reads=[kf], writes=[ki])
                    P.op("dve", lambda e: e.tensor_copy(out=k_ap, in_=ki_ap), reads=[ki], writes=[kf])
                    P.op("dve", lambda e: e.scalar_tensor_tensor(out=dst, in0=k_ap, scalar=-CW1, in1=src, op0=ALU.mult, op1=ALU.add),
                         reads=[kf, ang], writes=[dbuf])
                    P.op("dve", lambda e: e.scalar_tensor_tensor(out=dst, in0=k_ap, scalar=-CW2, in1=dst, op0=ALU.mult, op1=ALU.add),
                         reads=[kf, dbuf], writes=[dbuf])
                    P.op("dve", lambda e: e.scalar_tensor_tensor(out=dst, in0=k_ap, scalar=-CW3, in1=dst, op0=ALU.mult, op1=ALU.add),
                         reads=[kf, dbuf], writes=[dbuf])
                    P.op("dve", lambda e: e.tensor_scalar(out=dst, in0=dst, scalar1=shift, scalar2=None, op0=ALU.add), reads=[dbuf], writes=[dbuf])
                    P.op("dve", lambda e: e.tensor_scalar(out=dst, in0=dst, scalar1=math.pi, scalar2=-math.pi, op0=ALU.min, op1=ALU.max),
                         reads=[dbuf], writes=[dbuf])
                    P.op("act", lambda e: e.activation(out=dst, in_=dst, func=AF.Sin), reads=[dbuf], writes=[dbuf])

                P.op("dve", lambda e: e.tensor_tensor(out=ang.ap[:], in0=bc(sm.ap[:, TH, :].unsqueeze(2), [128, 32, NTB]),
                                                      in1=bc(tpos.ap[:].unsqueeze(1), [128, 32, NTB]), op=ALU.mult),
                     reads=[sm, tpos], writes=[ang])
                sincos(Ei, 0.0)
                sincos(Er, math.pi / 2)
                sm_op(lambda e: e.tensor_tensor(out=sm.ap[:, C1, :], in0=sm.ap[:, R, :], in1=Er.ap[:, :, 1], op=ALU.mult), extra=[Er])
                sm_op(lambda e: e.tensor_tensor(out=sm.ap[:, S1, :], in0=sm.ap[:, R, :], in1=Ei.ap[:, :, 1], op=ALU.mult), extra=[Ei])
                sm_op(lambda e: e.tensor_scalar(out=sm.ap[:, C1, :], in0=sm.ap[:, C1, :], scalar1=-1.0, scalar2=None, op0=ALU.add))
                sm_op(lambda e: e.tensor_tensor(out=sm.ap[:, TMP1, :], in0=sm.ap[:, A_RE, :], in1=sm.ap[:, A_RE, :], op=ALU.mult))
                sm_op(lambda e: e.tensor_tensor(out=sm.ap[:, TMP2, :], in0=sm.ap[:, A_IM, :], in1=sm.ap[:, A_IM, :], op=ALU.mult))
                sm_op(lambda e: e.tensor_tensor(out=sm.ap[:, TMP1, :], in0=sm.ap[:, TMP1, :], in1=sm.ap[:, TMP2, :], op=ALU.add))
                sm_op(lambda e: e.reciprocal(out=sm.ap[:, TMP1, :], in_=sm.ap[:, TMP1, :]))
                sm_op(lambda e: e.tensor_tensor(out=sm.ap[:, TMP2, :], in0=sm.ap[:, C1, :], in1=sm.ap[:, A_RE, :], op=ALU.mult))
                sm_op(lambda e: e.tensor_tensor(out=sm.ap[:, TMP3, :], in0=sm.ap[:, S1, :], in1=sm.ap[:, A_IM, :], op=ALU.mult))
                sm_op(lambda e: e.tensor_tensor(out=sm.ap[:, TMP2, :], in0=sm.ap[:, TMP2, :], in1=sm.ap[:, TMP3, :], op=ALU.add))
                sm_op(lambda e: e.tensor_tensor(out=sm.ap[:, FRE, :], in0=sm.ap[:, TMP2, :], in1=sm.ap[:, TMP1, :], op=ALU.mult))
                sm_op(lambda e: e.tensor_tensor(out=sm.ap[:, TMP2, :], in0=sm.ap[:, S1, :], in1=sm.ap[:, A_RE, :], op=ALU.mult))
                sm_op(lambda e: e.tensor_tensor(out=sm.ap[:, TMP3, :], in0=sm.ap[:, C1, :], in1=sm.ap[:, A_IM, :], op=ALU.mult))
                sm_op(lambda e: e.tensor_tensor(out=sm.ap[:, TMP2, :], in0=sm.ap[:, TMP2, :], in1=sm.ap[:, TMP3, :], op=ALU.subtract))
                sm_op(lambda e: e.tensor_tensor(out=sm.ap[:, FIM, :], in0=sm.ap[:, TMP2, :], in1=sm.ap[:, TMP1, :], op=ALU.mult))
                tt = sp.sb([128, 32, 64], F32, "tt")
                fre_b = bc(sm.ap[:, FRE, :].unsqueeze(2), [128, 32, 64])
                fim_b = bc(sm.ap[:, FIM, :].unsqueeze(2), [128, 32, 64])
                P.op("dve", lambda e: e.tensor_tensor(out=Tr.ap[:], in0=Er.ap[:, :, 0:64], in1=fre_b, op=ALU.mult), reads=[Er, sm], writes=[Tr])
                P.op("dve", lambda e: e.tensor_tensor(out=tt.ap[:], in0=Ei.ap[:, :, 0:64], in1=fim_b, op=ALU.mult), reads=[Ei, sm], writes=[tt])
                P.op("dve", lambda e: e.tensor_tensor(out=Tr.ap[:], in0=Tr.ap[:], in1=tt.ap[:], op=ALU.add), reads=[Tr, tt], writes=[Tr])
                P.op("dve", lambda e: e.tensor_tensor(out=Ti.ap[:], in0=Er.ap[:, :, 0:64], in1=fim_b, op=ALU.mult), reads=[Er, sm], writes=[Ti])
                P.op("dve", lambda e: e.tensor_tensor(out=tt.ap[:], in0=Ei.ap[:, :, 0:64], in1=fre_b, op=ALU.mult), reads=[Ei, sm, Ti], writes=[tt])
                P.op("dve", lambda e: e.tensor_tensor(out=Ti.ap[:], in0=Ti.ap[:], in1=tt.ap[:], op=ALU.subtract), reads=[Ti, tt], writes=[Ti])

                stg = [sp.sb([128, 8, 128], F32, "stg") for _ in range(2)]
                stg_sem = [P.no_dsem("stg") for _ in range(2)]
                it = 0
                for kind in range(4):
                    src = [W["s5_b_re"], W["s5_b_im"], W["s5_c_re"], W["s5_c_im"]][kind]
                    for jb in range(4):
                        sg = stg[it % 2]
                        P.op("pool", lambda e, sg=sg: e.memset(sg.ap[:], 0.0), writes=[sg])
                        for jj in range(8):
                            j = jb * 8 + jj
                            for gg in range(2):
                                g = 2 * j + gg
                                if kind < 2:
                                    P.dma("sp", sg.ap[gg * 64:(gg + 1) * 64, jj, (g % 8) * 16:(g % 8) * 16 + 16], src[g], pwrites=[sg], sem=stg_sem[it % 2])
                                else:
                                    P.dma("sp", sg.ap[(g % 8) * 16:(g % 8) * 16 + 16, jj, gg * 64:(gg + 1) * 64], src[g], pwrites=[sg], sem=stg_sem[it % 2])
                        for q4 in range(2):
                            psb = PS[(it * 2 + q4) % 4]
                            for u in range(4):
                                jj = q4 * 4 + u
                                P.op("pe", lambda e, psb=psb, u=u, jj=jj, sg=sg: e.transpose(out=psb.ap[:, u * 128:(u + 1) * 128], in_=sg.ap[:, jj, :], identity=ident_f.ap[:]),
                                     reads=[sg, ident_f], writes=[psb])
                            dst = (Bbd[kind] if kind < 2 else Cbd[kind - 2])
                            j0 = jb * 8 + q4 * 4
                            sc = -1.0 if kind == 3 else 1.0
                            P.op("act", lambda e, psb=psb, dst=dst, j0=j0, sc=sc: e.activation(out=dst.ap[:, j0:j0 + 4, :], in_=psb.ap[:].rearrange("p (u c) -> p u c", c=128), func=AF.Copy, scale=sc),
                                 reads=[psb], pwrites=[dst])
                        it += 1
                load_T(sp, dT, dT.ap[:, :], W["s5_d"].rearrange("(c p) -> c p", p=128), 8, PS[4])

            xt = [st.sb([128, D], F32, "x") for _ in range(2)]
            xsem = [P.no_dsem("x") for _ in range(2)]
            nctx = NormCtx(st, PS[0], PS[7])
            hTs = [st.sb([128, 8, 128], BF16, "hT") for _ in range(2)]
            wre = st.sb([128, 32, 128], BF16, "wre")
            wim = st.sb([128, 32, 128], BF16, "wim")
            gre = st.sb([128, 32, 128], F32, "gre")
            gim = st.sb([128, 32, 128], F32, "gim")
            hsr = st.sb([128, 32, 128], BF16, "hsr")
            hsi = st.sb([128, 32, 128], BF16, "hsi")
            tmp = [st.sb([128, 512], F32, "tmp") for _ in range(4)]
            init = st.sb([128, 2, 2, 32], F32, "init")
            hend = st.sb([128, 2, 2, 32], F32, "hend")
            ctmp = st.sb([128, 4, 32], F32, "ctmp")
            ysb = st.sb([128, 8, 128], F32, "ysb")
            zT = [st.sb([128, 8, 128], BF16, "zT") for _ in range(2)]
            P.op("pool", lambda e: e.memset(init.ap[:], 0.0), writes=[init])

            def cmul_small(dbuf, dst_re, dst_im, a_re, a_im, b_re, b_im, deps_r):
                P.op("dve", lambda e: e.tensor_tensor(out=ctmp.ap[:, 0, :], in0=a_re, in1=b_re, op=ALU.mult), reads=deps_r, pwrites=[ctmp])
                P.op("dve", lambda e: e.tensor_tensor(out=ctmp.ap[:, 1, :], in0=a_im, in1=b_im, op=ALU.mult), reads=deps_r, pwrites=[ctmp])
                P.op("dve", lambda e: e.tensor_tensor(out=ctmp.ap[:, 2, :], in0=a_re, in1=b_im, op=ALU.mult), reads=deps_r, pwrites=[ctmp])
                P.op("dve", lambda e: e.tensor_tensor(out=ctmp.ap[:, 3, :], in0=a_im, in1=b_re, op=ALU.mult), reads=deps_r, pwrites=[ctmp])
                P.op("dve", lambda e: e.tensor_tensor(out=dst_re, in0=ctmp.ap[:, 0, :], in1=ctmp.ap[:, 1, :], op=ALU.subtract), reads=[ctmp], pwrites=[dbuf])
                P.op("dve", lambda e: e.tensor_tensor(out=dst_im, in0=ctmp.ap[:, 2, :], in1=ctmp.ap[:, 3, :], op=ALU.add), reads=[ctmp], pwrites=[dbuf])

            tlist = list(range(NT)) if upto >= 1 else ([] if upto < 0.55 else ([0] if upto < 0.65 else [0, NPT]))
            for t in tlist:
                is_s = (t == NPT)
                i = t % 2
                x = xt[i]
                src, srcb = x_src(t, "in")
                P.dma("sp", x.ap[:], src, writes=[x], sem=xsem[i])
                hT = hTs[i]
                nctx.run(x, x.ap[:], hT, hT.ap[:], 0, 0, is_s)
                if is_s:
                    for s in range(2):
                        cmul_small(init, init.ap[:, 0, s, :], init.ap[:, 1, s, :], h0.ap[:, 0, s, :], h0.ap[:, 1, s, :],
                                   Er.ap[:, :, 1], Ei.ap[:, :, 1], [h0, Er, Ei])
                import os as _os
                CUT = int(_os.environ.get('KD_CUT', '99'))
                if CUT < 2:
                    continue
                for m in range(8):
                    pr, pi_ = PS[1 + (m % 2) * 2], PS[2 + (m % 2) * 2]
                    for u in range(4):
                        j = m * 4 + u
                        P.op("pe", lambda e, pr=pr, u=u, j=j, m=m, hT=hT: e.matmul(out=pr.ap[:, u * 128:(u + 1) * 128], lhsT=Bbd[0].ap[:, j, :], rhs=hT.ap[:, m, :], start=True, stop=True),
                             reads=[Bbd[0], hT], writes=[pr])
                    for u in range(4):
                        j = m * 4 + u
                        P.op("pe", lambda e, pi_=pi_, u=u, j=j, m=m, hT=hT: e.matmul(out=pi_.ap[:, u * 128:(u + 1) * 128], lhsT=Bbd[1].ap[:, j, :], rhs=hT.ap[:, m, :], start=True, stop=True),
                             reads=[Bbd[1], hT], writes=[pi_])
                    prv = pr.ap[:].rearrange("p (u h t) -> p u h t", u=4, h=2)
                    piv = pi_.ap[:].rearrange("p (u h t) -> p u h t", u=4, h=2)
                    Trv = bc(Tr.ap[:, m * 4:(m + 1) * 4, :].unsqueeze(2), [128, 4, 2, 64])
                    Tiv = bc(Ti.ap[:, m * 4:(m + 1) * 4, :].unsqueeze(2), [128, 4, 2, 64])
                    tv = [tb.ap[:].rearrange("p (u h t) -> p u h t", u=4, h=2) for tb in tmp]
                    wrv = wre.ap[:, m * 4:(m + 1) * 4, :].rearrange("p u (h t) -> p u h t", h=2)
                    wiv = wim.ap[:, m * 4:(m + 1) * 4, :].rearrange("p u (h t) -> p u h t", h=2)
                    P.op("dve", lambda e, prv=prv, Trv=Trv, tv=tv: e.tensor_tensor(out=tv[0], in0=prv, in1=Trv, op=ALU.mult), reads=[pr, Tr], writes=[tmp[0]])
                    P.op("dve", lambda e, piv=piv, Tiv=Tiv, tv=tv: e.tensor_tensor(out=tv[1], in0=piv, in1=Tiv, op=ALU.mult), reads=[pi_, Ti], writes=[tmp[1]])
                    P.op("dve", lambda e, prv=prv, Tiv=Tiv, tv=tv: e.tensor_tensor(out=tv[2], in0=prv, in1=Tiv, op=ALU.mult), reads=[pr, Ti], writes=[tmp[2]])
                    P.op("dve", lambda e, piv=piv, Trv=Trv, tv=tv: e.tensor_tensor(out=tv[3], in0=piv, in1=Trv, op=ALU.mult), reads=[pi_, Tr], writes=[tmp[3]])
                    P.op("pool", lambda e, wrv=wrv, tv=tv: e.tensor_tensor(out=wrv, in0=tv[0], in1=tv[1], op=ALU.subtract), reads=[tmp[0], tmp[1]], pwrites=[wre])
                    P.op("pool", lambda e, wiv=wiv, tv=tv: e.tensor_tensor(out=wiv, in0=tv[2], in1=tv[3], op=ALU.add), reads=[tmp[2], tmp[3]], pwrites=[wim])
                if CUT < 3:
                    continue
                for hf in range(2):
                    for j in range(32):
                        for ri, (wsrc, gdst) in enumerate(((wre, gre), (wim, gim))):
                            P.op("dve", lambda e, j=j, hf=hf, ri=ri, wsrc=wsrc, gdst=gdst: e.tensor_tensor_scan(
                                out=gdst.ap[:, j, hf * 64:(hf + 1) * 64], data0=bc(sm.ap[:, R, j:j + 1], [128, 64]),
                                data1=wsrc.ap[:, j, hf * 64:(hf + 1) * 64], initial=init.ap[:, ri, hf, j:j + 1], op0=ALU.mult, op1=ALU.add),
                                reads=[sm, wsrc, init], pwrites=[gdst])
                    ge_r = gre.ap[:, :, hf * 64 + 63]
                    ge_i = gim.ap[:, :, hf * 64 + 63]
                    if not is_s:
                        nh = 1 - hf
                        cmul_small(init, init.ap[:, 0, nh, :], init.ap[:, 1, nh, :], ge_r, ge_i, Er.ap[:, :, 64], Ei.ap[:, :, 64], [gre, gim, Er, Ei])
                    if is_s or (t == NPT - 1 and hf == 1):
                        cmul_small(hend, hend.ap[:, 0, hf, :], hend.ap[:, 1, hf, :], ge_r, ge_i, Er.ap[:, :, 63], Ei.ap[:, :, 63], [gre, gim, Er, Ei])
                        if is_s:
                            P.dma("sp", O["s5s_re"][hf].rearrange("(j p) -> p j", p=128), hend.ap[:, 0, hf, :], reads=[hend], pwrites=[out_bufs["s5s_re"]], sem=osem, allow_slow_non_contiguous=True)
                            P.dma("sp", O["s5s_im"][hf].rearrange("(j p) -> p j", p=128), hend.ap[:, 1, hf, :], reads=[hend], pwrites=[out_bufs["s5s_im"]], sem=osem, allow_slow_non_contiguous=True)
                        else:
                            P.dma("sp", O["s5p_re"].rearrange("(j p) -> p j", p=128), hend.ap[:, 0, hf, :], reads=[hend], pwrites=[out_bufs["s5p_re"]], sem=osem, allow_slow_non_contiguous=True)
                            P.dma("sp", O["s5p_im"].rearrange("(j p) -> p j", p=128), hend.ap[:, 1, hf, :], reads=[hend], pwrites=[out_bufs["s5p_im"]], sem=osem, allow_slow_non_contiguous=True)
                if CUT < 4:
                    continue
                z = zT[i]
                for m in range(8):
                    grv = gre.ap[:, m * 4:(m + 1) * 4, :].rearrange("p u (h t) -> p u h t", h=2)
                    giv = gim.ap[:, m * 4:(m + 1) * 4, :].rearrange("p u (h t) -> p u h t", h=2)
                    Erv = bc(Er.ap[:, m * 4:(m + 1) * 4, 0:64].unsqueeze(2), [128, 4, 2, 64])
                    Eiv = bc(Ei.ap[:, m * 4:(m + 1) * 4, 0:64].unsqueeze(2), [128, 4, 2, 64])
                    tv = [tb.ap[:].rearrange("p (u h t) -> p u h t", u=4, h=2) for tb in tmp]
                    hrv = hsr.ap[:, m * 4:(m + 1) * 4, :].rearrange("p u (h t) -> p u h t", h=2)
                    hiv = hsi.ap[:, m * 4:(m + 1) * 4, :].rearrange("p u (h t) -> p u h t", h=2)
                    P.op("dve", lambda e, grv=grv, Erv=Erv, tv=tv: e.tensor_tensor(out=tv[0], in0=grv, in1=Erv, op=ALU.mult), reads=[gre, Er], writes=[tmp[0]])
                    P.op("pool", lambda e, giv=giv, Eiv=Eiv, tv=tv: e.tensor_tensor(out=tv[1], in0=giv, in1=Eiv, op=ALU.mult), reads=[gim, Ei], writes=[tmp[1]])
                    P.op("pool", lambda e, grv=grv, Eiv=Eiv, tv=tv: e.tensor_tensor(out=tv[2], in0=grv, in1=Eiv, op=ALU.mult), reads=[gre, Ei], writes=[tmp[2]])
                    P.op("pool", lambda e, giv=giv, Erv=Erv, tv=tv: e.tensor_tensor(out=tv[3], in0=giv, in1=Erv, op=ALU.mult), reads=[gim, Er], writes=[tmp[3]])
                    P.op("dve", lambda e, hrv=hrv, tv=tv: e.tensor_tensor(out=hrv, in0=tv[0], in1=tv[1], op=ALU.subtract), reads=[tmp[0], tmp[1]], pwrites=[hsr])
                    P.op("pool", lambda e, hiv=hiv, tv=tv: e.tensor_tensor(out=hiv, in0=tv[2], in1=tv[3], op=ALU.add), reads=[tmp[2], tmp[3]], pwrites=[hsi])
                    py = PS[5 + (m // 4)]
                    for u in range(4):
                        j = m * 4 + u
                        P.op("pe", lambda e, py=py, m=m, j=j, u=u: e.matmul(out=py.ap[:, (m % 4) * 128:(m % 4 + 1) * 128], lhsT=Cbd[0].ap[:, j, :], rhs=hsr.ap[:, j, :], start=(u == 0), stop=False),
                             reads=[Cbd[0], hsr], writes=[py])
                    for u in range(4):
                        j = m * 4 + u
                        P.op("pe", lambda e, py=py, m=m, j=j, u=u: e.matmul(out=py.ap[:, (m % 4) * 128:(m % 4 + 1) * 128], lhsT=Cbd[1].ap[:, j, :], rhs=hsi.ap[:, j, :], start=False, stop=(u == 3)),
                             reads=[Cbd[1], hsi], writes=[py])
                    P.op("dve", lambda e, py=py, m=m, hT=hT: e.scalar_tensor_tensor(out=ysb.ap[:, m, :], in0=hT.ap[:, m, :], scalar=dT.ap[:, m:m + 1],
                                                                              in1=py.ap[:, (m % 4) * 128:(m % 4 + 1) * 128], op0=ALU.mult, op1=ALU.add),
                         reads=[hT, dT, py], pwrites=[ysb])
                if CUT < 5:
                    continue
                P.op("act", lambda e, z=z: e.activation(out=z.ap[:], in_=ysb.ap[:], func=AF.Gelu_apprx_tanh), reads=[ysb], writes=[z])
                P.dma("sp", zT_d[t], z.ap[:].rearrange("p k t -> p (k t)"), reads=[z], pwrites=[zT_b], sem=ssem)

        if en(2):
          with Stage(P, "l0g") as st:
            Wa = st.sb([128, 8, D], BF16, "Wa")
            Wb = st.sb([128, 8, D], BF16, "Wb")
            wsem = P.no_dsem("wglu")
            load_w_bf16(Wa, lambda kc: Wa.ap[:, kc, :], W["s5_w_glu_a"], 8, wsem)
            load_w_bf16(Wb, lambda kc: Wb.ap[:, kc, :], W["s5_w_glu_b"], 8, wsem)
            gate = Gate(st, 0, 0)
            xt = [st.sb([128, D], F32, "x") for _ in range(2)]
            xsem = [P.no_dsem("x") for _ in range(2)]
            zt = [st.sb([128, 8, 128], BF16, "z") for _ in range(2)]
            zsem = [P.no_dsem("z") for _ in range(2)]
            sig = [st.sb([128, D], F32, "sig") for _ in range(2)]
            for t in range(NT):
                is_s = (t == NPT)
                i = t % 2
                x, z, sg_ = xt[i], zt[i], sig[i]
                src, _ = x_src(t, "in")
                P.dma("sp", x.ap[:], src, writes=[x], sem=xsem[i])
                P.dma("sp", z.ap[:].rearrange("p k t -> p (k t)"), zT_d[t], reads=[zT_b], writes=[z], sem=zsem[i])
                g = gate.get(is_s)
                for cbk in range(2):
                    pa, pb = PS[cbk * 2], PS[1 + cbk * 2]
                    for kc in range(8):
                        P.op("pe", lambda e, pa=pa, kc=kc, cbk=cbk, z=z: e.matmul(out=pa.ap[:], lhsT=z.ap[:, kc, :], rhs=Wa.ap[:, kc, cbk * 512:(cbk + 1) * 512], start=(kc == 0), stop=(kc == 7)),
                             reads=[z, Wa], writes=[pa])
                    for kc in range(8):
                        P.op("pe", lambda e, pb=pb, kc=kc, cbk=cbk, z=z: e.matmul(out=pb.ap[:], lhsT=z.ap[:, kc, :], rhs=Wb.ap[:, kc, cbk * 512:(cbk + 1) * 512], start=(kc == 0), stop=(kc == 7)),
                             reads=[z, Wb], writes=[pb])
                    sl = slice(cbk * 512, (cbk + 1) * 512)
                    P.op("act", lambda e, pb=pb, sl=sl, sg_=sg_: e.activation(out=sg_.ap[:, sl], in_=pb.ap[:], func=AF.Sigmoid), reads=[pb], pwrites=[sg_])
                    P.op("dve", lambda e, pa=pa, sl=sl, sg_=sg_: e.tensor_tensor(out=sg_.ap[:, sl], in0=pa.ap[:], in1=sg_.ap[:, sl], op=ALU.mult), reads=[pa, sg_], pwrites=[sg_])
                    P.op("pool", lambda e, sl=sl, g=g, sg_=sg_: e.tensor_tensor(out=sg_.ap[:, sl], in0=sg_.ap[:, sl], in1=g.ap[:, sl], op=ALU.mult), reads=[sg_, g], pwrites=[sg_])
                    P.op("pool", lambda e, sl=sl, sg_=sg_, x=x: e.tensor_tensor(out=sg_.ap[:, sl], in0=sg_.ap[:, sl], in1=x.ap[:, sl], op=ALU.add), reads=[sg_, x], pwrites=[sg_])
                dst, dstb = x_dst(t, "a")
                P.dma("sp", dst, sg_.ap[:], reads=[sg_], pwrites=[dstb], sem=ssem)

        def mlp_stage(l, src_w, dst_w, final=False):
          with Stage(P, "mlp%d" % l) as st:
            wup = st.sb([128, 8, 4 * D], BF16, "wup")
            wdn = st.sb([128, 32, D], BF16, "wdn")
            wsem = P.no_dsem("wmlp")
            for kc in range(8):
                for c0 in range(0, 4 * D, 1024):
                    P.dma("pool", wup.ap[:, kc, c0:c0 + 1024], W["w_up"][l, kc * 128:(kc + 1) * 128, c0:c0 + 1024], pwrites=[wup], sem=wsem)
            for kc in range(32):
                P.dma("pool", wdn.ap[:, kc, :], W["w_down"][l, kc * 128:(kc + 1) * 128, :], pwrites=[wdn], sem=wsem)
            gate = Gate(st, l, 1)
            if final:
                gfin = st.sb([128, D], F32, "gfin")
                gsem = P.no_dsem("gfin")
                P.dma("sp", gfin.ap[:], W["g_final"].partition_broadcast(128), writes=[gfin], sem=gsem)
                fstat = st.sb([128, 4], F32, "fstat")
            xsl = [st.sb([128, D], F32, "xs") for _ in range(4)]
            xsem = P.no_dsem("x")
            nctx = NormCtx(st, PS[0], PS[3])
            hT = st.sb([128, 8, 512], BF16, "hT")
            actT = st.sb([128, 32, 512], BF16, "actT")
            rl = [st.sb([128, 512], BF16, "rl") for _ in range(2)]
            tiles = [(s * 4, 4) for s in range(NPT // 4)] + [(NPT, 1)]
            for (t0, nt) in tiles:
                is_s = (t0 == NPT)
                ntok = nt * 128
                for a in range(nt):
                    src, srcb = x_src(t0 + a, src_w)
                    P.dma("sp", xsl[a].ap[:], src, reads=([srcb] if srcb else []), writes=[xsl[a]], sem=xsem)
                for a in range(nt):
                    hv = hT.ap[:, :, a * 128:(a + 1) * 128]
                    nctx.run(xsl[a], xsl[a].ap[:], hT, hv, l, 1, is_s)
                for oc in range(32):
                    pu = PS[(1, 2, 4, 5, 6, 7)[oc % 6]]
                    for kc in range(8):
                        P.op("pe", lambda e, pu=pu, kc=kc, oc=oc, ntok=ntok: e.matmul(out=pu.ap[:, 0:ntok], lhsT=wup.ap[:, kc, oc * 128:(oc + 1) * 128], rhs=hT.ap[:, kc, 0:ntok], start=(kc == 0), stop=(kc == 7)),
                             reads=[wup, hT], writes=[pu])
                    r = rl[oc % 2]
                    P.op("act", lambda e, pu=pu, r=r, ntok=ntok: e.activation(out=r.ap[:, 0:ntok], in_=pu.ap[:, 0:ntok], func=AF.Relu), reads=[pu], writes=[r])
                    eng = "dve" if oc % 2 == 0 else "pool"
                    P.op(eng, lambda e, r=r, oc=oc, ntok=ntok: e.tensor_tensor(out=actT.ap[:, oc, 0:ntok], in0=r.ap[:, 0:ntok], in1=r.ap[:, 0:ntok], op=ALU.mult), reads=[r], pwrites=[actT])
                g = gate.get(is_s)
                for a in range(nt):
                    for cbk in range(2):
                        pd = PS[4 + (a * 2 + cbk) % 4]
                        for kc in range(32):
                            P.op("pe", lambda e, pd=pd, kc=kc, a=a, cbk=cbk: e.matmul(out=pd.ap[:], lhsT=actT.ap[:, kc, a * 128:(a + 1) * 128], rhs=wdn.ap[:, kc, cbk * 512:(cbk + 1) * 512], start=(kc == 0), stop=(kc == 31)),
                                 reads=[actT, wdn], writes=[pd])
                        sl = slice(cbk * 512, (cbk + 1) * 512)
                        tmpb = rl
                        P.op("dve", lambda e, pd=pd, a=a, sl=sl, g=g: e.tensor_tensor(out=pd.ap[:], in0=pd.ap[:], in1=g.ap[:, sl], op=ALU.mult), reads=[pd, g], writes=[pd])
                        P.op("dve", lambda e, pd=pd, a=a, sl=sl: e.tensor_tensor(out=xsl[a].ap[:, sl], in0=pd.ap[:], in1=xsl[a].ap[:, sl], op=ALU.add), reads=[pd, xsl[a]], pwrites=[xsl[a]])
                    if final:
                        junk, stat = nctx.junk, fstat
                        P.op("act", lambda e, a=a: e.activation(out=junk.ap[:], in_=xsl[a].ap[:], func=AF.Square, accum_out=stat.ap[:, 0:1]), reads=[xsl[a]], writes=[junk, stat])
                        P.op("act", lambda e: e.activation(out=stat.ap[:, 1:2], in_=stat.ap[:, 0:1], func=AF.Sqrt, scale=1.0 / D, bias=EPS), reads=[stat], writes=[stat])
                        P.op("dve", lambda e: e.reciprocal(out=stat.ap[:, 2:3], in_=stat.ap[:, 1:2]), reads=[stat], writes=[stat])
                        P.op("dve", lambda e, a=a: e.scalar_tensor_tensor(out=xsl[a].ap[:], in0=xsl[a].ap[:], scalar=stat.ap[:, 2:3], in1=gfin.ap[:], op0=ALU.mult, op1=ALU.mult),
                             reads=[xsl[a], stat, gfin], pwrites=[xsl[a]])
                        t = t0 + a
                        if t < NPT:
                            P.dma("sp", O["yp"][t * 128:(t + 1) * 128, :], xsl[a].ap[:], reads=[xsl[a]], pwrites=[out_bufs["yp"]], sem=osem)
                        else:
                            P.dma("sp", O["ys"], xsl[a].ap[:], reads=[xsl[a]], pwrites=[out_bufs["ys"]], sem=osem)
                    else:
                        dst, dstb = x_dst(t0 + a, dst_w)
                        P.dma("sp", dst, xsl[a].ap[:], reads=[xsl[a]], pwrites=[dstb], sem=ssem)

        if en(3):
            mlp_stage(0, "a", "b")

        def sin_reduce(st, dbuf, abuf, shift, shape):
            kf = st.sb(shape, F32, "kf")
            ki = st.sb(shape, I32, "ki")
            dst, src, k_ap, ki_ap = dbuf.ap[:], abuf.ap[:], kf.ap[:], ki.ap[:]
            P.op("dve", lambda e: e.tensor_scalar(out=k_ap, in0=src, scalar1=shift, scalar2=1.0 / TWO_PI, op0=ALU.add, op1=ALU.mult), reads=[abuf], writes=[kf])
            P.op("dve", lambda e: e.tensor_copy(out=ki_ap, in_=k_ap), reads=[kf], writes=[ki])
            P.op("dve", lambda e: e.tensor_copy(out=k_ap, in_=ki_ap), reads=[ki], writes=[kf])
            P.op("dve", lambda e: e.scalar_tensor_tensor(out=dst, in0=k_ap, scalar=-CW1, in1=src, op0=ALU.mult, op1=ALU.add), reads=[kf, abuf], writes=[dbuf])
            P.op("dve", lambda e: e.scalar_tensor_tensor(out=dst, in0=k_ap, scalar=-CW2, in1=dst, op0=ALU.mult, op1=ALU.add), reads=[kf, dbuf], writes=[dbuf])
            P.op("dve", lambda e: e.scalar_tensor_tensor(out=dst, in0=k_ap, scalar=-CW3, in1=dst, op0=ALU.mult, op1=ALU.add), reads=[kf, dbuf], writes=[dbuf])
            P.op("dve", lambda e: e.tensor_scalar(out=dst, in0=dst, scalar1=shift, scalar2=None, op0=ALU.add), reads=[dbuf], writes=[dbuf])
            P.op("dve", lambda e: e.tensor_scalar(out=dst, in0=dst, scalar1=math.pi, scalar2=-math.pi, op0=ALU.min, op1=ALU.max), reads=[dbuf], writes=[dbuf])
            P.op("act", lambda e: e.activation(out=dst, in_=dst, func=AF.Sin), reads=[dbuf], writes=[dbuf])

        def rope_tables(st, half):
            cosT = st.sb([128, NT, half], F32, "cosT")
            sinT = st.sb([128, NT, half], F32, "sinT")
            with Stage(P, "rp") as sp:
                pos = sp.sb([128, NT], F32, "pos")
                invf = sp.sb([128, half], F32, "invf")
                ang = sp.sb([128, NT, half], F32, "ang")
                P.op("pool", lambda e: e.iota(pos.ap[:], pattern=[[128, NT]], base=0, channel_multiplier=1, allow_small_or_imprecise_dtypes=True), writes=[pos])
                P.op("pool", lambda e: e.iota(pos.ap[0:64, NPT:NPT + 1], pattern=[[0, 1]], base=PAST, channel_multiplier=1, allow_small_or_imprecise_dtypes=True), pwrites=[pos])
                P.op("pool", lambda e: e.iota(pos.ap[64:128, NPT:NPT + 1], pattern=[[0, 1]], base=PAST, channel_multiplier=1, allow_small_or_imprecise_dtypes=True), pwrites=[pos])
                for i in range(half):
                    P.op("pool", lambda e, i=i: e.memset(invf.ap[:, i:i + 1], float(np.float32(10000.0) ** np.float32(-i / half))), pwrites=[invf])
                P.op("dve", lambda e: e.tensor_tensor(out=ang.ap[:], in0=bc(pos.ap[:].unsqueeze(2), [128, NT, half]), in1=bc(invf.ap[:].unsqueeze(1), [128, NT, half]), op=ALU.mult),
                     reads=[pos, invf], writes=[ang])
                sin_reduce(sp, sinT, ang, 0.0, [128, NT, half])
                sin_reduce(sp, cosT, ang, math.pi / 2, [128, NT, half])
            return cosT, sinT

        def rope_apply(src_ps, src_ap, dbuf, dst_ap, ngrp, half, cosT, sinT, t, tmps):
            c = bc(cosT.ap[:, t, :].unsqueeze(1), [128, ngrp, half])
            s = bc(sinT.ap[:, t, :].unsqueeze(1), [128, ngrp, half])
            x1, x2 = src_ap[:, :, 0:half], src_ap[:, :, half:2 * half]
            tv = [tb.ap[:, 0:ngrp * half].rearrange("p (g i) -> p g i", i=half) for tb in tmps]
            P.op("dve", lambda e: e.tensor_tensor(out=tv[0], in0=x1, in1=c, op=ALU.mult), reads=[src_ps, cosT], writes=[tmps[0]])
            P.op("dve", lambda e: e.tensor_tensor(out=tv[1], in0=x2, in1=s, op=ALU.mult), reads=[src_ps, sinT], writes=[tmps[1]])
            P.op("dve", lambda e: e.tensor_tensor(out=tv[2], in0=x1, in1=s, op=ALU.mult), reads=[src_ps, sinT], writes=[tmps[2]])
            P.op("dve", lambda e: e.tensor_tensor(out=tv[3], in0=x2, in1=c, op=ALU.mult), reads=[src_ps, cosT], writes=[tmps[3]])
            P.op("pool", lambda e: e.tensor_tensor(out=dst_ap[:, :, 0:half], in0=tv[0], in1=tv[1], op=ALU.subtract), reads=[tmps[0], tmps[1]], pwrites=[dbuf])
            P.op("pool", lambda e: e.tensor_tensor(out=dst_ap[:, :, half:2 * half], in0=tv[2], in1=tv[3], op=ALU.add), reads=[tmps[2], tmps[3]], pwrites=[dbuf])

        QT_d = dscr("QT_d", [8, 128, NTOK], BF16)
        KT_d = dscr("KT_d", [8, 128, NTOK], BF16)
        V_d = dscr("V_d", [NTOK, D], BF16)
        OT_d = dscr("OT_d", [8, 128, NTOK], BF16)
        QT_b, KT_b, V_b, OT_b = Buf(QT_d, "QT_d"), Buf(KT_d, "KT_d"), Buf(V_d, "V_d"), Buf(OT_d, "OT_d")

        if en(4):
          with Stage(P, "l1a") as st:
            cosT, sinT = rope_tables(st, 32)
            wqkv = st.sb([128, 8, 3 * D], BF16, "wqkv")
            for kc in range(8):
                for c0 in range(0, 3 * D, 1024):
                    P.dma("pool", wqkv.ap[:, kc, c0:c0 + 1024], W["diff_w_qkv"][kc * 128:(kc + 1) * 128, c0:c0 + 1024], pwrites=[wqkv])
            xt = [st.sb([128, D], F32, "x") for _ in range(2)]
            nctx = NormCtx(st, PS[0], PS[7])
            hTs = [st.sb([128, 8, 128], BF16, "hT") for _ in range(2)]
            qk = [st.sb([128, 2, D], F32, "qk") for _ in range(2)]
            qkb = [st.sb([128, 2, D], BF16, "qkb") for _ in range(2)]
            vf = [st.sb([128, D], F32, "vf") for _ in range(2)]
            vb = [st.sb([128, D], BF16, "vb") for _ in range(2)]
            tmps = [st.sb([128, 256], F32, "rt") for _ in range(4)]
            qkT = [st.sb([128, 16, 128], BF16, "qkT") for _ in range(2)]
            tlist = list(range(NT)) if upto >= 4.5 else [0, NPT]
            for t in tlist:
                is_s = (t == NPT)
                i = t % 2
                x, hT = xt[i], hTs[i]
                src, srcb = x_src(t, "b")
                P.dma("sp", x.ap[:], src, reads=[srcb], writes=[x])
                nctx.run(x, x.ap[:], hT, hT.ap[:], 1, 0, is_s)
                for cb in range(6):
                    pq = PS[1 + cb % 4]
                    for kc in range(8):
                        P.op("pe", lambda e, pq=pq, kc=kc, cb=cb, hT=hT: e.matmul(out=pq.ap[:], lhsT=hT.ap[:, kc, :], rhs=wqkv.ap[:, kc, cb * 512:(cb + 1) * 512], start=(kc == 0), stop=(kc == 7)),
                             reads=[hT, wqkv], writes=[pq])
                    if cb < 4:
                        which, half_ = cb // 2, cb % 2
                        dst = qk[i].ap[:, which, half_ * 512:(half_ + 1) * 512].rearrange("p (g d) -> p g d", d=64)
                        rope_apply(pq, pq.ap[:].rearrange("p (g d) -> p g d", d=64), qk[i], dst, 8, 32, cosT, sinT, t, tmps)
                    else:
                        sl = slice((cb - 4) * 512, (cb - 3) * 512)
                        P.op("act", lambda e, pq=pq, sl=sl, i=i: e.activation(out=vf[i].ap[:, sl], in_=pq.ap[:], func=AF.Copy), reads=[pq], pwrites=[vf[i]])
                        P.op("dve", lambda e, pq=pq, sl=sl, i=i: e.tensor_copy(out=vb[i].ap[:, sl], in_=vf[i].ap[:, sl]), reads=[vf[i]], pwrites=[vb[i]])
                if not is_s:
                    P.dma("sp", O["dkp"][t * 128:(t + 1) * 128, :], qk[i].ap[:, 1, :], reads=[qk[i]], pwrites=[out_bufs["dkp"]])
                    P.dma("sp", O["dvp"][t * 128:(t + 1) * 128, :], vf[i].ap[:], reads=[vf[i]], pwrites=[out_bufs["dvp"]])
                else:
                    P.dma("sp", O["dks"], qk[i].ap[:, 1, :], reads=[qk[i]], pwrites=[out_bufs["dks"]])
                    P.dma("sp", O["dvs"], vf[i].ap[:], reads=[vf[i]], pwrites=[out_bufs["dvs"]])
                P.dma("sp", V_d[t * 128:(t + 1) * 128, :], vb[i].ap[:], reads=[vb[i]], pwrites=[V_b])
                P.op("act", lambda e, i=i: e.activation(out=qkb[i].ap[:], in_=qk[i].ap[:], func=AF.Copy), reads=[qk[i]], writes=[qkb[i]])
                for which in range(2):
                    pt = PS[5 + which]
                    ptv = pt.ap[:].bitcast(BF16).rearrange("p (h t) -> p h t", t=128)
                    for h in range(8):
                        P.op("pe", lambda e, ptv=ptv, h=h, which=which, i=i: e.transpose(out=ptv[:, h, :], in_=qkb[i].ap[:, which, h * 128:(h + 1) * 128], identity=ident_b.ap[:]),
                             reads=[qkb[i], ident_b], writes=[pt])
                    eng = "act" if which == 0 else "dve"
                    if eng == "act":
                        P.op("act", lambda e, ptv=ptv, which=which, i=i: e.activation(out=qkT[i].ap[:, which * 8:(which + 1) * 8, :], in_=ptv, func=AF.Copy), reads=[pt], pwrites=[qkT[i]])
                    else:
                        P.op("dve", lambda e, ptv=ptv, which=which, i=i: e.tensor_copy(out=qkT[i].ap[:, which * 8:(which + 1) * 8, :], in_=ptv), reads=[pt], pwrites=[qkT[i]])
                P.dma("sp", QT_d[:, :, t * 128:(t + 1) * 128].rearrange("h p t -> p h t"), qkT[i].ap[:, 0:8, :], reads=[qkT[i]], pwrites=[QT_b])
                P.dma("sp", KT_d[:, :, t * 128:(t + 1) * 128].rearrange("h p t -> p h t"), qkT[i].ap[:, 8:16, :], reads=[qkT[i]], pwrites=[KT_b])

        if en(5):
          with Stage(P, "l1b") as st:
            lv = st.sb([128, 4, 64], F32, "lv")
            for k_i, nm in enumerate(["diff_lambda_q1", "diff_lambda_k1", "diff_lambda_q2", "diff_lambda_k2"]):
                P.dma("sp", lv.ap[:, k_i, :], W[nm].partition_broadcast(128), pwrites=[lv])
            lsc = st.sb([128, 8], F32, "lsc")
            lpr = st.sb([128, 2, 64], F32, "lpr")
            P.op("dve", lambda e: e.tensor_tensor(out=lpr.ap[:, 0, :], in0=lv.ap[:, 0, :], in1=lv.ap[:, 1, :], op=ALU.mult), reads=[lv], pwrites=[lpr])
            P.op("dve", lambda e: e.tensor_tensor(out=lpr.ap[:, 1, :], in0=lv.ap[:, 2, :], in1=lv.ap[:, 3, :], op=ALU.mult), reads=[lv], pwrites=[lpr])
            P.op("dve", lambda e: e.tensor_reduce(out=lsc.ap[:, 0:2], in_=lpr.ap[:], axis=mybir.AxisListType.X, op=ALU.add), reads=[lpr], writes=[lsc])
            P.op("act", lambda e: e.activation(out=lsc.ap[:, 2:4], in_=lsc.ap[:, 0:2], func=AF.Exp), reads=[lsc], writes=[lsc])
            P.op("dve", lambda e: e.tensor_tensor(out=lsc.ap[:, 4:5], in0=lsc.ap[:, 3:4], in1=lsc.ap[:, 2:3], op=ALU.subtract), reads=[lsc], writes=[lsc])
            P.op("dve", lambda e: e.tensor_scalar(out=lsc.ap[:, 5:6], in0=lsc.ap[:, 4:5], scalar1=-LAMBDA_INIT, scalar2=None, op0=ALU.add), reads=[lsc], writes=[lsc])
            neglam = lsc.ap[:, 5:6]
            SC = 64 ** -0.5

            KT = [st.sb([128, SEQ], BF16, "KT") for _ in range(2)]
            QT = [st.sb([128, SEQ], BF16, "QT") for _ in range(2)]
            Vh = [st.sb([128, NPT, 128], BF16, "Vh") for _ in range(2)]
            PT = [[st.sb([128, 512], BF16, "PT") for _ in range(3)] for _ in range(2)]
            R = [st.sb([128, 512], F32, "R") for _ in range(2)]
            o12 = [st.sb([128, 512], F32, "o12") for _ in range(2)]
            ob = [st.sb([128, 512], BF16, "ob") for _ in range(2)]
            Sb = [[PS[0], PS[1]], [PS[2], PS[3]]]
            Ob = [PS[4], PS[5]]
            Lb = [PS[6], PS[7]]
            acc0 = st.sb([128, 512], F32, "acc0")
            accb0 = st.sb([128, 512], BF16, "accb0")
            nheads = 8 if upto >= 5.5 else 1
            if _os.environ.get('KD_SKIP_L1B'):
                nheads = 0
            nQ = 16 if upto >= 5.5 else 2
            for h in range(nheads):
                sl_ = h % 2
                kt, qt, vh = KT[sl_], QT[sl_], Vh[sl_]
                for c4 in range(4):
                    cs = slice(c4 * 2048, (c4 + 1) * 2048)
                    P.dma("sp", kt.ap[:, cs], KT_d[h, :, cs], reads=[KT_b], pwrites=[kt])
                    P.dma("sp", qt.ap[:, cs], QT_d[h, :, cs], reads=[QT_b], pwrites=[qt])
                    bs = slice(c4 * 16, (c4 + 1) * 16)
                    P.dma("sp", vh.ap[:, bs, :], V_d[c4 * 2048:(c4 + 1) * 2048, h * 128:(h + 1) * 128].rearrange("(b p) e -> p b e", p=128), reads=[V_b], pwrites=[vh])
                for Q in range(nQ):
                    blocks = [(kb, 0, False) for kb in range(4 * Q)] + [(4 * Q + i_, 128 * i_, True) for i_ in range(4)]
                    n = len(blocks)

                    def issue_S(idx):
                        kb, col0, diag = blocks[idx]
                        for c in range(2):
                            sb_ = Sb[c][idx % 2]
                            rs = slice(c * 64, (c + 1) * 64)
                            P.op("pe", lambda e, sb_=sb_, rs=rs, kb=kb, col0=col0, kt=kt, qt=qt, Q=Q: e.matmul(
                                out=sb_.ap[:, col0:512], lhsT=kt.ap[rs, kb * 128:(kb + 1) * 128], rhs=qt.ap[rs, Q * 512 + col0:(Q + 1) * 512], start=True, stop=True),
                                reads=[kt, qt], writes=[sb_])

                    P.op("dve", lambda e: e.memset(acc0.ap[:], 0.0), writes=[acc0])
                    issue_S(0)
                    for idx in range(n):
                        kb, col0, diag = blocks[idx]
                        for c in range(2):
                            sb_, pt = Sb[c][idx % 2], PT[c][idx % 3]
                            if not diag:
                                P.op("act", lambda e, sb_=sb_, pt=pt: e.activation(out=pt.ap[:], in_=sb_.ap[:], func=AF.Exp, scale=SC), reads=[sb_], writes=[pt])
                            else:
                                P.op("act", lambda e, sb_=sb_, pt=pt, col0=col0: e.activation(out=pt.ap[0:64, col0:512], in_=sb_.ap[0:64, col0:512], func=AF.Exp, scale=SC), reads=[sb_], writes=[pt])
                                P.op("act", lambda e, sb_=sb_, pt=pt, col0=col0: e.activation(out=pt.ap[64:128, col0:col0 + 64], in_=sb_.ap[64:128, col0:col0 + 64], func=AF.Copy, scale=0.0), reads=[sb_], pwrites=[pt])
                                P.op("act", lambda e, sb_=sb_, pt=pt, col0=col0: e.activation(out=pt.ap[64:128, col0 + 64:512], in_=sb_.ap[64:128, col0 + 64:512], func=AF.Exp, scale=SC), reads=[sb_], pwrites=[pt])
                        if idx + 1 < n:
                            issue_S(idx + 1)
                        for c in range(2):
                            pt = PT[c][idx % 3]
                            P.op("pe", lambda e, c=c, pt=pt, kb=kb, col0=col0, idx=idx, diag=diag, vh=vh: e.matmul(
                                out=Ob[c].ap[:, col0:512], lhsT=vh.ap[:, kb, :], rhs=pt.ap[:, col0:512], start=(idx == 0), stop=diag, skip_group_check=True),
                                reads=[vh, pt], writes=[Ob[c]])
                            if c == 1:
                                P.op("pe", lambda e, c=c, pt=pt, col0=col0, idx=idx, diag=diag: e.matmul(
                                    out=Lb[c].ap[:, col0:512], lhsT=ones_b.ap[:], rhs=pt.ap[:, col0:512], start=(idx == 0), stop=diag, skip_group_check=True),
                                    reads=[ones_b, pt], writes=[Lb[c]])
                            else:
                                P.op("dve", lambda e, pt=pt, col0=col0: e.tensor_tensor(out=acc0.ap[:, col0:512], in0=acc0.ap[:, col0:512], in1=pt.ap[:, col0:512], op=ALU.add),
                                     reads=[acc0, pt], writes=[acc0])
                    P.op("dve", lambda e: e.tensor_copy(out=accb0.ap[:], in_=acc0.ap[:]), reads=[acc0], writes=[accb0])
                    P.op("pe", lambda e: e.matmul(out=Lb[0].ap[:], lhsT=ones_b.ap[:], rhs=accb0.ap[:], start=True, stop=True), reads=[ones_b, accb0], writes=[Lb[0]])
                    for c in range(2):
                        P.op("dve", lambda e, c=c: e.reciprocal(out=R[c].ap[:], in_=Lb[c].ap[:]), reads=[Lb[c]], writes=[R[c]])
                        P.op("dve", lambda e, c=c: e.tensor_tensor(out=o12[c].ap[:], in0=Ob[c].ap[:], in1=R[c].ap[:], op=ALU.mult), reads=[Ob[c], R[c]], writes=[o12[c]])
                    obq = ob[Q % 2]
                    P.op("dve", lambda e, obq=obq: e.scalar_tensor_tensor(out=obq.ap[:], in0=o12[1].ap[:], scalar=neglam, in1=o12[0].ap[:], op0=ALU.mult, op1=ALU.add),
                         reads=[o12[0], o12[1], lsc], writes=[obq])
                    P.dma("sp", OT_d[h, :, Q * 512:(Q + 1) * 512], obq.ap[:], reads=[obq], pwrites=[OT_b])

          with Stage(P, "l1s") as st:
            lv = st.sb([128, 4, 64], F32, "lv")
            for k_i, nm in enumerate(["diff_lambda_q1", "diff_lambda_k1", "diff_lambda_q2", "diff_lambda_k2"]):
                P.dma("sp", lv.ap[:, k_i, :], W[nm].partition_broadcast(128), pwrites=[lv])
            lsc = st.sb([128, 8], F32, "lsc")
            lpr = st.sb([128, 2, 64], F32, "lpr")
            P.op("dve", lambda e: e.tensor_tensor(out=lpr.ap[:, 0, :], in0=lv.ap[:, 0, :], in1=lv.ap[:, 1, :], op=ALU.mult), reads=[lv], pwrites=[lpr])
            P.op("dve", lambda e: e.tensor_tensor(out=lpr.ap[:, 1, :], in0=lv.ap[:, 2, :], in1=lv.ap[:, 3, :], op=ALU.mult), reads=[lv], pwrites=[lpr])
            P.op("dve", lambda e: e.tensor_reduce(out=lsc.ap[:, 0:2], in_=lpr.ap[:], axis=mybir.AxisListType.X, op=ALU.add), reads=[lpr], writes=[lsc])
            P.op("act", lambda e: e.activation(out=lsc.ap[:, 2:4], in_=lsc.ap[:, 0:2], func=AF.Exp), reads=[lsc], writes=[lsc])
            P.op("dve", lambda e: e.tensor_tensor(out=lsc.ap[:, 4:5], in0=lsc.ap[:, 3:4], in1=lsc.ap[:, 2:3], op=ALU.subtract), reads=[lsc], writes=[lsc])
            P.op("dve", lambda e: e.tensor_scalar(out=lsc.ap[:, 5:6], in0=lsc.ap[:, 4:5], scalar1=-LAMBDA_INIT, scalar2=None, op0=ALU.add), reads=[lsc], writes=[lsc])
            neglam = lsc.ap[:, 5:6]
            SC = 64 ** -0.5
            zer = st.sb([128, 512], BF16, "zer")
            P.op("pool", lambda e: e.memset(zer.ap[:], 0.0), writes=[zer])
            Kt = [st.sb([128, D], BF16, "Kt") for _ in range(2)]
            Vt = [st.sb([128, D], BF16, "Vt") for _ in range(2)]
            KTb = [st.sb([128, 8, 128], BF16, "KTb") for _ in range(2)]
            QTs = st.sb([128, 8, 64], BF16, "QTs")
            PTs = [st.sb([128, 2, 512], BF16, "PTs") for _ in range(2)]
            Rr = st.sb([128, 1024], F32, "Rr")
            oo = st.sb([128, 1024], F32, "oo")
            obs = st.sb([128, 8, 64], BF16, "obs")
            Sbk, Obk, Lbk, Tbk = [PS[0], PS[1]], [PS[2], PS[3]], [PS[4], PS[5]], PS[6]
            NKB = PAST // 128
            for s in range(0 if _os.environ.get('KD_SKIP_L1S') else 2):
                tok0 = NPT * 128 + s * 64
                P.dma("sp", QTs.ap[:], QT_d[:, :, tok0:tok0 + 64].rearrange("h p t -> p h t"), reads=[QT_b], writes=[QTs])
                for b in Obk + Lbk:
                    P.op("pe", lambda e, b=b: e.matmul(out=b.ap[:], lhsT=zer.ap[:, 0:128], rhs=zer.ap[:], start=True, stop=False, skip_group_check=True), reads=[zer], writes=[b])
                _part = int(_os.environ.get('KD_L1S_PART', '9'))
                _kbs = [int(v) for v in _os.environ.get('KD_L1S_KBS', '').split(',')] if _os.environ.get('KD_L1S_KBS') else list(range(NKB + 1))
                for kb in _kbs:
                    i = kb % 2
                    last = (kb == NKB)
                    nk = 64 if last else 128
                    ktb = KTb[i]
                    if not last:
                        P.dma("pool", Kt[i].ap[:], I["cdk"][s, kb * 128:(kb + 1) * 128, :], writes=[Kt[i]])
                        P.dma("pool", Vt[i].ap[:], I["cdv"][s, kb * 128:(kb + 1) * 128, :], writes=[Vt[i]])
                        tv = Tbk.ap[:].bitcast(BF16).rearrange("p (h t) -> p h t", t=128)
                        for h in range(8):
                            P.op("pe", lambda e, tv=tv, h=h, i=i: e.transpose(out=tv[:, h, :], in_=Kt[i].ap[:, h * 128:(h + 1) * 128], identity=ident_b.ap[:]), reads=[Kt[i], ident_b], writes=[Tbk])
                        P.op("dve", lambda e, tv=tv, ktb=ktb: e.tensor_copy(out=ktb.ap[:], in_=tv), reads=[Tbk], writes=[ktb])
                    else:
                        P.dma("sp", ktb.ap[:, :, 0:64], KT_d[:, :, tok0:tok0 + 64].rearrange("h p t -> p h t"), reads=[KT_b], writes=[ktb])
                        P.dma("pool", Vt[i].ap[0:64, :], V_d[tok0:tok0 + 64, :], reads=[V_b], writes=[Vt[i]])
                    if _part < 2:
                        continue
                    for h in range(8):
                        for c in range(2):
                            sb_ = Sbk[c]
                            rs = slice(c * 64, (c + 1) * 64)
                            P.op("pe", lambda e, sb_=sb_, rs=rs, h=h, nk=nk, ktb=ktb: e.matmul(
                                out=sb_.ap[0:nk, h * 64:h * 64 + 64], lhsT=ktb.ap[rs, h, 0:nk], rhs=QTs.ap[rs, h, :], start=True, stop=True),
                                reads=[ktb, QTs], writes=[sb_])
                    pts = PTs[i]
                    for j in range(2):
                        P.op("act", lambda e, j=j, nk=nk, pts=pts: e.activation(out=pts.ap[0:nk, j, :], in_=Sbk[j].ap[0:nk, :], func=AF.Exp, scale=SC), reads=[Sbk[j]], pwrites=[pts])
                    if _part < 3:
                        continue
                    for h in range(8):
                        for c in range(2):
                            j, cs = c, slice(h * 64, h * 64 + 64)
                            P.op("pe", lambda e, j=j, cs=cs, h=h, nk=nk, i=i, pts=pts: e.matmul(
                                out=Obk[j].ap[:, cs], lhsT=Vt[i].ap[0:nk, h * 128:(h + 1) * 128], rhs=pts.ap[0:nk, j, cs], start=False, stop=True, skip_group_check=True),
                                reads=[Vt[i], pts], writes=[Obk[j]])
                            P.op("pe", lambda e, j=j, cs=cs, nk=nk, pts=pts: e.matmul(
                                out=Lbk[j].ap[:, cs], lhsT=ones_b.ap[0:nk, :], rhs=pts.ap[0:nk, j, cs], start=False, stop=True, skip_group_check=True),
                                reads=[ones_b, pts], writes=[Lbk[j]])
                if _part < 4:
                    continue
                for j in range(2):
                    js = slice(j * 512, (j + 1) * 512)
                    P.op("dve", lambda e, j=j, js=js: e.reciprocal(out=Rr.ap[:, js], in_=Lbk[j].ap[:]), reads=[Lbk[j]], pwrites=[Rr])
                    P.op("dve", lambda e, j=j, js=js: e.tensor_tensor(out=oo.ap[:, js], in0=Obk[j].ap[:], in1=Rr.ap[:, js], op=ALU.mult), reads=[Obk[j], Rr], pwrites=[oo])
                ov = oo.ap[:].rearrange("p (c h q) -> p c h q", c=2, q=64)
                P.op("dve", lambda e, ov=ov: e.scalar_tensor_tensor(out=obs.ap[:], in0=ov[:, 1, :, :], scalar=neglam, in1=ov[:, 0, :, :], op0=ALU.mult, op1=ALU.add),
                     reads=[oo, lsc], writes=[obs])
                P.dma("sp", OT_d[:, :, tok0:tok0 + 64].rearrange("h p t -> p h t"), obs.ap[:], reads=[obs], pwrites=[OT_b])

        def attn_out_stage(name, l, OT_src, OT_srcb, w_o_name, nh, e_dim, subnorm, gsub_name, src_w, dst_w, tile_ok):
          with Stage(P, name) as st:
            nk = nh * e_dim // 128
            wo = st.sb([128, nk, D], BF16, "wo")
            if subnorm:
                gsub = st.sb([128, 1], F32, "gsub")
                P.dma("sp", gsub.ap[:], W[gsub_name].rearrange("(p o) -> p o", o=1), writes=[gsub])
                wstg = [st.sb([128, D], F32, "wstg") for _ in range(2)]
                for kc in range(nk):
                    ws = wstg[kc % 2]
                    P.dma("sp", ws.ap[:], W[w_o_name][kc * 128:(kc + 1) * 128, :], writes=[ws])
                    P.op("dve", lambda e, ws=ws, kc=kc: e.tensor_scalar(out=wo.ap[:, kc, :], in0=ws.ap[:], scalar1=gsub.ap[:, 0:1], scalar2=(1.0 - LAMBDA_INIT), op0=ALU.mult, op1=ALU.mult),
                         reads=[ws, gsub], pwrites=[wo])
            else:
                for kc in range(nk):
                    P.dma("pool", wo.ap[:, kc, :], W[w_o_name][kc * 128:(kc + 1) * 128, :], pwrites=[wo])
            gate = Gate(st, l, 0)
            xt = [st.sb([128, D], F32, "x") for _ in range(2)]
            ot = [st.sb([128, nk, 128], BF16, "ot") for _ in range(2)]
            sq = st.sb([128, nk, 128], BF16, "sq")
            rs_ = st.sb([128, nk * 128], F32, "rs")
            on = [st.sb([128, nk, 128], BF16, "on") for _ in range(2)]
            for t in range(NT):
                if not tile_ok(t):
                    continue
                is_s = (t == NPT)
                i = t % 2
                x, o_ = xt[i], ot[i]
                src, srcb = x_src(t, src_w)
                P.dma("sp", x.ap[:], src, reads=[srcb], writes=[x])
                P.dma("sp", o_.ap[:], OT_src[:, :, t * 128:(t + 1) * 128].rearrange("h p t -> p h t"), reads=[OT_srcb], writes=[o_])
                if subnorm:
                    P.op("act", lambda e, o_=o_: e.activation(out=sq.ap[:], in_=o_.ap[:], func=AF.Square), reads=[o_], writes=[sq])
                    for j in range(2):
                        P.op("pe", lambda e, j=j: e.matmul(out=PS[j].ap[:], lhsT=ones_b.ap[:], rhs=sq.ap[:, j * 4:(j + 1) * 4, :].rearrange("p h t -> p (h t)"), start=True, stop=True),
                             reads=[ones_b, sq], writes=[PS[j]])
                        js = slice(j * 512, (j + 1) * 512)
                        P.op("act", lambda e, j=j, js=js: e.activation(out=rs_.ap[:, js], in_=PS[j].ap[:], func=AF.Sqrt, scale=1.0 / 128, bias=EPS), reads=[PS[j]], pwrites=[rs_])
                    P.op("dve", lambda e: e.reciprocal(out=rs_.ap[:], in_=rs_.ap[:]), reads=[rs_], writes=[rs_])
                    lhs = on[i]
                    P.op("dve", lambda e, o_=o_, lhs=lhs: e.tensor_tensor(out=lhs.ap[:].rearrange("p h t -> p (h t)"), in0=o_.ap[:].rearrange("p h t -> p (h t)"), in1=rs_.ap[:], op=ALU.mult),
                         reads=[o_, rs_], writes=[lhs])
                else:
                    lhs = o_
                g = gate.get(is_s)
                for cbk in range(2):
                    po = PS[2 + cbk + 2 * (t % 2)]
                    for kc in range(nk):
                        P.op("pe", lambda e, po=po, kc=kc, cbk=cbk, lhs=lhs: e.matmul(out=po.ap[:], lhsT=lhs.ap[:, kc, :], rhs=wo.ap[:, kc, cbk * 512:(cbk + 1) * 512], start=(kc == 0), stop=(kc == nk - 1)),
                             reads=[lhs, wo], writes=[po])
                    sl = slice(cbk * 512, (cbk + 1) * 512)
                    P.op("dve", lambda e, po=po, sl=sl, g=g: e.tensor_tensor(out=po.ap[:], in0=po.ap[:], in1=g.ap[:, sl], op=ALU.mult), reads=[po, g], writes=[po])
                    P.op("dve", lambda e, po=po, sl=sl, x=x: e.tensor_tensor(out=x.ap[:, sl], in0=po.ap[:], in1=x.ap[:, sl], op=ALU.add), reads=[po, x], pwrites=[x])
                dst, dstb = x_dst(t, dst_w)
                P.dma("sp", dst, x.ap[:], reads=[x], pwrites=[dstb])

        if en(6):
            attn_out_stage("l1c", 1, OT_d, OT_b, "diff_w_o", 8, 128, True, "diff_g_sub", "b", "a", lambda t: True)
        if en(7):
            mlp_stage(1, "a", "b")

        QA_d = dscr("QA_d", [16, 128, NTOK], BF16)
        QR_d = dscr("QR_d", [16, 32, NTOK], BF16)
        CK_d = dscr("CK_d", [NTOK, 128], BF16)
        CKT_d = dscr("CKT_d", [128, NTOK], BF16)
        KRT_d = dscr("KRT_d", [32, NTOK], BF16)
        OT2_d = dscr("OT2_d", [8, 128, NTOK], BF16)
        QA_b, QR_b, CK_b, CKT_b, KRT_b, OT2_b = (Buf(QA_d, "QA_d"), Buf(QR_d, "QR_d"), Buf(CK_d, "CK_d"), Buf(CKT_d, "CKT_d"),
                                                 Buf(KRT_d, "KRT_d"), Buf(OT2_d, "OT2_d"))
        MSC = 96 ** -0.5

        if en(8):
          with Stage(P, "l2a") as st:
            cos2, sin2 = rope_tables(st, 16)
            wdq = st.sb([128, 8, 416], BF16, "wdq")
            for kc in range(8):
                P.dma("pool", wdq.ap[:, kc, 0:256], W["mla_w_dq"][kc * 128:(kc + 1) * 128, :], pwrites=[wdq])
                P.dma("pool", wdq.ap[:, kc, 256:416], W["mla_w_dkv"][kc * 128:(kc + 1) * 128, :], pwrites=[wdq])
            gq = st.sb([128, 2], F32, "gq")
            P.dma("sp", gq.ap[:], W["mla_g_q"].rearrange("(c p) -> p c", p=128), writes=[gq], allow_slow_non_contiguous=True)
            wuq = st.sb([128, 2, 1536], BF16, "wuq")
            wst = [st.sb([128, 1536], F32, "wst") for _ in range(2)]
            for kc in range(2):
                P.dma("sp", wst[kc].ap[:], W["mla_w_uq"][kc * 128:(kc + 1) * 128, :], writes=[wst[kc]])
                P.op("dve", lambda e, kc=kc: e.tensor_scalar(out=wuq.ap[:, kc, :], in0=wst[kc].ap[:], scalar1=gq.ap[:, kc:kc + 1], scalar2=None, op0=ALU.mult), reads=[wst[kc], gq], pwrites=[wuq])
            wuk = st.sb([128, D], BF16, "wuk")
            P.dma("pool", wuk.ap[:], W["mla_w_uk"], writes=[wuk])
            wukT = st.sb([64, 16, 128], BF16, "wukT")
            for g4 in range(2):
                pb = PS[4 + g4]
                pv = pb.ap[:].bitcast(BF16).rearrange("p (h t) -> p h t", t=128)
                for u in range(8):
                    h = g4 * 8 + u
                    P.op("pe", lambda e, pv=pv, u=u, h=h: e.transpose(out=pv[0:64, u, :], in_=wuk.ap[:, h * 64:(h + 1) * 64], identity=ident_b.ap[:]), reads=[wuk, ident_b], writes=[pb])
                P.op("dve", lambda e, pv=pv, g4=g4: e.tensor_copy(out=wukT.ap[:, g4 * 8:(g4 + 1) * 8, :], in_=pv[0:64, :, :]), reads=[pb], pwrites=[wukT])
            gkv = st.sb([128, 128], F32, "gkv")
            P.dma("sp", gkv.ap[:], W["mla_g_kv"].partition_broadcast(128), writes=[gkv])
            xt = [st.sb([128, D], F32, "x") for _ in range(2)]
            nctx = NormCtx(st, PS[0], PS[7])
            hTs = [st.sb([128, 8, 128], BF16, "hT") for _ in range(2)]
            st4 = st.sb([128, 8], F32, "st4")
            jk = st.sb([128, 256], BF16, "jk")
            qn = st.sb([128, 256], BF16, "qn")
            qnT = st.sb([128, 2, 128], BF16, "qnT")
            qb = st.sb([128, 16, 96], BF16, "qb")
            qT = st.sb([96, 16, 128], BF16, "qT")
            qa = [st.sb([128, 16, 128], BF16, "qa") for _ in range(2)]
            ckf = [st.sb([128, 128], F32, "ckf") for _ in range(2)]
            ckb = [st.sb([128, 128], BF16, "ckb") for _ in range(2)]
            krf = [st.sb([128, 32], F32, "krf") for _ in range(2)]
            krb = [st.sb([128, 32], BF16, "krb") for _ in range(2)]
            ckT = [st.sb([128, 128], BF16, "ckT") for _ in range(2)]
            krT = [st.sb([32, 128], BF16, "krT") for _ in range(2)]
            tmps = [st.sb([128, 256], F32, "rt") for _ in range(4)]
            for t in range(NT):
                is_s = (t == NPT)
                i = t % 2
                x, hT = xt[i], hTs[i]
                src, srcb = x_src(t, "b")
                P.dma("sp", x.ap[:], src, reads=[srcb], writes=[x])
                nctx.run(x, x.ap[:], hT, hT.ap[:], 2, 0, is_s)
                pp = PS[1]
                for kc in range(8):
                    P.op("pe", lambda e, kc=kc, hT=hT: e.matmul(out=pp.ap[:, 0:416], lhsT=hT.ap[:, kc, :], rhs=wdq.ap[:, kc, :], start=(kc == 0), stop=(kc == 7)), reads=[hT, wdq], writes=[pp])
                P.op("act", lambda e: e.activation(out=jk.ap[:], in_=pp.ap[:, 0:256], func=AF.Square, accum_out=st4.ap[:, 0:1]), reads=[pp], writes=[jk, st4])
                P.op("act", lambda e: e.activation(out=st4.ap[:, 1:2], in_=st4.ap[:, 0:1], func=AF.Sqrt, scale=1.0 / 256, bias=EPS), reads=[st4], writes=[st4])
                P.op("dve", lambda e: e.reciprocal(out=st4.ap[:, 2:3], in_=st4.ap[:, 1:2]), reads=[st4], writes=[st4])
                P.op("act", lambda e: e.activation(out=qn.ap[:], in_=pp.ap[:, 0:256], func=AF.Copy, scale=st4.ap[:, 2:3]), reads=[pp, st4], writes=[qn])
                P.op("act", lambda e: e.activation(out=jk.ap[:, 0:128], in_=pp.ap[:, 256:384], func=AF.Square, accum_out=st4.ap[:, 4:5]), reads=[pp], writes=[jk, st4])
                P.op("act", lambda e: e.activation(out=st4.ap[:, 5:6], in_=st4.ap[:, 4:5], func=AF.Sqrt, scale=1.0 / 128, bias=EPS), reads=[st4], writes=[st4])
                P.op("dve", lambda e: e.reciprocal(out=st4.ap[:, 6:7], in_=st4.ap[:, 5:6]), reads=[st4], writes=[st4])
                P.op("dve", lambda e, i=i: e.scalar_tensor_tensor(out=ckf[i].ap[:], in0=pp.ap[:, 256:384], scalar=st4.ap[:, 6:7], in1=gkv.ap[:], op0=ALU.mult, op1=ALU.mult),
                     reads=[pp, st4, gkv], writes=[ckf[i]])
                P.op("dve", lambda e, i=i: e.tensor_copy(out=ckb[i].ap[:], in_=ckf[i].ap[:]), reads=[ckf[i]], writes=[ckb[i]])
                rope_apply(pp, pp.ap[:, 384:416].rearrange("p (g d) -> p g d", g=1), krf[i], krf[i].ap[:].rearrange("p (g d) -> p g d", g=1), 1, 16, cos2, sin2, t, tmps)
                P.op("dve", lambda e, i=i: e.tensor_copy(out=krb[i].ap[:], in_=krf[i].ap[:]), reads=[krf[i]], writes=[krb[i]])
                if not is_s:
                    P.dma("sp", O["ckvp"][t * 128:(t + 1) * 128, :], ckf[i].ap[:], reads=[ckf[i]], pwrites=[out_bufs["ckvp"]])
                    P.dma("sp", O["krp"][t * 128:(t + 1) * 128, :], krf[i].ap[:], reads=[krf[i]], pwrites=[out_bufs["krp"]])
                else:
                    P.dma("sp", O["ckvs"], ckf[i].ap[:], reads=[ckf[i]], pwrites=[out_bufs["ckvs"]])
                    P.dma("sp", O["krs"], krf[i].ap[:], reads=[krf[i]], pwrites=[out_bufs["krs"]])
                P.dma("sp", CK_d[t * 128:(t + 1) * 128, :], ckb[i].ap[:], reads=[ckb[i]], pwrites=[CK_b])
                p6 = PS[6]
                p6v = p6.ap[:].bitcast(BF16)
                P.op("pe", lambda e, i=i: e.transpose(out=p6v[:, 0:128], in_=ckb[i].ap[:], identity=ident_b.ap[:]), reads=[ckb[i], ident_b], writes=[p6])
                P.op("pe", lambda e, i=i: e.transpose(out=p6v[0:32, 128:256], in_=krb[i].ap[:], identity=ident_b.ap[:]), reads=[krb[i], ident_b], writes=[p6])
                for kc in range(2):
                    P.op("pe", lambda e, kc=kc: e.transpose(out=p6v[:, 256 + kc * 128:384 + kc * 128], in_=qn.ap[:, kc * 128:(kc + 1) * 128], identity=ident_b.ap[:]), reads=[qn, ident_b], writes=[p6])
                P.op("dve", lambda e, i=i: e.tensor_copy(out=ckT[i].ap[:], in_=p6v[:, 0:128]), reads=[p6], writes=[ckT[i]])
                P.op("dve", lambda e, i=i: e.tensor_copy(out=krT[i].ap[:], in_=p6v[0:32, 128:256]), reads=[p6], writes=[krT[i]])
                P.op("dve", lambda e: e.tensor_copy(out=qnT.ap[:], in_=p6v[:, 256:512].rearrange("p (k t) -> p k t", t=128)), reads=[p6], writes=[qnT])
                P.dma("sp", CKT_d[:, t * 128:(t + 1) * 128], ckT[i].ap[:], reads=[ckT[i]], pwrites=[CKT_b])
                P.dma("sp", KRT_d[:, t * 128:(t + 1) * 128], krT[i].ap[:], reads=[krT[i]], pwrites=[KRT_b])
                for qblk in range(4):
                    pq = PS[2 + qblk % 2]
                    for kc in range(2):
                        P.op("pe", lambda e, pq=pq, kc=kc, qblk=qblk: e.matmul(out=pq.ap[:, 0:384], lhsT=qnT.ap[:, kc, :], rhs=wuq.ap[:, kc, qblk * 384:(qblk + 1) * 384], start=(kc == 0), stop=(kc == 1)),
                             reads=[qnT, wuq], writes=[pq])
                    pqv = pq.ap[:, 0:384].rearrange("p (h d) -> p h d", d=96)
                    hs = slice(qblk * 4, (qblk + 1) * 4)
                    P.op("act", lambda e, pqv=pqv, hs=hs: e.activation(out=qb.ap[:, hs, 0:64], in_=pqv[:, :, 0:64], func=AF.Copy), reads=[pq], pwrites=[qb])
                    rope_apply(pq, pqv[:, :, 64:96], qb, qb.ap[:, hs, 64:96], 4, 16, cos2, sin2, t, tmps)
                for g4 in range(2):
                    pb = PS[4 + g4]
                    pv = pb.ap[:].bitcast(BF16).rearrange("p (h t) -> p h t", t=128)
                    for u in range(8):
                        h = g4 * 8 + u
                        P.op("pe", lambda e, pv=pv, u=u, h=h: e.transpose(out=pv[0:96, u, :], in_=qb.ap[:, h, :], identity=ident_b.ap[:]), reads=[qb, ident_b], writes=[pb])
                    if g4 == 0:
                        P.op("act", lambda e, pv=pv, g4=g4: e.activation(out=qT.ap[:, g4 * 8:(g4 + 1) * 8, :], in_=pv[0:96, :, :], func=AF.Copy), reads=[pb], pwrites=[qT])
                    else:
                        P.op("dve", lambda e, pv=pv, g4=g4: e.tensor_copy(out=qT.ap[:, g4 * 8:(g4 + 1) * 8, :], in_=pv[0:96, :, :]), reads=[pb], pwrites=[qT])
                P.dma("sp", QR_d[:, :, t * 128:(t + 1) * 128].rearrange("h p t -> p h t"), qT.ap[64:96, :, :], reads=[qT], pwrites=[QR_b])
                qa_ = qa[i]
                for g4 in range(4):
                    pa = PS[2 + g4 % 2]
                    for u in range(4):
                        h = g4 * 4 + u
                        P.op("pe", lambda e, pa=pa, u=u, h=h: e.matmul(out=pa.ap[:, u * 128:(u + 1) * 128], lhsT=wukT.ap[:, h, :], rhs=qT.ap[0:64, h, :], start=True, stop=True), reads=[wukT, qT], writes=[pa])
                    if g4 % 2 == 0:
                        P.op("act", lambda e, pa=pa, g4=g4, qa_=qa_: e.activation(out=qa_.ap[:, g4 * 4:(g4 + 1) * 4, :], in_=pa.ap[:].rearrange("p (u t) -> p u t", t=128), func=AF.Copy), reads=[pa], pwrites=[qa_])
                    else:
                        P.op("dve", lambda e, pa=pa, g4=g4, qa_=qa_: e.tensor_copy(out=qa_.ap[:, g4 * 4:(g4 + 1) * 4, :], in_=pa.ap[:].rearrange("p (u t) -> p u t", t=128)), reads=[pa], pwrites=[qa_])
                P.dma("sp", QA_d[:, :, t * 128:(t + 1) * 128].rearrange("h p t -> p h t"), qa_.ap[:], reads=[qa_], pwrites=[QA_b])

        if en(9):
          with Stage(P, "l2b") as st:
            wuv = st.sb([128, D], BF16, "wuv")
            P.dma("pool", wuv.ap[:], W["mla_w_uv"], writes=[wuv])
            CKT = st.sb([128, SEQ], BF16, "CKT")
            KRT = st.sb([128, SEQ], BF16, "KRT")
            CK = st.sb([128, NPT, 128], BF16, "CK")
            P.op("pool", lambda e: e.memset(KRT.ap[:], 0.0), writes=[KRT])
            for c4 in range(4):
                cs = slice(c4 * 2048, (c4 + 1) * 2048)
                P.dma("sp", CKT.ap[:, cs], CKT_d[:, cs], reads=[CKT_b], pwrites=[CKT])
                P.dma("sp", KRT.ap[0:32, cs], KRT_d[:, cs], reads=[KRT_b], pwrites=[KRT])
                P.dma("sp", CK.ap[:, c4 * 16:(c4 + 1) * 16, :], CK_d[c4 * 2048:(c4 + 1) * 2048, :].rearrange("(b p) e -> p b e", p=128), reads=[CK_b], pwrites=[CK])
            QA = [st.sb([128, SEQ], BF16, "QA") for _ in range(2)]
            QR = [st.sb([128, SEQ], BF16, "QR") for _ in range(2)]
            for b in QR:
                P.op("pool", lambda e, b=b: e.memset(b.ap[:], 0.0), writes=[b])
            PT = [st.sb([128, 512], BF16, "PT") for _ in range(4)]
            Rb = st.sb([128, 512], F32, "Rb")
            ol = st.sb([128, 512], BF16, "ol")
            ohb = [st.sb([64, 512], BF16, "ohb") for _ in range(2)]
            Sb, Obs, Lb, Eb = [PS[0], PS[1], PS[5], PS[6]], [PS[2], PS[3]], PS[7], PS[4]
            accs = [[st.sb([128, 512], F32, "acc") for _ in range(2)] for _ in range(2)]
            accb = st.sb([128, 512], BF16, "accb")
            aeng = ["dve", "dve"]
            nheads = 16 if upto >= 9.5 else 1
            nQ = 16 if upto >= 9.5 else 2
            for h in range(nheads):
                qa_, qr_ = QA[h % 2], QR[h % 2]
                for c4 in range(4):
                    cs = slice(c4 * 2048, (c4 + 1) * 2048)
                    P.dma("sp", qa_.ap[:, cs], QA_d[h, :, cs], reads=[QA_b], pwrites=[qa_])
                    P.dma("sp", qr_.ap[0:32, cs], QR_d[h, :, cs], reads=[QR_b], pwrites=[qr_])
                for Q in range(nQ):
                    blocks = [(kb, 0, False) for kb in range(4 * Q)] + [(4 * Q + i_, 128 * i_, True) for i_ in range(4)]
                    n = len(blocks)
                    qp = Q % 2
                    Ob = Obs[qp]
                    for k_ in range(2):
                        P.op(aeng[k_], lambda e, k_=k_, qp=qp: e.memset(accs[k_][qp].ap[:], 0.0), writes=[accs[k_][qp]])

                    def issue_S(idx, Q=Q, qa_=qa_, qr_=qr_, blocks=blocks):
                        kb, col0, diag = blocks[idx]
                        sb_ = Sb[idx % 4]
                        P.op("pe", lambda e, sb_=sb_, kb=kb, col0=col0: e.matmul(out=sb_.ap[:, col0:512], lhsT=CKT.ap[:, kb * 128:(kb + 1) * 128], rhs=qa_.ap[:, Q * 512 + col0:(Q + 1) * 512], start=True, stop=False),
                             reads=[CKT, qa_], writes=[sb_])
                        P.op("pe", lambda e, sb_=sb_, kb=kb, col0=col0: e.matmul(out=sb_.ap[:, col0:512], lhsT=KRT.ap[:, kb * 128:(kb + 1) * 128], rhs=qr_.ap[:, Q * 512 + col0:(Q + 1) * 512], start=False, stop=True),
                             reads=[KRT, qr_], writes=[sb_])

                    issue_S(0)
                    if n > 1:
                        issue_S(1)
                    for idx in range(n):
                        kb, col0, diag = blocks[idx]
                        sb_, pt = Sb[idx % 4], PT[idx % 4]
                        if not diag:
                            P.op("act", lambda e, sb_=sb_, pt=pt: e.activation(out=pt.ap[:], in_=sb_.ap[:], func=AF.Exp, scale=MSC), reads=[sb_], writes=[pt])
                        else:
                            P.op("act", lambda e, sb_=sb_, pt=pt, col0=col0: e.activation(out=pt.ap[0:64, col0:512], in_=sb_.ap[0:64, col0:512], func=AF.Exp, scale=MSC), reads=[sb_], writes=[pt])
                            P.op("act", lambda e, sb_=sb_, pt=pt, col0=col0: e.activation(out=pt.ap[64:128, col0:col0 + 64], in_=sb_.ap[64:128, col0:col0 + 64], func=AF.Copy, scale=0.0), reads=[sb_], pwrites=[pt])
                            P.op("act", lambda e, sb_=sb_, pt=pt, col0=col0: e.activation(out=pt.ap[64:128, col0 + 64:512], in_=sb_.ap[64:128, col0 + 64:512], func=AF.Exp, scale=MSC), reads=[sb_], pwrites=[pt])
                        if idx + 2 < n:
                            issue_S(idx + 2)
                        P.op("pe", lambda e, Ob=Ob, pt=pt, kb=kb, col0=col0, idx=idx, diag=diag: e.matmul(out=Ob.ap[:, col0:512], lhsT=CK.ap[:, kb, :], rhs=pt.ap[:, col0:512], start=(idx == 0), stop=diag, skip_group_check=True),
                             reads=[CK, pt], writes=[Ob])
                        ac = accs[idx % 2][qp]
                        P.op(aeng[idx % 2], lambda e, ac=ac, pt=pt, col0=col0: e.tensor_tensor(out=ac.ap[:, col0:512], in0=ac.ap[:, col0:512], in1=pt.ap[:, col0:512], op=ALU.add),
                             reads=[ac, pt], writes=[ac])
                    a0, a1 = accs[0][qp], accs[1][qp]
                    P.op("dve", lambda e, a0=a0, a1=a1: e.tensor_tensor(out=accb.ap[:], in0=a0.ap[:], in1=a1.ap[:], op=ALU.add), reads=[a0, a1], writes=[accb])
                    P.op("pe", lambda e: e.matmul(out=Lb.ap[:], lhsT=ones_b.ap[:], rhs=accb.ap[:], start=True, stop=True), reads=[ones_b, accb], writes=[Lb])
                    P.op("dve", lambda e: e.reciprocal(out=Rb.ap[:], in_=Lb.ap[:]), reads=[Lb], writes=[Rb])
                    P.op("dve", lambda e, Ob=Ob: e.tensor_tensor(out=ol.ap[:], in0=Ob.ap[:], in1=Rb.ap[:], op=ALU.mult), reads=[Ob, Rb], writes=[ol])
                    P.op("pe", lambda e, h=h: e.matmul(out=Eb.ap[0:64, :], lhsT=wuv.ap[:, h * 64:(h + 1) * 64], rhs=ol.ap[:], start=True, stop=True), reads=[wuv, ol], writes=[Eb])
                    oh = ohb[Q % 2]
                    P.op("act", lambda e, oh=oh: e.activation(out=oh.ap[:], in_=Eb.ap[0:64, :], func=AF.Copy), reads=[Eb], writes=[oh])
                    P.dma("sp", OT2_d[h // 2, (h % 2) * 64:(h % 2) * 64 + 64, Q * 512:(Q + 1) * 512], oh.ap[:], reads=[oh], pwrites=[OT2_b])

          with Stage(P, "l2s") as st:
            wuv = st.sb([128, D], BF16, "wuv")
            P.dma("pool", wuv.ap[:], W["mla_w_uv"], writes=[wuv])
            zer = st.sb([128, 512], BF16, "zer")
            P.op("pool", lambda e: e.memset(zer.ap[:], 0.0), writes=[zer])
            ckr = [st.sb([128, 128], BF16, "ckr") for _ in range(2)]
            krr = [st.sb([128, 32], BF16, "krr") for _ in range(2)]
            ckT = [st.sb([128, 128], BF16, "ckT") for _ in range(2)]
            krT = [st.sb([128, 128], BF16, "krT") for _ in range(2)]
            for b in krT:
                P.op("pool", lambda e, b=b: e.memset(b.ap[:], 0.0), writes=[b])
            QAs = st.sb([128, 16, 64], BF16, "QAs")
            QRs = st.sb([128, 16, 64], BF16, "QRs")
            P.op("pool", lambda e: e.memset(QRs.ap[:], 0.0), writes=[QRs])
            PTs = [st.sb([128, 1024], BF16, "PTs") for _ in range(2)]
            Rr = st.sb([128, 1024], F32, "Rr")
            olb = st.sb([128, 1024], BF16, "olb")
            ohs = st.sb([64, 16, 64], BF16, "ohs")
            Sbk, Obk, Lbk, Tbk, Ebk = [PS[0], PS[1]], [PS[2], PS[3]], [PS[4], PS[5]], PS[6], PS[7]
            NKB = PAST // 128
            for s in range(2):
                tok0 = NPT * 128 + s * 64
                P.dma("sp", QAs.ap[:], QA_d[:, :, tok0:tok0 + 64].rearrange("h p t -> p h t"), reads=[QA_b], writes=[QAs])
                P.dma("sp", QRs.ap[0:32, :, :], QR_d[:, :, tok0:tok0 + 64].rearrange("h p t -> p h t"), reads=[QR_b], pwrites=[QRs])
                for b in Obk + Lbk:
                    P.op("pe", lambda e, b=b: e.matmul(out=b.ap[:], lhsT=zer.ap[:, 0:128], rhs=zer.ap[:], start=True, stop=False, skip_group_check=True), reads=[zer], writes=[b])
                for kb in range(NKB + 1):
                    i = kb % 2
                    last = (kb == NKB)
                    nk = 64 if last else 128
                    if not last:
                        P.dma("pool", ckr[i].ap[:], I["cck"][s, kb * 128:(kb + 1) * 128, :], writes=[ckr[i]])
                        P.dma("pool", krr[i].ap[:], I["ckr"][s, kb * 128:(kb + 1) * 128, :], writes=[krr[i]])
                        tv = Tbk.ap[:].bitcast(BF16)
                        P.op("pe", lambda e, tv=tv, i=i: e.transpose(out=tv[:, 0:128], in_=ckr[i].ap[:], identity=ident_b.ap[:]), reads=[ckr[i], ident_b], writes=[Tbk])
                        P.op("pe", lambda e, tv=tv, i=i: e.transpose(out=tv[0:32, 128:256], in_=krr[i].ap[:], identity=ident_b.ap[:]), reads=[krr[i], ident_b], writes=[Tbk])
                        P.op("dve", lambda e, tv=tv, i=i: e.tensor_copy(out=ckT[i].ap[:], in_=tv[:, 0:128]), reads=[Tbk], writes=[ckT[i]])
                        P.op("dve", lambda e, tv=tv, i=i: e.tensor_copy(out=krT[i].ap[0:32, :], in_=tv[0:32, 128:256]), reads=[Tbk], pwrites=[krT[i]])
                    else:
                        P.dma("pool", ckr[i].ap[0:64, :], CK_d[tok0:tok0 + 64, :], reads=[CK_b], writes=[ckr[i]])
                        P.dma("sp", ckT[i].ap[:, 0:64], CKT_d[:, tok0:tok0 + 64], reads=[CKT_b], writes=[ckT[i]])
                        P.dma("sp", krT[i].ap[0:32, 0:64], KRT_d[:, tok0:tok0 + 64], reads=[KRT_b], pwrites=[krT[i]])
                    for h in range(16):
                        sb_ = Sbk[h // 8]
                        cs = slice((h % 8) * 64, (h % 8) * 64 + 64)
                        P.op("pe", lambda e, sb_=sb_, cs=cs, h=h, nk=nk, i=i: e.matmul(out=sb_.ap[0:nk, cs], lhsT=ckT[i].ap[:, 0:nk], rhs=QAs.ap[:, h, :], start=True, stop=False), reads=[ckT[i], QAs], writes=[sb_])
                        P.op("pe", lambda e, sb_=sb_, cs=cs, h=h, nk=nk, i=i: e.matmul(out=sb_.ap[0:nk, cs], lhsT=krT[i].ap[:, 0:nk], rhs=QRs.ap[:, h, :], start=False, stop=True), reads=[krT[i], QRs], writes=[sb_])
                    pts = PTs[i]
                    for j in range(2):
                        P.op("act", lambda e, j=j, nk=nk, pts=pts: e.activation(out=pts.ap[0:nk, j * 512:(j + 1) * 512], in_=Sbk[j].ap[0:nk, :], func=AF.Exp, scale=MSC), reads=[Sbk[j]], pwrites=[pts])
                    for j in range(2):
                        js = slice(j * 512, (j + 1) * 512)
                        P.op("pe", lambda e, j=j, js=js, nk=nk, i=i, pts=pts: e.matmul(out=Obk[j].ap[:], lhsT=ckr[i].ap[0:nk, :], rhs=pts.ap[0:nk, js], start=False, stop=True, skip_group_check=True),
                             reads=[ckr[i], pts], writes=[Obk[j]])
                        P.op("pe", lambda e, j=j, js=js, nk=nk, pts=pts: e.matmul(out=Lbk[j].ap[:], lhsT=ones_b.ap[0:nk, :], rhs=pts.ap[0:nk, js], start=False, stop=True, skip_group_check=True),
                             reads=[ones_b, pts], writes=[Lbk[j]])
                for j in range(2):
                    js = slice(j * 512, (j + 1) * 512)
                    P.op("dve", lambda e, j=j, js=js: e.reciprocal(out=Rr.ap[:, js], in_=Lbk[j].ap[:]), reads=[Lbk[j]], pwrites=[Rr])
                    P.op("dve", lambda e, j=j, js=js: e.tensor_tensor(out=olb.ap[:, js], in0=Obk[j].ap[:], in1=Rr.ap[:, js], op=ALU.mult), reads=[Obk[j], Rr], pwrites=[olb])
                for g2 in range(2):
                    for u in range(8):
                        h = g2 * 8 + u
                        P.op("pe", lambda e, h=h, u=u: e.matmul(out=Ebk.ap[0:64, u * 64:(u + 1) * 64], lhsT=wuv.ap[:, h * 64:(h + 1) * 64], rhs=olb.ap[:, h * 64:(h + 1) * 64], start=True, stop=True), reads=[wuv, olb], writes=[Ebk])
                    P.op("act", lambda e, g2=g2: e.activation(out=ohs.ap[:, g2 * 8:(g2 + 1) * 8, :], in_=Ebk.ap[0:64, :].rearrange("p (u q) -> p u q", q=64), func=AF.Copy), reads=[Ebk], pwrites=[ohs])
                for h in range(16):
                    P.dma("sp", OT2_d[h // 2, (h % 2) * 64:(h % 2) * 64 + 64, tok0:tok0 + 64], ohs.ap[:, h, :], reads=[ohs], pwrites=[OT2_b])

        if en(10):
            attn_out_stage("l2c", 2, OT2_d, OT2_b, "mla_w_o", 16, 64, False, None, "b", "a", lambda t: True)
        if en(11):
            mlp_stage(2, "a", "b")

        if en(12):
          with Stage(P, "l3") as st:
            win = st.sb([128, 8, 4 * D], BF16, "win")
            for kc in range(8):
                for c0 in range(0, 4 * D, 1024):
                    P.dma("pool", win.ap[:, kc, c0:c0 + 1024], W["sgu_w_in"][kc * 128:(kc + 1) * 128, c0:c0 + 1024], pwrites=[win])
            wout = st.sb([128, 16, D], BF16, "wout")
            for kc in range(16):
                P.dma("pool", wout.ap[:, kc, :], W["sgu_w_out"][kc * 128:(kc + 1) * 128, :], pwrites=[wout])
            gvb = st.sb([128, 2 * D], F32, "gvb")
            P.dma("sp", gvb.ap[:], W["sgu_g_v"].partition_broadcast(128), writes=[gvb])
            Bs = [st.sb([128, 8, 128], F32, "Bs") for _ in range(2)]
            P.dma("sp", Bs[0].ap[:], W["sgu_b_s"].partition_broadcast(128), writes=[Bs[0]])
            for hh in range(2):
                P.dma("sp", Bs[1].ap[:, :, hh * 64:(hh + 1) * 64], W["sgu_b_s"][:, 0:64].partition_broadcast(128), pwrites=[Bs[1]])
            WgT = [st.sb([128, 8, 128], BF16, "WgT") for _ in range(2)]
            with Stage(P, "l3p") as sp:
                stg = [sp.sb([128, 128], F32, "stg") for _ in range(2)]
                k_ = 0
                for ps_i in range(2):
                    for g in range(8):
                        sg = stg[k_ % 2]
                        if ps_i == 0:
                            P.dma("sp", sg.ap[:], W["sgu_w_s"][g], writes=[sg])
                        else:
                            P.op("pool", lambda e, sg=sg: e.memset(sg.ap[:], 0.0), writes=[sg])
                            for hh in range(2):
                                P.dma("sp", sg.ap[hh * 64:(hh + 1) * 64, hh * 64:(hh + 1) * 64], W["sgu_w_s"][g, 0:64, 0:64], pwrites=[sg])
                        P.op("pool", lambda e, sg=sg: e.affine_select(out=sg.ap[:], in_=sg.ap[:], pattern=[[-1, 128]], compare_op=ALU.is_ge, fill=0.0, base=0, channel_multiplier=1),
                             reads=[sg], writes=[sg])
                        pb = PS[1 + k_ % 2]
                        P.op("pe", lambda e, sg=sg, pb=pb: e.transpose(out=pb.ap[:, 0:128], in_=sg.ap[:], identity=ident_f.ap[:]), reads=[sg, ident_f], writes=[pb])
                        P.op("dve", lambda e, pb=pb, ps_i=ps_i, g=g: e.tensor_copy(out=WgT[ps_i].ap[:, g, :], in_=pb.ap[:, 0:128]), reads=[pb], pwrites=[WgT[ps_i]])
                        k_ += 1
            gate = Gate(st, 3, 0)
            xt = [st.sb([128, D], F32, "x") for _ in range(2)]
            nctx = NormCtx(st, PS[0], PS[7])
            hTs = [st.sb([128, 8, 128], BF16, "hT") for _ in range(2)]
            uT = [st.sb([128, 16, 128], BF16, "uT") for _ in range(2)]
            vg = st.sb([128, 2 * D], F32, "vg")
            vn = st.sb([128, 2 * D], F32, "vn")
            vnb = st.sb([128, 2 * D], BF16, "vnb")
            jk2 = st.sb([128, 2 * D], BF16, "jk2")
            st5 = st.sb([128, 4], F32, "st5")
            svt = [st.sb([128, 512], F32, "svt") for _ in range(2)]
            pT = [st.sb([128, 16, 128], BF16, "pT") for _ in range(2)]
            tl3 = list(range(NT)) if upto >= 12.5 else [0, NPT]
            for t in tl3:
                is_s = (t == NPT)
                i = t % 2
                x, hT = xt[i], hTs[i]
                src, srcb = x_src(t, "b")
                P.dma("sp", x.ap[:], src, reads=[srcb], writes=[x])
                nctx.run(x, x.ap[:], hT, hT.ap[:], 3, 0, is_s)
                for q4 in range(4):
                    pu = PS[1 + q4 % 2]
                    for u in range(4):
                        oc = q4 * 4 + u
                        for kc in range(8):
                            P.op("pe", lambda e, pu=pu, u=u, oc=oc, kc=kc, hT=hT: e.matmul(out=pu.ap[:, u * 128:(u + 1) * 128], lhsT=win.ap[:, kc, oc * 128:(oc + 1) * 128], rhs=hT.ap[:, kc, :], start=(kc == 0), stop=(kc == 7)),
                                 reads=[win, hT], writes=[pu])
                    P.op("act", lambda e, pu=pu, q4=q4, i=i: e.activation(out=uT[i].ap[:, q4 * 4:(q4 + 1) * 4, :], in_=pu.ap[:].rearrange("p (u t) -> p u t", t=128), func=AF.Gelu_apprx_tanh), reads=[pu], pwrites=[uT[i]])
                for blk in range(4):
                    pv_ = PS[3 + blk % 2]
                    for kc in range(8):
                        P.op("pe", lambda e, pv_=pv_, blk=blk, kc=kc, hT=hT: e.matmul(out=pv_.ap[:], lhsT=hT.ap[:, kc, :], rhs=win.ap[:, kc, 2048 + blk * 512:2048 + (blk + 1) * 512], start=(kc == 0), stop=(kc == 7)),
                             reads=[hT, win], writes=[pv_])
                    P.op("act", lambda e, pv_=pv_, blk=blk: e.activation(out=vg.ap[:, blk * 512:(blk + 1) * 512], in_=pv_.ap[:], func=AF.Gelu_apprx_tanh), reads=[pv_], pwrites=[vg])
                P.op("act", lambda e: e.activation(out=jk2.ap[:], in_=vg.ap[:], func=AF.Square, accum_out=st5.ap[:, 0:1]), reads=[vg], writes=[jk2, st5])
                P.op("act", lambda e: e.activation(out=st5.ap[:, 1:2], in_=st5.ap[:, 0:1], func=AF.Sqrt, scale=1.0 / (2 * D), bias=EPS), reads=[st5], writes=[st5])
                P.op("dve", lambda e: e.reciprocal(out=st5.ap[:, 2:3], in_=st5.ap[:, 1:2]), reads=[st5], writes=[st5])
                P.op("dve", lambda e: e.scalar_tensor_tensor(out=vn.ap[:], in0=vg.ap[:], scalar=st5.ap[:, 2:3], in1=gvb.ap[:], op0=ALU.mult, op1=ALU.mult), reads=[vg, st5, gvb], writes=[vn])
                P.op("pool", lambda e: e.tensor_copy(out=vnb.ap[:], in_=vn.ap[:]), reads=[vn], writes=[vnb])
                if is_s:
                    P.dma("sp", O["sguv"], vn.ap[:], reads=[vn], pwrites=[out_bufs["sguv"]])
                wg, bs_ = WgT[1 if is_s else 0], Bs[1 if is_s else 0]
                p_ = pT[i]
                for q4 in range(4):
                    psv = PS[5 + q4 % 2]
                    for u in range(4):
                        fc = q4 * 4 + u
                        P.op("pe", lambda e, psv=psv, u=u, fc=fc, wg=wg: e.matmul(out=psv.ap[:, u * 128:(u + 1) * 128], lhsT=vnb.ap[:, fc * 128:(fc + 1) * 128], rhs=wg.ap[:, fc // 2, :], start=True, stop=True),
                             reads=[vnb, wg], writes=[psv])
                    sv_ = svt[q4 % 2]
                    P.op("dve", lambda e, psv=psv, sv_=sv_, q4=q4, bs_=bs_: e.tensor_tensor(out=sv_.ap[:].rearrange("p (g u t) -> p g u t", g=2, u=2), in0=psv.ap[:].rearrange("p (g u t) -> p g u t", g=2, u=2),
                                                                                  in1=bc(bs_.ap[:, q4 * 2:(q4 + 1) * 2, :].unsqueeze(2), [128, 2, 2, 128]), op=ALU.add), reads=[psv, bs_], writes=[sv_])
                    P.op("pool", lambda e, sv_=sv_, q4=q4, i=i, p_=p_: e.tensor_tensor(out=p_.ap[:, q4 * 4:(q4 + 1) * 4, :].rearrange("p u t -> p (u t)"), in0=sv_.ap[:], in1=uT[i].ap[:, q4 * 4:(q4 + 1) * 4, :].rearrange("p u t -> p (u t)"), op=ALU.mult),
                         reads=[sv_, uT[i]], pwrites=[p_])
                g = gate.get(is_s)
                for cbk in range(2):
                    po = PS[1 + cbk]
                    for kc in range(16):
                        P.op("pe", lambda e, po=po, kc=kc, cbk=cbk, p_=p_: e.matmul(out=po.ap[:], lhsT=p_.ap[:, kc, :], rhs=wout.ap[:, kc, cbk * 512:(cbk + 1) * 512], start=(kc == 0), stop=(kc == 15)),
                             reads=[p_, wout], writes=[po])
                    sl = slice(cbk * 512, (cbk + 1) * 512)
                    P.op("dve", lambda e, po=po, sl=sl, g=g: e.tensor_tensor(out=po.ap[:], in0=po.ap[:], in1=g.ap[:, sl], op=ALU.mult), reads=[po, g], writes=[po])
                    P.op("dve", lambda e, po=po, sl=sl, x=x: e.tensor_tensor(out=x.ap[:, sl], in0=po.ap[:], in1=x.ap[:, sl], op=ALU.add), reads=[po, x], pwrites=[x])
                dst, dstb = x_dst(t, "a")
                P.dma("sp", dst, x.ap[:], reads=[x], pwrites=[dstb])

        if en(13):
            mlp_stage(3, "a", None, final=True)

        P.barrier()
        P.emit()
        print("n_inst", P.n_inst, "n_dsem", P.n_dsem)
    return nc


_NC_CACHE = {}


def kernel(**inp):
    f = lambda a: np.ascontiguousarray(np.asarray(a, dtype=np.float32))
    upto = float(inp.get("_upto", 99))
    if upto not in _NC_CACHE:
        _NC_CACHE[upto] = build_program(upto)
    nc = _NC_CACHE[upto]
    wnames = ["w_ada", "b_ada", "g_mix", "g_ffn", "w_up", "w_down", "g_final", "s5_a_re", "s5_a_im", "s5_b_re", "s5_b_im",
              "s5_c_re", "s5_c_im", "s5_d", "s5_log_dt", "s5_w_glu_a", "s5_w_glu_b", "diff_w_qkv", "diff_lambda_q1",
              "diff_lambda_k1", "diff_lambda_q2", "diff_lambda_k2", "diff_g_sub", "diff_w_o", "mla_w_dq", "mla_g_q",
              "mla_w_uq", "mla_w_dkv", "mla_g_kv", "mla_w_uk", "mla_w_uv", "mla_w_o", "sgu_w_in", "sgu_g_v", "sgu_w_s",
              "sgu_b_s", "sgu_w_out"]
    wd = {k: f(inp[k]) for k in wnames}
    wd["s5_a_re"] = wd["s5_a_re"].reshape(4096)
    wd["s5_a_im"] = wd["s5_a_im"].reshape(4096)
    wd["mla_w_uk"] = wd["mla_w_uk"].reshape(128, 1024)
    wd["mla_w_uv"] = wd["mla_w_uv"].reshape(128, 1024)
    xp, xs = f(inp["x_prompt"]), f(inp["x_sample"])
    cp, cs = f(inp["c_prompt"]), f(inp["c_sample"])
    sre, sim = f(inp["state_s5_re"]), f(inp["state_s5_im"])
    cdk, cdv = f(inp["cache_diff_k"]), f(inp["cache_diff_v"])
    cck, ckr = f(inp["cache_mla_ckv"]), f(inp["cache_mla_krope"])
    in_maps = []
    for c in range(8):
        b = c % 4
        s0 = 2 * c
        m = dict(wd)
        m["xp"] = xp[b]
        m["xs"] = xs[s0:s0 + 2].reshape(128, D)
        m["c3"] = np.concatenate([cp[b:b + 1], cs[s0:s0 + 2]], axis=0)
        m["h0re"] = sre[s0:s0 + 2].reshape(2, 4096)
        m["h0im"] = sim[s0:s0 + 2].reshape(2, 4096)
        m["cdk"] = cdk[s0:s0 + 2].reshape(2, PAST, D)
        m["cdv"] = cdv[s0:s0 + 2].reshape(2, PAST, D)
        m["cck"] = cck[s0:s0 + 2]
        m["ckr"] = ckr[s0:s0 + 2]
        in_maps.append(m)
    ncores = int(inp.get("_ncores", 8))
    if ncores < 8:
        res = run_bass_kernel_spmd(nc, in_maps[:ncores], core_ids=list(range(ncores))).results
        res = [res[c % ncores] for c in range(8)]
    else:
        res = run_bass_kernel_spmd(nc, in_maps, core_ids=list(range(8))).results
    global _LAST_RES
    _LAST_RES = res
    R = lambda k, cores: [res[c][k] for c in cores]
    p4, a8 = range(4), range(8)
    y_prompt = np.stack(R("yp", p4)).reshape(4, SEQ, D)
    y_sample = np.concatenate(R("ys", a8)).reshape(16, 64, D)
    s5p_re = np.stack(R("s5p_re", p4)).reshape(4, 64, 64)
    s5p_im = np.stack(R("s5p_im", p4)).reshape(4, 64, 64)
    s5s_re = np.concatenate(R("s5s_re", a8)).reshape(16, 64, 64)
    s5s_im = np.concatenate(R("s5s_im", a8)).reshape(16, 64, 64)
    dkp = np.stack(R("dkp", p4)).reshape(4, SEQ, 8, 128)
    dvp = np.stack(R("dvp", p4)).reshape(4, SEQ, 8, 128)
    dks = np.concatenate(R("dks", a8)).reshape(16, 64, 8, 128)
    dvs = np.concatenate(R("dvs", a8)).reshape(16, 64, 8, 128)
    ckvp = np.stack(R("ckvp", p4)).reshape(4, SEQ, 128)
    krp = np.stack(R("krp", p4)).reshape(4, SEQ, 32)
    ckvs = np.concatenate(R("ckvs", a8)).reshape(16, 64, 128)
    krs = np.concatenate(R("krs", a8)).reshape(16, 64, 32)
    sguv = np.concatenate(R("sguv", a8)).reshape(16, 64, 2 * D)
    return (y_prompt, y_sample, s5p_re, s5p_im, s5s_re, s5s_im, dkp, dvp, dks, dvs, ckvp, krp, ckvs, krs, sguv)
```
